# Optimizing a Trainium2 kernel written in Bass

```python
import math
import jax, jax.numpy as jnp
from jax import lax
import numpy as np

D_MODEL = 1024
BATCH = 4
SEQ = 4096
DEPTH = 2

GRID_W = 64
CTX_LEN = 256
Q_BLOCK = 128
ROPE_THETA = 10000.0
EPS = 1e-6

A_HEADS = 8
A_KV_HEADS = 2
A_GROUP = A_HEADS // A_KV_HEADS
A_HEAD_DIM = 64
A_WIDTH = A_HEADS * A_HEAD_DIM

S5_CH = 512
S5_GROUP_CH = 16
S5_GROUPS = S5_CH // S5_GROUP_CH
S5_STATE = 64

C_HEADS = 8
C_NOPE = 64
C_ROPE = 32
C_VDIM = 64
C_Q_RANK = 768
C_KV_RANK = 256
C_QK_DIM = C_NOPE + C_ROPE
C_WIDTH = C_HEADS * C_VDIM

D_FF = 4 * D_MODEL
N_BRANCH = 3
N_MOD = 6
DEEPNORM_ALPHA = (2.0 * DEPTH) ** 0.25
DEEPNORM_BETA = (8.0 * DEPTH) ** -0.25

OFF_AK = 0
OFF_AV = OFF_AK + A_KV_HEADS * A_HEAD_DIM
OFF_CKV = OFF_AV + A_KV_HEADS * A_HEAD_DIM
OFF_CKR = OFF_CKV + C_KV_RANK
OFF_U = OFF_CKR + C_ROPE
N_STATE_COLS = OFF_U + S5_CH
OFF_AQ = N_STATE_COLS
OFF_CQ = OFF_AQ + A_WIDTH
OFF_GATE = OFF_CQ + C_Q_RANK
N_IN_COLS = OFF_GATE + N_BRANCH * D_MODEL

kernel_name = 'hybrid_gqa_s5_mla_dit_block'


def layer_norm(x, g=None, b=None):
    x32 = x.astype(jnp.float32)
    mu = jnp.mean(x32, axis=-1, keepdims=True)
    var = jnp.mean(jnp.square(x32 - mu), axis=-1, keepdims=True)
    y = (x32 - mu) * lax.rsqrt(var + EPS)
    if g is not None:
        y = y * g.astype(jnp.float32) + b.astype(jnp.float32)
    return y.astype(x.dtype)


def rms_norm(x, g):
    x32 = x.astype(jnp.float32)
    y = x32 * lax.rsqrt(jnp.mean(jnp.square(x32), axis=-1, keepdims=True) + EPS)
    return (y * g.astype(jnp.float32)).astype(x.dtype)


def modulate(x, shift, scale):
    return layer_norm(x) * (1.0 + scale) + shift


def post_norm(x, y, g, b):
    return layer_norm(DEEPNORM_ALPHA * x + y, g, b)


def axial_rope_tables(rows, dim):
    half = dim // 2
    inv = ROPE_THETA ** (-jnp.arange(0, half, 2, dtype=jnp.float32) / half)
    row = jnp.repeat(jnp.arange(rows, dtype=jnp.float32), GRID_W)
    col = jnp.tile(jnp.arange(GRID_W, dtype=jnp.float32), rows)
    ang_r = row[:, None] * inv
    ang_c = col[:, None] * inv
    ang = jnp.concatenate([ang_r, ang_r, ang_c, ang_c], axis=-1)
    return jnp.cos(ang), jnp.sin(ang)


def apply_rope(x, cos, sin):
    half = x.shape[-1] // 2
    q = half // 2

    def rot(v):
        return jnp.concatenate([-v[..., q:], v[..., :q]], axis=-1)

    xr = jnp.concatenate([rot(x[..., :half]), rot(x[..., half:])], axis=-1)
    out = x.astype(jnp.float32) * cos[:, None, :] + xr.astype(jnp.float32) * sin[:, None, :]
    return out.astype(x.dtype)


def block_attention(q, k, v, scale):
    bsz, kvh, grp, lq, dk = q.shape
    nb = lq // Q_BLOCK
    qb = jnp.moveaxis(q.reshape(bsz, kvh, grp, nb, Q_BLOCK, dk), 3, 0)

    def one_block(qblk):
        s = jnp.einsum('bhgqd,bhkd->bhgqk', qblk, k, preferred_element_type=jnp.float32) * scale
        p = jax.nn.softmax(s, axis=-1).astype(v.dtype)
        return jnp.einsum('bhgqk,bhkd->bhgqd', p, v)

    o = lax.map(one_block, qb)
    return jnp.moveaxis(o, 0, 3).reshape(bsz, kvh, grp, lq, v.shape[-1])


def gqa_kv(proj, k_gain, rope):
    bsz, L = proj.shape[:2]
    k = rms_norm(proj[..., OFF_AK:OFF_AV].reshape(bsz, L, A_KV_HEADS, A_HEAD_DIM), k_gain)
    v = proj[..., OFF_AV:OFF_CKV].reshape(bsz, L, A_KV_HEADS, A_HEAD_DIM)
    if rope is not None:
        k = apply_rope(k, *rope)
    return k.transpose(0, 2, 1, 3), v.transpose(0, 2, 1, 3)


def mla_kv(proj, kv_a_gain, w_kvb, rope):
    bsz, L = proj.shape[:2]
    c_kv = rms_norm(proj[..., OFF_CKV:OFF_CKR], kv_a_gain)
    kv = (c_kv @ w_kvb).reshape(bsz, L, C_HEADS, C_NOPE + C_VDIM)
    k_nope, v = kv[..., :C_NOPE], kv[..., C_NOPE:]
    k_rope = proj[..., OFF_CKR:OFF_U][:, :, None, :]
    if rope is not None:
        k_rope = apply_rope(k_rope, *rope)
    k = jnp.concatenate([k_nope, jnp.broadcast_to(k_rope, (bsz, L, C_HEADS, C_ROPE))], axis=-1)
    return k.transpose(0, 2, 1, 3), v.transpose(0, 2, 1, 3)


def attend_queries(proj, ka, va, kc, vc, lp, rope_a, rope_c):
    bsz, L = proj.shape[:2]
    qa = rms_norm(proj[..., OFF_AQ:OFF_CQ].reshape(bsz, L, A_HEADS, A_HEAD_DIM), lp['a_q_gain'])
    if rope_a is not None:
        qa = apply_rope(qa, *rope_a)
    qa = qa.reshape(bsz, L, A_KV_HEADS, A_GROUP, A_HEAD_DIM).transpose(0, 2, 3, 1, 4)
    oa = block_attention(qa, ka, va, A_HEAD_DIM ** -0.5)
    ya = oa.transpose(0, 3, 1, 2, 4).reshape(bsz, L, A_WIDTH)
    cq = rms_norm(proj[..., OFF_CQ:OFF_GATE], lp['c_q_a_gain'])
    qc = (cq @ lp['c_w_qb']).reshape(bsz, L, C_HEADS, C_QK_DIM)
    q_nope, q_rope = qc[..., :C_NOPE], qc[..., C_NOPE:]
    if rope_c is not None:
        q_rope = apply_rope(q_rope, *rope_c)
    qc = jnp.concatenate([q_nope, q_rope], axis=-1).transpose(0, 2, 1, 3)[:, :, None]
    oc = block_attention(qc, kc, vc, C_QK_DIM ** -0.5)
    yc = oc[:, :, 0].transpose(0, 2, 1, 3).reshape(bsz, L, C_WIDTH)
    return ya, yc


def s5_discretize(a_re, a_im, log_dt, b_re, b_im):
    f32 = jnp.float32
    a_re, a_im = a_re.astype(f32), a_im.astype(f32)
    dt = jnp.exp(log_dt.astype(f32))[:, None]
    mag = jnp.exp(a_re * dt)
    abar_r = mag * jnp.cos(a_im * dt)
    abar_i = mag * jnp.sin(a_im * dt)
    den = a_re * a_re + a_im * a_im
    nr = abar_r - 1.0
    coef_r = (nr * a_re + abar_i * a_im) / den
    coef_i = (abar_i * a_re - nr * a_im) / den
    b_re, b_im = b_re.astype(f32), b_im.astype(f32)
    bbar_r = coef_r[..., None] * b_re - coef_i[..., None] * b_im
    bbar_i = coef_r[..., None] * b_im + coef_i[..., None] * b_re
    return abar_r, abar_i, bbar_r, bbar_i


def _complex_affine_combine(earlier, later):
    a1r, a1i, b1r, b1i = earlier
    a2r, a2i, b2r, b2i = later
    return (a1r * a2r - a1i * a2i,
            a1r * a2i + a1i * a2r,
            a2r * b1r - a2i * b1i + b2r,
            a2r * b1i + a2i * b1r + b2i)


def s5_states(u, disc, x0, reverse):
    abar_r, abar_i, bbar_r, bbar_i = disc
    u32 = u.astype(jnp.float32)
    bu_r = jnp.einsum('blgc,gpc->blgp', u32, bbar_r)
    bu_i = jnp.einsum('blgc,gpc->blgp', u32, bbar_i)
    L = u.shape[1]
    ar = jnp.broadcast_to(abar_r, (1, L) + abar_r.shape)
    ai = jnp.broadcast_to(abar_i, (1, L) + abar_i.shape)
    cum_r, cum_i, s_r, s_i = lax.associative_scan(
        _complex_affine_combine, (ar, ai, bu_r, bu_i), axis=1, reverse=reverse)
    if x0 is not None:
        x0r, x0i = x0[0][:, None], x0[1][:, None]
        s_r = s_r + cum_r * x0r - cum_i * x0i
        s_i = s_i + cum_r * x0i + cum_i * x0r
    return s_r, s_i


def s5_readout(s_r, s_i, c_re, c_im):
    return (jnp.einsum('gcp,blgp->blgc', c_re.astype(jnp.float32), s_r)
            - jnp.einsum('gcp,blgp->blgc', c_im.astype(jnp.float32), s_i))


def s5_input(proj):
    return proj[..., OFF_U:N_STATE_COLS].reshape(proj.shape[0], proj.shape[1], S5_GROUPS, S5_GROUP_CH)


def s5_glu(ys, u, d, w_glu):
    y = ys + d.astype(jnp.float32).reshape(S5_GROUPS, S5_GROUP_CH) * u.astype(jnp.float32)
    y = y.reshape(u.shape[0], u.shape[1], S5_CH).astype(u.dtype)
    h = jax.nn.gelu(y) @ w_glu
    a, g = jnp.split(h, 2, axis=-1)
    return a * jax.nn.sigmoid(g)


def merge_branches(proj, ya, ys, yc, lp):
    bsz, L = proj.shape[:2]
    g = jax.nn.sigmoid(proj[..., OFF_GATE:]).reshape(bsz, L, N_BRANCH, D_MODEL)
    merged = (g[..., 0, :] * (ya @ lp['w_branch_a'])
              + g[..., 1, :] * (ys @ lp['w_branch_s5'])
              + g[..., 2, :] * (yc @ lp['w_branch_c']))
    return merged @ lp['w_out']


def squared_relu_mlp(h, w_up, w_down):
    return jnp.square(jax.nn.relu(h @ w_up)) @ w_down


def setup_inputs(seed: int = 0) -> dict:
    key = jax.random.key(seed)
    ks = iter(jax.random.split(key, 64))

    def nrm(shape, scale):
        return scale * jax.random.normal(next(ks), shape, jnp.float32)

    G, P, CH = S5_GROUPS, S5_STATE, S5_GROUP_CH
    return {
        'x': nrm((BATCH, SEQ, D_MODEL), 1.0),
        'c': nrm((BATCH, D_MODEL), 1.0),
        'ctx': nrm((BATCH, CTX_LEN, D_MODEL), 1.0),
        'c_ctx': nrm((D_MODEL,), 1.0),
        'w_mod': nrm((DEPTH, D_MODEL, N_MOD * D_MODEL), 0.5 * D_MODEL ** -0.5),
        'b_mod': nrm((DEPTH, N_MOD * D_MODEL), 0.01),
        'w_in': nrm((DEPTH, D_MODEL, N_IN_COLS), D_MODEL ** -0.5),
        'a_q_gain': 1.0 + nrm((DEPTH, A_HEAD_DIM), 0.02),
        'a_k_gain': 1.0 + nrm((DEPTH, A_HEAD_DIM), 0.02),
        'c_q_a_gain': 1.0 + nrm((DEPTH, C_Q_RANK), 0.02),
        'c_kv_a_gain': 1.0 + nrm((DEPTH, C_KV_RANK), 0.02),
        'c_w_qb': nrm((DEPTH, C_Q_RANK, C_HEADS * C_QK_DIM), C_Q_RANK ** -0.5),
        'c_w_kvb': nrm((DEPTH, C_KV_RANK, C_HEADS * (C_NOPE + C_VDIM)), C_KV_RANK ** -0.5),
        's5_a_re': -0.5 + nrm((DEPTH, 2, G, P), 0.01),
        's5_a_im': jnp.pi * jnp.arange(P, dtype=jnp.float32) + nrm((DEPTH, 2, G, P), 0.01),
        's5_log_dt': jax.random.uniform(next(ks), (DEPTH, 2, G), jnp.float32, math.log(1e-3), math.log(1e-1)),
        's5_b_re': nrm((DEPTH, 2, G, P, CH), (2.0 * CH) ** -0.5),
        's5_b_im': nrm((DEPTH, 2, G, P, CH), (2.0 * CH) ** -0.5),
        's5_c_re': nrm((DEPTH, 2, G, CH, P), P ** -0.5),
        's5_c_im': nrm((DEPTH, 2, G, CH, P), P ** -0.5),
        's5_d': nrm((DEPTH, S5_CH), 1.0),
        's5_w_glu': nrm((DEPTH, S5_CH, 2 * S5_CH), S5_CH ** -0.5),
        'w_branch_a': nrm((DEPTH, A_WIDTH, D_MODEL), A_WIDTH ** -0.5),
        'w_branch_s5': nrm((DEPTH, S5_CH, D_MODEL), S5_CH ** -0.5),
        'w_branch_c': nrm((DEPTH, C_WIDTH, D_MODEL), C_WIDTH ** -0.5),
        'w_out': nrm((DEPTH, D_MODEL, D_MODEL), DEEPNORM_BETA * D_MODEL ** -0.5),
        'ln1_g': 1.0 + nrm((DEPTH, D_MODEL), 0.02),
        'ln1_b': nrm((DEPTH, D_MODEL), 0.02),
        'w_up': nrm((DEPTH, D_MODEL, D_FF), D_MODEL ** -0.5),
        'w_down': nrm((DEPTH, D_FF, D_MODEL), DEEPNORM_BETA * D_FF ** -0.5),
        'ln2_g': 1.0 + nrm((DEPTH, D_MODEL), 0.02),
        'ln2_b': nrm((DEPTH, D_MODEL), 0.02),
    }


def reference(x, c, ctx, c_ctx, w_mod, b_mod, w_in, a_q_gain, a_k_gain, c_q_a_gain, c_kv_a_gain,
              c_w_qb, c_w_kvb, s5_a_re, s5_a_im, s5_log_dt, s5_b_re, s5_b_im, s5_c_re, s5_c_im,
              s5_d, s5_w_glu, w_branch_a, w_branch_s5, w_branch_c, w_out, ln1_g, ln1_b,
              w_up, w_down, ln2_g, ln2_b):
    rows = x.shape[1] // GRID_W
    rope_a = axial_rope_tables(rows, A_HEAD_DIM)
    rope_c = axial_rope_tables(rows, C_ROPE)
    for l in range(DEPTH):
        last = l == DEPTH - 1
        lp = {'a_q_gain': a_q_gain[l], 'a_k_gain': a_k_gain[l],
              'c_q_a_gain': c_q_a_gain[l], 'c_kv_a_gain': c_kv_a_gain[l],
              'c_w_qb': c_w_qb[l], 'c_w_kvb': c_w_kvb[l],
              'w_branch_a': w_branch_a[l], 'w_branch_s5': w_branch_s5[l],
              'w_branch_c': w_branch_c[l], 'w_out': w_out[l]}
        disc = [s5_discretize(s5_a_re[l, dr], s5_a_im[l, dr], s5_log_dt[l, dr],
                              s5_b_re[l, dr], s5_b_im[l, dr]) for dr in range(2)]

        mod = jax.nn.silu(c) @ w_mod[l] + b_mod[l]
        sh1, sc1, g1, sh2, sc2, g2 = jnp.split(mod[:, None, :], N_MOD, axis=-1)
        n_ctx_mod = 2 if last else N_MOD
        mod_c = jax.nn.silu(c_ctx) @ w_mod[l][:, :n_ctx_mod * D_MODEL] + b_mod[l][:n_ctx_mod * D_MODEL]
        mods_c = jnp.split(mod_c, n_ctx_mod)

        h = modulate(x, sh1, sc1)
        hc = modulate(ctx, mods_c[0], mods_c[1])
        proj = h @ w_in[l]
        proj_c = hc @ (w_in[l][:, :N_STATE_COLS] if last else w_in[l])

        ka_c, va_c = gqa_kv(proj_c, lp['a_k_gain'], None)
        kc_c, vc_c = mla_kv(proj_c, lp['c_kv_a_gain'], lp['c_w_kvb'], None)
        u_c = s5_input(proj_c)
        st_f = s5_states(u_c, disc[0], None, False)
        st_b = s5_states(u_c, disc[1], None, True)

        ka, va = gqa_kv(proj, lp['a_k_gain'], rope_a)
        kc, vc = mla_kv(proj, lp['c_kv_a_gain'], lp['c_w_kvb'], rope_c)
        ya, yc = attend_queries(proj,
                                jnp.concatenate([ka_c, ka], axis=2), jnp.concatenate([va_c, va], axis=2),
                                jnp.concatenate([kc_c, kc], axis=2), jnp.concatenate([vc_c, vc], axis=2),
                                lp, rope_a, rope_c)
        u = s5_input(proj)
        ys = (s5_readout(*s5_states(u, disc[0], (st_f[0][:, -1], st_f[1][:, -1]), False), s5_c_re[l, 0], s5_c_im[l, 0])
              + s5_readout(*s5_states(u, disc[1], (st_b[0][:, 0], st_b[1][:, 0]), True), s5_c_re[l, 1], s5_c_im[l, 1]))
        ys = s5_glu(ys, u, s5_d[l], s5_w_glu[l])
        mix = merge_branches(proj, ya, ys, yc, lp)
        x_mid = post_norm(x, g1 * mix, ln1_g[l], ln1_b[l])
        ff = squared_relu_mlp(modulate(x_mid, sh2, sc2), w_up[l], w_down[l])
        x_next = post_norm(x_mid, g2 * ff, ln2_g[l], ln2_b[l])

        if not last:
            ya_c, yc_c = attend_queries(proj_c, ka_c, va_c, kc_c, vc_c, lp, None, None)
            ys_c = (s5_readout(*st_f, s5_c_re[l, 0], s5_c_im[l, 0])
                    + s5_readout(*st_b, s5_c_re[l, 1], s5_c_im[l, 1]))
            ys_c = s5_glu(ys_c, u_c, s5_d[l], s5_w_glu[l])
            mix_c = merge_branches(proj_c, ya_c, ys_c, yc_c, lp)
            ctx_mid = post_norm(ctx, mods_c[2] * mix_c, ln1_g[l], ln1_b[l])
            ff_c = squared_relu_mlp(modulate(ctx_mid, mods_c[3], mods_c[4]), w_up[l], w_down[l])
            ctx = post_norm(ctx_mid, mods_c[5] * ff_c, ln2_g[l], ln2_b[l])
        x = x_next
    return x
```

```python
import numpy as np
import concourse.bass as bass
import concourse.mybir as mybir
from concourse.bass_utils import run_bass_kernel_spmd
from contextlib import ExitStack, contextmanager

F32 = mybir.dt.float32
BF16 = mybir.dt.bfloat16
I32 = mybir.dt.int32
AF = mybir.ActivationFunctionType
ALU = mybir.AluOpType
AX = mybir.AxisListType

ENGS = ["pe", "act", "dve", "pool", "sp"]
DMA_WIN = 8
SAME_ENG_SYNC = True


class T:
    __slots__ = ("ap", "keys")

    def __init__(self, ap, keys):
        self.ap = ap
        self.keys = tuple(keys)

    def __getitem__(self, sl):
        return T(self.ap[sl], self.keys)

    def v(self, ap):
        return T(ap, self.keys)

    def k(self, *sub):
        return T(self.ap, [(self.keys[0],) + tuple(sub)])


class Prog:
    def __init__(self, nc, stack):
        self.nc = nc
        self.stack = stack
        self.cur = stack
        self.streams = {e: [] for e in ENGS}
        self.last_writer = {}
        self.readers = {}
        self.ndma = {e: 0 for e in ENGS}
        self.sigcount = {e: 0 for e in ENGS}
        self.waited = {e: {} for e in ENGS}
        self.nt = 0
        self.psum_banks = []
        self.psum_i = 0
        self.sem = {e: stack.enter_context(nc.semaphore(f"s_{e}")) for e in ENGS}
        self.dsem = {e: [stack.enter_context(nc.semaphore(f"d_{e}{i}")) for i in range(DMA_WIN)]
                     for e in ("sp", "pool", "act")}
        self.dbg = {}
        self.nops = {e: 0 for e in ENGS}

    def sb(self, shape, dt, name=None):
        self.nt += 1
        name = name or "t"
        nm = f"{name}_{self.nt}"
        t = self.cur.enter_context(self.nc.sbuf_tensor(nm, list(shape), dt))
        return T(t[:], [nm])

    def dram(self, shape, dt, name):
        self.nt += 1
        nm = f"{name}_{self.nt}"
        t = self.nc.dram_tensor(nm, list(shape), dt, kind="Internal")
        return T(t.ap(), [nm])

    def init_psum(self, n=8):
        for i in range(n):
            t = self.stack.enter_context(self.nc.psum_tensor(f"bank{i}", [128, 512], F32))
            self.psum_banks.append(T(t[:], [f"bank{i}"]))

    def ps(self):
        b = self.psum_banks[self.psum_i % len(self.psum_banks)]
        self.psum_i += 1
        return b

    @contextmanager
    def scope(self):
        prev = self.cur
        with ExitStack() as st:
            self.cur = st
            yield
            self.flush()
        self.cur = prev

    def add(self, eng, fn, reads=(), writes=(), dma=False):
        deps = set()
        rk = [k for t in reads for k in t.keys]
        wk = [k for t in writes for k in t.keys]
        for k in rk:
            if k in self.last_writer:
                deps.add(self.last_writer[k])
        for k in wk:
            if k in self.last_writer:
                deps.add(self.last_writer[k])
            for r in self.readers.get(k, ()):
                deps.add(r)
        idx = len(self.streams[eng])
        me = (eng, idx)
        deps.discard(me)
        op = dict(fn=fn, deps=deps, dma=dma, signal=False, dman=None)
        if dma:
            op["dman"] = self.ndma[eng]
            self.ndma[eng] += 1
        self.streams[eng].append(op)
        for k in rk:
            self.readers.setdefault(k, []).append(me)
        for k in wk:
            self.last_writer[k] = me
            self.readers[k] = []
        return me

    def dma(self, out, in_, eng="sp", **kw):
        o = out.ap
        i = in_.ap
        return self.add(eng, lambda e: e.dma_start(out=o, in_=i, **kw), [in_], [out], dma=True)

    def mm(self, out, lhsT, rhs, start=True, stop=True, **kw):
        return self.add("pe", lambda e: e.matmul(out.ap, lhsT.ap, rhs.ap, start=start, stop=stop, **kw),
                        [lhsT, rhs], [out])

    def transpose(self, out, in_, ident):
        return self.add("pe", lambda e: e.transpose(out.ap, in_.ap, ident.ap), [in_, ident], [out])

    def act(self, out, in_, func, bias=None, scale=None, eng="act", accum_out=None):
        reads = [in_]
        kw = {}
        if bias is not None:
            if isinstance(bias, T):
                reads.append(bias); kw["bias"] = bias.ap
            else:
                kw["bias"] = bias
        if scale is not None:
            if isinstance(scale, T):
                reads.append(scale); kw["scale"] = scale.ap
            else:
                kw["scale"] = scale
        writes = [out]
        if accum_out is not None:
            kw["accum_out"] = accum_out.ap; writes.append(accum_out)
        return self.add(eng, lambda e: e.activation(out.ap, in_.ap, func, **kw), reads, writes)

    def tt(self, out, a, b, op, eng="dve"):
        return self.add(eng, lambda e: e.tensor_tensor(out.ap, a.ap, b.ap, op), [a, b], [out])

    def ts(self, out, a, s1, s2=None, op0=ALU.mult, op1=None, eng="dve"):
        reads = [a]
        v1 = s1.ap if isinstance(s1, T) else s1
        if isinstance(s1, T): reads.append(s1)
        v2 = s2.ap if isinstance(s2, T) else s2
        if isinstance(s2, T): reads.append(s2)
        if op1 is None:
            return self.add(eng, lambda e: e.tensor_scalar(out.ap, a.ap, v1, None, op0), reads, [out])
        return self.add(eng, lambda e: e.tensor_scalar(out.ap, a.ap, v1, v2, op0, op1), reads, [out])

    def stt(self, out, a, s, b, op0, op1, eng="dve"):
        reads = [a, b]
        v = s.ap if isinstance(s, T) else s
        if isinstance(s, T): reads.append(s)
        return self.add(eng, lambda e: e.scalar_tensor_tensor(out.ap, a.ap, v, b.ap, op0, op1), reads, [out])

    def copy(self, out, in_, eng="dve"):
        if eng == "act":
            return self.add("act", lambda e: e.copy(out.ap, in_.ap), [in_], [out])
        return self.add(eng, lambda e: e.tensor_copy(out.ap, in_.ap), [in_], [out])

    def memset(self, out, val, eng="pool"):
        return self.add(eng, lambda e: e.memset(out.ap, val), [], [out])

    def recip(self, out, in_, eng="dve"):
        return self.add(eng, lambda e: e.reciprocal(out.ap, in_.ap), [in_], [out])

    def bn_stats(self, out, in_):
        return self.add("dve", lambda e: e.bn_stats(out.ap, in_.ap), [in_], [out])

    def bn_aggr(self, out, in_):
        return self.add("dve", lambda e: e.bn_aggr(out.ap, in_.ap), [in_], [out])

    def debug_out(self, name, t, shape, dt=F32):
        d = self.nc.dram_tensor(name, list(shape), dt, kind="ExternalOutput").ap()
        self.dbg[name] = d
        return self.dma(T(d, [name]), t)

    def flush(self):
        nc = self.nc
        streams = self.streams
        lasts = []
        for e in ENGS:
            for j in range(len(streams[e]) - 1, -1, -1):
                op = streams[e][j]
                if op["fn"] is not None and not op["dma"]:
                    lasts.append((e, j))
                    break
        dmas = [(e, j) for e in ENGS for j, op in enumerate(streams[e]) if op["dma"]]
        for e in ENGS:
            deps = set(l for l in lasts if l[0] != e) | set(dmas)
            streams[e].append(dict(fn=None, deps=deps, dma=False, signal=False, dman=None))
        for e in ENGS:
            for op in streams[e]:
                for (f, j) in op["deps"]:
                    d = streams[f][j]
                    if not d["dma"]:
                        if f == e and not SAME_ENG_SYNC:
                            continue
                        d["signal"] = True
        for e in ENGS:
            for op in streams[e]:
                if op["signal"]:
                    self.sigcount[e] += 1
                    op["sigval"] = self.sigcount[e]
        sem, dsem = self.sem, self.dsem

        def run(ename):
            def body(eng):
                waited = self.waited[ename]

                def wait(s, v, key):
                    if waited.get(key, 0) >= v:
                        return
                    waited[key] = v
                    eng.wait_ge(s, v)

                for op in streams[ename]:
                    for (f, j) in sorted(op["deps"]):
                        d = streams[f][j]
                        if d["dma"]:
                            n = d["dman"]
                            wait(dsem[f][n % DMA_WIN], 16 * (n // DMA_WIN + 1), (f, n % DMA_WIN))
                        else:
                            if f == ename and not SAME_ENG_SYNC:
                                continue
                            wait(sem[f], d["sigval"], f)
                    if op["dma"]:
                        n = op["dman"]
                        if n >= DMA_WIN:
                            wait(dsem[ename][n % DMA_WIN], 16 * (n // DMA_WIN), (ename, n % DMA_WIN))
                    if op["fn"] is None:
                        continue
                    ins = op["fn"](eng)
                    if op["dma"]:
                        ins.then_inc(dsem[ename][op["dman"] % DMA_WIN], 16)
                    elif op["signal"]:
                        ins.then_inc(sem[ename], 1)
            return body

        with nc.Block() as block:
            block.tensor(run("pe"))
            block.scalar(run("act"))
            block.vector(run("dve"))
            block.gpsimd(run("pool"))
            block.sync(run("sp"))
        for e in ENGS:
            self.nops[e] += len(streams[e])
        self.streams = {e: [] for e in ENGS}
        self.last_writer = {}
        self.readers = {}


class Rot:
    def __init__(self, P, shape, dt, name, n):
        self.tiles = [P.sb(shape, dt, f"{name}{i}") for i in range(n)]
        self.i = 0

    def __call__(self):
        t = self.tiles[self.i % len(self.tiles)]
        self.i += 1
        return t


def _ps6(self):
    b = self.psum_banks[self.psum_i % 6]
    self.psum_i += 1
    return b


def _psacc(self):
    self.acc_i = getattr(self, "acc_i", 0) + 1
    return self.psum_banks[6 + self.acc_i % 2]


Prog.ps = _ps6
Prog.ps_acc = _psacc


D = 1024
L = 4096
LH = 2048
CTX = 256
NK = CTX + L
NKT = NK // 128
EPS = 1e-6
OFF_AK, OFF_AV, OFF_CKV, OFF_CKR, OFF_U, NST = 0, 128, 256, 512, 544, 1056
OFF_AQ, OFF_CQ, OFF_GATE, NIN = 1056, 1568, 2336, 5408
ALPHA = (2.0 * 2) ** 0.25


def dT(ap, name):
    return T(ap, [name])


class Ctx:
    pass


class RowSrc:
    def __init__(self, fn):
        self.fn = fn

    def rows(self, r0):
        return self.fn(r0)


def flat_src(t):
    return RowSrc(lambda r0: t.v(t.ap[r0:r0 + 128, :]))


WEIGHT_SHAPES = {
    "w_mod": [D, 6 * D], "b_mod": [6 * D], "w_in": [D, NIN], "a_q_gain": [64], "a_k_gain": [64],
    "c_q_a_gain": [768], "c_kv_a_gain": [256], "c_w_qb": [768, 768], "c_w_kvb": [256, 1024],
    "s5_a_re": [2, 32, 64], "s5_a_im": [2, 32, 64], "s5_log_dt": [2, 32],
    "s5_b_re": [2, 32, 64, 16], "s5_b_im": [2, 32, 64, 16], "s5_c_re": [2, 32, 16, 64], "s5_c_im": [2, 32, 16, 64],
    "s5_d": [512], "s5_w_glu": [512, 1024], "w_branch_a": [512, D], "w_branch_s5": [512, D], "w_branch_c": [512, D],
    "w_out": [D, D], "ln1_g": [D], "ln1_b": [D], "w_up": [D, 4 * D], "w_down": [4 * D, D], "ln2_g": [D], "ln2_b": [D],
}
CONST_SHAPES = {
    "x_all": [L, D], "x_own": [LH, D], "ctx": [CTX, D], "cvec": [2, D],
    "ident": [128, 128], "blk64": [128, 128], "perm64": [128, 128], "perm32": [32, 32],
    "ropek_cos": [128, L], "ropek_sin": [128, L], "ropeq_cos": [128, LH], "ropeq_sin": [128, LH],
    "rope32k_cos": [32, L], "rope32k_sin": [32, L], "rope32q_cos": [32, LH], "rope32q_sin": [32, LH],
    "sel": [128, 2], "swap": [128, 128], "mask_f": [128, 128], "mask_b": [128, 128],
    "sel8": [128, 64, 128], "sel8T": [128, 64, 128],
}


def declare_consts(nc):
    return {k: dT(nc.dram_tensor(k, list(v), F32, kind="ExternalInput").ap(), k) for k, v in CONST_SHAPES.items()}


def declare_weights(nc, l):
    return {k: dT(nc.dram_tensor(f"{k}_{l}", list(v), F32, kind="ExternalInput").ap(), f"{k}_{l}")
            for k, v in WEIGHT_SHAPES.items()}


def declare_inputs(nc, last):
    I = declare_consts(nc)
    I.update(declare_weights(nc, 1 if last else 0))
    return I


def host_constants():
    import math
    C = {}
    C["ident"] = np.eye(128, dtype=np.float32)
    blk = np.zeros((128, 128), np.float32); blk[:64, :64] = 1 / 64; blk[64:, 64:] = 1 / 64
    C["blk64"] = blk

    def perm_and_sign(dim):
        half = dim // 2; q = half // 2
        Pm = np.zeros((dim, dim), np.float32)
        sg = np.zeros(dim, np.float32)
        for m in range(dim):
            if (m % half) < q:
                Pm[m + q, m] = 1; sg[m] = -1
            else:
                Pm[m - q, m] = 1; sg[m] = 1
        return Pm, sg
    P64, s64 = perm_and_sign(64)
    p128 = np.zeros((128, 128), np.float32); p128[:64, :64] = P64; p128[64:, 64:] = P64
    C["perm64"] = p128
    P32, s32 = perm_and_sign(32)
    C["perm32"] = P32

    def tables(dim):
        half = dim // 2
        inv = 10000.0 ** (-np.arange(0, half, 2, dtype=np.float32) / half)
        rows = L // 64
        row = np.repeat(np.arange(rows, dtype=np.float32), 64)
        col = np.tile(np.arange(64, dtype=np.float32), rows)
        ang_r = row[:, None] * inv; ang_c = col[:, None] * inv
        ang = np.concatenate([ang_r, ang_r, ang_c, ang_c], axis=-1).astype(np.float32)
        return np.cos(ang).T.astype(np.float32), np.sin(ang).T.astype(np.float32)
    c64, s64t = tables(64)
    s64t = s64t * s64[:, None]
    C["ropek_cos"] = np.concatenate([c64, c64], 0); C["ropek_sin"] = np.concatenate([s64t, s64t], 0)
    c32, s32t = tables(32)
    s32t = s32t * s32[:, None]
    C["rope32k_cos"] = c32; C["rope32k_sin"] = s32t
    sw = np.zeros((128, 128), np.float32)
    for p in range(64):
        sw[p, 64 + p] = 1; sw[64 + p, p] = 1
    C["swap"] = sw
    ii = np.arange(128) // 16
    C["mask_f"] = (ii[:, None] <= ii[None, :]).astype(np.float32)
    C["mask_b"] = (ii[:, None] >= ii[None, :]).astype(np.float32)
    sel = np.zeros((128, 64, 128), np.float32)
    for gl in range(8):
        for i in range(8):
            for c in range(16):
                sel[gl * 16 + c, gl * 8 + i, i * 16 + c] = 1
    C["sel8"] = sel
    C["sel8T"] = np.ascontiguousarray(sel.transpose(2, 1, 0))
    return C


def per_core_inputs(inputs, core, C, layers=(0, 1)):
    b, hh = core // 2, core % 2
    m = {}
    xb = inputs["x"][b]
    m["x_all"] = xb; m["x_own"] = xb[hh * LH:(hh + 1) * LH]; m["ctx"] = inputs["ctx"][b]
    m["cvec"] = np.stack([inputs["c"][b], inputs["c_ctx"]], 0)
    for l in layers:
        for k in WEIGHT_SHAPES:
            m[f"{k}_{l}"] = inputs[k][l]
    for k in ["ident", "blk64", "perm64", "perm32", "ropek_cos", "ropek_sin", "rope32k_cos", "rope32k_sin",
              "swap", "mask_f", "mask_b", "sel8", "sel8T"]:
        m[k] = C[k]
    sl = slice(hh * LH, (hh + 1) * LH)
    m["ropeq_cos"] = C["ropek_cos"][:, sl]; m["ropeq_sin"] = C["ropek_sin"][:, sl]
    m["rope32q_cos"] = C["rope32k_cos"][:, sl]; m["rope32q_sin"] = C["rope32k_sin"][:, sl]
    s = np.zeros((128, 2), np.float32); s[:, hh] = 1
    m["sel"] = s
    return {k: np.ascontiguousarray(v, dtype=np.float32) for k, v in m.items()}


def rstd_from_ms(P, out, ms, n, eps=EPS, eng_a="act"):
    P.ts(out, ms, eps, None, op0=ALU.add)
    P.act(out, out, AF.Sqrt)
    P.recip(out, out)


def stage_prep(P, I, G):
    G.ident = P.sb([128, 128], BF16, "ident"); P.dma(G.ident, I["ident"], eng="pool")
    G.blk64 = P.sb([128, 128], BF16, "blk64"); P.dma(G.blk64, I["blk64"], eng="pool")
    G.perm64 = P.sb([128, 128], BF16, "perm64"); P.dma(G.perm64, I["perm64"], eng="pool")
    G.perm32 = P.sb([32, 32], BF16, "perm32"); P.dma(G.perm32, I["perm32"], eng="pool")
    G.ones = P.sb([128, 128], BF16, "ones"); P.memset(G.ones, 1.0)
    G.modT = P.sb([128, 48, 2], F32, "modT")
    G.sel = P.sb([128, 2], F32, "sel"); P.dma(G.sel, I["sel"])
    with P.scope():
        cT = P.sb([128, 8, 2], F32, "cT")
        for t in range(2):
            src = I["cvec"].ap[t, :].rearrange("(k p) -> p k", p=128)
            P.dma(cT[:, :, t], dT(src, "cvec"), allow_slow_non_contiguous=True)
        sT = P.sb([128, 8, 2], BF16, "sT")
        P.act(sT, cT, AF.Silu)
        bT = P.sb([128, 48], F32, "bT")
        P.dma(bT, dT(I["b_mod"].ap.rearrange("(j p) -> p j", p=128), "b_mod"), allow_slow_non_contiguous=True)
        wbufs = [P.sb([128, 8, 128], BF16, f"wm{i}") for i in range(3)]
        for j in range(48):
            wb = wbufs[j % 3]
            src = I["w_mod"].ap[:, j * 128:(j + 1) * 128].rearrange("(k p) c -> p k c", p=128)
            P.dma(wb, dT(src, "w_mod"), eng="pool")
            ps = P.ps()
            for kc in range(8):
                P.mm(ps[:, 0:2], wb[:, kc, :], sT[:, kc, :], start=(kc == 0), stop=(kc == 7))
            P.ts(G.modT[:, j, :], ps[:, 0:2], bT[:, j:j + 1], None, op0=ALU.add)
        for w in (1, 4):
            P.ts(G.modT[:, w * 8:(w + 1) * 8, :], G.modT[:, w * 8:(w + 1) * 8, :], 1.0, None, op0=ALU.add)
        G.modD = P.dram([2, 6 * D], F32, "modD")
        for t in range(2):
            dst = G.modD.ap[t, :].rearrange("(j p) -> p j", p=128)
            P.dma(G.modD.v(dst), G.modT[:, :, t], allow_slow_non_contiguous=True)


def ln_tile_to_hT(P, G, xt, hT_dst, t_idx, which_sh, which_sc):
    st = P.sb([128, 2, 6], F32, "bnst")
    mv = P.sb([128, 2], F32, "mv")
    for hh in range(2):
        P.bn_stats(st[:, hh, :], xt[:, hh * 512:(hh + 1) * 512])
    P.bn_aggr(mv, st)
    rs = P.sb([128, 1], F32, "rs")
    rstd_from_ms(P, rs, mv[:, 1:2], 1)
    xn = P.sb([128, D], BF16, "xn")
    P.ts(xn, xt, mv[:, 0:1], rs, op0=ALU.subtract, op1=ALU.mult)
    ps = P.ps()
    psb = ps.v(ps.ap.bitcast(BF16))
    for kc in range(8):
        P.transpose(psb[:, kc * 128:(kc + 1) * 128], xn[:, kc * 128:(kc + 1) * 128], G.ident)
    for kc in range(8):
        P.act(hT_dst[:, kc, :], psb[:, kc * 128:(kc + 1) * 128], AF.Identity,
              bias=G.modT[:, which_sh * 8 + kc, t_idx:t_idx + 1], scale=G.modT[:, which_sc * 8 + kc, t_idx:t_idx + 1])


_CAST = {"i": 0}


def wload(P, stg, dst, src, engs=("pool", "dve", "act")):
    shape = list(dst.ap.shape)[1:]
    n = 1
    for v in shape:
        n *= v
    st = stg()
    sv = st.ap[0:dst.ap.shape[0], 0:n]
    if len(shape) == 2:
        sv = sv.rearrange("p (a b) -> p a b", a=shape[0])
    svt = st.v(sv)
    P.dma(svt, src)
    e = engs[_CAST["i"] % len(engs)]
    _CAST["i"] += 1
    P.copy(dst, svt, eng=e)


def mmg(P, items, K):
    for k in range(K):
        for (out, lf, rf) in items:
            P.mm(out, lf(k), rf(k), start=(k == 0), stop=(k == K - 1))


def make_ln_pools(P):
    R = Ctx()
    R.xt = Rot(P, [128, D], F32, "xt", 2)
    R.st = Rot(P, [128, 2, 6], F32, "bnst", 2)
    R.mv = Rot(P, [128, 2], F32, "mv", 2)
    R.rs = Rot(P, [128, 1], F32, "rs", 2)
    R.xn = Rot(P, [128, D], BF16, "xn", 2)
    return R


def ln_tile_to_hT2(P, G, R, src_dram, hT_dst, t_idx, which_sh, which_sc):
    xt = R.xt()
    P.dma(xt, src_dram)
    st = R.st(); mv = R.mv(); rs = R.rs(); xn = R.xn()
    for hh in range(2):
        P.bn_stats(st[:, hh, :], xt[:, hh * 512:(hh + 1) * 512])
    P.bn_aggr(mv, st)
    rstd_from_ms(P, rs, mv[:, 1:2], 1)
    P.ts(xn, xt, mv[:, 0:1], rs, op0=ALU.subtract, op1=ALU.mult)
    ps = P.ps()
    psb = ps.v(ps.ap.bitcast(BF16))
    for kc in range(8):
        P.transpose(psb[:, kc * 128:(kc + 1) * 128], xn[:, kc * 128:(kc + 1) * 128], G.ident)
    for kc in range(8):
        P.act(hT_dst[:, kc, :], psb[:, kc * 128:(kc + 1) * 128], AF.Identity,
              bias=G.modT[:, which_sh * 8 + kc, t_idx:t_idx + 1], scale=G.modT[:, which_sc * 8 + kc, t_idx:t_idx + 1])


def rope_apply(P, dst, src_bf, perm, cos, sin, tmp, n, rows=128, psfn=None):
    ps = (psfn or P.ps)()
    P.mm(ps[0:rows, 0:n], perm, src_bf)
    P.tt(tmp, src_bf, cos, ALU.mult)
    P.tt(dst, ps[0:rows, 0:n], sin, ALU.mult)
    P.tt(dst, dst, tmp, ALU.add)


def alloc_persist(P, G):
    G.kT = P.sb([128, NK], BF16, "kT")
    G.Vg = P.sb([128, NKT, 2, 128], BF16, "Vg")
    G.ckvT = P.sb([128, 2, NK], BF16, "ckvT")
    G.krT = P.sb([32, NK], BF16, "krT")


def stage_A(P, I, G):
    P.memset(G.Vg[:, :, :, 64:128], 1.0)
    with P.scope():
        w_st = P.sb([128, 8, NST], BF16, "w_st")
        stg = Rot(P, [128, NST], F32, "stg", 2)
        for kc in range(8):
            wload(P, stg, w_st[:, kc, :], dT(I["w_in"].ap[kc * 128:(kc + 1) * 128, 0:NST], "w_in"))
        kg = P.sb([128, 1], F32, "kg")
        for r in range(2):
            P.dma(kg[r * 64:(r + 1) * 64, :], dT(I["a_k_gain"].ap.rearrange("(p o) -> p o", o=1), "akg"))
        cg = P.sb([128, 2], F32, "cg")
        P.dma(cg, dT(I["c_kv_a_gain"].ap.rearrange("(j p) -> p j", p=128), "ckg"), allow_slow_non_contiguous=True)
        R = make_ln_pools(P)
        hTs = Rot(P, [128, 8, 512], BF16, "hT", 2)
        sq = Rot(P, [128, 512], BF16, "sq", 2)
        rst = Rot(P, [128, 512], F32, "rst", 2)
        knb = Rot(P, [128, 512], BF16, "knb", 3)
        tmp = Rot(P, [128, 512], F32, "tmp", 2)
        cosb = Rot(P, [128, 512], F32, "cosb", 2)
        sinb = Rot(P, [128, 512], F32, "sinb", 2)
        cos32 = Rot(P, [32, 512], F32, "cos32", 2)
        sin32 = Rot(P, [32, 512], F32, "sin32", 2)
        blocks = [(0, 2, True)] + [(2 + 4 * i, 4, False) for i in range(8)]
        for (t0, nt, is_ctx) in blocks:
            n = nt * 128
            c0 = t0 * 128
            hT = hTs()
            for ti in range(nt):
                t = t0 + ti
                src = I["ctx"].rows(t * 128) if is_ctx else I["x_all"].rows((t - 2) * 128)
                ln_tile_to_hT2(P, G, R, src, hT[:, :, ti * 128:(ti + 1) * 128], 1 if is_ctx else 0, 0, 1)
            if not is_ctx:
                lc = c0 - CTX
                cb, sb_, c32, s32 = cosb(), sinb(), cos32(), sin32()
                P.dma(cb[:, 0:n], dT(I["ropek_cos"].ap[:, lc:lc + n], "rc"))
                P.dma(sb_[:, 0:n], dT(I["ropek_sin"].ap[:, lc:lc + n], "rs"))
                P.dma(c32[:, 0:n], dT(I["rope32k_cos"].ap[:, lc:lc + n], "rc32"))
                P.dma(s32[:, 0:n], dT(I["rope32k_sin"].ap[:, lc:lc + n], "rs32"))
            pk = P.ps(); pc = [P.ps(), P.ps()]; pr = P.ps()
            mmg(P, [(pk[:, 0:n], lambda k: w_st[:, k, OFF_AK:OFF_AK + 128], lambda k: hT[:, k, 0:n]),
                    (pc[0][:, 0:n], lambda k: w_st[:, k, OFF_CKV:OFF_CKV + 128], lambda k: hT[:, k, 0:n]),
                    (pc[1][:, 0:n], lambda k: w_st[:, k, OFF_CKV + 128:OFF_CKV + 256], lambda k: hT[:, k, 0:n]),
                    (pr[0:32, 0:n], lambda k: w_st[:, k, OFF_CKR:OFF_CKR + 32], lambda k: hT[:, k, 0:n])], 8)
            s = sq()
            P.act(s[:, 0:n], pk[:, 0:n], AF.Square)
            pm = P.ps()
            P.mm(pm[:, 0:n], G.blk64, s[:, 0:n])
            rs = rst()
            rstd_from_ms(P, rs[:, 0:n], pm[:, 0:n], n)
            kn = knb()
            P.stt(kn[:, 0:n], pk[:, 0:n], kg[:, 0:1], rs[:, 0:n], ALU.mult, ALU.mult)
            if is_ctx:
                P.copy(G.kT[:, c0:c0 + n], kn[:, 0:n])
            else:
                rope_apply(P, G.kT[:, c0:c0 + n], kn[:, 0:n], G.perm64, cb[:, 0:n], sb_[:, 0:n], tmp()[:, 0:n], n)
            ss = [sq(), sq()]
            for j in range(2):
                P.act(ss[j][:, 0:n], pc[j][:, 0:n], AF.Square)
            pm = P.ps()
            for j in range(2):
                P.mm(pm[:, 0:n], G.ones, ss[j][:, 0:n], start=(j == 0), stop=(j == 1))
            rs = rst()
            P.ts(rs[:, 0:n], pm[:, 0:n], 1.0 / 256, EPS, op0=ALU.mult, op1=ALU.add)
            P.act(rs[:, 0:n], rs[:, 0:n], AF.Sqrt)
            P.recip(rs[:, 0:n], rs[:, 0:n])
            for j in range(2):
                P.stt(G.ckvT[:, j, c0:c0 + n], pc[j][:, 0:n], cg[:, j:j + 1], rs[:, 0:n], ALU.mult, ALU.mult)
            if is_ctx:
                P.copy(G.krT[:, c0:c0 + n], pr[0:32, 0:n])
            else:
                kr = knb()
                P.copy(kr[0:32, 0:n], pr[0:32, 0:n])
                rope_apply(P, G.krT[:, c0:c0 + n], kr[0:32, 0:n], G.perm32, c32[:, 0:n], s32[:, 0:n], tmp()[0:32, 0:n], n, rows=32)
            pvs = [P.ps() for _ in range(nt)]
            mmg(P, [(pvs[ti][:, 0:128], (lambda k, ti=ti: hT[:, k, ti * 128:(ti + 1) * 128]),
                     lambda k: w_st[:, k, OFF_AV:OFF_AV + 128]) for ti in range(nt)], 8)
            for ti in range(nt):
                pv = pvs[ti]
                P.copy(G.Vg[:, t0 + ti, :, 0:64], pv.v(pv.ap[:, 0:128].rearrange("p (a b) -> p a b", a=2)), eng="act")
            pus = [P.ps() for _ in range(4)]
            mmg(P, [(pus[j][:, 0:n], (lambda k, j=j: w_st[:, k, OFF_U + j * 128:OFF_U + (j + 1) * 128]),
                     lambda k: hT[:, k, 0:n]) for j in range(4)], 8)
            for j in range(4):
                P.copy(G.uT[:, j, c0:c0 + n], pus[j][:, 0:n], eng=("act" if j % 2 else "dve"))


def own_blocks(last):
    bl = [(i * 512, 512, False, i * 512) for i in range(4)]
    if not last:
        bl.append((LH, 256, True, 0))
    return bl


def stage_B1(P, I, G, last):
    NQ = LH + (0 if last else CTX)
    G.NQ = NQ
    G.qg = P.sb([128, 4, NQ], BF16, "qg")
    G.qm = P.sb([96, 8, NQ], BF16, "qm")
    G.gD = P.dram([3 * D, NQ], BF16, "gD")
    with P.scope():
        hT = P.sb([128, 8, NQ], BF16, "hTall")
        with P.scope():
            R = make_ln_pools(P)
            for (c0, n, is_ctx, r0) in own_blocks(last):
                for ti in range(n // 128):
                    src = (I["ctx"] if is_ctx else I["x_own"]).rows(r0 + ti * 128)
                    ln_tile_to_hT2(P, G, R, src, hT[:, :, c0 + ti * 128:c0 + (ti + 1) * 128], 1 if is_ctx else 0, 0, 1)
        qgain = P.sb([128, 1], F32, "qgain")
        for r in range(2):
            P.dma(qgain[r * 64:(r + 1) * 64, :], dT(I["a_q_gain"].ap.rearrange("(p o) -> p o", o=1), "aqg"))
        cqg = P.sb([128, 6], F32, "cqg")
        P.dma(cqg, dT(I["c_q_a_gain"].ap.rearrange("(j p) -> p j", p=128), "cqg"), allow_slow_non_contiguous=True)
        perm32h = P.sb([96, 32], BF16, "perm32h")
        P.dma(perm32h[64:96, :], I["perm32"], eng="pool")
        cosqR = Rot(P, [128, 512], F32, "cosq", 2)
        sinqR = Rot(P, [128, 512], F32, "sinq", 2)
        cos32R = Rot(P, [96, 512], F32, "cos32q", 2)
        sin32R = Rot(P, [96, 512], F32, "sin32q", 2)
        sq = Rot(P, [128, 512], BF16, "sq", 6)
        rst = Rot(P, [128, 512], F32, "rst", 2)
        knb = Rot(P, [128, 512], BF16, "knb", 2)
        tmp = Rot(P, [128, 512], F32, "tmp", 2)
        with P.scope():
            wq = P.sb([128, 8, 4, 128], BF16, "wq")
            stg = Rot(P, [128, 768], F32, "stg", 2)
            for kc in range(8):
                for hf in range(2):
                    src = I["w_in"].ap[kc * 128:(kc + 1) * 128, OFF_AQ + hf * 256:OFF_AQ + (hf + 1) * 256].rearrange("p (a b) -> p a b", a=4)
                    wload(P, stg, wq[:, kc, :, hf * 64:(hf + 1) * 64], dT(src, "w_in"))
            for (c0, n, is_ctx, r0) in own_blocks(last):
                if not is_ctx:
                    cosq = cosqR(); sinq = sinqR()
                    P.dma(cosq, dT(I["ropeq_cos"].ap[:, c0:c0 + n], "rqc"))
                    P.dma(sinq, dT(I["ropeq_sin"].ap[:, c0:c0 + n], "rqs"))
                pks = [P.ps() for _ in range(4)]
                mmg(P, [(pks[hd][:, 0:n], (lambda k, hd=hd: wq[:, k, hd, :]), lambda k: hT[:, k, c0:c0 + n]) for hd in range(4)], 8)
                for hd in range(4):
                    pk = pks[hd]
                    s = sq()
                    P.act(s[:, 0:n], pk[:, 0:n], AF.Square)
                    pm = P.ps_acc()
                    P.mm(pm[:, 0:n], G.blk64, s[:, 0:n])
                    rs = rst()
                    rstd_from_ms(P, rs[:, 0:n], pm[:, 0:n], n)
                    if is_ctx:
                        P.stt(G.qg[:, hd, c0:c0 + n], pk[:, 0:n], qgain[:, 0:1], rs[:, 0:n], ALU.mult, ALU.mult)
                    else:
                        kn = knb()
                        P.stt(kn[:, 0:n], pk[:, 0:n], qgain[:, 0:1], rs[:, 0:n], ALU.mult, ALU.mult)
                        rope_apply(P, G.qg[:, hd, c0:c0 + n], kn[:, 0:n], G.perm64, cosq[:, 0:n], sinq[:, 0:n],
                                   tmp()[:, 0:n], n, psfn=P.ps_acc)
        with P.scope():
            wc = P.sb([128, 8, 768], BF16, "wc")
            stg = Rot(P, [128, 768], F32, "stg", 1)
            for kc in range(8):
                wload(P, stg, wc[:, kc, :], dT(I["w_in"].ap[kc * 128:(kc + 1) * 128, OFF_CQ:OFF_CQ + 768], "w_in"))
            wqb = P.sb([128, 6, 768], BF16, "wqb")
            for j in range(6):
                wload(P, stg, wqb[:, j, :], dT(I["c_w_qb"].ap[j * 128:(j + 1) * 128, :], "wqb"))
            cqn = P.sb([128, 6, 512], BF16, "cqn")
            qrb = Rot(P, [96, 512], BF16, "qrb", 2)
            for (c0, n, is_ctx, r0) in own_blocks(last):
                if not is_ctx:
                    cos32 = cos32R(); sin32 = sin32R()
                    P.dma(cos32[64:96, :], dT(I["rope32q_cos"].ap[:, c0:c0 + n], "rqc32"))
                    P.dma(sin32[64:96, :], dT(I["rope32q_sin"].ap[:, c0:c0 + n], "rqs32"))
                pcs = [P.ps() for _ in range(6)]
                mmg(P, [(pcs[j][:, 0:n], (lambda k, j=j: wc[:, k, j * 128:(j + 1) * 128]), lambda k: hT[:, k, c0:c0 + n]) for j in range(6)], 8)
                sqs = []
                for j in range(6):
                    s = sq()
                    P.act(s[:, 0:n], pcs[j][:, 0:n], AF.Square)
                    sqs.append(s)
                pm = P.ps_acc()
                for j in range(6):
                    P.mm(pm[:, 0:n], G.ones, sqs[j][:, 0:n], start=(j == 0), stop=(j == 5))
                rs = rst()
                P.ts(rs[:, 0:n], pm[:, 0:n], 1.0 / 768, EPS, op0=ALU.mult, op1=ALU.add)
                P.act(rs[:, 0:n], rs[:, 0:n], AF.Sqrt)
                P.recip(rs[:, 0:n], rs[:, 0:n])
                for j in range(6):
                    P.stt(cqn[:, j, 0:n], pcs[j][:, 0:n], cqg[:, j:j + 1], rs[:, 0:n], ALU.mult, ALU.mult)
                for hg in range(2):
                    pqs = [P.ps() for _ in range(4)]
                    mmg(P, [(pqs[i][0:96, 0:n], (lambda k, h=hg * 4 + i: wqb[:, k, h * 96:(h + 1) * 96]), lambda k: cqn[:, k, 0:n]) for i in range(4)], 6)
                    for i in range(4):
                        h = hg * 4 + i
                        pq = pqs[i]
                        if is_ctx:
                            P.copy(G.qm[:, h, c0:c0 + n], pq[0:96, 0:n], eng="act")
                        else:
                            P.copy(G.qm[0:64, h, c0:c0 + n], pq[0:64, 0:n], eng="act")
                            qr = qrb()
                            P.copy(qr[64:96, 0:n], pq[64:96, 0:n])
                            pr = P.ps_acc()
                            P.mm(pr[64:96, 0:n], perm32h[64:96, :], qr[64:96, 0:n])
                            t = tmp()
                            P.tt(t[64:96, 0:n], qr[64:96, 0:n], cos32[64:96, 0:n], ALU.mult)
                            t2 = tmp()
                            P.tt(t2[64:96, 0:n], pr[64:96, 0:n], sin32[64:96, 0:n], ALU.mult)
                            P.tt(G.qm[64:96, h, c0:c0 + n], t[64:96, 0:n], t2[64:96, 0:n], ALU.add)
        with P.scope():
            wg = Rot(P, [128, 8, 512], BF16, "wg", 2)
            stg = Rot(P, [128, 512], F32, "stg", 3)
            gb = Rot(P, [128, 512], BF16, "gb", 3)
            for gi in range(6):
                w = wg()
                for kc in range(8):
                    wload(P, stg, w[:, kc, :], dT(I["w_in"].ap[kc * 128:(kc + 1) * 128, OFF_GATE + gi * 512:OFF_GATE + (gi + 1) * 512], "w_in"), engs=("pool", "dve"))
                for (c0, n, is_ctx, r0) in own_blocks(last):
                    pgs = [P.ps() for _ in range(4)]
                    mmg(P, [(pgs[oc][:, 0:n], (lambda k, oc=oc: w[:, k, oc * 128:(oc + 1) * 128]), lambda k: hT[:, k, c0:c0 + n]) for oc in range(4)], 8)
                    for oc in range(4):
                        g = gb()
                        P.act(g[:, 0:n], pgs[oc][:, 0:n], AF.Sigmoid)
                        row = (gi * 4 + oc) * 128
                        P.dma(G.gD.v(G.gD.ap[row:row + 128, c0:c0 + n]), g[:, 0:n])


def run_attn(P, chains, pT, scale):
    LA = 2
    nkt = chains[0][1]
    pls = [dict() for _ in chains]
    for kt in range(nkt + LA):
        if kt < nkt:
            for ci, (po, _, n, slf, srhs, vlf) in enumerate(chains):
                pss = P.ps()
                P.mm(pss[:, 0:n], slf(kt), srhs)
                p = pT()
                P.act(p[:, 0:n], pss[:, 0:n], AF.Exp, scale=scale)
                pls[ci][kt] = p
        jj = kt - LA
        if jj >= 0:
            for ci, (po, _, n, slf, srhs, vlf) in enumerate(chains):
                P.mm(po[:, 0:n], vlf(jj), pls[ci].pop(jj)[:, 0:n], start=(jj == 0), stop=(jj == nkt - 1))


def block_groups(last):
    bl = own_blocks(last)
    groups = [bl[0:2], bl[2:4]]
    if not last:
        groups.append(bl[4:5])
    return groups


def attn_finish(P, po, n, rec, yo, dst):
    r = rec()
    P.recip(r[64:128, 0:n], po[64:128, 0:n])
    y = yo()
    P.tt(y[:, 0:n], po[0:64, 0:n], r[64:128, 0:n], ALU.mult)
    P.dma(dst, y[:, 0:n])


def stage_B2(P, I, G, last):
    NQ = G.NQ
    G.yaD = P.dram([512, NQ], BF16, "yaD")
    with P.scope():
        pT = Rot(P, [128, 512], BF16, "pT", 8)
        rec = Rot(P, [128, 512], F32, "rec", 2)
        yo = Rot(P, [64, 512], BF16, "yo", 2)
        kTp = P.sb([128, 2, NK], BF16, "kTp")
        P.memset(kTp, 0.0)
        P.copy(kTp[0:64, 0, :], G.kT[0:64, :], eng="pool")
        P.copy(kTp[64:128, 1, :], G.kT[64:128, :], eng="dve")
        for hd in range(4):
            for kvh in range(2):
                head = hd + 4 * kvh
                for grp in block_groups(last):
                    chains = []
                    for (c0, n, is_ctx, r0) in grp:
                        nkt = 2 if is_ctx else NKT
                        chains.append((P.ps_acc(), nkt, n, (lambda kt: kTp[:, kvh, kt * 128:(kt + 1) * 128]),
                                       G.qg[:, hd, c0:c0 + n], (lambda kt: G.Vg[:, kt, kvh, :])))
                    run_attn(P, chains, pT, 0.125)
                    for (po, _, n, _, _, _), (c0, _, _, _) in zip(chains, grp):
                        attn_finish(P, po, n, rec, yo, G.yaD.v(G.yaD.ap[head * 64:(head + 1) * 64, c0:c0 + n]))


def stage_B3(P, I, G, last):
    NQ = G.NQ
    G.ycD = P.dram([512, NQ], BF16, "ycD")
    with P.scope():
        wkv = P.sb([128, 2, 1024], BF16, "wkv")
        stg = Rot(P, [128, 1024], F32, "stg", 1)
        for j in range(2):
            wload(P, stg, wkv[:, j, :], dT(I["c_w_kvb"].ap[j * 128:(j + 1) * 128, :], "wkvb"))
        Kh = Rot(P, [96, NK], BF16, "Kh", 2)
        Vh = [P.sb([128, NKT, 128], BF16, f"Vh{i}") for i in range(2)]
        for v in Vh:
            P.memset(v[:, :, 64:128], 1.0)
        pT = Rot(P, [128, 512], BF16, "pT", 8)
        rec = Rot(P, [128, 512], F32, "rec", 2)
        yo = Rot(P, [64, 512], BF16, "yo", 2)
        scale = 96 ** -0.5
        for h in range(8):
            K = Kh(); V = Vh[h % 2]
            cbs = [(cb * 512, min(512, NK - cb * 512)) for cb in range(9)]
            for g0 in range(0, 9, 3):
                grp = cbs[g0:g0 + 3]
                pks = [P.ps() for _ in grp]
                mmg(P, [(pks[i][0:64, 0:n], lambda k: wkv[:, k, h * 128:h * 128 + 64], (lambda k, k0=k0, n=n: G.ckvT[:, k, k0:k0 + n]))
                        for i, (k0, n) in enumerate(grp)], 2)
                for i, (k0, n) in enumerate(grp):
                    P.copy(K[0:64, k0:k0 + n], pks[i][0:64, 0:n], eng=("act" if i % 2 else "dve"))
            P.copy(K[64:96, :], G.krT[0:32, :], eng="pool")
            for g0 in range(0, NKT, 4):
                kts = list(range(g0, min(g0 + 4, NKT)))
                pvs = [P.ps() for _ in kts]
                mmg(P, [(pvs[i][:, 0:64], (lambda k, kt=kt: G.ckvT[:, k, kt * 128:(kt + 1) * 128]),
                         lambda k: wkv[:, k, h * 128 + 64:h * 128 + 128]) for i, kt in enumerate(kts)], 2)
                for i, kt in enumerate(kts):
                    P.copy(V[:, kt, 0:64], pvs[i][:, 0:64], eng=("act" if kt % 2 else "dve"))
            for grp in block_groups(last):
                chains = []
                for (c0, n, is_ctx, r0) in grp:
                    nkt = 2 if is_ctx else NKT
                    chains.append((P.ps_acc(), nkt, n, (lambda kt: K[0:96, kt * 128:(kt + 1) * 128]),
                                   G.qm[0:96, h, c0:c0 + n], (lambda kt: V[:, kt, :])))
                run_attn(P, chains, pT, scale)
                for (po, _, n, _, _, _), (c0, _, _, _) in zip(chains, grp):
                    attn_finish(P, po, n, rec, yo, G.ycD.v(G.ycD.ap[h * 64:(h + 1) * 64, c0:c0 + n]))


def bcast_load(P, dst, src_ap_1d):
    P.dma(dst, dT(src_ap_1d.partition_broadcast(128), "bc"))


def stage_B4a(P, I, G, last):
    NQ = G.NQ
    G.xmidD = P.dram([NQ, D], F32, "xmidD")
    G.h2D = P.dram([D, NQ], BF16, "h2D")
    with P.scope():
        wglu = P.sb([128, 4, 1024], BF16, "wglu")
        wb = [P.sb([128, 4, 1024], BF16, f"wb{i}") for i in range(3)]
        wout = P.sb([128, 8, 1024], BF16, "wout")
        stg = Rot(P, [128, 1024], F32, "stg", 2)
        for j in range(4):
            wload(P, stg, wglu[:, j, :], dT(I["s5_w_glu"].ap[j * 128:(j + 1) * 128, :], "w"))
            for i, nm in enumerate(["w_branch_a", "w_branch_s5", "w_branch_c"]):
                wload(P, stg, wb[i][:, j, :], dT(I[nm].ap[j * 128:(j + 1) * 128, :], "w"))
        for kc in range(8):
            wload(P, stg, wout[:, kc, :], dT(I["w_out"].ap[kc * 128:(kc + 1) * 128, :], "w"))
        g1b = [P.sb([128, D], F32, f"g1b{t}") for t in range(2)]
        for t in range(2):
            P.dma(g1b[t], dT(G.modD.ap[t, 2 * D:3 * D].partition_broadcast(128), "modD_r"))
        lng = P.sb([128, D], F32, "lng"); bcast_load(P, lng, I["ln1_g"].ap)
        lnb = P.sb([128, D], F32, "lnb"); bcast_load(P, lnb, I["ln1_b"].ap)
        gates = P.sb([128, 24, 512], BF16, "gates")
        srcs = [P.sb([128, 4, 512], BF16, f"src{i}") for i in range(3)]
        yT = P.sb([128, 4, 512], BF16, "yT")
        t1 = P.sb([128, 4, 512], F32, "t1")
        sg = Rot(P, [128, 512], F32, "sg", 2)
        acc = Rot(P, [128, 512], F32, "acc", 2)
        tmpm = Rot(P, [128, 512], F32, "tmpm", 2)
        merged = P.sb([128, 8, 512], BF16, "merged")
        xtR = Rot(P, [128, D], F32, "xt", 2)
        tsR = Rot(P, [128, D], F32, "tsum", 2)
        xmR = Rot(P, [128, D], F32, "xm", 2)
        stR = Rot(P, [128, 2, 6], F32, "bnst", 2); mvR = Rot(P, [128, 2], F32, "mv", 2); rsR = Rot(P, [128, 1], F32, "rs", 2)
        xnR = Rot(P, [128, D], BF16, "xn", 2)
        h2T = P.sb([128, 8, 512], BF16, "h2T")
        for (c0, n, is_ctx, r0) in own_blocks(last):
            tix = 1 if is_ctx else 0
            for j in range(4):
                P.dma(yT[:, j, 0:n], G.ysD.v(G.ysD.ap[j * 128:(j + 1) * 128, c0:c0 + n]))
                P.dma(srcs[0][:, j, 0:n], G.yaD.v(G.yaD.ap[j * 128:(j + 1) * 128, c0:c0 + n]))
                P.dma(srcs[2][:, j, 0:n], G.ycD.v(G.ycD.ap[j * 128:(j + 1) * 128, c0:c0 + n]))
            for gi in range(24):
                P.dma(gates[:, gi, 0:n], G.gD.v(G.gD.ap[gi * 128:(gi + 1) * 128, c0:c0 + n]))
            P.tt(t1[:, :, 0:n], yT[:, :, 0:n], yT[:, :, 0:n], ALU.mult)
            P.ts(t1[:, :, 0:n], t1[:, :, 0:n], 0.044715, 1.0, op0=ALU.mult, op1=ALU.add)
            P.tt(t1[:, :, 0:n], t1[:, :, 0:n], yT[:, :, 0:n], ALU.mult)
            P.act(t1[:, :, 0:n], t1[:, :, 0:n], AF.Sigmoid, scale=1.5957691216)
            ge = P.sb([128, 4, 512], BF16, "ge") if c0 == 0 else ge
            P.tt(ge[:, :, 0:n], t1[:, :, 0:n], yT[:, :, 0:n], ALU.mult)
            for op_ in range(2):
                pa = [P.ps(), P.ps()]; pg = [P.ps(), P.ps()]
                items = []
                for q in range(2):
                    oc = op_ * 2 + q
                    items.append((pa[q][:, 0:n], (lambda k, oc=oc: wglu[:, k, oc * 128:(oc + 1) * 128]), lambda k: ge[:, k, 0:n]))
                    items.append((pg[q][:, 0:n], (lambda k, oc=oc: wglu[:, k, 512 + oc * 128:512 + (oc + 1) * 128]), lambda k: ge[:, k, 0:n]))
                mmg(P, items, 4)
                for q in range(2):
                    oc = op_ * 2 + q
                    s = sg()
                    P.act(s[:, 0:n], pg[q][:, 0:n], AF.Sigmoid)
                    P.tt(srcs[1][:, oc, 0:n], pa[q][:, 0:n], s[:, 0:n], ALU.mult)
            for oc in range(8):
                a = acc()
                pbs = [P.ps() for _ in range(3)]
                mmg(P, [(pbs[br][:, 0:n], (lambda k, br=br: wb[br][:, k, oc * 128:(oc + 1) * 128]), (lambda k, br=br: srcs[br][:, k, 0:n])) for br in range(3)], 4)
                for br in range(3):
                    pb = pbs[br]
                    if br == 0:
                        P.tt(a[:, 0:n], pb[:, 0:n], gates[:, br * 8 + oc, 0:n], ALU.mult)
                    else:
                        tm = tmpm()
                        P.tt(tm[:, 0:n], pb[:, 0:n], gates[:, br * 8 + oc, 0:n], ALU.mult)
                        if br == 1:
                            P.tt(a[:, 0:n], a[:, 0:n], tm[:, 0:n], ALU.add, eng="pool")
                        else:
                            P.tt(merged[:, oc, 0:n], a[:, 0:n], tm[:, 0:n], ALU.add, eng="pool")
            for ti in range(n // 128):
                xt = xtR(); ts_ = tsR()
                P.dma(xt, (I["ctx"] if is_ctx else I["x_own"]).rows(r0 + ti * 128))
                pms = [P.ps(), P.ps()]
                mmg(P, [(pms[half][:, 0:512], lambda k: merged[:, k, ti * 128:(ti + 1) * 128],
                         (lambda k, half=half: wout[:, k, half * 512:(half + 1) * 512])) for half in range(2)], 8)
                for half in range(2):
                    P.tt(ts_[:, half * 512:(half + 1) * 512], pms[half][:, 0:512], g1b[tix][:, half * 512:(half + 1) * 512], ALU.mult)
                P.stt(ts_, xt, ALPHA, ts_, ALU.mult, ALU.add)
                st = stR(); mv = mvR(); rs = rsR(); xm = xmR()
                for hh in range(2):
                    P.bn_stats(st[:, hh, :], ts_[:, hh * 512:(hh + 1) * 512])
                P.bn_aggr(mv, st)
                rstd_from_ms(P, rs, mv[:, 1:2], 1)
                P.ts(xm, ts_, mv[:, 0:1], rs, op0=ALU.subtract, op1=ALU.mult)
                P.tt(xm, xm, lng, ALU.mult, eng="pool")
                P.tt(xm, xm, lnb, ALU.add, eng="pool")
                P.dma(G.xmidD.v(G.xmidD.ap[c0 + ti * 128:c0 + (ti + 1) * 128, :]), xm)
                st = stR(); mv = mvR(); rs = rsR(); xn = xnR()
                for hh in range(2):
                    P.bn_stats(st[:, hh, :], xm[:, hh * 512:(hh + 1) * 512])
                P.bn_aggr(mv, st)
                rstd_from_ms(P, rs, mv[:, 1:2], 1)
                P.ts(xn, xm, mv[:, 0:1], rs, op0=ALU.subtract, op1=ALU.mult)
                ps = P.ps()
                psb = ps.v(ps.ap.bitcast(BF16))
                for kc in range(8):
                    P.transpose(psb[:, kc * 128:(kc + 1) * 128], xn[:, kc * 128:(kc + 1) * 128], G.ident)
                for kc in range(8):
                    P.act(h2T[:, kc, ti * 128:(ti + 1) * 128], psb[:, kc * 128:(kc + 1) * 128], AF.Identity,
                          bias=G.modT[:, 3 * 8 + kc, tix:tix + 1], scale=G.modT[:, 4 * 8 + kc, tix:tix + 1])
            for kc in range(8):
                P.dma(G.h2D.v(G.h2D.ap[kc * 128:(kc + 1) * 128, c0:c0 + n]), h2T[:, kc, 0:n])


def stage_B4b(P, I, G, last, out_own, out_ctx):
    NQ = G.NQ
    NT = NQ // 128
    with P.scope():
        g2b = [P.sb([128, D], F32, f"g2b{t}") for t in range(2)]
        for t in range(2):
            P.dma(g2b[t], dT(G.modD.ap[t, 5 * D:6 * D].partition_broadcast(128), "modD_r"))
        lng = P.sb([128, D], F32, "lng"); bcast_load(P, lng, I["ln2_g"].ap)
        lnb = P.sb([128, D], F32, "lnb"); bcast_load(P, lnb, I["ln2_b"].ap)
        h2T = P.sb([128, 8, NQ], BF16, "h2Tall")
        for kc in range(8):
            P.dma(h2T[:, kc, :], G.h2D.v(G.h2D.ap[kc * 128:(kc + 1) * 128, :]))
        tsum = P.sb([128, NT, D], F32, "tsum2")
        wuR = Rot(P, [128, 8, 512], BF16, "wu", 2)
        wdR = Rot(P, [128, 4, D], BF16, "wd", 2)
        stg = Rot(P, [128, 1024], F32, "stg", 3)
        aR = Rot(P, [128, 4, 512], BF16, "aog", 2)
        rl = Rot(P, [128, 512], BF16, "rl", 3)
        xmR = Rot(P, [128, D], F32, "xm", 2)
        stR = Rot(P, [128, 2, 6], F32, "bnst", 2); mvR = Rot(P, [128, 2], F32, "mv", 2); rsR = Rot(P, [128, 1], F32, "rs", 2)
        for og in range(8):
            wu = wuR(); wd = wdR()
            for kc in range(8):
                wload(P, stg, wu[:, kc, :], dT(I["w_up"].ap[kc * 128:(kc + 1) * 128, og * 512:(og + 1) * 512], "w"), engs=("pool", "act"))
            for oc in range(4):
                wload(P, stg, wd[:, oc, :], dT(I["w_down"].ap[og * 512 + oc * 128:og * 512 + (oc + 1) * 128, :], "w"), engs=("pool", "act"))
            for (c0, n, is_ctx, r0) in own_blocks(last):
                a = aR()
                pus = [P.ps() for _ in range(4)]
                mmg(P, [(pus[oc][:, 0:n], (lambda k, oc=oc: wu[:, k, oc * 128:(oc + 1) * 128]), lambda k: h2T[:, k, c0:c0 + n]) for oc in range(4)], 8)
                for oc in range(4):
                    r = rl()
                    P.act(r[:, 0:n], pus[oc][:, 0:n], AF.Relu)
                    P.tt(a[:, oc, 0:n], r[:, 0:n], r[:, 0:n], ALU.mult, eng="pool")
                combos = [(ti, half) for ti in range(n // 128) for half in range(2)]
                for g0 in range(0, len(combos), 4):
                    grp = combos[g0:g0 + 4]
                    pds = [P.ps() for _ in grp]
                    mmg(P, [(pds[i][:, 0:512], (lambda k, ti=ti: a[:, k, ti * 128:(ti + 1) * 128]),
                             (lambda k, half=half: wd[:, k, half * 512:(half + 1) * 512])) for i, (ti, half) in enumerate(grp)], 4)
                    for i, (ti, half) in enumerate(grp):
                        tile = c0 // 128 + ti
                        dst = tsum[:, tile, half * 512:(half + 1) * 512]
                        if og == 0:
                            P.copy(dst, pds[i][:, 0:512])
                        else:
                            P.tt(dst, pds[i][:, 0:512], dst, ALU.add)
        for (c0, n, is_ctx, r0) in own_blocks(last):
            tix = 1 if is_ctx else 0
            for ti in range(n // 128):
                tile = c0 // 128 + ti
                xm = xmR()
                P.dma(xm, G.xmidD.v(G.xmidD.ap[c0 + ti * 128:c0 + (ti + 1) * 128, :]))
                P.tt(tsum[:, tile, :], tsum[:, tile, :], g2b[tix], ALU.mult, eng="pool")
                P.stt(tsum[:, tile, :], xm, ALPHA, tsum[:, tile, :], ALU.mult, ALU.add)
                st = stR(); mv = mvR(); rs = rsR()
                for hh in range(2):
                    P.bn_stats(st[:, hh, :], tsum[:, tile, hh * 512:(hh + 1) * 512])
                P.bn_aggr(mv, st)
                rstd_from_ms(P, rs, mv[:, 1:2], 1)
                o = xm
                P.ts(o, tsum[:, tile, :], mv[:, 0:1], rs, op0=ALU.subtract, op1=ALU.mult)
                P.tt(o, o, lng, ALU.mult, eng="pool")
                P.tt(o, o, lnb, ALU.add, eng="pool")
                dst = out_ctx if is_ctx else out_own
                P.dma(dst.rows(r0 + ti * 128), o)


def bc(t, pattern):
    a = t.ap
    return t.v(bass.AP(a.tensor, a.offset, [list(a.ap[0])] + [list(p) for p in pattern]))


MAGIC = 12582912.0
TWO_PI = 6.283185307179586


def stage_S5(P, I, G, last):
    NQ = LH + (0 if last else CTX)
    G.ysD = P.dram([512, NQ], BF16, "ysD")
    with P.scope():
        PR = P.sb([128, 32, 64], F32, "PR"); NPI = P.sb([128, 32, 64], F32, "NPI")
        DA = P.sb([128, 10, 64], F32, "DA"); DB = P.sb([128, 10, 64], F32, "DB")
        BX1 = P.sb([128, 64, 16], F32, "BX1"); BX2 = P.sb([128, 64, 16], F32, "BX2")
        CX1 = P.sb([128, 64, 16], F32, "CX1"); CX2 = P.sb([128, 64, 16], F32, "CX2")
        Dcol = P.sb([128, 32], F32, "Dcol")
        identF = G.ident
        swapF = P.sb([128, 128], BF16, "swapF"); P.dma(swapF, I["swap"], eng="pool")
        maskf = P.sb([128, 128], BF16, "maskf"); P.dma(maskf, I["mask_f"], eng="pool")
        maskb = P.sb([128, 128], BF16, "maskb"); P.dma(maskb, I["mask_b"], eng="pool")
        sgn = P.sb([128, 1], F32, "sgn"); P.memset(sgn[0:64, :], 1.0); P.memset(sgn[64:128, :], -1.0)
        for i in range(8):
            P.dma(Dcol[i * 16:(i + 1) * 16, :], dT(I["s5_d"].ap.rearrange("(g c) -> c g", c=16), "s5d"), allow_slow_non_contiguous=True)
        with P.scope():
            are = P.sb([128, 64], F32, "are"); aim = P.sb([128, 64], F32, "aim"); ldt = P.sb([128, 64], F32, "ldt")
            for hf in range(2):
                sl = slice(hf * 64, (hf + 1) * 64)
                P.dma(are[sl, :], dT(I["s5_a_re"].ap.rearrange("d g p -> p (d g)"), "a"), allow_slow_non_contiguous=True)
                P.dma(aim[sl, :], dT(I["s5_a_im"].ap.rearrange("d g p -> p (d g)"), "a"), allow_slow_non_contiguous=True)
            P.dma(ldt, dT(I["s5_log_dt"].ap.rearrange("d g -> (d g)").partition_broadcast(128), "a"))
            dt_ = P.sb([128, 64], F32, "dt")
            P.act(dt_, ldt, AF.Exp)
            lr = P.sb([128, 64], F32, "lr"); li = P.sb([128, 64], F32, "li")
            P.tt(lr, are, dt_, ALU.mult); P.tt(li, aim, dt_, ALU.mult)
            with P.scope():
                elist = [t - 7 for t in range(16)] + [8 - t for t in range(16)]
                LR = P.sb([128, 32, 64], F32, "LR"); LI = P.sb([128, 32, 64], F32, "LI")
                for idx, e in enumerate(elist):
                    P.ts(LR[:, idx, :], lr, float(e), None, op0=ALU.mult)
                    P.ts(LI[:, idx, :], li, float(e), None, op0=ALU.mult, eng="pool")
                mag = P.sb([128, 32, 64], F32, "mag")
                P.act(mag, LR, AF.Exp)
                rr = P.sb([128, 32, 64], F32, "rr"); kk = P.sb([128, 32, 64], F32, "kk")

                def sin_of(dst, ang_t, shift):
                    P.ts(rr, ang_t, 1.0 / TWO_PI, shift / TWO_PI, op0=ALU.mult, op1=ALU.add)
                    P.ts(kk, rr, MAGIC, None, op0=ALU.add)
                    P.ts(kk, kk, MAGIC, None, op0=ALU.subtract)
                    P.tt(rr, rr, kk, ALU.subtract)
                    P.ts(rr, rr, TWO_PI, None, op0=ALU.mult)
                    P.ts(rr, rr, 3.1415925, -3.1415925, op0=ALU.min, op1=ALU.max)
                    P.act(dst, rr, AF.Sin)
                sn = LR
                sin_of(sn, LI, 0.0)
                P.stt(NPI, mag, -1.0, sn, ALU.mult, ALU.mult)
                sin_of(sn, LI, TWO_PI / 4)
                P.tt(PR, mag, sn, ALU.mult)
            cr_ = P.sb([128, 64], F32, "cr"); ci_ = P.sb([128, 64], F32, "ci")
            t1 = P.sb([128, 64], F32, "t1"); t2 = P.sb([128, 64], F32, "t2")
            P.copy(DA[:, 0, :], PR[:, 15, :])
            P.ts(DB[:, 0, :], NPI[:, 15, :], -1.0, None, op0=ALU.mult)
            for m in range(1, 10):
                P.tt(t1, DA[:, m - 1, :], DA[:, m - 1, :], ALU.mult)
                P.tt(t2, DB[:, m - 1, :], DB[:, m - 1, :], ALU.mult)
                P.tt(DA[:, m, :], t1, t2, ALU.subtract)
                P.stt(DB[:, m, :], DA[:, m - 1, :], 2.0, DB[:, m - 1, :], ALU.mult, ALU.mult)
            P.ts(DB, DB, sgn[:, 0:1], None, op0=ALU.mult)
            den = P.sb([128, 64], F32, "den"); nr = P.sb([128, 64], F32, "nr"); abi = P.sb([128, 64], F32, "abi")
            P.tt(t1, are, are, ALU.mult); P.tt(t2, aim, aim, ALU.mult); P.tt(den, t1, t2, ALU.add); P.recip(den, den)
            P.ts(nr, PR[:, 8, :], -1.0, None, op0=ALU.add)
            P.ts(abi, NPI[:, 8, :], -1.0, None, op0=ALU.mult)
            P.tt(t1, nr, are, ALU.mult); P.tt(t2, abi, aim, ALU.mult); P.tt(cr_, t1, t2, ALU.add); P.tt(cr_, cr_, den, ALU.mult)
            P.tt(t1, abi, are, ALU.mult); P.tt(t2, nr, aim, ALU.mult); P.tt(ci_, t1, t2, ALU.subtract); P.tt(ci_, ci_, den, ALU.mult)
            crb = bc(cr_, [[1, 64], [0, 16]]); cib = bc(ci_, [[1, 64], [0, 16]])
            Bre = P.sb([128, 64, 16], F32, "Bre"); Bim = P.sb([128, 64, 16], F32, "Bim")
            Cre = P.sb([128, 64, 16], F32, "Cre"); Cim = P.sb([128, 64, 16], F32, "Cim")
            for hf in range(2):
                sl = slice(hf * 64, (hf + 1) * 64)
                P.dma(Bre[sl], dT(I["s5_b_re"].ap.rearrange("d g p c -> p (d g) c"), "a"))
                P.dma(Bim[sl], dT(I["s5_b_im"].ap.rearrange("d g p c -> p (d g) c"), "a"))
                P.dma(Cre[sl], dT(I["s5_c_re"].ap.rearrange("d g c p -> p (d g) c"), "a"), allow_slow_non_contiguous=True)
                P.dma(Cim[sl], dT(I["s5_c_im"].ap.rearrange("d g c p -> p (d g) c"), "a"), allow_slow_non_contiguous=True)
            bbr = P.sb([128, 64, 16], F32, "bbr"); bbi = P.sb([128, 64, 16], F32, "bbi"); t3 = P.sb([128, 64, 16], F32, "t3")
            P.tt(bbr, Bre, crb, ALU.mult); P.tt(t3, Bim, cib, ALU.mult); P.tt(bbr, bbr, t3, ALU.subtract)
            P.tt(bbi, Bim, crb, ALU.mult); P.tt(t3, Bre, cib, ALU.mult); P.tt(bbi, bbi, t3, ALU.add)
            P.copy(BX1[0:64], bbr[0:64]); P.copy(BX1[64:128], bbi[64:128])
            P.copy(BX2[0:64], bbi[0:64]); P.ts(BX2[64:128], bbr[64:128], -1.0, None, op0=ALU.mult)
            P.copy(CX1[0:64], Cre[0:64]); P.ts(CX1[64:128], Cim[64:128], -1.0, None, op0=ALU.mult)
            P.copy(CX2[0:64], Cim[0:64]); P.copy(CX2[64:128], Cre[64:128])
        Sel = P.sb([128, 64, 128], BF16, "Sel"); SelT = P.sb([128, 64, 128], BF16, "SelT")
        for q in range(4):
            P.dma(Sel[:, q * 16:(q + 1) * 16, :], dT(I["sel8"].ap[:, q * 16:(q + 1) * 16, :], "sel8"), eng="pool")
            P.dma(SelT[:, q * 16:(q + 1) * 16, :], dT(I["sel8T"].ap[:, q * 16:(q + 1) * 16, :], "sel8T"), eng="pool")
        KQ = {nm: Rot(P, [128, 32, 16], BF16, nm, 2) for nm in ["Kf", "Qf", "Kb", "Qb"]}
        tA = Rot(P, [128, 32, 16], F32, "tA", 1); tB = Rot(P, [128, 32, 16], F32, "tB", 1)
        tC = Rot(P, [128, 32, 16], F32, "tC", 1); tD = Rot(P, [128, 32, 16], F32, "tD", 1)
        SgR = Rot(P, [128, 128], BF16, "Sg", 2)
        WeR = Rot(P, [128, 128], BF16, "We", 4)
        s1R = Rot(P, [128, 128], F32, "s1", 1); s2R = Rot(P, [128, 128], F32, "s2", 1)
        UcR = Rot(P, [128, 576], BF16, "Uc", 2)
        XR = {d: [P.sb([128, 545], BF16, f"X{d}{i}") for i in range(2)] for d in "fb"}
        for d in "fb":
            for x in XR[d]:
                P.memset(x, 0.0)
        MdR = {d: Rot(P, [128, 10, 128], BF16, "Md" + d, 1) for d in "fb"}
        mA = Rot(P, [128, 10, 128], BF16, "mA", 1); mB = Rot(P, [128, 10, 128], BF16, "mB", 1)
        Yt = P.sb([128, 8, 544], BF16, "Yt")
        tqR = Rot(P, [128, 256], F32, "tq", 2); yctx = P.sb([128, CTX], BF16, "yctx")
        yown = Rot(P, [128, LH], BF16, "yown", 1)
        identB = G.ident
        flip = 0
        for g in range(32):
            j, gl = g // 8, g % 8
            mats = {}
            for dname, gd in (("f", g), ("b", 32 + g)):
                for kind, X1, X2, eng in (("K", BX1, BX2, "dve"), ("Q", CX1, CX2, "pool")):
                    out = KQ[kind + dname]()
                    a = tA() if kind == "K" else tC(); b_ = tB() if kind == "K" else tD()
                    prb = bc(PR[:, :, gd], [[64, 32], [0, 16]]); npb = bc(NPI[:, :, gd], [[64, 32], [0, 16]])
                    x1b = bc(X1[:, gd, :], [[0, 32], [1, 16]]); x2b = bc(X2[:, gd, :], [[0, 32], [1, 16]])
                    P.tt(a, prb, x1b, ALU.mult, eng=eng)
                    P.tt(b_, npb, x2b, ALU.mult, eng=eng)
                    P.tt(out, a, b_, ALU.add, eng=eng)
                    mats[kind + dname] = out

            def m128(t, lo):
                return t.v(t.ap[:, lo:lo + 8, :].rearrange("p a b -> p (a b)"))
            Kf, Qf, Kb, Qb = mats["Kf"], mats["Qf"], mats["Kb"], mats["Qb"]
            psf = P.ps(); psb_ = P.ps()
            P.mm(psf[:, 0:128], m128(Kf, 24), m128(Qf, 7))
            P.mm(psb_[:, 0:128], m128(Kb, 7), m128(Qb, 24))
            s1 = s1R(); s2 = s2R(); Sg = SgR()
            P.tt(s1, psf[:, 0:128], maskf, ALU.mult)
            P.tt(s2, psb_[:, 0:128], maskb, ALU.mult)
            P.tt(s1, s1, s2, ALU.add)
            P.stt(Sg, identF, Dcol[:, g:g + 1], s1, ALU.mult, ALU.add)
            We = {}
            for dname, src in (("f", m128(Kf, 17)), ("b", m128(Kb, 7))):
                pt = P.ps()
                ptb = pt.v(pt.ap.bitcast(BF16))
                P.transpose(ptb[:, 0:128], src, identB)
                w = WeR()
                P.copy(w, ptb[:, 0:128], eng="act")
                We[dname] = w
            Wo = {"f": m128(Qf, 8), "b": m128(Qb, 16)}
            Uc = UcR()
            pu = P.ps(); pu2 = P.ps()
            for i in range(8):
                rhs = G.uT.v(G.uT.ap[:, j, CTX + i:NK:8])
                P.mm(pu[:, 0:512], Sel[:, gl * 8 + i, :], rhs, start=(i == 0), stop=(i == 7))
                rhs = G.uT.v(G.uT.ap[:, j, i:CTX:8])
                P.mm(pu2[:, 0:32], Sel[:, gl * 8 + i, :], rhs, start=(i == 0), stop=(i == 7))
            P.copy(Uc[:, 32:544], pu[:, 0:512], eng="act")
            P.copy(Uc[:, 0:32], pu2[:, 0:32])
            P.copy(Uc[:, 544:576], pu2[:, 0:32])
            Xfin = {}
            st_ = {}
            for dname, gd in (("f", g), ("b", 32 + g)):
                ucoff = 0 if dname == "f" else 32
                xoff = 1 if dname == "f" else 0
                cur = XR[dname][0]; nxt = XR[dname][1]
                pa = P.ps(); pb2 = P.ps()
                P.mm(pa[:, 0:512], We[dname], Uc[:, ucoff:ucoff + 512])
                P.mm(pb2[:, 0:32], We[dname], Uc[:, ucoff + 512:ucoff + 544])
                P.copy(cur[:, xoff:xoff + 512], pa[:, 0:512], eng="act")
                P.copy(cur[:, xoff + 512:xoff + 544], pb2[:, 0:32])
                Mall = MdR[dname]()
                ta = mA(); tb = mB()
                idb = bc(identF, [[0, 10], [1, 128]]); swb = bc(swapF, [[0, 10], [1, 128]])
                dab = bc(DA[:, :, gd], [[64, 10], [0, 128]]); dbb = bc(DB[:, :, gd], [[64, 10], [0, 128]])
                P.tt(ta, idb, dab, ALU.mult, eng="pool")
                P.tt(tb, swb, dbb, ALU.mult, eng="pool")
                P.tt(Mall, ta, tb, ALU.add, eng="pool")
                st_[dname] = [cur, nxt, xoff, Mall]
            for m in range(10):
                d = 1 << m
                work = []
                for dname in ("f", "b"):
                    cur, nxt, xoff, Mall = st_[dname]
                    for (lo, hi) in ((0, 272), (272, 544)):
                        ps = P.ps()
                        if dname == "f":
                            s_ = max(lo, d)
                            has = s_ < hi
                            shift = (ps[:, s_ - lo:hi - lo], cur[:, xoff + s_ - d:xoff + hi - d]) if has else None
                        else:
                            e_ = min(hi, 544 - d)
                            has = lo < e_
                            shift = (ps[:, 0:e_ - lo], cur[:, xoff + lo + d:xoff + e_ + d]) if has else None
                        work.append((dname, ps, lo, hi, shift, cur, nxt, xoff, Mall))
                for (dname, ps, lo, hi, shift, cur, nxt, xoff, Mall) in work:
                    P.mm(ps[:, 0:hi - lo], identB, cur[:, xoff + lo:xoff + hi], start=True, stop=(shift is None))
                for (dname, ps, lo, hi, shift, cur, nxt, xoff, Mall) in work:
                    if shift is not None:
                        P.mm(shift[0], Mall[:, m, :], shift[1], start=False, stop=True)
                for (dname, ps, lo, hi, shift, cur, nxt, xoff, Mall) in work:
                    flip ^= 1
                    P.copy(nxt[:, xoff + lo:xoff + hi], ps[:, 0:hi - lo], eng=("act" if flip else "dve"))
                for dname in ("f", "b"):
                    st_[dname][0], st_[dname][1] = st_[dname][1], st_[dname][0]
            Xfin = {dname: st_[dname][0] for dname in ("f", "b")}
            Xf, Xb = Xfin["f"], Xfin["b"]
            py = P.ps()
            pyc = P.ps() if not last else None
            ytl = [(Sg, Uc[:, 32:544], Uc[:, 0:32]), (Wo["f"], Xf[:, 32:544], Xf[:, 0:32]), (Wo["b"], Xb[:, 1:513], Xb[:, 513:545])]
            for q, (lh, r1, r2) in enumerate(ytl):
                P.mm(py[:, 0:512], lh, r1, start=(q == 0), stop=(q == 2))
                if not last:
                    P.mm(pyc[:, 0:32], lh, r2, start=(q == 0), stop=(q == 2))
            P.copy(Yt[:, gl, 0:512], py[:, 0:512], eng="act")
            if not last:
                P.copy(Yt[:, gl, 512:544], pyc[:, 0:32])
            if gl == 7:
                yo = yown()
                for i in range(8):
                    ps = P.ps()
                    ps2 = P.ps() if not last else None
                    for g2 in range(8):
                        P.mm(ps[:, 0:512], SelT[:, g2 * 8 + i, :], Yt[:, g2, 0:512], start=(g2 == 0), stop=(g2 == 7))
                        if not last:
                            P.mm(ps2[:, 0:32], SelT[:, g2 * 8 + i, :], Yt[:, g2, 512:544], start=(g2 == 0), stop=(g2 == 7))
                    tq = tqR()
                    P.act(tq, ps[:, 0:256], AF.Copy, scale=G.sel[:, 0:1])
                    P.stt(yo.v(yo.ap[:, i:LH:8]), ps[:, 256:512], G.sel[:, 1:2], tq, ALU.mult, ALU.add)
                    if not last:
                        P.copy(yctx.v(yctx.ap[:, i:CTX:8]), ps2[:, 0:32])
                P.dma(G.ysD.v(G.ysD.ap[j * 128:(j + 1) * 128, 0:LH]), yo)
                if not last:
                    P.dma(G.ysD.v(G.ysD.ap[j * 128:(j + 1) * 128, LH:LH + CTX]), yctx)


def emit_layer(P, I, G, last, out_own, out_ctx):
    stage_prep(P, I, G)
    with P.scope():
        alloc_persist(P, G)
        with P.scope():
            G.uT = P.sb([128, 4, NK], BF16, "uT")
            stage_A(P, I, G)
            stage_S5(P, I, G, last)
        stage_B1(P, I, G, last)
        stage_B2(P, I, G, last)
        stage_B3(P, I, G, last)
    stage_B4a(P, I, G, last)
    stage_B4b(P, I, G, last, out_own, out_ctx)
    P.flush()


def build_fused():
    nc = bass.Bass("TRN2", target_bir_lowering=False)
    Cn = declare_consts(nc)
    W = [declare_weights(nc, l) for l in range(2)]
    y_out = dT(nc.dram_tensor("y_own", [LH, D], F32, kind="ExternalOutput").ap(), "y_own")
    with ExitStack() as st:
        P = Prog(nc, st)
        P.init_psum()
        NCH = 4
        CR = LH // NCH
        x1o = [P.dram([CR, D], F32, f"x1o{c}") for c in range(NCH)]
        x1g = [P.dram([2 * CR, D], F32, f"x1g{c}") for c in range(NCH)]
        ctx1 = P.dram([CTX, D], F32, "ctx1")
        own_src = RowSrc(lambda r0: x1o[r0 // CR].v(x1o[r0 // CR].ap[r0 % CR:r0 % CR + 128, :]))

        def all_fn(r0):
            half, rr = r0 // LH, r0 % LH
            c, i = rr // CR, rr % CR
            return x1g[c].v(x1g[c].ap[half * CR + i:half * CR + i + 128, :])
        with P.scope():
            G = Ctx()
            I0 = dict(Cn); I0.update(W[0])
            for k in ("x_all", "x_own", "ctx"):
                I0[k] = flat_src(Cn[k])
            emit_layer(P, I0, G, False, own_src, flat_src(ctx1))
        groups = [[0, 1], [2, 3], [4, 5], [6, 7]]
        for c in range(NCH):
            P.add("pool", lambda e, c=c: e.collective_compute("AllGather", ALU.bypass, replica_groups=groups,
                                                              ins=[x1o[c].ap.opt()], outs=[x1g[c].ap.opt()]), [x1o[c]], [x1g[c]])
            P.add("pool", None, [x1g[c]], [])
        P.flush()
        with P.scope():
            G = Ctx()
            I1 = dict(Cn); I1.update(W[1])
            I1["x_all"] = RowSrc(all_fn); I1["x_own"] = own_src; I1["ctx"] = flat_src(ctx1)
            emit_layer(P, I1, G, True, flat_src(y_out), None)
    return nc


_NC_CACHE = {}


def kernel(**inputs):
    inputs = {k: np.asarray(v) for k, v in inputs.items()}
    C = host_constants()
    if "nc" not in _NC_CACHE:
        _NC_CACHE["nc"] = build_fused()
    nc = _NC_CACHE["nc"]
    in_maps = [per_core_inputs(inputs, core, C) for core in range(8)]
    res = run_bass_kernel_spmd(nc, in_maps, core_ids=list(range(8)))
    out = np.empty((4, L, D), np.float32)
    for core in range(8):
        b, hh = core // 2, core % 2
        out[b, hh * LH:(hh + 1) * LH] = np.asarray(res.results[core]["y_own"])
    return out
```

```python
import numpy as np
import concourse.bass as bass
import concourse.mybir as mybir
from concourse.bass_utils import run_bass_kernel_spmd
from contextlib import ExitStack, contextmanager

F32 = mybir.dt.float32
BF16 = mybir.dt.bfloat16
I32 = mybir.dt.int32
AF = mybir.ActivationFunctionType
ALU = mybir.AluOpType
AX = mybir.AxisListType

ENGS = ["pe", "act", "dve", "pool", "sp"]
DMA_WIN = 8
SAME_ENG_SYNC = True


class T:
    __slots__ = ("ap", "keys")

    def __init__(self, ap, keys):
        self.ap = ap
        self.keys = tuple(keys)

    def __getitem__(self, sl):
        return T(self.ap[sl], self.keys)

    def v(self, ap):
        return T(ap, self.keys)

    def k(self, *sub):
        return T(self.ap, [(self.keys[0],) + tuple(sub)])


class Prog:
    def __init__(self, nc, stack):
        self.nc = nc
        self.stack = stack
        self.cur = stack
        self.streams = {e: [] for e in ENGS}
        self.last_writer = {}
        self.readers = {}
        self.ndma = {e: 0 for e in ENGS}
        self.sigcount = {e: 0 for e in ENGS}
        self.waited = {e: {} for e in ENGS}
        self.nt = 0
        self.psum_banks = []
        self.psum_i = 0
        self.sem = {e: stack.enter_context(nc.semaphore(f"s_{e}")) for e in ENGS}
        self.dsem = {e: [stack.enter_context(nc.semaphore(f"d_{e}{i}")) for i in range(DMA_WIN)]
                     for e in ("sp", "pool", "act")}
        self.dbg = {}
        self.nops = {e: 0 for e in ENGS}

    def sb(self, shape, dt, name=None):
        self.nt += 1
        name = name or "t"
        nm = f"{name}_{self.nt}"
        t = self.cur.enter_context(self.nc.sbuf_tensor(nm, list(shape), dt))
        return T(t[:], [nm])

    def dram(self, shape, dt, name):
        self.nt += 1
        nm = f"{name}_{self.nt}"
        t = self.nc.dram_tensor(nm, list(shape), dt, kind="Internal")
        return T(t.ap(), [nm])

    def init_psum(self, n=8):
        for i in range(n):
            t = self.stack.enter_context(self.nc.psum_tensor(f"bank{i}", [128, 512], F32))
            self.psum_banks.append(T(t[:], [f"bank{i}"]))

    def ps(self):
        b = self.psum_banks[self.psum_i % len(self.psum_banks)]
        self.psum_i += 1
        return b

    @contextmanager
    def scope(self):
        prev = self.cur
        with ExitStack() as st:
            self.cur = st
            yield
            self.flush()
        self.cur = prev

    def add(self, eng, fn, reads=(), writes=(), dma=False):
        deps = set()
        rk = [k for t in reads for k in t.keys]
        wk = [k for t in writes for k in t.keys]
        for k in rk:
            if k in self.last_writer:
                deps.add(self.last_writer[k])
        for k in wk:
            if k in self.last_writer:
                deps.add(self.last_writer[k])
            for r in self.readers.get(k, ()):
                deps.add(r)
        idx = len(self.streams[eng])
        me = (eng, idx)
        deps.discard(me)
        op = dict(fn=fn, deps=deps, dma=dma, signal=False, dman=None)
        if dma:
            op["dman"] = self.ndma[eng]
            self.ndma[eng] += 1
        self.streams[eng].append(op)
        for k in rk:
            self.readers.setdefault(k, []).append(me)
        for k in wk:
            self.last_writer[k] = me
            self.readers[k] = []
        return me

    def dma(self, out, in_, eng="sp", **kw):
        o = out.ap
        i = in_.ap
        return self.add(eng, lambda e: e.dma_start(out=o, in_=i, **kw), [in_], [out], dma=True)

    def mm(self, out, lhsT, rhs, start=True, stop=True, **kw):
        return self.add("pe", lambda e: e.matmul(out.ap, lhsT.ap, rhs.ap, start=start, stop=stop, **kw),
                        [lhsT, rhs], [out])

    def transpose(self, out, in_, ident):
        return self.add("pe", lambda e: e.transpose(out.ap, in_.ap, ident.ap), [in_, ident], [out])

    def act(self, out, in_, func, bias=None, scale=None, eng="act", accum_out=None):
        reads = [in_]
        kw = {}
        if bias is not None:
            if isinstance(bias, T):
                reads.append(bias); kw["bias"] = bias.ap
            else:
                kw["bias"] = bias
        if scale is not None:
            if isinstance(scale, T):
                reads.append(scale); kw["scale"] = scale.ap
            else:
                kw["scale"] = scale
        writes = [out]
        if accum_out is not None:
            kw["accum_out"] = accum_out.ap; writes.append(accum_out)
        return self.add(eng, lambda e: e.activation(out.ap, in_.ap, func, **kw), reads, writes)

    def tt(self, out, a, b, op, eng="dve"):
        return self.add(eng, lambda e: e.tensor_tensor(out.ap, a.ap, b.ap, op), [a, b], [out])

    def ts(self, out, a, s1, s2=None, op0=ALU.mult, op1=None, eng="dve"):
        reads = [a]
        v1 = s1.ap if isinstance(s1, T) else s1
        if isinstance(s1, T): reads.append(s1)
        v2 = s2.ap if isinstance(s2, T) else s2
        if isinstance(s2, T): reads.append(s2)
        if op1 is None:
            return self.add(eng, lambda e: e.tensor_scalar(out.ap, a.ap, v1, None, op0), reads, [out])
        return self.add(eng, lambda e: e.tensor_scalar(out.ap, a.ap, v1, v2, op0, op1), reads, [out])

    def stt(self, out, a, s, b, op0, op1, eng="dve"):
        reads = [a, b]
        v = s.ap if isinstance(s, T) else s
        if isinstance(s, T): reads.append(s)
        return self.add(eng, lambda e: e.scalar_tensor_tensor(out.ap, a.ap, v, b.ap, op0, op1), reads, [out])

    def copy(self, out, in_, eng="dve"):
        if eng == "act":
            return self.add("act", lambda e: e.copy(out.ap, in_.ap), [in_], [out])
        return self.add(eng, lambda e: e.tensor_copy(out.ap, in_.ap), [in_], [out])

    def memset(self, out, val, eng="pool"):
        return self.add(eng, lambda e: e.memset(out.ap, val), [], [out])

    def recip(self, out, in_, eng="dve"):
        return self.add(eng, lambda e: e.reciprocal(out.ap, in_.ap), [in_], [out])

    def bn_stats(self, out, in_):
        return self.add("dve", lambda e: e.bn_stats(out.ap, in_.ap), [in_], [out])

    def bn_aggr(self, out, in_):
        return self.add("dve", lambda e: e.bn_aggr(out.ap, in_.ap), [in_], [out])

    def debug_out(self, name, t, shape, dt=F32):
        d = self.nc.dram_tensor(name, list(shape), dt, kind="ExternalOutput").ap()
        self.dbg[name] = d
        return self.dma(T(d, [name]), t)

    def flush(self):
        nc = self.nc
        streams = self.streams
        lasts = []
        for e in ENGS:
            for j in range(len(streams[e]) - 1, -1, -1):
                op = streams[e][j]
                if op["fn"] is not None and not op["dma"]:
                    lasts.append((e, j))
                    break
        dmas = [(e, j) for e in ENGS for j, op in enumerate(streams[e]) if op["dma"]]
        for e in ENGS:
            deps = set(l for l in lasts if l[0] != e) | set(dmas)
            streams[e].append(dict(fn=None, deps=deps, dma=False, signal=False, dman=None))
        for e in ENGS:
            for op in streams[e]:
                for (f, j) in op["deps"]:
                    d = streams[f][j]
                    if not d["dma"]:
                        if f == e and not SAME_ENG_SYNC:
                            continue
                        d["signal"] = True
        for e in ENGS:
            for op in streams[e]:
                if op["signal"]:
                    self.sigcount[e] += 1
                    op["sigval"] = self.sigcount[e]
        sem, dsem = self.sem, self.dsem

        def run(ename):
            def body(eng):
                waited = self.waited[ename]

                def wait(s, v, key):
                    if waited.get(key, 0) >= v:
                        return
                    waited[key] = v
                    eng.wait_ge(s, v)

                for op in streams[ename]:
                    for (f, j) in sorted(op["deps"]):
                        d = streams[f][j]
                        if d["dma"]:
                            n = d["dman"]
                            wait(dsem[f][n % DMA_WIN], 16 * (n // DMA_WIN + 1), (f, n % DMA_WIN))
                        else:
                            if f == ename and not SAME_ENG_SYNC:
                                continue
                            wait(sem[f], d["sigval"], f)
                    if op["dma"]:
                        n = op["dman"]
                        if n >= DMA_WIN:
                            wait(dsem[ename][n % DMA_WIN], 16 * (n // DMA_WIN), (ename, n % DMA_WIN))
                    if op["fn"] is None:
                        continue
                    ins = op["fn"](eng)
                    if op["dma"]:
                        ins.then_inc(dsem[ename][op["dman"] % DMA_WIN], 16)
                    elif op["signal"]:
                        ins.then_inc(sem[ename], 1)
            return body

        with nc.Block() as block:
            block.tensor(run("pe"))
            block.scalar(run("act"))
            block.vector(run("dve"))
            block.gpsimd(run("pool"))
            block.sync(run("sp"))
        for e in ENGS:
            self.nops[e] += len(streams[e])
        self.streams = {e: [] for e in ENGS}
        self.last_writer = {}
        self.readers = {}


class Rot:
    def __init__(self, P, shape, dt, name, n):
        self.tiles = [P.sb(shape, dt, f"{name}{i}") for i in range(n)]
        self.i = 0

    def __call__(self):
        t = self.tiles[self.i % len(self.tiles)]
        self.i += 1
        return t


def _ps6(self):
    b = self.psum_banks[self.psum_i % 6]
    self.psum_i += 1
    return b


def _psacc(self):
    self.acc_i = getattr(self, "acc_i", 0) + 1
    return self.psum_banks[6 + self.acc_i % 2]


Prog.ps = _ps6
Prog.ps_acc = _psacc


D = 1024
L = 4096
LH = 2048
CTX = 256
NK = CTX + L
NKT = NK // 128
EPS = 1e-6
OFF_AK, OFF_AV, OFF_CKV, OFF_CKR, OFF_U, NST = 0, 128, 256, 512, 544, 1056
OFF_AQ, OFF_CQ, OFF_GATE, NIN = 1056, 1568, 2336, 5408
ALPHA = (2.0 * 2) ** 0.25


def dT(ap, name):
    return T(ap, [name])


class Ctx:
    pass


class RowSrc:
    def __init__(self, fn):
        self.fn = fn

    def rows(self, r0):
        return self.fn(r0)


def flat_src(t):
    return RowSrc(lambda r0: t.v(t.ap[r0:r0 + 128, :]))


WEIGHT_SHAPES = {
    "w_mod": [D, 6 * D], "b_mod": [6 * D], "w_in": [D, NIN], "a_q_gain": [64], "a_k_gain": [64],
    "c_q_a_gain": [768], "c_kv_a_gain": [256], "c_w_qb": [768, 768], "c_w_kvb": [256, 1024],
    "s5_a_re": [2, 32, 64], "s5_a_im": [2, 32, 64], "s5_log_dt": [2, 32],
    "s5_b_re": [2, 32, 64, 16], "s5_b_im": [2, 32, 64, 16], "s5_c_re": [2, 32, 16, 64], "s5_c_im": [2, 32, 16, 64],
    "s5_d": [512], "s5_w_glu": [512, 1024], "w_branch_a": [512, D], "w_branch_s5": [512, D], "w_branch_c": [512, D],
    "w_out": [D, D], "ln1_g": [D], "ln1_b": [D], "w_up": [D, 4 * D], "w_down": [4 * D, D], "ln2_g": [D], "ln2_b": [D],
}
CONST_SHAPES = {
    "x_all": [L, D], "x_own": [LH, D], "ctx": [CTX, D], "cvec": [2, D],
    "ident": [128, 128], "blk64": [128, 128], "perm64": [128, 128], "perm32": [32, 32],
    "ropek_cos": [128, L], "ropek_sin": [128, L], "ropeq_cos": [128, LH], "ropeq_sin": [128, LH],
    "rope32k_cos": [32, L], "rope32k_sin": [32, L], "rope32q_cos": [32, LH], "rope32q_sin": [32, LH],
    "sel": [128, 2], "swap": [128, 128], "mask_f": [128, 128], "mask_b": [128, 128],
    "sel8": [128, 64, 128], "sel8T": [128, 64, 128],
}


def declare_consts(nc):
    return {k: dT(nc.dram_tensor(k, list(v), F32, kind="ExternalInput").ap(), k) for k, v in CONST_SHAPES.items()}


def declare_weights(nc, l):
    return {k: dT(nc.dram_tensor(f"{k}_{l}", list(v), F32, kind="ExternalInput").ap(), f"{k}_{l}")
            for k, v in WEIGHT_SHAPES.items()}


def declare_inputs(nc, last):
    I = declare_consts(nc)
    I.update(declare_weights(nc, 1 if last else 0))
    return I


def host_constants():
    import math
    C = {}
    C["ident"] = np.eye(128, dtype=np.float32)
    blk = np.zeros((128, 128), np.float32); blk[:64, :64] = 1 / 64; blk[64:, 64:] = 1 / 64
    C["blk64"] = blk

    def perm_and_sign(dim):
        half = dim // 2; q = half // 2
        Pm = np.zeros((dim, dim), np.float32)
        sg = np.zeros(dim, np.float32)
        for m in range(dim):
            if (m % half) < q:
                Pm[m + q, m] = 1; sg[m] = -1
            else:
                Pm[m - q, m] = 1; sg[m] = 1
        return Pm, sg
    P64, s64 = perm_and_sign(64)
    p128 = np.zeros((128, 128), np.float32); p128[:64, :64] = P64; p128[64:, 64:] = P64
    C["perm64"] = p128
    P32, s32 = perm_and_sign(32)
    C["perm32"] = P32

    def tables(dim):
        half = dim // 2
        inv = 10000.0 ** (-np.arange(0, half, 2, dtype=np.float32) / half)
        rows = L // 64
        row = np.repeat(np.arange(rows, dtype=np.float32), 64)
        col = np.tile(np.arange(64, dtype=np.float32), rows)
        ang_r = row[:, None] * inv; ang_c = col[:, None] * inv
        ang = np.concatenate([ang_r, ang_r, ang_c, ang_c], axis=-1).astype(np.float32)
        return np.cos(ang).T.astype(np.float32), np.sin(ang).T.astype(np.float32)
    c64, s64t = tables(64)
    s64t = s64t * s64[:, None]
    C["ropek_cos"] = np.concatenate([c64, c64], 0); C["ropek_sin"] = np.concatenate([s64t, s64t], 0)
    c32, s32t = tables(32)
    s32t = s32t * s32[:, None]
    C["rope32k_cos"] = c32; C["rope32k_sin"] = s32t
    sw = np.zeros((128, 128), np.float32)
    for p in range(64):
        sw[p, 64 + p] = 1; sw[64 + p, p] = 1
    C["swap"] = sw
    ii = np.arange(128) // 16
    C["mask_f"] = (ii[:, None] <= ii[None, :]).astype(np.float32)
    C["mask_b"] = (ii[:, None] >= ii[None, :]).astype(np.float32)
    sel = np.zeros((128, 64, 128), np.float32)
    for gl in range(8):
        for i in range(8):
            for c in range(16):
                sel[gl * 16 + c, gl * 8 + i, i * 16 + c] = 1
    C["sel8"] = sel
    C["sel8T"] = np.ascontiguousarray(sel.transpose(2, 1, 0))
    return C


def per_core_inputs(inputs, core, C, layers=(0, 1)):
    b, hh = core // 2, core % 2
    m = {}
    xb = inputs["x"][b]
    m["x_all"] = xb; m["x_own"] = xb[hh * LH:(hh + 1) * LH]; m["ctx"] = inputs["ctx"][b]
    m["cvec"] = np.stack([inputs["c"][b], inputs["c_ctx"]], 0)
    for l in layers:
        for k in WEIGHT_SHAPES:
            m[f"{k}_{l}"] = inputs[k][l]
    for k in ["ident", "blk64", "perm64", "perm32", "ropek_cos", "ropek_sin", "rope32k_cos", "rope32k_sin",
              "swap", "mask_f", "mask_b", "sel8", "sel8T"]:
        m[k] = C[k]
    sl = slice(hh * LH, (hh + 1) * LH)
    m["ropeq_cos"] = C["ropek_cos"][:, sl]; m["ropeq_sin"] = C["ropek_sin"][:, sl]
    m["rope32q_cos"] = C["rope32k_cos"][:, sl]; m["rope32q_sin"] = C["rope32k_sin"][:, sl]
    s = np.zeros((128, 2), np.float32); s[:, hh] = 1
    m["sel"] = s
    return {k: np.ascontiguousarray(v, dtype=np.float32) for k, v in m.items()}


def rstd_from_ms(P, out, ms, n, eps=EPS, eng_a="act"):
    P.ts(out, ms, eps, None, op0=ALU.add)
    P.act(out, out, AF.Sqrt)
    P.recip(out, out)


def stage_prep(P, I, G):
    G.ident = P.sb([128, 128], BF16, "ident"); P.dma(G.ident, I["ident"], eng="pool")
    G.blk64 = P.sb([128, 128], BF16, "blk64"); P.dma(G.blk64, I["blk64"], eng="pool")
    G.perm64 = P.sb([128, 128], BF16, "perm64"); P.dma(G.perm64, I["perm64"], eng="pool")
    G.perm32 = P.sb([32, 32], BF16, "perm32"); P.dma(G.perm32, I["perm32"], eng="pool")
    G.ones = P.sb([128, 128], BF16, "ones"); P.memset(G.ones, 1.0)
    G.modT = P.sb([128, 48, 2], F32, "modT")
    G.sel = P.sb([128, 2], F32, "sel"); P.dma(G.sel, I["sel"])
    with P.scope():
        cT = P.sb([128, 8, 2], F32, "cT")
        for t in range(2):
            src = I["cvec"].ap[t, :].rearrange("(k p) -> p k", p=128)
            P.dma(cT[:, :, t], dT(src, "cvec"), allow_slow_non_contiguous=True)
        sT = P.sb([128, 8, 2], BF16, "sT")
        P.act(sT, cT, AF.Silu)
        bT = P.sb([128, 48], F32, "bT")
        P.dma(bT, dT(I["b_mod"].ap.rearrange("(j p) -> p j", p=128), "b_mod"), allow_slow_non_contiguous=True)
        wbufs = [P.sb([128, 8, 128], BF16, f"wm{i}") for i in range(3)]
        for j in range(48):
            wb = wbufs[j % 3]
            src = I["w_mod"].ap[:, j * 128:(j + 1) * 128].rearrange("(k p) c -> p k c", p=128)
            P.dma(wb, dT(src, "w_mod"), eng="pool")
            ps = P.ps()
            for kc in range(8):
                P.mm(ps[:, 0:2], wb[:, kc, :], sT[:, kc, :], start=(kc == 0), stop=(kc == 7))
            P.ts(G.modT[:, j, :], ps[:, 0:2], bT[:, j:j + 1], None, op0=ALU.add)
        for w in (1, 4):
            P.ts(G.modT[:, w * 8:(w + 1) * 8, :], G.modT[:, w * 8:(w + 1) * 8, :], 1.0, None, op0=ALU.add)
        G.modD = P.dram([2, 6 * D], F32, "modD")
        for t in range(2):
            dst = G.modD.ap[t, :].rearrange("(j p) -> p j", p=128)
            P.dma(G.modD.v(dst), G.modT[:, :, t], allow_slow_non_contiguous=True)


def ln_tile_to_hT(P, G, xt, hT_dst, t_idx, which_sh, which_sc):
    st = P.sb([128, 2, 6], F32, "bnst")
    mv = P.sb([128, 2], F32, "mv")
    for hh in range(2):
        P.bn_stats(st[:, hh, :], xt[:, hh * 512:(hh + 1) * 512])
    P.bn_aggr(mv, st)
    rs = P.sb([128, 1], F32, "rs")
    rstd_from_ms(P, rs, mv[:, 1:2], 1)
    xn = P.sb([128, D], BF16, "xn")
    P.ts(xn, xt, mv[:, 0:1], rs, op0=ALU.subtract, op1=ALU.mult)
    ps = P.ps()
    psb = ps.v(ps.ap.bitcast(BF16))
    for kc in range(8):
        P.transpose(psb[:, kc * 128:(kc + 1) * 128], xn[:, kc * 128:(kc + 1) * 128], G.ident)
    for kc in range(8):
        P.act(hT_dst[:, kc, :], psb[:, kc * 128:(kc + 1) * 128], AF.Identity,
              bias=G.modT[:, which_sh * 8 + kc, t_idx:t_idx + 1], scale=G.modT[:, which_sc * 8 + kc, t_idx:t_idx + 1])


_CAST = {"i": 0}


def wload(P, stg, dst, src, engs=("pool", "dve", "act")):
    shape = list(dst.ap.shape)[1:]
    n = 1
    for v in shape:
        n *= v
    st = stg()
    sv = st.ap[0:dst.ap.shape[0], 0:n]
    if len(shape) == 2:
        sv = sv.rearrange("p (a b) -> p a b", a=shape[0])
    svt = st.v(sv)
    P.dma(svt, src)
    e = engs[_CAST["i"] % len(engs)]
    _CAST["i"] += 1
    P.copy(dst, svt, eng=e)


def mmg(P, items, K):
    for k in range(K):
        for (out, lf, rf) in items:
            P.mm(out, lf(k), rf(k), start=(k == 0), stop=(k == K - 1))


def make_ln_pools(P):
    R = Ctx()
    R.xt = Rot(P, [128, D], F32, "xt", 2)
    R.st = Rot(P, [128, 2, 6], F32, "bnst", 2)
    R.mv = Rot(P, [128, 2], F32, "mv", 2)
    R.rs = Rot(P, [128, 1], F32, "rs", 2)
    R.xn = Rot(P, [128, D], BF16, "xn", 2)
    return R


def ln_tile_to_hT2(P, G, R, src_dram, hT_dst, t_idx, which_sh, which_sc):
    xt = R.xt()
    P.dma(xt, src_dram)
    st = R.st(); mv = R.mv(); rs = R.rs(); xn = R.xn()
    for hh in range(2):
        P.bn_stats(st[:, hh, :], xt[:, hh * 512:(hh + 1) * 512])
    P.bn_aggr(mv, st)
    rstd_from_ms(P, rs, mv[:, 1:2], 1)
    P.ts(xn, xt, mv[:, 0:1], rs, op0=ALU.subtract, op1=ALU.mult)
    ps = P.ps()
    psb = ps.v(ps.ap.bitcast(BF16))
    for kc in range(8):
        P.transpose(psb[:, kc * 128:(kc + 1) * 128], xn[:, kc * 128:(kc + 1) * 128], G.ident)
    for kc in range(8):
        P.act(hT_dst[:, kc, :], psb[:, kc * 128:(kc + 1) * 128], AF.Identity,
              bias=G.modT[:, which_sh * 8 + kc, t_idx:t_idx + 1], scale=G.modT[:, which_sc * 8 + kc, t_idx:t_idx + 1])


def rope_apply(P, dst, src_bf, perm, cos, sin, tmp, n, rows=128, psfn=None):
    ps = (psfn or P.ps)()
    P.mm(ps[0:rows, 0:n], perm, src_bf)
    P.tt(tmp, src_bf, cos, ALU.mult)
    P.tt(dst, ps[0:rows, 0:n], sin, ALU.mult)
    P.tt(dst, dst, tmp, ALU.add)


def alloc_persist(P, G):
    G.kT = P.sb([128, NK], BF16, "kT")
    G.Vg = P.sb([128, NKT, 2, 128], BF16, "Vg")
    G.ckvT = P.sb([128, 2, NK], BF16, "ckvT")
    G.krT = P.sb([32, NK], BF16, "krT")


def stage_A(P, I, G):
    P.memset(G.Vg[:, :, :, 64:128], 1.0)
    with P.scope():
        w_st = P.sb([128, 8, NST], BF16, "w_st")
        stg = Rot(P, [128, NST], F32, "stg", 2)
        for kc in range(8):
            wload(P, stg, w_st[:, kc, :], dT(I["w_in"].ap[kc * 128:(kc + 1) * 128, 0:NST], "w_in"))
        kg = P.sb([128, 1], F32, "kg")
        for r in range(2):
            P.dma(kg[r * 64:(r + 1) * 64, :], dT(I["a_k_gain"].ap.rearrange("(p o) -> p o", o=1), "akg"))
        cg = P.sb([128, 2], F32, "cg")
        P.dma(cg, dT(I["c_kv_a_gain"].ap.rearrange("(j p) -> p j", p=128), "ckg"), allow_slow_non_contiguous=True)
        R = make_ln_pools(P)
        hTs = Rot(P, [128, 8, 512], BF16, "hT", 2)
        sq = Rot(P, [128, 512], BF16, "sq", 2)
        rst = Rot(P, [128, 512], F32, "rst", 2)
        knb = Rot(P, [128, 512], BF16, "knb", 3)
        tmp = Rot(P, [128, 512], F32, "tmp", 2)
        cosb = Rot(P, [128, 512], F32, "cosb", 2)
        sinb = Rot(P, [128, 512], F32, "sinb", 2)
        cos32 = Rot(P, [32, 512], F32, "cos32", 2)
        sin32 = Rot(P, [32, 512], F32, "sin32", 2)
        blocks = [(0, 2, True)] + [(2 + 4 * i, 4, False) for i in range(8)]
        for (t0, nt, is_ctx) in blocks:
            n = nt * 128
            c0 = t0 * 128
            hT = hTs()
            for ti in range(nt):
                t = t0 + ti
                src = I["ctx"].rows(t * 128) if is_ctx else I["x_all"].rows((t - 2) * 128)
                ln_tile_to_hT2(P, G, R, src, hT[:, :, ti * 128:(ti + 1) * 128], 1 if is_ctx else 0, 0, 1)
            if not is_ctx:
                lc = c0 - CTX
                cb, sb_, c32, s32 = cosb(), sinb(), cos32(), sin32()
                P.dma(cb[:, 0:n], dT(I["ropek_cos"].ap[:, lc:lc + n], "rc"))
                P.dma(sb_[:, 0:n], dT(I["ropek_sin"].ap[:, lc:lc + n], "rs"))
                P.dma(c32[:, 0:n], dT(I["rope32k_cos"].ap[:, lc:lc + n], "rc32"))
                P.dma(s32[:, 0:n], dT(I["rope32k_sin"].ap[:, lc:lc + n], "rs32"))
            pk = P.ps(); pc = [P.ps(), P.ps()]; pr = P.ps()
            mmg(P, [(pk[:, 0:n], lambda k: w_st[:, k, OFF_AK:OFF_AK + 128], lambda k: hT[:, k, 0:n]),
                    (pc[0][:, 0:n], lambda k: w_st[:, k, OFF_CKV:OFF_CKV + 128], lambda k: hT[:, k, 0:n]),
                    (pc[1][:, 0:n], lambda k: w_st[:, k, OFF_CKV + 128:OFF_CKV + 256], lambda k: hT[:, k, 0:n]),
                    (pr[0:32, 0:n], lambda k: w_st[:, k, OFF_CKR:OFF_CKR + 32], lambda k: hT[:, k, 0:n])], 8)
            s = sq()
            P.act(s[:, 0:n], pk[:, 0:n], AF.Square)
            pm = P.ps()
            P.mm(pm[:, 0:n], G.blk64, s[:, 0:n])
            rs = rst()
            rstd_from_ms(P, rs[:, 0:n], pm[:, 0:n], n)
            kn = knb()
            P.stt(kn[:, 0:n], pk[:, 0:n], kg[:, 0:1], rs[:, 0:n], ALU.mult, ALU.mult)
            if is_ctx:
                P.copy(G.kT[:, c0:c0 + n], kn[:, 0:n])
            else:
                rope_apply(P, G.kT[:, c0:c0 + n], kn[:, 0:n], G.perm64, cb[:, 0:n], sb_[:, 0:n], tmp()[:, 0:n], n)
            ss = [sq(), sq()]
            for j in range(2):
                P.act(ss[j][:, 0:n], pc[j][:, 0:n], AF.Square)
            pm = P.ps()
            for j in range(2):
                P.mm(pm[:, 0:n], G.ones, ss[j][:, 0:n], start=(j == 0), stop=(j == 1))
            rs = rst()
            P.ts(rs[:, 0:n], pm[:, 0:n], 1.0 / 256, EPS, op0=ALU.mult, op1=ALU.add)
            P.act(rs[:, 0:n], rs[:, 0:n], AF.Sqrt)
            P.recip(rs[:, 0:n], rs[:, 0:n])
            for j in range(2):
                P.stt(G.ckvT[:, j, c0:c0 + n], pc[j][:, 0:n], cg[:, j:j + 1], rs[:, 0:n], ALU.mult, ALU.mult)
            if is_ctx:
                P.copy(G.krT[:, c0:c0 + n], pr[0:32, 0:n])
            else:
                kr = knb()
                P.copy(kr[0:32, 0:n], pr[0:32, 0:n])
                rope_apply(P, G.krT[:, c0:c0 + n], kr[0:32, 0:n], G.perm32, c32[:, 0:n], s32[:, 0:n], tmp()[0:32, 0:n], n, rows=32)
            pvs = [P.ps() for _ in range(nt)]
            mmg(P, [(pvs[ti][:, 0:128], (lambda k, ti=ti: hT[:, k, ti * 128:(ti + 1) * 128]),
                     lambda k: w_st[:, k, OFF_AV:OFF_AV + 128]) for ti in range(nt)], 8)
            for ti in range(nt):
                pv = pvs[ti]
                P.copy(G.Vg[:, t0 + ti, :, 0:64], pv.v(pv.ap[:, 0:128].rearrange("p (a b) -> p a b", a=2)), eng="act")
            pus = [P.ps() for _ in range(4)]
            mmg(P, [(pus[j][:, 0:n], (lambda k, j=j: w_st[:, k, OFF_U + j * 128:OFF_U + (j + 1) * 128]),
                     lambda k: hT[:, k, 0:n]) for j in range(4)], 8)
            for j in range(4):
                dst = G.uT.v(G.uT.ap[:, j, :, c0 // 8:(c0 + n) // 8].rearrange("p i c -> p c i"))
                src = pus[j].v(pus[j].ap[:, 0:n].rearrange("p (c i) -> p c i", i=8))
                P.copy(dst, src, eng=("act" if j % 2 else "dve"))


def own_blocks(last):
    bl = [(i * 512, 512, False, i * 512) for i in range(4)]
    if not last:
        bl.append((LH, 256, True, 0))
    return bl


def stage_B1(P, I, G, last):
    NQ = LH + (0 if last else CTX)
    G.NQ = NQ
    G.qg = P.sb([128, 4, NQ], BF16, "qg")
    G.qm = P.sb([96, 8, NQ], BF16, "qm")
    G.gD = P.dram([3 * D, NQ], BF16, "gD")
    with P.scope():
        hT = P.sb([128, 8, NQ], BF16, "hTall")
        with P.scope():
            R = make_ln_pools(P)
            for (c0, n, is_ctx, r0) in own_blocks(last):
                for ti in range(n // 128):
                    src = (I["ctx"] if is_ctx else I["x_own"]).rows(r0 + ti * 128)
                    ln_tile_to_hT2(P, G, R, src, hT[:, :, c0 + ti * 128:c0 + (ti + 1) * 128], 1 if is_ctx else 0, 0, 1)
        qgain = P.sb([128, 1], F32, "qgain")
        for r in range(2):
            P.dma(qgain[r * 64:(r + 1) * 64, :], dT(I["a_q_gain"].ap.rearrange("(p o) -> p o", o=1), "aqg"))
        cqg = P.sb([128, 6], F32, "cqg")
        P.dma(cqg, dT(I["c_q_a_gain"].ap.rearrange("(j p) -> p j", p=128), "cqg"), allow_slow_non_contiguous=True)
        perm32h = P.sb([96, 32], BF16, "perm32h")
        P.dma(perm32h[64:96, :], I["perm32"], eng="pool")
        cosqR = Rot(P, [128, 512], F32, "cosq", 2)
        sinqR = Rot(P, [128, 512], F32, "sinq", 2)
        cos32R = Rot(P, [96, 512], F32, "cos32q", 2)
        sin32R = Rot(P, [96, 512], F32, "sin32q", 2)
        sq = Rot(P, [128, 512], BF16, "sq", 6)
        rst = Rot(P, [128, 512], F32, "rst", 2)
        knb = Rot(P, [128, 512], BF16, "knb", 2)
        tmp = Rot(P, [128, 512], F32, "tmp", 2)
        with P.scope():
            wq = P.sb([128, 8, 4, 128], BF16, "wq")
            stg = Rot(P, [128, 768], F32, "stg", 2)
            for kc in range(8):
                for hf in range(2):
                    src = I["w_in"].ap[kc * 128:(kc + 1) * 128, OFF_AQ + hf * 256:OFF_AQ + (hf + 1) * 256].rearrange("p (a b) -> p a b", a=4)
                    wload(P, stg, wq[:, kc, :, hf * 64:(hf + 1) * 64], dT(src, "w_in"))
            for (c0, n, is_ctx, r0) in own_blocks(last):
                if not is_ctx:
                    cosq = cosqR(); sinq = sinqR()
                    P.dma(cosq, dT(I["ropeq_cos"].ap[:, c0:c0 + n], "rqc"))
                    P.dma(sinq, dT(I["ropeq_sin"].ap[:, c0:c0 + n], "rqs"))
                pks = [P.ps() for _ in range(4)]
                mmg(P, [(pks[hd][:, 0:n], (lambda k, hd=hd: wq[:, k, hd, :]), lambda k: hT[:, k, c0:c0 + n]) for hd in range(4)], 8)
                for hd in range(4):
                    pk = pks[hd]
                    s = sq()
                    P.act(s[:, 0:n], pk[:, 0:n], AF.Square)
                    pm = P.ps_acc()
                    P.mm(pm[:, 0:n], G.blk64, s[:, 0:n])
                    rs = rst()
                    rstd_from_ms(P, rs[:, 0:n], pm[:, 0:n], n)
                    if is_ctx:
                        P.stt(G.qg[:, hd, c0:c0 + n], pk[:, 0:n], qgain[:, 0:1], rs[:, 0:n], ALU.mult, ALU.mult)
                    else:
                        kn = knb()
                        P.stt(kn[:, 0:n], pk[:, 0:n], qgain[:, 0:1], rs[:, 0:n], ALU.mult, ALU.mult)
                        rope_apply(P, G.qg[:, hd, c0:c0 + n], kn[:, 0:n], G.perm64, cosq[:, 0:n], sinq[:, 0:n],
                                   tmp()[:, 0:n], n, psfn=P.ps_acc)
        with P.scope():
            wc = P.sb([128, 8, 768], BF16, "wc")
            stg = Rot(P, [128, 768], F32, "stg", 1)
            for kc in range(8):
                wload(P, stg, wc[:, kc, :], dT(I["w_in"].ap[kc * 128:(kc + 1) * 128, OFF_CQ:OFF_CQ + 768], "w_in"))
            wqb = P.sb([128, 6, 768], BF16, "wqb")
            for j in range(6):
                wload(P, stg, wqb[:, j, :], dT(I["c_w_qb"].ap[j * 128:(j + 1) * 128, :], "wqb"))
            cqn = P.sb([128, 6, 512], BF16, "cqn")
            qrb = Rot(P, [96, 512], BF16, "qrb", 2)
            for (c0, n, is_ctx, r0) in own_blocks(last):
                if not is_ctx:
                    cos32 = cos32R(); sin32 = sin32R()
                    P.dma(cos32[64:96, :], dT(I["rope32q_cos"].ap[:, c0:c0 + n], "rqc32"))
                    P.dma(sin32[64:96, :], dT(I["rope32q_sin"].ap[:, c0:c0 + n], "rqs32"))
                pcs = [P.ps() for _ in range(6)]
                mmg(P, [(pcs[j][:, 0:n], (lambda k, j=j: wc[:, k, j * 128:(j + 1) * 128]), lambda k: hT[:, k, c0:c0 + n]) for j in range(6)], 8)
                sqs = []
                for j in range(6):
                    s = sq()
                    P.act(s[:, 0:n], pcs[j][:, 0:n], AF.Square)
                    sqs.append(s)
                pm = P.ps_acc()
                for j in range(6):
                    P.mm(pm[:, 0:n], G.ones, sqs[j][:, 0:n], start=(j == 0), stop=(j == 5))
                rs = rst()
                P.ts(rs[:, 0:n], pm[:, 0:n], 1.0 / 768, EPS, op0=ALU.mult, op1=ALU.add)
                P.act(rs[:, 0:n], rs[:, 0:n], AF.Sqrt)
                P.recip(rs[:, 0:n], rs[:, 0:n])
                for j in range(6):
                    P.stt(cqn[:, j, 0:n], pcs[j][:, 0:n], cqg[:, j:j + 1], rs[:, 0:n], ALU.mult, ALU.mult)
                for hg in range(2):
                    pqs = [P.ps() for _ in range(4)]
                    mmg(P, [(pqs[i][0:96, 0:n], (lambda k, h=hg * 4 + i: wqb[:, k, h * 96:(h + 1) * 96]), lambda k: cqn[:, k, 0:n]) for i in range(4)], 6)
                    for i in range(4):
                        h = hg * 4 + i
                        pq = pqs[i]
                        if is_ctx:
                            P.copy(G.qm[:, h, c0:c0 + n], pq[0:96, 0:n], eng="act")
                        else:
                            P.copy(G.qm[0:64, h, c0:c0 + n], pq[0:64, 0:n], eng="act")
                            qr = qrb()
                            P.copy(qr[64:96, 0:n], pq[64:96, 0:n])
                            pr = P.ps_acc()
                            P.mm(pr[64:96, 0:n], perm32h[64:96, :], qr[64:96, 0:n])
                            t = tmp()
                            P.tt(t[64:96, 0:n], qr[64:96, 0:n], cos32[64:96, 0:n], ALU.mult)
                            t2 = tmp()
                            P.tt(t2[64:96, 0:n], pr[64:96, 0:n], sin32[64:96, 0:n], ALU.mult)
                            P.tt(G.qm[64:96, h, c0:c0 + n], t[64:96, 0:n], t2[64:96, 0:n], ALU.add)
        with P.scope():
            wg = Rot(P, [128, 8, 512], BF16, "wg", 2)
            stg = Rot(P, [128, 512], F32, "stg", 3)
            gb = Rot(P, [128, 512], BF16, "gb", 3)
            for gi in range(6):
                w = wg()
                for kc in range(8):
                    wload(P, stg, w[:, kc, :], dT(I["w_in"].ap[kc * 128:(kc + 1) * 128, OFF_GATE + gi * 512:OFF_GATE + (gi + 1) * 512], "w_in"), engs=("pool", "dve"))
                for (c0, n, is_ctx, r0) in own_blocks(last):
                    pgs = [P.ps() for _ in range(4)]
                    mmg(P, [(pgs[oc][:, 0:n], (lambda k, oc=oc: w[:, k, oc * 128:(oc + 1) * 128]), lambda k: hT[:, k, c0:c0 + n]) for oc in range(4)], 8)
                    for oc in range(4):
                        g = gb()
                        P.act(g[:, 0:n], pgs[oc][:, 0:n], AF.Sigmoid)
                        row = (gi * 4 + oc) * 128
                        P.dma(G.gD.v(G.gD.ap[row:row + 128, c0:c0 + n]), g[:, 0:n])


def run_attn(P, chains, pT, scale):
    LA = 2
    nkt = chains[0][1]
    pls = [dict() for _ in chains]
    for kt in range(nkt + LA):
        if kt < nkt:
            for ci, (po, _, n, slf, srhs, vlf) in enumerate(chains):
                pss = P.ps()
                P.mm(pss[:, 0:n], slf(kt), srhs)
                p = pT()
                P.act(p[:, 0:n], pss[:, 0:n], AF.Exp, scale=scale)
                pls[ci][kt] = p
        jj = kt - LA
        if jj >= 0:
            for ci, (po, _, n, slf, srhs, vlf) in enumerate(chains):
                P.mm(po[:, 0:n], vlf(jj), pls[ci].pop(jj)[:, 0:n], start=(jj == 0), stop=(jj == nkt - 1))


def block_groups(last):
    bl = own_blocks(last)
    groups = [bl[0:2], bl[2:4]]
    if not last:
        groups.append(bl[4:5])
    return groups


def attn_finish(P, po, n, rec, yo, dst):
    r = rec()
    P.recip(r[64:128, 0:n], po[64:128, 0:n])
    y = yo()
    P.tt(y[:, 0:n], po[0:64, 0:n], r[64:128, 0:n], ALU.mult)
    P.dma(dst, y[:, 0:n])


def stage_B2(P, I, G, last):
    NQ = G.NQ
    G.yaD = P.dram([512, NQ], BF16, "yaD")
    with P.scope():
        pT = Rot(P, [128, 512], BF16, "pT", 8)
        rec = Rot(P, [128, 512], F32, "rec", 2)
        yo = Rot(P, [64, 512], BF16, "yo", 2)
        kTp = P.sb([128, 2, NK], BF16, "kTp")
        P.memset(kTp, 0.0)
        P.copy(kTp[0:64, 0, :], G.kT[0:64, :], eng="pool")
        P.copy(kTp[64:128, 1, :], G.kT[64:128, :], eng="dve")
        for hd in range(4):
            for kvh in range(2):
                head = hd + 4 * kvh
                for grp in block_groups(last):
                    chains = []
                    for (c0, n, is_ctx, r0) in grp:
                        nkt = 2 if is_ctx else NKT
                        chains.append((P.ps_acc(), nkt, n, (lambda kt: kTp[:, kvh, kt * 128:(kt + 1) * 128]),
                                       G.qg[:, hd, c0:c0 + n], (lambda kt: G.Vg[:, kt, kvh, :])))
                    run_attn(P, chains, pT, 0.125)
                    for (po, _, n, _, _, _), (c0, _, _, _) in zip(chains, grp):
                        attn_finish(P, po, n, rec, yo, G.yaD.v(G.yaD.ap[head * 64:(head + 1) * 64, c0:c0 + n]))


def stage_B3(P, I, G, last):
    NQ = G.NQ
    G.ycD = P.dram([512, NQ], BF16, "ycD")
    with P.scope():
        wkv = P.sb([128, 2, 1024], BF16, "wkv")
        stg = Rot(P, [128, 1024], F32, "stg", 1)
        for j in range(2):
            wload(P, stg, wkv[:, j, :], dT(I["c_w_kvb"].ap[j * 128:(j + 1) * 128, :], "wkvb"))
        Kh = Rot(P, [96, NK], BF16, "Kh", 2)
        Vh = [P.sb([128, NKT, 128], BF16, f"Vh{i}") for i in range(2)]
        for v in Vh:
            P.memset(v[:, :, 64:128], 1.0)
        pT = Rot(P, [128, 512], BF16, "pT", 8)
        rec = Rot(P, [128, 512], F32, "rec", 2)
        yo = Rot(P, [64, 512], BF16, "yo", 2)
        scale = 96 ** -0.5
        for h in range(8):
            K = Kh(); V = Vh[h % 2]
            cbs = [(cb * 512, min(512, NK - cb * 512)) for cb in range(9)]
            for g0 in range(0, 9, 3):
                grp = cbs[g0:g0 + 3]
                pks = [P.ps() for _ in grp]
                mmg(P, [(pks[i][0:64, 0:n], lambda k: wkv[:, k, h * 128:h * 128 + 64], (lambda k, k0=k0, n=n: G.ckvT[:, k, k0:k0 + n]))
                        for i, (k0, n) in enumerate(grp)], 2)
                for i, (k0, n) in enumerate(grp):
                    P.copy(K[0:64, k0:k0 + n], pks[i][0:64, 0:n], eng=("act" if i % 2 else "dve"))
            P.copy(K[64:96, :], G.krT[0:32, :], eng="pool")
            for g0 in range(0, NKT, 4):
                kts = list(range(g0, min(g0 + 4, NKT)))
                pvs = [P.ps() for _ in kts]
                mmg(P, [(pvs[i][:, 0:64], (lambda k, kt=kt: G.ckvT[:, k, kt * 128:(kt + 1) * 128]),
                         lambda k: wkv[:, k, h * 128 + 64:h * 128 + 128]) for i, kt in enumerate(kts)], 2)
                for i, kt in enumerate(kts):
                    P.copy(V[:, kt, 0:64], pvs[i][:, 0:64], eng=("act" if kt % 2 else "dve"))
            for grp in block_groups(last):
                chains = []
                for (c0, n, is_ctx, r0) in grp:
                    nkt = 2 if is_ctx else NKT
                    chains.append((P.ps_acc(), nkt, n, (lambda kt: K[0:96, kt * 128:(kt + 1) * 128]),
                                   G.qm[0:96, h, c0:c0 + n], (lambda kt: V[:, kt, :])))
                run_attn(P, chains, pT, scale)
                for (po, _, n, _, _, _), (c0, _, _, _) in zip(chains, grp):
                    attn_finish(P, po, n, rec, yo, G.ycD.v(G.ycD.ap[h * 64:(h + 1) * 64, c0:c0 + n]))


def bcast_load(P, dst, src_ap_1d):
    P.dma(dst, dT(src_ap_1d.partition_broadcast(128), "bc"))


def stage_B4a(P, I, G, last):
    NQ = G.NQ
    G.xmidD = P.dram([NQ, D], F32, "xmidD")
    G.h2D = P.dram([D, NQ], BF16, "h2D")
    with P.scope():
        wglu = P.sb([128, 4, 1024], BF16, "wglu")
        wb = [P.sb([128, 4, 1024], BF16, f"wb{i}") for i in range(3)]
        wout = P.sb([128, 8, 1024], BF16, "wout")
        stg = Rot(P, [128, 1024], F32, "stg", 2)
        for j in range(4):
            wload(P, stg, wglu[:, j, :], dT(I["s5_w_glu"].ap[j * 128:(j + 1) * 128, :], "w"))
            for i, nm in enumerate(["w_branch_a", "w_branch_s5", "w_branch_c"]):
                wload(P, stg, wb[i][:, j, :], dT(I[nm].ap[j * 128:(j + 1) * 128, :], "w"))
        for kc in range(8):
            wload(P, stg, wout[:, kc, :], dT(I["w_out"].ap[kc * 128:(kc + 1) * 128, :], "w"))
        g1b = [P.sb([128, D], F32, f"g1b{t}") for t in range(2)]
        for t in range(2):
            P.dma(g1b[t], dT(G.modD.ap[t, 2 * D:3 * D].partition_broadcast(128), "modD_r"))
        lng = P.sb([128, D], F32, "lng"); bcast_load(P, lng, I["ln1_g"].ap)
        lnb = P.sb([128, D], F32, "lnb"); bcast_load(P, lnb, I["ln1_b"].ap)
        gates = P.sb([128, 24, 512], BF16, "gates")
        srcs = [P.sb([128, 4, 512], BF16, f"src{i}") for i in range(3)]
        yT = P.sb([128, 4, 512], BF16, "yT")
        t1 = P.sb([128, 4, 512], F32, "t1")
        sg = Rot(P, [128, 512], F32, "sg", 2)
        acc = Rot(P, [128, 512], F32, "acc", 2)
        tmpm = Rot(P, [128, 512], F32, "tmpm", 2)
        merged = P.sb([128, 8, 512], BF16, "merged")
        xtR = Rot(P, [128, D], F32, "xt", 2)
        tsR = Rot(P, [128, D], F32, "tsum", 3)
        xmR = Rot(P, [128, D], F32, "xm", 2)
        stR = Rot(P, [128, 2, 6], F32, "bnst", 2); mvR = Rot(P, [128, 2], F32, "mv", 2); rsR = Rot(P, [128, 1], F32, "rs", 2)
        xnR = Rot(P, [128, D], BF16, "xn", 2)
        h2T = P.sb([128, 8, 512], BF16, "h2T")
        for (c0, n, is_ctx, r0) in own_blocks(last):
            tix = 1 if is_ctx else 0
            for j in range(4):
                P.dma(yT[:, j, 0:n], G.ysD.v(G.ysD.ap[j * 128:(j + 1) * 128, c0:c0 + n]))
                P.dma(srcs[0][:, j, 0:n], G.yaD.v(G.yaD.ap[j * 128:(j + 1) * 128, c0:c0 + n]))
                P.dma(srcs[2][:, j, 0:n], G.ycD.v(G.ycD.ap[j * 128:(j + 1) * 128, c0:c0 + n]))
            for gi in range(24):
                P.dma(gates[:, gi, 0:n], G.gD.v(G.gD.ap[gi * 128:(gi + 1) * 128, c0:c0 + n]))
            P.tt(t1[:, :, 0:n], yT[:, :, 0:n], yT[:, :, 0:n], ALU.mult)
            P.ts(t1[:, :, 0:n], t1[:, :, 0:n], 0.044715, 1.0, op0=ALU.mult, op1=ALU.add)
            P.tt(t1[:, :, 0:n], t1[:, :, 0:n], yT[:, :, 0:n], ALU.mult)
            P.act(t1[:, :, 0:n], t1[:, :, 0:n], AF.Sigmoid, scale=1.5957691216)
            ge = P.sb([128, 4, 512], BF16, "ge") if c0 == 0 else ge
            P.tt(ge[:, :, 0:n], t1[:, :, 0:n], yT[:, :, 0:n], ALU.mult)
            for op_ in range(2):
                pa = [P.ps(), P.ps()]; pg = [P.ps(), P.ps()]
                items = []
                for q in range(2):
                    oc = op_ * 2 + q
                    items.append((pa[q][:, 0:n], (lambda k, oc=oc: wglu[:, k, oc * 128:(oc + 1) * 128]), lambda k: ge[:, k, 0:n]))
                    items.append((pg[q][:, 0:n], (lambda k, oc=oc: wglu[:, k, 512 + oc * 128:512 + (oc + 1) * 128]), lambda k: ge[:, k, 0:n]))
                mmg(P, items, 4)
                for q in range(2):
                    oc = op_ * 2 + q
                    s = sg()
                    P.act(s[:, 0:n], pg[q][:, 0:n], AF.Sigmoid)
                    P.tt(srcs[1][:, oc, 0:n], pa[q][:, 0:n], s[:, 0:n], ALU.mult)
            for oc in range(8):
                a = acc()
                pbs = [P.ps() for _ in range(3)]
                mmg(P, [(pbs[br][:, 0:n], (lambda k, br=br: wb[br][:, k, oc * 128:(oc + 1) * 128]), (lambda k, br=br: srcs[br][:, k, 0:n])) for br in range(3)], 4)
                for br in range(3):
                    pb = pbs[br]
                    if br == 0:
                        P.tt(a[:, 0:n], pb[:, 0:n], gates[:, br * 8 + oc, 0:n], ALU.mult)
                    else:
                        tm = tmpm()
                        P.tt(tm[:, 0:n], pb[:, 0:n], gates[:, br * 8 + oc, 0:n], ALU.mult)
                        if br == 1:
                            P.tt(a[:, 0:n], a[:, 0:n], tm[:, 0:n], ALU.add, eng="pool")
                        else:
                            P.tt(merged[:, oc, 0:n], a[:, 0:n], tm[:, 0:n], ALU.add, eng="pool")
            def mix(ti):
                xt = xtR(); ts_ = tsR()
                P.dma(xt, (I["ctx"] if is_ctx else I["x_own"]).rows(r0 + ti * 128))
                pms = [P.ps(), P.ps()]
                mmg(P, [(pms[half][:, 0:512], lambda k: merged[:, k, ti * 128:(ti + 1) * 128],
                         (lambda k, half=half: wout[:, k, half * 512:(half + 1) * 512])) for half in range(2)], 8)
                for half in range(2):
                    P.tt(ts_[:, half * 512:(half + 1) * 512], pms[half][:, 0:512], g1b[tix][:, half * 512:(half + 1) * 512], ALU.mult)
                P.stt(ts_, xt, ALPHA, ts_, ALU.mult, ALU.add)
                return ts_

            def chain(ti, ts_):
                st = stR(); mv = mvR(); rs = rsR(); xm = xmR()
                for hh in range(2):
                    P.bn_stats(st[:, hh, :], ts_[:, hh * 512:(hh + 1) * 512])
                P.bn_aggr(mv, st)
                rstd_from_ms(P, rs, mv[:, 1:2], 1)
                P.stt(xm, ts_, mv[:, 0:1], lng, ALU.subtract, ALU.mult)
                P.stt(xm, xm, rs, lnb, ALU.mult, ALU.add)
                P.dma(G.xmidD.v(G.xmidD.ap[c0 + ti * 128:c0 + (ti + 1) * 128, :]), xm)
                st = stR(); mv = mvR(); rs = rsR(); xn = xnR()
                for hh in range(2):
                    P.bn_stats(st[:, hh, :], xm[:, hh * 512:(hh + 1) * 512])
                P.bn_aggr(mv, st)
                rstd_from_ms(P, rs, mv[:, 1:2], 1)
                P.ts(xn, xm, mv[:, 0:1], rs, op0=ALU.subtract, op1=ALU.mult)
                return xn

            def xpose(ti, xn):
                ps = P.ps()
                psb = ps.v(ps.ap.bitcast(BF16))
                for kc in range(8):
                    P.transpose(psb[:, kc * 128:(kc + 1) * 128], xn[:, kc * 128:(kc + 1) * 128], G.ident)
                for kc in range(8):
                    P.act(h2T[:, kc, ti * 128:(ti + 1) * 128], psb[:, kc * 128:(kc + 1) * 128], AF.Identity,
                          bias=G.modT[:, 3 * 8 + kc, tix:tix + 1], scale=G.modT[:, 4 * 8 + kc, tix:tix + 1])
            nti = n // 128
            ts_cur = mix(0)
            for ti in range(nti):
                ts_nxt = mix(ti + 1) if ti + 1 < nti else None
                xn = chain(ti, ts_cur)
                xpose(ti, xn)
                ts_cur = ts_nxt
            for kc in range(8):
                P.dma(G.h2D.v(G.h2D.ap[kc * 128:(kc + 1) * 128, c0:c0 + n]), h2T[:, kc, 0:n])


def stage_B4b(P, I, G, last, out_own, out_ctx):
    NQ = G.NQ
    NT = NQ // 128
    with P.scope():
        g2b = [P.sb([128, D], F32, f"g2b{t}") for t in range(2)]
        for t in range(2):
            P.dma(g2b[t], dT(G.modD.ap[t, 5 * D:6 * D].partition_broadcast(128), "modD_r"))
        lng = P.sb([128, D], F32, "lng"); bcast_load(P, lng, I["ln2_g"].ap)
        lnb = P.sb([128, D], F32, "lnb"); bcast_load(P, lnb, I["ln2_b"].ap)
        h2T = P.sb([128, 8, NQ], BF16, "h2Tall")
        for kc in range(8):
            P.dma(h2T[:, kc, :], G.h2D.v(G.h2D.ap[kc * 128:(kc + 1) * 128, :]))
        tsum = P.sb([128, NT, D], F32, "tsum2")
        wuR = Rot(P, [128, 8, 512], BF16, "wu", 2)
        wdR = Rot(P, [128, 4, D], BF16, "wd", 2)
        stg = Rot(P, [128, 1024], F32, "stg", 3)
        aR = Rot(P, [128, 4, 512], BF16, "aog", 3)
        rl = Rot(P, [128, 512], BF16, "rl", 4)
        xmR = Rot(P, [128, D], F32, "xm", 2)
        stR = Rot(P, [128, 2, 6], F32, "bnst", 2); mvR = Rot(P, [128, 2], F32, "mv", 2); rsR = Rot(P, [128, 1], F32, "rs", 2)
        for og in range(8):
            wu = wuR(); wd = wdR()
            for kc in range(8):
                wload(P, stg, wu[:, kc, :], dT(I["w_up"].ap[kc * 128:(kc + 1) * 128, og * 512:(og + 1) * 512], "w"), engs=("pool", "act"))
            for oc in range(4):
                wload(P, stg, wd[:, oc, :], dT(I["w_down"].ap[og * 512 + oc * 128:og * 512 + (oc + 1) * 128, :], "w"), engs=("pool", "act"))
            blks = own_blocks(last)

            def up(blk):
                (c0, n, is_ctx, r0) = blk
                a = aR()
                pus = [P.ps() for _ in range(4)]
                mmg(P, [(pus[oc][:, 0:n], (lambda k, oc=oc: wu[:, k, oc * 128:(oc + 1) * 128]), lambda k: h2T[:, k, c0:c0 + n]) for oc in range(4)], 8)
                for oc in range(4):
                    r = rl()
                    P.act(r[:, 0:n], pus[oc][:, 0:n], AF.Relu)
                    P.tt(a[:, oc, 0:n], pus[oc][:, 0:n], r[:, 0:n], ALU.mult)
                return a

            def down(blk, a):
                (c0, n, is_ctx, r0) = blk
                combos = [(ti, half) for ti in range(n // 128) for half in range(2)]
                for g0 in range(0, len(combos), 4):
                    grp = combos[g0:g0 + 4]
                    pds = [P.ps() for _ in grp]
                    mmg(P, [(pds[i][:, 0:512], (lambda k, ti=ti: a[:, k, ti * 128:(ti + 1) * 128]),
                             (lambda k, half=half: wd[:, k, half * 512:(half + 1) * 512])) for i, (ti, half) in enumerate(grp)], 4)
                    for i, (ti, half) in enumerate(grp):
                        tile = c0 // 128 + ti
                        dst = tsum[:, tile, half * 512:(half + 1) * 512]
                        if og == 0:
                            P.copy(dst, pds[i][:, 0:512], eng="act")
                        else:
                            P.tt(dst, pds[i][:, 0:512], dst, ALU.add)
            a_cur = up(blks[0])
            for bi, blk in enumerate(blks):
                a_nxt = up(blks[bi + 1]) if bi + 1 < len(blks) else None
                down(blk, a_cur)
                a_cur = a_nxt
        for (c0, n, is_ctx, r0) in own_blocks(last):
            tix = 1 if is_ctx else 0
            for ti in range(n // 128):
                tile = c0 // 128 + ti
                xm = xmR()
                P.dma(xm, G.xmidD.v(G.xmidD.ap[c0 + ti * 128:c0 + (ti + 1) * 128, :]))
                P.tt(tsum[:, tile, :], tsum[:, tile, :], g2b[tix], ALU.mult, eng="pool")
                P.stt(tsum[:, tile, :], xm, ALPHA, tsum[:, tile, :], ALU.mult, ALU.add)
                st = stR(); mv = mvR(); rs = rsR()
                for hh in range(2):
                    P.bn_stats(st[:, hh, :], tsum[:, tile, hh * 512:(hh + 1) * 512])
                P.bn_aggr(mv, st)
                rstd_from_ms(P, rs, mv[:, 1:2], 1)
                o = xm
                P.stt(o, tsum[:, tile, :], mv[:, 0:1], lng, ALU.subtract, ALU.mult)
                P.stt(o, o, rs, lnb, ALU.mult, ALU.add)
                dst = out_ctx if is_ctx else out_own
                P.dma(dst.rows(r0 + ti * 128), o)


def bc(t, pattern):
    a = t.ap
    return t.v(bass.AP(a.tensor, a.offset, [list(a.ap[0])] + [list(p) for p in pattern]))


MAGIC = 12582912.0
TWO_PI = 6.283185307179586


def stage_S5(P, I, G, last):
    NQ = LH + (0 if last else CTX)
    G.ysD = P.dram([512, NQ], BF16, "ysD")
    with P.scope():
        PR = P.sb([128, 32, 64], F32, "PR"); NPI = P.sb([128, 32, 64], F32, "NPI")
        DA = P.sb([128, 10, 64], F32, "DA"); DB = P.sb([128, 10, 64], F32, "DB")
        BX1 = P.sb([128, 64, 16], F32, "BX1"); BX2 = P.sb([128, 64, 16], F32, "BX2")
        CX1 = P.sb([128, 64, 16], F32, "CX1"); CX2 = P.sb([128, 64, 16], F32, "CX2")
        Dcol = P.sb([128, 32], F32, "Dcol")
        identF = G.ident
        swapF = P.sb([128, 128], BF16, "swapF"); P.dma(swapF, I["swap"], eng="pool")
        maskf = P.sb([128, 128], BF16, "maskf"); P.dma(maskf, I["mask_f"], eng="pool")
        maskb = P.sb([128, 128], BF16, "maskb"); P.dma(maskb, I["mask_b"], eng="pool")
        sgn = P.sb([128, 1], F32, "sgn"); P.memset(sgn[0:64, :], 1.0); P.memset(sgn[64:128, :], -1.0)
        for i in range(8):
            P.dma(Dcol[i * 16:(i + 1) * 16, :], dT(I["s5_d"].ap.rearrange("(g c) -> c g", c=16), "s5d"), allow_slow_non_contiguous=True)
        with P.scope():
            are = P.sb([128, 64], F32, "are"); aim = P.sb([128, 64], F32, "aim"); ldt = P.sb([128, 64], F32, "ldt")
            for hf in range(2):
                sl = slice(hf * 64, (hf + 1) * 64)
                P.dma(are[sl, :], dT(I["s5_a_re"].ap.rearrange("d g p -> p (d g)"), "a"), allow_slow_non_contiguous=True)
                P.dma(aim[sl, :], dT(I["s5_a_im"].ap.rearrange("d g p -> p (d g)"), "a"), allow_slow_non_contiguous=True)
            P.dma(ldt, dT(I["s5_log_dt"].ap.rearrange("d g -> (d g)").partition_broadcast(128), "a"))
            dt_ = P.sb([128, 64], F32, "dt")
            P.act(dt_, ldt, AF.Exp)
            lr = P.sb([128, 64], F32, "lr"); li = P.sb([128, 64], F32, "li")
            P.tt(lr, are, dt_, ALU.mult); P.tt(li, aim, dt_, ALU.mult)
            with P.scope():
                elist = [t - 7 for t in range(16)] + [8 - t for t in range(16)]
                LR = P.sb([128, 32, 64], F32, "LR"); LI = P.sb([128, 32, 64], F32, "LI")
                for idx, e in enumerate(elist):
                    P.ts(LR[:, idx, :], lr, float(e), None, op0=ALU.mult)
                    P.ts(LI[:, idx, :], li, float(e), None, op0=ALU.mult, eng="pool")
                mag = P.sb([128, 32, 64], F32, "mag")
                P.act(mag, LR, AF.Exp)
                rr = P.sb([128, 32, 64], F32, "rr"); kk = P.sb([128, 32, 64], F32, "kk")

                def sin_of(dst, ang_t, shift):
                    P.ts(rr, ang_t, 1.0 / TWO_PI, shift / TWO_PI, op0=ALU.mult, op1=ALU.add)
                    P.ts(kk, rr, MAGIC, None, op0=ALU.add)
                    P.ts(kk, kk, MAGIC, None, op0=ALU.subtract)
                    P.tt(rr, rr, kk, ALU.subtract)
                    P.ts(rr, rr, TWO_PI, None, op0=ALU.mult)
                    P.ts(rr, rr, 3.1415925, -3.1415925, op0=ALU.min, op1=ALU.max)
                    P.act(dst, rr, AF.Sin)
                sn = LR
                sin_of(sn, LI, 0.0)
                P.stt(NPI, mag, -1.0, sn, ALU.mult, ALU.mult)
                sin_of(sn, LI, TWO_PI / 4)
                P.tt(PR, mag, sn, ALU.mult)
            cr_ = P.sb([128, 64], F32, "cr"); ci_ = P.sb([128, 64], F32, "ci")
            t1 = P.sb([128, 64], F32, "t1"); t2 = P.sb([128, 64], F32, "t2")
            P.copy(DA[:, 0, :], PR[:, 15, :])
            P.ts(DB[:, 0, :], NPI[:, 15, :], -1.0, None, op0=ALU.mult)
            for m in range(1, 10):
                P.tt(t1, DA[:, m - 1, :], DA[:, m - 1, :], ALU.mult)
                P.tt(t2, DB[:, m - 1, :], DB[:, m - 1, :], ALU.mult)
                P.tt(DA[:, m, :], t1, t2, ALU.subtract)
                P.stt(DB[:, m, :], DA[:, m - 1, :], 2.0, DB[:, m - 1, :], ALU.mult, ALU.mult)
            P.ts(DB, DB, sgn[:, 0:1], None, op0=ALU.mult)
            den = P.sb([128, 64], F32, "den"); nr = P.sb([128, 64], F32, "nr"); abi = P.sb([128, 64], F32, "abi")
            P.tt(t1, are, are, ALU.mult); P.tt(t2, aim, aim, ALU.mult); P.tt(den, t1, t2, ALU.add); P.recip(den, den)
            P.ts(nr, PR[:, 8, :], -1.0, None, op0=ALU.add)
            P.ts(abi, NPI[:, 8, :], -1.0, None, op0=ALU.mult)
            P.tt(t1, nr, are, ALU.mult); P.tt(t2, abi, aim, ALU.mult); P.tt(cr_, t1, t2, ALU.add); P.tt(cr_, cr_, den, ALU.mult)
            P.tt(t1, abi, are, ALU.mult); P.tt(t2, nr, aim, ALU.mult); P.tt(ci_, t1, t2, ALU.subtract); P.tt(ci_, ci_, den, ALU.mult)
            crb = bc(cr_, [[1, 64], [0, 16]]); cib = bc(ci_, [[1, 64], [0, 16]])
            Bre = P.sb([128, 64, 16], F32, "Bre"); Bim = P.sb([128, 64, 16], F32, "Bim")
            Cre = P.sb([128, 64, 16], F32, "Cre"); Cim = P.sb([128, 64, 16], F32, "Cim")
            for hf in range(2):
                sl = slice(hf * 64, (hf + 1) * 64)
                P.dma(Bre[sl], dT(I["s5_b_re"].ap.rearrange("d g p c -> p (d g) c"), "a"))
                P.dma(Bim[sl], dT(I["s5_b_im"].ap.rearrange("d g p c -> p (d g) c"), "a"))
                P.dma(Cre[sl], dT(I["s5_c_re"].ap.rearrange("d g c p -> p (d g) c"), "a"), allow_slow_non_contiguous=True)
                P.dma(Cim[sl], dT(I["s5_c_im"].ap.rearrange("d g c p -> p (d g) c"), "a"), allow_slow_non_contiguous=True)
            bbr = P.sb([128, 64, 16], F32, "bbr"); bbi = P.sb([128, 64, 16], F32, "bbi"); t3 = P.sb([128, 64, 16], F32, "t3")
            P.tt(bbr, Bre, crb, ALU.mult); P.tt(t3, Bim, cib, ALU.mult); P.tt(bbr, bbr, t3, ALU.subtract)
            P.tt(bbi, Bim, crb, ALU.mult); P.tt(t3, Bre, cib, ALU.mult); P.tt(bbi, bbi, t3, ALU.add)
            P.copy(BX1[0:64], bbr[0:64]); P.copy(BX1[64:128], bbi[64:128])
            P.copy(BX2[0:64], bbi[0:64]); P.ts(BX2[64:128], bbr[64:128], -1.0, None, op0=ALU.mult)
            P.copy(CX1[0:64], Cre[0:64]); P.ts(CX1[64:128], Cim[64:128], -1.0, None, op0=ALU.mult)
            P.copy(CX2[0:64], Cim[0:64]); P.copy(CX2[64:128], Cre[64:128])
        Sel = P.sb([128, 64, 128], BF16, "Sel"); SelT = P.sb([128, 64, 128], BF16, "SelT")
        for q in range(4):
            P.dma(Sel[:, q * 16:(q + 1) * 16, :], dT(I["sel8"].ap[:, q * 16:(q + 1) * 16, :], "sel8"), eng="pool")
            P.dma(SelT[:, q * 16:(q + 1) * 16, :], dT(I["sel8T"].ap[:, q * 16:(q + 1) * 16, :], "sel8T"), eng="pool")
        KQ = {nm: Rot(P, [128, 16, 16], BF16, nm, 2) for nm in ["Kf", "Qf", "Kb", "Qb"]}
        tA = Rot(P, [128, 16, 16], F32, "tA", 1); tB = Rot(P, [128, 16, 16], F32, "tB", 1)
        tC = Rot(P, [128, 16, 16], F32, "tC", 1); tD = Rot(P, [128, 16, 16], F32, "tD", 1)
        SgR = Rot(P, [128, 128], BF16, "Sg", 2)
        WeR = Rot(P, [128, 128], BF16, "We", 4)
        s1R = Rot(P, [128, 128], F32, "s1", 1); s2R = Rot(P, [128, 128], F32, "s2", 1)
        UcR = Rot(P, [128, 576], BF16, "Uc", 2)
        XR = {d: [P.sb([128, 545], BF16, f"X{d}{i}") for i in range(2)] for d in "fb"}
        for d in "fb":
            for x in XR[d]:
                P.memset(x, 0.0)
        MdR = {d: Rot(P, [128, 10, 128], BF16, "Md" + d, 2) for d in "fb"}
        mA = Rot(P, [128, 10, 128], BF16, "mA", 1); mB = Rot(P, [128, 10, 128], BF16, "mB", 1)
        Yt = P.sb([128, 8, 544], BF16, "Yt")
        tqR = Rot(P, [128, 256], F32, "tq", 2); yctx = P.sb([128, CTX], BF16, "yctx")
        yown = Rot(P, [128, LH], BF16, "yown", 1)
        identB = G.ident
        flip = 0
        for g in range(32):
            j, gl = g // 8, g % 8
            mats = {}
            spec = {"Kf": (17, 15), "Qf": (7, 9), "Kb": (7, 8), "Qb": (16, 16)}
            for dname, gd in (("f", g), ("b", 32 + g)):
                for kind, X1, X2, eng in (("K", BX1, BX2, "dve"), ("Q", CX1, CX2, "pool")):
                    t0_, ne = spec[kind + dname]
                    out = KQ[kind + dname]()[:, 0:ne, :]
                    a = (tA() if kind == "K" else tC())[:, 0:ne, :]
                    b_ = (tB() if kind == "K" else tD())[:, 0:ne, :]
                    prb = bc(PR[:, t0_, gd:gd + 1], [[64, ne], [0, 16]]); npb = bc(NPI[:, t0_, gd:gd + 1], [[64, ne], [0, 16]])
                    x1b = bc(X1[:, gd, :], [[0, ne], [1, 16]]); x2b = bc(X2[:, gd, :], [[0, ne], [1, 16]])
                    P.tt(a, prb, x1b, ALU.mult, eng=eng)
                    P.tt(b_, npb, x2b, ALU.mult, eng=eng)
                    P.tt(out, a, b_, ALU.add, eng=eng)
                    mats[kind + dname] = out

            def m128(t, lo):
                return t.v(t.ap[:, lo:lo + 8, :].rearrange("p a b -> p (a b)"))
            Kf, Qf, Kb, Qb = mats["Kf"], mats["Qf"], mats["Kb"], mats["Qb"]
            psf = P.ps(); psb_ = P.ps()
            P.mm(psf[:, 0:128], m128(Kf, 7), m128(Qf, 0))
            P.mm(psb_[:, 0:128], m128(Kb, 0), m128(Qb, 8))
            s1 = s1R(); s2 = s2R(); Sg = SgR()
            P.tt(s1, psf[:, 0:128], maskf, ALU.mult)
            P.tt(s2, psb_[:, 0:128], maskb, ALU.mult)
            P.tt(s1, s1, s2, ALU.add)
            P.stt(Sg, identF, Dcol[:, g:g + 1], s1, ALU.mult, ALU.add)
            We = {}
            for dname, src in (("f", m128(Kf, 0)), ("b", m128(Kb, 0))):
                pt = P.ps()
                ptb = pt.v(pt.ap.bitcast(BF16))
                P.transpose(ptb[:, 0:128], src, identB)
                w = WeR()
                P.copy(w, ptb[:, 0:128], eng="act")
                We[dname] = w
            Wo = {"f": m128(Qf, 1), "b": m128(Qb, 0)}
            Uc = UcR()
            pu = P.ps(); pu2 = P.ps()
            for i in range(8):
                P.mm(pu[:, 0:512], Sel[:, gl * 8 + i, :], G.uT[:, j, i, 32:544], start=(i == 0), stop=(i == 7))
                P.mm(pu2[:, 0:32], Sel[:, gl * 8 + i, :], G.uT[:, j, i, 0:32], start=(i == 0), stop=(i == 7))
            P.copy(Uc[:, 32:544], pu[:, 0:512], eng="act")
            P.copy(Uc[:, 0:32], pu2[:, 0:32])
            P.copy(Uc[:, 544:576], pu2[:, 0:32])
            Xfin = {}
            st_ = {}
            for dname, gd in (("f", g), ("b", 32 + g)):
                ucoff = 0 if dname == "f" else 32
                xoff = 1 if dname == "f" else 0
                cur = XR[dname][0]; nxt = XR[dname][1]
                pa = P.ps(); pb2 = P.ps()
                P.mm(pa[:, 0:512], We[dname], Uc[:, ucoff:ucoff + 512])
                P.mm(pb2[:, 0:32], We[dname], Uc[:, ucoff + 512:ucoff + 544])
                P.copy(cur[:, xoff:xoff + 512], pa[:, 0:512], eng="act")
                P.copy(cur[:, xoff + 512:xoff + 544], pb2[:, 0:32])
                Mall = MdR[dname]()
                ta = mA(); tb = mB()
                idb = bc(identF, [[0, 10], [1, 128]]); swb = bc(swapF, [[0, 10], [1, 128]])
                dab = bc(DA[:, :, gd], [[64, 10], [0, 128]]); dbb = bc(DB[:, :, gd], [[64, 10], [0, 128]])
                P.tt(ta, idb, dab, ALU.mult, eng="pool")
                P.tt(tb, swb, dbb, ALU.mult, eng="pool")
                P.tt(Mall, ta, tb, ALU.add, eng="pool")
                st_[dname] = [cur, nxt, xoff, Mall]
            for m in range(10):
                d = 1 << m
                work = []
                for dname in ("f", "b"):
                    cur, nxt, xoff, Mall = st_[dname]
                    for (lo, hi) in ((0, 272), (272, 544)):
                        ps = P.ps()
                        if dname == "f":
                            s_ = max(lo, d)
                            has = s_ < hi
                            shift = (ps[:, s_ - lo:hi - lo], cur[:, xoff + s_ - d:xoff + hi - d]) if has else None
                        else:
                            e_ = min(hi, 544 - d)
                            has = lo < e_
                            shift = (ps[:, 0:e_ - lo], cur[:, xoff + lo + d:xoff + e_ + d]) if has else None
                        work.append((dname, ps, lo, hi, shift, cur, nxt, xoff, Mall))
                for (dname, ps, lo, hi, shift, cur, nxt, xoff, Mall) in work:
                    P.mm(ps[:, 0:hi - lo], identB, cur[:, xoff + lo:xoff + hi], start=True, stop=(shift is None))
                for (dname, ps, lo, hi, shift, cur, nxt, xoff, Mall) in work:
                    if shift is not None:
                        P.mm(shift[0], Mall[:, m, :], shift[1], start=False, stop=True)
                for (dname, ps, lo, hi, shift, cur, nxt, xoff, Mall) in work:
                    flip ^= 1
                    P.copy(nxt[:, xoff + lo:xoff + hi], ps[:, 0:hi - lo], eng=("act" if flip else "dve"))
                for dname in ("f", "b"):
                    st_[dname][0], st_[dname][1] = st_[dname][1], st_[dname][0]
            Xfin = {dname: st_[dname][0] for dname in ("f", "b")}
            Xf, Xb = Xfin["f"], Xfin["b"]
            py = P.ps()
            pyc = P.ps() if not last else None
            ytl = [(Sg, Uc[:, 32:544], Uc[:, 0:32]), (Wo["f"], Xf[:, 32:544], Xf[:, 0:32]), (Wo["b"], Xb[:, 1:513], Xb[:, 513:545])]
            for q, (lh, r1, r2) in enumerate(ytl):
                P.mm(py[:, 0:512], lh, r1, start=(q == 0), stop=(q == 2))
                if not last:
                    P.mm(pyc[:, 0:32], lh, r2, start=(q == 0), stop=(q == 2))
            P.copy(Yt[:, gl, 0:512], py[:, 0:512], eng="act")
            if not last:
                P.copy(Yt[:, gl, 512:544], pyc[:, 0:32])
            if gl == 7:
                yo = yown()
                for i in range(8):
                    ps = P.ps()
                    ps2 = P.ps() if not last else None
                    for g2 in range(8):
                        P.mm(ps[:, 0:512], SelT[:, g2 * 8 + i, :], Yt[:, g2, 0:512], start=(g2 == 0), stop=(g2 == 7))
                        if not last:
                            P.mm(ps2[:, 0:32], SelT[:, g2 * 8 + i, :], Yt[:, g2, 512:544], start=(g2 == 0), stop=(g2 == 7))
                    tq = tqR()
                    P.act(tq, ps[:, 0:256], AF.Copy, scale=G.sel[:, 0:1])
                    P.stt(yo.v(yo.ap[:, i:LH:8]), ps[:, 256:512], G.sel[:, 1:2], tq, ALU.mult, ALU.add)
                    if not last:
                        P.copy(yctx.v(yctx.ap[:, i:CTX:8]), ps2[:, 0:32])
                P.dma(G.ysD.v(G.ysD.ap[j * 128:(j + 1) * 128, 0:LH]), yo)
                if not last:
                    P.dma(G.ysD.v(G.ysD.ap[j * 128:(j + 1) * 128, LH:LH + CTX]), yctx)


def emit_layer(P, I, G, last, out_own, out_ctx):
    stage_prep(P, I, G)
    with P.scope():
        alloc_persist(P, G)
        with P.scope():
            G.uT = P.sb([128, 4, 8, NK // 8], BF16, "uT")
            stage_A(P, I, G)
            stage_S5(P, I, G, last)
        stage_B1(P, I, G, last)
        stage_B2(P, I, G, last)
        stage_B3(P, I, G, last)
    stage_B4a(P, I, G, last)
    stage_B4b(P, I, G, last, out_own, out_ctx)
    P.flush()


def build_fused():
    nc = bass.Bass("TRN2", target_bir_lowering=False)
    Cn = declare_consts(nc)
    W = [declare_weights(nc, l) for l in range(2)]
    y_out = dT(nc.dram_tensor("y_own", [LH, D], F32, kind="ExternalOutput").ap(), "y_own")
    with ExitStack() as st:
        P = Prog(nc, st)
        P.init_psum()
        NCH = 4
        CR = LH // NCH
        x1o = [P.dram([CR, D], F32, f"x1o{c}") for c in range(NCH)]
        x1g = [P.dram([2 * CR, D], F32, f"x1g{c}") for c in range(NCH)]
        ctx1 = P.dram([CTX, D], F32, "ctx1")
        own_src = RowSrc(lambda r0: x1o[r0 // CR].v(x1o[r0 // CR].ap[r0 % CR:r0 % CR + 128, :]))

        def all_fn(r0):
            half, rr = r0 // LH, r0 % LH
            c, i = rr // CR, rr % CR
            return x1g[c].v(x1g[c].ap[half * CR + i:half * CR + i + 128, :])
        with P.scope():
            G = Ctx()
            I0 = dict(Cn); I0.update(W[0])
            for k in ("x_all", "x_own", "ctx"):
                I0[k] = flat_src(Cn[k])
            emit_layer(P, I0, G, False, own_src, flat_src(ctx1))
        groups = [[0, 1], [2, 3], [4, 5], [6, 7]]
        for c in range(NCH):
            P.add("pool", lambda e, c=c: e.collective_compute("AllGather", ALU.bypass, replica_groups=groups,
                                                              ins=[x1o[c].ap.opt()], outs=[x1g[c].ap.opt()]), [x1o[c]], [x1g[c]])
            P.add("pool", None, [x1g[c]], [])
        P.flush()
        with P.scope():
            G = Ctx()
            I1 = dict(Cn); I1.update(W[1])
            I1["x_all"] = RowSrc(all_fn); I1["x_own"] = own_src; I1["ctx"] = flat_src(ctx1)
            emit_layer(P, I1, G, True, flat_src(y_out), None)
    return nc


_NC_CACHE = {}


def kernel(**inputs):
    inputs = {k: np.asarray(v) for k, v in inputs.items()}
    C = host_constants()
    if "nc" not in _NC_CACHE:
        _NC_CACHE["nc"] = build_fused()
    nc = _NC_CACHE["nc"]
    in_maps = [per_core_inputs(inputs, core, C) for core in range(8)]
    res = run_bass_kernel_spmd(nc, in_maps, core_ids=list(range(8)))
    out = np.empty((4, L, D), np.float32)
    for core in range(8):
        b, hh = core // 2, core % 2
        out[b, hh * LH:(hh + 1) * LH] = np.asarray(res.results[core]["y_own"])
    return out
```

```python
import numpy as np
import concourse.bass as bass
import concourse.mybir as mybir
from concourse.bass_utils import run_bass_kernel_spmd
from contextlib import ExitStack, contextmanager

F32 = mybir.dt.float32
BF16 = mybir.dt.bfloat16
I32 = mybir.dt.int32
AF = mybir.ActivationFunctionType
ALU = mybir.AluOpType
AX = mybir.AxisListType

ENGS = ["pe", "act", "dve", "pool", "sp"]
DMA_WIN = 8
SAME_ENG_SYNC = True


class T:
    __slots__ = ("ap", "keys")

    def __init__(self, ap, keys):
        self.ap = ap
        self.keys = tuple(keys)

    def __getitem__(self, sl):
        return T(self.ap[sl], self.keys)

    def v(self, ap):
        return T(ap, self.keys)

    def k(self, *sub):
        return T(self.ap, [(self.keys[0],) + tuple(sub)])


class Prog:
    def __init__(self, nc, stack):
        self.nc = nc
        self.stack = stack
        self.cur = stack
        self.streams = {e: [] for e in ENGS}
        self.last_writer = {}
        self.readers = {}
        self.ndma = {e: 0 for e in ENGS}
        self.sigcount = {e: 0 for e in ENGS}
        self.waited = {e: {} for e in ENGS}
        self.nt = 0
        self.psum_banks = []
        self.psum_i = 0
        self.sem = {e: stack.enter_context(nc.semaphore(f"s_{e}")) for e in ENGS}
        self.dsem = {e: [stack.enter_context(nc.semaphore(f"d_{e}{i}")) for i in range(DMA_WIN)]
                     for e in ("sp", "pool", "act")}
        self.dbg = {}
        self.nops = {e: 0 for e in ENGS}

    def sb(self, shape, dt, name=None):
        self.nt += 1
        name = name or "t"
        nm = f"{name}_{self.nt}"
        t = self.cur.enter_context(self.nc.sbuf_tensor(nm, list(shape), dt))
        return T(t[:], [nm])

    def dram(self, shape, dt, name):
        self.nt += 1
        nm = f"{name}_{self.nt}"
        t = self.nc.dram_tensor(nm, list(shape), dt, kind="Internal")
        return T(t.ap(), [nm])

    def init_psum(self, n=8):
        for i in range(n):
            t = self.stack.enter_context(self.nc.psum_tensor(f"bank{i}", [128, 512], F32))
            self.psum_banks.append(T(t[:], [f"bank{i}"]))

    def ps(self):
        b = self.psum_banks[self.psum_i % len(self.psum_banks)]
        self.psum_i += 1
        return b

    @contextmanager
    def scope(self):
        prev = self.cur
        with ExitStack() as st:
            self.cur = st
            yield
            self.flush()
        self.cur = prev

    def add(self, eng, fn, reads=(), writes=(), dma=False):
        deps = set()
        rk = [k for t in reads for k in t.keys]
        wk = [k for t in writes for k in t.keys]
        for k in rk:
            if k in self.last_writer:
                deps.add(self.last_writer[k])
        for k in wk:
            if k in self.last_writer:
                deps.add(self.last_writer[k])
            for r in self.readers.get(k, ()):
                deps.add(r)
        idx = len(self.streams[eng])
        me = (eng, idx)
        deps.discard(me)
        op = dict(fn=fn, deps=deps, dma=dma, signal=False, dman=None)
        if dma:
            op["dman"] = self.ndma[eng]
            self.ndma[eng] += 1
        self.streams[eng].append(op)
        for k in rk:
            self.readers.setdefault(k, []).append(me)
        for k in wk:
            self.last_writer[k] = me
            self.readers[k] = []
        return me

    def dma(self, out, in_, eng="sp", **kw):
        o = out.ap
        i = in_.ap
        return self.add(eng, lambda e: e.dma_start(out=o, in_=i, **kw), [in_], [out], dma=True)

    def mm(self, out, lhsT, rhs, start=True, stop=True, **kw):
        return self.add("pe", lambda e: e.matmul(out.ap, lhsT.ap, rhs.ap, start=start, stop=stop, **kw),
                        [lhsT, rhs], [out])

    def transpose(self, out, in_, ident):
        return self.add("pe", lambda e: e.transpose(out.ap, in_.ap, ident.ap), [in_, ident], [out])

    def act(self, out, in_, func, bias=None, scale=None, eng="act", accum_out=None):
        reads = [in_]
        kw = {}
        if bias is not None:
            if isinstance(bias, T):
                reads.append(bias); kw["bias"] = bias.ap
            else:
                kw["bias"] = bias
        if scale is not None:
            if isinstance(scale, T):
                reads.append(scale); kw["scale"] = scale.ap
            else:
                kw["scale"] = scale
        writes = [out]
        if accum_out is not None:
            kw["accum_out"] = accum_out.ap; writes.append(accum_out)
        return self.add(eng, lambda e: e.activation(out.ap, in_.ap, func, **kw), reads, writes)

    def tt(self, out, a, b, op, eng="dve"):
        return self.add(eng, lambda e: e.tensor_tensor(out.ap, a.ap, b.ap, op), [a, b], [out])

    def ts(self, out, a, s1, s2=None, op0=ALU.mult, op1=None, eng="dve"):
        reads = [a]
        v1 = s1.ap if isinstance(s1, T) else s1
        if isinstance(s1, T): reads.append(s1)
        v2 = s2.ap if isinstance(s2, T) else s2
        if isinstance(s2, T): reads.append(s2)
        if op1 is None:
            return self.add(eng, lambda e: e.tensor_scalar(out.ap, a.ap, v1, None, op0), reads, [out])
        return self.add(eng, lambda e: e.tensor_scalar(out.ap, a.ap, v1, v2, op0, op1), reads, [out])

    def stt(self, out, a, s, b, op0, op1, eng="dve"):
        reads = [a, b]
        v = s.ap if isinstance(s, T) else s
        if isinstance(s, T): reads.append(s)
        return self.add(eng, lambda e: e.scalar_tensor_tensor(out.ap, a.ap, v, b.ap, op0, op1), reads, [out])

    def copy(self, out, in_, eng="dve"):
        if eng == "act":
            return self.add("act", lambda e: e.copy(out.ap, in_.ap), [in_], [out])
        return self.add(eng, lambda e: e.tensor_copy(out.ap, in_.ap), [in_], [out])

    def memset(self, out, val, eng="pool"):
        return self.add(eng, lambda e: e.memset(out.ap, val), [], [out])

    def recip(self, out, in_, eng="dve"):
        return self.add(eng, lambda e: e.reciprocal(out.ap, in_.ap), [in_], [out])

    def bn_stats(self, out, in_):
        return self.add("dve", lambda e: e.bn_stats(out.ap, in_.ap), [in_], [out])

    def bn_aggr(self, out, in_):
        return self.add("dve", lambda e: e.bn_aggr(out.ap, in_.ap), [in_], [out])

    def debug_out(self, name, t, shape, dt=F32):
        d = self.nc.dram_tensor(name, list(shape), dt, kind="ExternalOutput").ap()
        self.dbg[name] = d
        return self.dma(T(d, [name]), t)

    def flush(self):
        nc = self.nc
        streams = self.streams
        lasts = []
        for e in ENGS:
            for j in range(len(streams[e]) - 1, -1, -1):
                op = streams[e][j]
                if op["fn"] is not None and not op["dma"]:
                    lasts.append((e, j))
                    break
        dmas = [(e, j) for e in ENGS for j, op in enumerate(streams[e]) if op["dma"]]
        for e in ENGS:
            deps = set(l for l in lasts if l[0] != e) | set(dmas)
            streams[e].append(dict(fn=None, deps=deps, dma=False, signal=False, dman=None))
        for e in ENGS:
            for op in streams[e]:
                for (f, j) in op["deps"]:
                    d = streams[f][j]
                    if not d["dma"]:
                        if f == e and not SAME_ENG_SYNC:
                            continue
                        d["signal"] = True
        for e in ENGS:
            for op in streams[e]:
                if op["signal"]:
                    self.sigcount[e] += 1
                    op["sigval"] = self.sigcount[e]
        sem, dsem = self.sem, self.dsem

        def run(ename):
            def body(eng):
                waited = self.waited[ename]

                def wait(s, v, key):
                    if waited.get(key, 0) >= v:
                        return
                    waited[key] = v
                    eng.wait_ge(s, v)

                for op in streams[ename]:
                    for (f, j) in sorted(op["deps"]):
                        d = streams[f][j]
                        if d["dma"]:
                            n = d["dman"]
                            wait(dsem[f][n % DMA_WIN], 16 * (n // DMA_WIN + 1), (f, n % DMA_WIN))
                        else:
                            if f == ename and not SAME_ENG_SYNC:
                                continue
                            wait(sem[f], d["sigval"], f)
                    if op["dma"]:
                        n = op["dman"]
                        if n >= DMA_WIN:
                            wait(dsem[ename][n % DMA_WIN], 16 * (n // DMA_WIN), (ename, n % DMA_WIN))
                    if op["fn"] is None:
                        continue
                    ins = op["fn"](eng)
                    if op["dma"]:
                        ins.then_inc(dsem[ename][op["dman"] % DMA_WIN], 16)
                    elif op["signal"]:
                        ins.then_inc(sem[ename], 1)
            return body

        with nc.Block() as block:
            block.tensor(run("pe"))
            block.scalar(run("act"))
            block.vector(run("dve"))
            block.gpsimd(run("pool"))
            block.sync(run("sp"))
        for e in ENGS:
            self.nops[e] += len(streams[e])
        self.streams = {e: [] for e in ENGS}
        self.last_writer = {}
        self.readers = {}


class Rot:
    def __init__(self, P, shape, dt, name, n):
        self.tiles = [P.sb(shape, dt, f"{name}{i}") for i in range(n)]
        self.i = 0

    def __call__(self):
        t = self.tiles[self.i % len(self.tiles)]
        self.i += 1
        return t


def _ps6(self):
    b = self.psum_banks[self.psum_i % 6]
    self.psum_i += 1
    return b


def _psacc(self):
    self.acc_i = getattr(self, "acc_i", 0) + 1
    return self.psum_banks[6 + self.acc_i % 2]


Prog.ps = _ps6
Prog.ps_acc = _psacc


D = 1024
L = 4096
LH = 2048
CTX = 256
NK = CTX + L
NKT = NK // 128
EPS = 1e-6
OFF_AK, OFF_AV, OFF_CKV, OFF_CKR, OFF_U, NST = 0, 128, 256, 512, 544, 1056
OFF_AQ, OFF_CQ, OFF_GATE, NIN = 1056, 1568, 2336, 5408
ALPHA = (2.0 * 2) ** 0.25


def dT(ap, name):
    return T(ap, [name])


class Ctx:
    pass


class RowSrc:
    def __init__(self, fn):
        self.fn = fn

    def rows(self, r0):
        return self.fn(r0)


def flat_src(t):
    return RowSrc(lambda r0: t.v(t.ap[r0:r0 + 128, :]))


WEIGHT_SHAPES = {
    "w_mod": [D, 6 * D], "b_mod": [6 * D], "w_in": [D, NIN], "a_q_gain": [64], "a_k_gain": [64],
    "c_q_a_gain": [768], "c_kv_a_gain": [256], "c_w_qb": [768, 768], "c_w_kvb": [256, 1024],
    "s5_a_re": [2, 32, 64], "s5_a_im": [2, 32, 64], "s5_log_dt": [2, 32],
    "s5_b_re": [2, 32, 64, 16], "s5_b_im": [2, 32, 64, 16], "s5_c_re": [2, 32, 16, 64], "s5_c_im": [2, 32, 16, 64],
    "s5_d": [512], "s5_w_glu": [512, 1024], "w_branch_a": [512, D], "w_branch_s5": [512, D], "w_branch_c": [512, D],
    "w_out": [D, D], "ln1_g": [D], "ln1_b": [D], "w_up": [D, 4 * D], "w_down": [4 * D, D], "ln2_g": [D], "ln2_b": [D],
}
CONST_SHAPES = {
    "x_all": [L, D], "x_own": [LH, D], "ctx": [CTX, D], "cvec": [2, D],
    "ident": [128, 128], "blk64": [128, 128], "perm64": [128, 128], "perm32": [32, 32],
    "ropek_cos": [128, L], "ropek_sin": [128, L], "ropeq_cos": [128, LH], "ropeq_sin": [128, LH],
    "rope32k_cos": [32, L], "rope32k_sin": [32, L], "rope32q_cos": [32, LH], "rope32q_sin": [32, LH],
    "sel": [128, 2], "swap": [128, 128], "mask_f": [128, 128], "mask_b": [128, 128],
    "sel8": [128, 64, 128], "sel8T": [128, 64, 128],
}


def declare_consts(nc):
    return {k: dT(nc.dram_tensor(k, list(v), F32, kind="ExternalInput").ap(), k) for k, v in CONST_SHAPES.items()}


def declare_weights(nc, l):
    return {k: dT(nc.dram_tensor(f"{k}_{l}", list(v), F32, kind="ExternalInput").ap(), f"{k}_{l}")
            for k, v in WEIGHT_SHAPES.items()}


def declare_inputs(nc, last):
    I = declare_consts(nc)
    I.update(declare_weights(nc, 1 if last else 0))
    return I


def host_constants():
    import math
    C = {}
    C["ident"] = np.eye(128, dtype=np.float32)
    blk = np.zeros((128, 128), np.float32); blk[:64, :64] = 1 / 64; blk[64:, 64:] = 1 / 64
    C["blk64"] = blk

    def perm_and_sign(dim):
        half = dim // 2; q = half // 2
        Pm = np.zeros((dim, dim), np.float32)
        sg = np.zeros(dim, np.float32)
        for m in range(dim):
            if (m % half) < q:
                Pm[m + q, m] = 1; sg[m] = -1
            else:
                Pm[m - q, m] = 1; sg[m] = 1
        return Pm, sg
    P64, s64 = perm_and_sign(64)
    p128 = np.zeros((128, 128), np.float32); p128[:64, :64] = P64; p128[64:, 64:] = P64
    C["perm64"] = p128
    P32, s32 = perm_and_sign(32)
    C["perm32"] = P32

    def tables(dim):
        half = dim // 2
        inv = 10000.0 ** (-np.arange(0, half, 2, dtype=np.float32) / half)
        rows = L // 64
        row = np.repeat(np.arange(rows, dtype=np.float32), 64)
        col = np.tile(np.arange(64, dtype=np.float32), rows)
        ang_r = row[:, None] * inv; ang_c = col[:, None] * inv
        ang = np.concatenate([ang_r, ang_r, ang_c, ang_c], axis=-1).astype(np.float32)
        return np.cos(ang).T.astype(np.float32), np.sin(ang).T.astype(np.float32)
    c64, s64t = tables(64)
    s64t = s64t * s64[:, None]
    C["ropek_cos"] = np.concatenate([c64, c64], 0); C["ropek_sin"] = np.concatenate([s64t, s64t], 0)
    c32, s32t = tables(32)
    s32t = s32t * s32[:, None]
    C["rope32k_cos"] = c32; C["rope32k_sin"] = s32t
    sw = np.zeros((128, 128), np.float32)
    for p in range(64):
        sw[p, 64 + p] = 1; sw[64 + p, p] = 1
    C["swap"] = sw
    ii = np.arange(128) // 16
    C["mask_f"] = (ii[:, None] <= ii[None, :]).astype(np.float32)
    C["mask_b"] = (ii[:, None] >= ii[None, :]).astype(np.float32)
    sel = np.zeros((128, 64, 128), np.float32)
    for gl in range(8):
        for i in range(8):
            for c in range(16):
                sel[gl * 16 + c, gl * 8 + i, i * 16 + c] = 1
    C["sel8"] = sel
    C["sel8T"] = np.ascontiguousarray(sel.transpose(2, 1, 0))
    return C


def per_core_inputs(inputs, core, C, layers=(0, 1)):
    b, hh = core // 2, core % 2
    m = {}
    xb = inputs["x"][b]
    m["x_all"] = xb; m["x_own"] = xb[hh * LH:(hh + 1) * LH]; m["ctx"] = inputs["ctx"][b]
    m["cvec"] = np.stack([inputs["c"][b], inputs["c_ctx"]], 0)
    for l in layers:
        for k in WEIGHT_SHAPES:
            m[f"{k}_{l}"] = inputs[k][l]
    for k in ["ident", "blk64", "perm64", "perm32", "ropek_cos", "ropek_sin", "rope32k_cos", "rope32k_sin",
              "swap", "mask_f", "mask_b", "sel8", "sel8T"]:
        m[k] = C[k]
    sl = slice(hh * LH, (hh + 1) * LH)
    m["ropeq_cos"] = C["ropek_cos"][:, sl]; m["ropeq_sin"] = C["ropek_sin"][:, sl]
    m["rope32q_cos"] = C["rope32k_cos"][:, sl]; m["rope32q_sin"] = C["rope32k_sin"][:, sl]
    s = np.zeros((128, 2), np.float32); s[:, hh] = 1
    m["sel"] = s
    return {k: np.ascontiguousarray(v, dtype=np.float32) for k, v in m.items()}


def rstd_from_ms(P, out, ms, n, eps=EPS, eng_a="act"):
    P.ts(out, ms, eps, None, op0=ALU.add)
    P.act(out, out, AF.Sqrt)
    P.recip(out, out)


def stage_prep(P, I, G):
    G.ident = P.sb([128, 128], BF16, "ident"); P.dma(G.ident, I["ident"], eng="pool")
    G.blk64 = P.sb([128, 128], BF16, "blk64"); P.dma(G.blk64, I["blk64"], eng="pool")
    G.perm64 = P.sb([128, 128], BF16, "perm64"); P.dma(G.perm64, I["perm64"], eng="pool")
    G.perm32 = P.sb([32, 32], BF16, "perm32"); P.dma(G.perm32, I["perm32"], eng="pool")
    G.ones = P.sb([128, 128], BF16, "ones"); P.memset(G.ones, 1.0)
    G.modT = P.sb([128, 48, 2], F32, "modT")
    G.sel = P.sb([128, 2], F32, "sel"); P.dma(G.sel, I["sel"])
    with P.scope():
        cT = P.sb([128, 8, 2], F32, "cT")
        for t in range(2):
            src = I["cvec"].ap[t, :].rearrange("(k p) -> p k", p=128)
            P.dma(cT[:, :, t], dT(src, "cvec"), allow_slow_non_contiguous=True)
        sT = P.sb([128, 8, 2], BF16, "sT")
        P.act(sT, cT, AF.Silu)
        bT = P.sb([128, 48], F32, "bT")
        P.dma(bT, dT(I["b_mod"].ap.rearrange("(j p) -> p j", p=128), "b_mod"), allow_slow_non_contiguous=True)
        wbufs = [P.sb([128, 8, 128], BF16, f"wm{i}") for i in range(3)]
        for j in range(48):
            wb = wbufs[j % 3]
            src = I["w_mod"].ap[:, j * 128:(j + 1) * 128].rearrange("(k p) c -> p k c", p=128)
            P.dma(wb, dT(src, "w_mod"), eng="pool")
            ps = P.ps()
            for kc in range(8):
                P.mm(ps[:, 0:2], wb[:, kc, :], sT[:, kc, :], start=(kc == 0), stop=(kc == 7))
            P.ts(G.modT[:, j, :], ps[:, 0:2], bT[:, j:j + 1], None, op0=ALU.add)
        for w in (1, 4):
            P.ts(G.modT[:, w * 8:(w + 1) * 8, :], G.modT[:, w * 8:(w + 1) * 8, :], 1.0, None, op0=ALU.add)
        G.modD = P.dram([2, 6 * D], F32, "modD")
        for t in range(2):
            dst = G.modD.ap[t, :].rearrange("(j p) -> p j", p=128)
            P.dma(G.modD.v(dst), G.modT[:, :, t], allow_slow_non_contiguous=True)


def ln_tile_to_hT(P, G, xt, hT_dst, t_idx, which_sh, which_sc):
    st = P.sb([128, 2, 6], F32, "bnst")
    mv = P.sb([128, 2], F32, "mv")
    for hh in range(2):
        P.bn_stats(st[:, hh, :], xt[:, hh * 512:(hh + 1) * 512])
    P.bn_aggr(mv, st)
    rs = P.sb([128, 1], F32, "rs")
    rstd_from_ms(P, rs, mv[:, 1:2], 1)
    xn = P.sb([128, D], BF16, "xn")
    P.ts(xn, xt, mv[:, 0:1], rs, op0=ALU.subtract, op1=ALU.mult)
    ps = P.ps()
    psb = ps.v(ps.ap.bitcast(BF16))
    for kc in range(8):
        P.transpose(psb[:, kc * 128:(kc + 1) * 128], xn[:, kc * 128:(kc + 1) * 128], G.ident)
    for kc in range(8):
        P.act(hT_dst[:, kc, :], psb[:, kc * 128:(kc + 1) * 128], AF.Identity,
              bias=G.modT[:, which_sh * 8 + kc, t_idx:t_idx + 1], scale=G.modT[:, which_sc * 8 + kc, t_idx:t_idx + 1])


_CAST = {"i": 0}


def wload(P, stg, dst, src, engs=("pool", "dve", "act")):
    shape = list(dst.ap.shape)[1:]
    n = 1
    for v in shape:
        n *= v
    st = stg()
    sv = st.ap[0:dst.ap.shape[0], 0:n]
    if len(shape) == 2:
        sv = sv.rearrange("p (a b) -> p a b", a=shape[0])
    svt = st.v(sv)
    P.dma(svt, src)
    e = engs[_CAST["i"] % len(engs)]
    _CAST["i"] += 1
    P.copy(dst, svt, eng=e)


def mmg(P, items, K):
    for k in range(K):
        for (out, lf, rf) in items:
            P.mm(out, lf(k), rf(k), start=(k == 0), stop=(k == K - 1))


def make_ln_pools(P):
    R = Ctx()
    R.xt = Rot(P, [128, D], F32, "xt", 2)
    R.st = Rot(P, [128, 2, 6], F32, "bnst", 2)
    R.mv = Rot(P, [128, 2], F32, "mv", 2)
    R.rs = Rot(P, [128, 1], F32, "rs", 2)
    R.xn = Rot(P, [128, D], BF16, "xn", 2)
    return R


def ln_tile_to_hT2(P, G, R, src_dram, hT_dst, t_idx, which_sh, which_sc):
    xt = R.xt()
    P.dma(xt, src_dram)
    st = R.st(); mv = R.mv(); rs = R.rs(); xn = R.xn()
    for hh in range(2):
        P.bn_stats(st[:, hh, :], xt[:, hh * 512:(hh + 1) * 512])
    P.bn_aggr(mv, st)
    rstd_from_ms(P, rs, mv[:, 1:2], 1)
    P.ts(xn, xt, mv[:, 0:1], rs, op0=ALU.subtract, op1=ALU.mult)
    ps = P.ps()
    psb = ps.v(ps.ap.bitcast(BF16))
    for kc in range(8):
        P.transpose(psb[:, kc * 128:(kc + 1) * 128], xn[:, kc * 128:(kc + 1) * 128], G.ident)
    for kc in range(8):
        P.act(hT_dst[:, kc, :], psb[:, kc * 128:(kc + 1) * 128], AF.Identity,
              bias=G.modT[:, which_sh * 8 + kc, t_idx:t_idx + 1], scale=G.modT[:, which_sc * 8 + kc, t_idx:t_idx + 1])


def rope_apply(P, dst, src_bf, perm, cos, sin, tmp, n, rows=128, psfn=None):
    ps = (psfn or P.ps)()
    P.mm(ps[0:rows, 0:n], perm, src_bf)
    P.tt(tmp, src_bf, cos, ALU.mult)
    P.tt(dst, ps[0:rows, 0:n], sin, ALU.mult)
    P.tt(dst, dst, tmp, ALU.add)


def alloc_persist(P, G):
    G.kT = P.sb([128, NK], BF16, "kT")
    G.Vg = P.sb([128, NKT, 2, 128], BF16, "Vg")
    G.ckvT = P.sb([128, 2, NK], BF16, "ckvT")
    G.krT = P.sb([32, NK], BF16, "krT")


def stage_A(P, I, G):
    P.memset(G.Vg[:, :, :, 64:128], 1.0)
    with P.scope():
        w_st = P.sb([128, 8, NST], BF16, "w_st")
        stg = Rot(P, [128, NST], F32, "stg", 1)
        for kc in range(8):
            wload(P, stg, w_st[:, kc, :], dT(I["w_in"].ap[kc * 128:(kc + 1) * 128, 0:NST], "w_in"))
        kg = P.sb([128, 1], F32, "kg")
        for r in range(2):
            P.dma(kg[r * 64:(r + 1) * 64, :], dT(I["a_k_gain"].ap.rearrange("(p o) -> p o", o=1), "akg"))
        cg = P.sb([128, 2], F32, "cg")
        P.dma(cg, dT(I["c_kv_a_gain"].ap.rearrange("(j p) -> p j", p=128), "ckg"), allow_slow_non_contiguous=True)
        R = make_ln_pools(P)
        hTs = Rot(P, [128, 8, 512], BF16, "hT", 2)
        sq = Rot(P, [128, 512], BF16, "sq", 2)
        rst = Rot(P, [128, 512], F32, "rst", 1)
        knb = Rot(P, [128, 512], BF16, "knb", 3)
        tmp = Rot(P, [128, 512], F32, "tmp", 1)
        cosb = Rot(P, [128, 512], F32, "cosb", 1)
        sinb = Rot(P, [128, 512], F32, "sinb", 1)
        cos32 = Rot(P, [32, 512], F32, "cos32", 1)
        sin32 = Rot(P, [32, 512], F32, "sin32", 1)
        blocks = [(0, 2, True)] + [(2 + 4 * i, 4, False) for i in range(8)]
        for (t0, nt, is_ctx) in blocks:
            n = nt * 128
            c0 = t0 * 128
            hT = hTs()
            for ti in range(nt):
                t = t0 + ti
                src = I["ctx"].rows(t * 128) if is_ctx else I["x_all"].rows((t - 2) * 128)
                ln_tile_to_hT2(P, G, R, src, hT[:, :, ti * 128:(ti + 1) * 128], 1 if is_ctx else 0, 0, 1)
            if not is_ctx:
                lc = c0 - CTX
                cb, sb_, c32, s32 = cosb(), sinb(), cos32(), sin32()
                P.dma(cb[:, 0:n], dT(I["ropek_cos"].ap[:, lc:lc + n], "rc"))
                P.dma(sb_[:, 0:n], dT(I["ropek_sin"].ap[:, lc:lc + n], "rs"))
                P.dma(c32[:, 0:n], dT(I["rope32k_cos"].ap[:, lc:lc + n], "rc32"))
                P.dma(s32[:, 0:n], dT(I["rope32k_sin"].ap[:, lc:lc + n], "rs32"))
            pk = P.ps(); pc = [P.ps(), P.ps()]; pr = P.ps()
            mmg(P, [(pk[:, 0:n], lambda k: w_st[:, k, OFF_AK:OFF_AK + 128], lambda k: hT[:, k, 0:n]),
                    (pc[0][:, 0:n], lambda k: w_st[:, k, OFF_CKV:OFF_CKV + 128], lambda k: hT[:, k, 0:n]),
                    (pc[1][:, 0:n], lambda k: w_st[:, k, OFF_CKV + 128:OFF_CKV + 256], lambda k: hT[:, k, 0:n]),
                    (pr[0:32, 0:n], lambda k: w_st[:, k, OFF_CKR:OFF_CKR + 32], lambda k: hT[:, k, 0:n])], 8)
            s = sq()
            P.act(s[:, 0:n], pk[:, 0:n], AF.Square)
            pm = P.ps()
            P.mm(pm[:, 0:n], G.blk64, s[:, 0:n])
            rs = rst()
            rstd_from_ms(P, rs[:, 0:n], pm[:, 0:n], n)
            kn = knb()
            P.stt(kn[:, 0:n], pk[:, 0:n], kg[:, 0:1], rs[:, 0:n], ALU.mult, ALU.mult)
            if is_ctx:
                P.copy(G.kT[:, c0:c0 + n], kn[:, 0:n])
            else:
                rope_apply(P, G.kT[:, c0:c0 + n], kn[:, 0:n], G.perm64, cb[:, 0:n], sb_[:, 0:n], tmp()[:, 0:n], n)
            ss = [sq(), sq()]
            for j in range(2):
                P.act(ss[j][:, 0:n], pc[j][:, 0:n], AF.Square)
            pm = P.ps()
            for j in range(2):
                P.mm(pm[:, 0:n], G.ones, ss[j][:, 0:n], start=(j == 0), stop=(j == 1))
            rs = rst()
            P.ts(rs[:, 0:n], pm[:, 0:n], 1.0 / 256, EPS, op0=ALU.mult, op1=ALU.add)
            P.act(rs[:, 0:n], rs[:, 0:n], AF.Sqrt)
            P.recip(rs[:, 0:n], rs[:, 0:n])
            for j in range(2):
                P.stt(G.ckvT[:, j, c0:c0 + n], pc[j][:, 0:n], cg[:, j:j + 1], rs[:, 0:n], ALU.mult, ALU.mult)
            if is_ctx:
                P.copy(G.krT[:, c0:c0 + n], pr[0:32, 0:n])
            else:
                kr = knb()
                P.copy(kr[0:32, 0:n], pr[0:32, 0:n])
                rope_apply(P, G.krT[:, c0:c0 + n], kr[0:32, 0:n], G.perm32, c32[:, 0:n], s32[:, 0:n], tmp()[0:32, 0:n], n, rows=32)
            pvs = [P.ps() for _ in range(nt)]
            mmg(P, [(pvs[ti][:, 0:128], (lambda k, ti=ti: hT[:, k, ti * 128:(ti + 1) * 128]),
                     lambda k: w_st[:, k, OFF_AV:OFF_AV + 128]) for ti in range(nt)], 8)
            for ti in range(nt):
                pv = pvs[ti]
                P.copy(G.Vg[:, t0 + ti, :, 0:64], pv.v(pv.ap[:, 0:128].rearrange("p (a b) -> p a b", a=2)), eng="act")
            pus = [P.ps() for _ in range(4)]
            mmg(P, [(pus[j][:, 0:n], (lambda k, j=j: w_st[:, k, OFF_U + j * 128:OFF_U + (j + 1) * 128]),
                     lambda k: hT[:, k, 0:n]) for j in range(4)], 8)
            for j in range(4):
                dst = G.uT.v(G.uT.ap[:, j, :, c0 // 8:(c0 + n) // 8].rearrange("p i c -> p c i"))
                src = pus[j].v(pus[j].ap[:, 0:n].rearrange("p (c i) -> p c i", i=8))
                P.copy(dst, src, eng=("act" if j % 2 else "dve"))


def own_blocks(last):
    bl = [(i * 512, 512, False, i * 512) for i in range(4)]
    if not last:
        bl.append((LH, 256, True, 0))
    return bl


def stage_B1(P, I, G, last):
    NQ = LH + (0 if last else CTX)
    G.NQ = NQ
    G.qg = P.sb([128, 4, NQ], BF16, "qg")
    G.qm = P.sb([96, 8, NQ], BF16, "qm")
    G.gD = P.dram([3 * D, NQ], BF16, "gD")
    with P.scope():
        hT = P.sb([128, 8, NQ], BF16, "hTall")
        with P.scope():
            R = make_ln_pools(P)
            for (c0, n, is_ctx, r0) in own_blocks(last):
                for ti in range(n // 128):
                    src = (I["ctx"] if is_ctx else I["x_own"]).rows(r0 + ti * 128)
                    ln_tile_to_hT2(P, G, R, src, hT[:, :, c0 + ti * 128:c0 + (ti + 1) * 128], 1 if is_ctx else 0, 0, 1)
        qgain = P.sb([128, 1], F32, "qgain")
        for r in range(2):
            P.dma(qgain[r * 64:(r + 1) * 64, :], dT(I["a_q_gain"].ap.rearrange("(p o) -> p o", o=1), "aqg"))
        cqg = P.sb([128, 6], F32, "cqg")
        P.dma(cqg, dT(I["c_q_a_gain"].ap.rearrange("(j p) -> p j", p=128), "cqg"), allow_slow_non_contiguous=True)
        perm32h = P.sb([96, 32], BF16, "perm32h")
        P.dma(perm32h[64:96, :], I["perm32"], eng="pool")
        cosqR = Rot(P, [128, 512], F32, "cosq", 2)
        sinqR = Rot(P, [128, 512], F32, "sinq", 2)
        cos32R = Rot(P, [96, 512], F32, "cos32q", 2)
        sin32R = Rot(P, [96, 512], F32, "sin32q", 2)
        sq = Rot(P, [128, 512], BF16, "sq", 6)
        rst = Rot(P, [128, 512], F32, "rst", 2)
        knb = Rot(P, [128, 512], BF16, "knb", 2)
        tmp = Rot(P, [128, 512], F32, "tmp", 2)
        with P.scope():
            wq = P.sb([128, 8, 4, 128], BF16, "wq")
            stg = Rot(P, [128, 768], F32, "stg", 2)
            for kc in range(8):
                for hf in range(2):
                    src = I["w_in"].ap[kc * 128:(kc + 1) * 128, OFF_AQ + hf * 256:OFF_AQ + (hf + 1) * 256].rearrange("p (a b) -> p a b", a=4)
                    wload(P, stg, wq[:, kc, :, hf * 64:(hf + 1) * 64], dT(src, "w_in"))
            for (c0, n, is_ctx, r0) in own_blocks(last):
                if not is_ctx:
                    cosq = cosqR(); sinq = sinqR()
                    P.dma(cosq, dT(I["ropeq_cos"].ap[:, c0:c0 + n], "rqc"))
                    P.dma(sinq, dT(I["ropeq_sin"].ap[:, c0:c0 + n], "rqs"))
                pks = [P.ps() for _ in range(4)]
                mmg(P, [(pks[hd][:, 0:n], (lambda k, hd=hd: wq[:, k, hd, :]), lambda k: hT[:, k, c0:c0 + n]) for hd in range(4)], 8)
                for hd in range(4):
                    pk = pks[hd]
                    s = sq()
                    P.act(s[:, 0:n], pk[:, 0:n], AF.Square)
                    pm = P.ps_acc()
                    P.mm(pm[:, 0:n], G.blk64, s[:, 0:n])
                    rs = rst()
                    rstd_from_ms(P, rs[:, 0:n], pm[:, 0:n], n)
                    if is_ctx:
                        P.stt(G.qg[:, hd, c0:c0 + n], pk[:, 0:n], qgain[:, 0:1], rs[:, 0:n], ALU.mult, ALU.mult)
                    else:
                        kn = knb()
                        P.stt(kn[:, 0:n], pk[:, 0:n], qgain[:, 0:1], rs[:, 0:n], ALU.mult, ALU.mult)
                        rope_apply(P, G.qg[:, hd, c0:c0 + n], kn[:, 0:n], G.perm64, cosq[:, 0:n], sinq[:, 0:n],
                                   tmp()[:, 0:n], n, psfn=P.ps_acc)
        with P.scope():
            wc = P.sb([128, 8, 768], BF16, "wc")
            stg = Rot(P, [128, 768], F32, "stg", 1)
            for kc in range(8):
                wload(P, stg, wc[:, kc, :], dT(I["w_in"].ap[kc * 128:(kc + 1) * 128, OFF_CQ:OFF_CQ + 768], "w_in"))
            wqb = P.sb([128, 6, 768], BF16, "wqb")
            for j in range(6):
                wload(P, stg, wqb[:, j, :], dT(I["c_w_qb"].ap[j * 128:(j + 1) * 128, :], "wqb"))
            cqn = P.sb([128, 6, 512], BF16, "cqn")
            qrb = Rot(P, [96, 512], BF16, "qrb", 2)
            for (c0, n, is_ctx, r0) in own_blocks(last):
                if not is_ctx:
                    cos32 = cos32R(); sin32 = sin32R()
                    P.dma(cos32[64:96, :], dT(I["rope32q_cos"].ap[:, c0:c0 + n], "rqc32"))
                    P.dma(sin32[64:96, :], dT(I["rope32q_sin"].ap[:, c0:c0 + n], "rqs32"))
                pcs = [P.ps() for _ in range(6)]
                mmg(P, [(pcs[j][:, 0:n], (lambda k, j=j: wc[:, k, j * 128:(j + 1) * 128]), lambda k: hT[:, k, c0:c0 + n]) for j in range(6)], 8)
                sqs = []
                for j in range(6):
                    s = sq()
                    P.act(s[:, 0:n], pcs[j][:, 0:n], AF.Square)
                    sqs.append(s)
                pm = P.ps_acc()
                for j in range(6):
                    P.mm(pm[:, 0:n], G.ones, sqs[j][:, 0:n], start=(j == 0), stop=(j == 5))
                rs = rst()
                P.ts(rs[:, 0:n], pm[:, 0:n], 1.0 / 768, EPS, op0=ALU.mult, op1=ALU.add)
                P.act(rs[:, 0:n], rs[:, 0:n], AF.Sqrt)
                P.recip(rs[:, 0:n], rs[:, 0:n])
                for j in range(6):
                    P.stt(cqn[:, j, 0:n], pcs[j][:, 0:n], cqg[:, j:j + 1], rs[:, 0:n], ALU.mult, ALU.mult)
                for hg in range(2):
                    pqs = [P.ps() for _ in range(4)]
                    mmg(P, [(pqs[i][0:96, 0:n], (lambda k, h=hg * 4 + i: wqb[:, k, h * 96:(h + 1) * 96]), lambda k: cqn[:, k, 0:n]) for i in range(4)], 6)
                    for i in range(4):
                        h = hg * 4 + i
                        pq = pqs[i]
                        if is_ctx:
                            P.copy(G.qm[:, h, c0:c0 + n], pq[0:96, 0:n], eng="act")
                        else:
                            P.copy(G.qm[0:64, h, c0:c0 + n], pq[0:64, 0:n], eng="act")
                            qr = qrb()
                            P.copy(qr[64:96, 0:n], pq[64:96, 0:n])
                            pr = P.ps_acc()
                            P.mm(pr[64:96, 0:n], perm32h[64:96, :], qr[64:96, 0:n])
                            t = tmp()
                            P.tt(t[64:96, 0:n], qr[64:96, 0:n], cos32[64:96, 0:n], ALU.mult)
                            t2 = tmp()
                            P.tt(t2[64:96, 0:n], pr[64:96, 0:n], sin32[64:96, 0:n], ALU.mult)
                            P.tt(G.qm[64:96, h, c0:c0 + n], t[64:96, 0:n], t2[64:96, 0:n], ALU.add)
        with P.scope():
            wg = Rot(P, [128, 8, 512], BF16, "wg", 2)
            stg = Rot(P, [128, 512], F32, "stg", 3)
            gb = Rot(P, [128, 512], BF16, "gb", 3)
            for gi in range(6):
                w = wg()
                for kc in range(8):
                    wload(P, stg, w[:, kc, :], dT(I["w_in"].ap[kc * 128:(kc + 1) * 128, OFF_GATE + gi * 512:OFF_GATE + (gi + 1) * 512], "w_in"), engs=("pool", "dve"))
                for (c0, n, is_ctx, r0) in own_blocks(last):
                    pgs = [P.ps() for _ in range(4)]
                    mmg(P, [(pgs[oc][:, 0:n], (lambda k, oc=oc: w[:, k, oc * 128:(oc + 1) * 128]), lambda k: hT[:, k, c0:c0 + n]) for oc in range(4)], 8)
                    for oc in range(4):
                        g = gb()
                        P.act(g[:, 0:n], pgs[oc][:, 0:n], AF.Sigmoid)
                        row = (gi * 4 + oc) * 128
                        P.dma(G.gD.v(G.gD.ap[row:row + 128, c0:c0 + n]), g[:, 0:n])


def run_attn(P, chains, pT, scale):
    LA = 2
    nkt = chains[0][1]
    pls = [dict() for _ in chains]
    for kt in range(nkt + LA):
        if kt < nkt:
            for ci, (po, _, n, slf, srhs, vlf) in enumerate(chains):
                pss = P.ps()
                P.mm(pss[:, 0:n], slf(kt), srhs)
                p = pT()
                P.act(p[:, 0:n], pss[:, 0:n], AF.Exp, scale=scale)
                pls[ci][kt] = p
        jj = kt - LA
        if jj >= 0:
            for ci, (po, _, n, slf, srhs, vlf) in enumerate(chains):
                P.mm(po[:, 0:n], vlf(jj), pls[ci].pop(jj)[:, 0:n], start=(jj == 0), stop=(jj == nkt - 1))


def block_groups(last):
    bl = own_blocks(last)
    groups = [bl[0:2], bl[2:4]]
    if not last:
        groups.append(bl[4:5])
    return groups


def attn_finish(P, po, n, rec, yo, dst):
    r = rec()
    P.recip(r[64:128, 0:n], po[64:128, 0:n])
    y = yo()
    P.tt(y[:, 0:n], po[0:64, 0:n], r[64:128, 0:n], ALU.mult)
    P.dma(dst, y[:, 0:n])


def stage_B2(P, I, G, last):
    NQ = G.NQ
    G.yaD = P.dram([512, NQ], BF16, "yaD")
    with P.scope():
        pT = Rot(P, [128, 512], BF16, "pT", 8)
        rec = Rot(P, [128, 512], F32, "rec", 2)
        yo = Rot(P, [64, 512], BF16, "yo", 2)
        kTp = P.sb([128, 2, NK], BF16, "kTp")
        P.memset(kTp, 0.0)
        P.copy(kTp[0:64, 0, :], G.kT[0:64, :], eng="pool")
        P.copy(kTp[64:128, 1, :], G.kT[64:128, :], eng="dve")
        for hd in range(4):
            for kvh in range(2):
                head = hd + 4 * kvh
                for grp in block_groups(last):
                    chains = []
                    for (c0, n, is_ctx, r0) in grp:
                        nkt = 2 if is_ctx else NKT
                        chains.append((P.ps_acc(), nkt, n, (lambda kt: kTp[:, kvh, kt * 128:(kt + 1) * 128]),
                                       G.qg[:, hd, c0:c0 + n], (lambda kt: G.Vg[:, kt, kvh, :])))
                    run_attn(P, chains, pT, 0.125)
                    for (po, _, n, _, _, _), (c0, _, _, _) in zip(chains, grp):
                        attn_finish(P, po, n, rec, yo, G.yaD.v(G.yaD.ap[head * 64:(head + 1) * 64, c0:c0 + n]))


def stage_B3(P, I, G, last):
    NQ = G.NQ
    G.ycD = P.dram([512, NQ], BF16, "ycD")
    with P.scope():
        wkv = P.sb([128, 2, 1024], BF16, "wkv")
        stg = Rot(P, [128, 1024], F32, "stg", 1)
        for j in range(2):
            wload(P, stg, wkv[:, j, :], dT(I["c_w_kvb"].ap[j * 128:(j + 1) * 128, :], "wkvb"))
        Kh = Rot(P, [96, NK], BF16, "Kh", 2)
        Vh = [P.sb([128, NKT, 128], BF16, f"Vh{i}") for i in range(2)]
        for v in Vh:
            P.memset(v[:, :, 64:128], 1.0)
        pT = Rot(P, [128, 512], BF16, "pT", 8)
        rec = Rot(P, [128, 512], F32, "rec", 2)
        yo = Rot(P, [64, 512], BF16, "yo", 2)
        scale = 96 ** -0.5
        for h in range(8):
            K = Kh(); V = Vh[h % 2]
            cbs = [(cb * 512, min(512, NK - cb * 512)) for cb in range(9)]
            for g0 in range(0, 9, 3):
                grp = cbs[g0:g0 + 3]
                pks = [P.ps() for _ in grp]
                mmg(P, [(pks[i][0:64, 0:n], lambda k: wkv[:, k, h * 128:h * 128 + 64], (lambda k, k0=k0, n=n: G.ckvT[:, k, k0:k0 + n]))
                        for i, (k0, n) in enumerate(grp)], 2)
                for i, (k0, n) in enumerate(grp):
                    P.copy(K[0:64, k0:k0 + n], pks[i][0:64, 0:n], eng=("act" if i % 2 else "dve"))
            P.copy(K[64:96, :], G.krT[0:32, :], eng="pool")
            for g0 in range(0, NKT, 4):
                kts = list(range(g0, min(g0 + 4, NKT)))
                pvs = [P.ps() for _ in kts]
                mmg(P, [(pvs[i][:, 0:64], (lambda k, kt=kt: G.ckvT[:, k, kt * 128:(kt + 1) * 128]),
                         lambda k: wkv[:, k, h * 128 + 64:h * 128 + 128]) for i, kt in enumerate(kts)], 2)
                for i, kt in enumerate(kts):
                    P.copy(V[:, kt, 0:64], pvs[i][:, 0:64], eng=("act" if kt % 2 else "dve"))
            for grp in block_groups(last):
                chains = []
                for (c0, n, is_ctx, r0) in grp:
                    nkt = 2 if is_ctx else NKT
                    chains.append((P.ps_acc(), nkt, n, (lambda kt: K[0:96, kt * 128:(kt + 1) * 128]),
                                   G.qm[0:96, h, c0:c0 + n], (lambda kt: V[:, kt, :])))
                run_attn(P, chains, pT, scale)
                for (po, _, n, _, _, _), (c0, _, _, _) in zip(chains, grp):
                    attn_finish(P, po, n, rec, yo, G.ycD.v(G.ycD.ap[h * 64:(h + 1) * 64, c0:c0 + n]))


def bcast_load(P, dst, src_ap_1d):
    P.dma(dst, dT(src_ap_1d.partition_broadcast(128), "bc"))


def stage_B4a(P, I, G, last):
    NQ = G.NQ
    G.xmidD = P.dram([NQ, D], F32, "xmidD")
    G.h2D = P.dram([D, NQ], BF16, "h2D")
    with P.scope():
        wglu = P.sb([128, 4, 1024], BF16, "wglu")
        wb = [P.sb([128, 4, 1024], BF16, f"wb{i}") for i in range(3)]
        wout = P.sb([128, 8, 1024], BF16, "wout")
        stg = Rot(P, [128, 1024], F32, "stg", 2)
        for j in range(4):
            wload(P, stg, wglu[:, j, :], dT(I["s5_w_glu"].ap[j * 128:(j + 1) * 128, :], "w"))
            for i, nm in enumerate(["w_branch_a", "w_branch_s5", "w_branch_c"]):
                wload(P, stg, wb[i][:, j, :], dT(I[nm].ap[j * 128:(j + 1) * 128, :], "w"))
        for kc in range(8):
            wload(P, stg, wout[:, kc, :], dT(I["w_out"].ap[kc * 128:(kc + 1) * 128, :], "w"))
        g1b = [P.sb([128, D], F32, f"g1b{t}") for t in range(2)]
        for t in range(2):
            P.dma(g1b[t], dT(G.modD.ap[t, 2 * D:3 * D].partition_broadcast(128), "modD_r"))
        lng = P.sb([128, D], F32, "lng"); bcast_load(P, lng, I["ln1_g"].ap)
        lnb = P.sb([128, D], F32, "lnb"); bcast_load(P, lnb, I["ln1_b"].ap)
        gates = P.sb([128, 24, 512], BF16, "gates")
        srcs = [P.sb([128, 4, 512], BF16, f"src{i}") for i in range(3)]
        yT = P.sb([128, 4, 512], BF16, "yT")
        t1 = P.sb([128, 4, 512], F32, "t1")
        sg = Rot(P, [128, 512], F32, "sg", 2)
        acc = Rot(P, [128, 512], F32, "acc", 2)
        tmpm = Rot(P, [128, 512], F32, "tmpm", 2)
        merged = P.sb([128, 8, 512], BF16, "merged")
        xtR = Rot(P, [128, D], F32, "xt", 2)
        tsR = Rot(P, [128, D], F32, "tsum", 3)
        xmR = Rot(P, [128, D], F32, "xm", 2)
        stR = Rot(P, [128, 2, 6], F32, "bnst", 2); mvR = Rot(P, [128, 2], F32, "mv", 2); rsR = Rot(P, [128, 1], F32, "rs", 2)
        xnR = Rot(P, [128, D], BF16, "xn", 2)
        h2T = P.sb([128, 8, 512], BF16, "h2T")
        for (c0, n, is_ctx, r0) in own_blocks(last):
            tix = 1 if is_ctx else 0
            for j in range(4):
                P.dma(yT[:, j, 0:n], G.ysD.v(G.ysD.ap[j * 128:(j + 1) * 128, c0:c0 + n]))
                P.dma(srcs[0][:, j, 0:n], G.yaD.v(G.yaD.ap[j * 128:(j + 1) * 128, c0:c0 + n]))
                P.dma(srcs[2][:, j, 0:n], G.ycD.v(G.ycD.ap[j * 128:(j + 1) * 128, c0:c0 + n]))
            for gi in range(24):
                P.dma(gates[:, gi, 0:n], G.gD.v(G.gD.ap[gi * 128:(gi + 1) * 128, c0:c0 + n]))
            P.tt(t1[:, :, 0:n], yT[:, :, 0:n], yT[:, :, 0:n], ALU.mult)
            P.ts(t1[:, :, 0:n], t1[:, :, 0:n], 0.044715, 1.0, op0=ALU.mult, op1=ALU.add)
            P.tt(t1[:, :, 0:n], t1[:, :, 0:n], yT[:, :, 0:n], ALU.mult)
            P.act(t1[:, :, 0:n], t1[:, :, 0:n], AF.Sigmoid, scale=1.5957691216)
            ge = P.sb([128, 4, 512], BF16, "ge") if c0 == 0 else ge
            P.tt(ge[:, :, 0:n], t1[:, :, 0:n], yT[:, :, 0:n], ALU.mult)
            for op_ in range(2):
                pa = [P.ps(), P.ps()]; pg = [P.ps(), P.ps()]
                items = []
                for q in range(2):
                    oc = op_ * 2 + q
                    items.append((pa[q][:, 0:n], (lambda k, oc=oc: wglu[:, k, oc * 128:(oc + 1) * 128]), lambda k: ge[:, k, 0:n]))
                    items.append((pg[q][:, 0:n], (lambda k, oc=oc: wglu[:, k, 512 + oc * 128:512 + (oc + 1) * 128]), lambda k: ge[:, k, 0:n]))
                mmg(P, items, 4)
                for q in range(2):
                    oc = op_ * 2 + q
                    s = sg()
                    P.act(s[:, 0:n], pg[q][:, 0:n], AF.Sigmoid)
                    P.tt(srcs[1][:, oc, 0:n], pa[q][:, 0:n], s[:, 0:n], ALU.mult)
            for oc in range(8):
                a = acc()
                pbs = [P.ps() for _ in range(3)]
                mmg(P, [(pbs[br][:, 0:n], (lambda k, br=br: wb[br][:, k, oc * 128:(oc + 1) * 128]), (lambda k, br=br: srcs[br][:, k, 0:n])) for br in range(3)], 4)
                for br in range(3):
                    pb = pbs[br]
                    if br == 0:
                        P.tt(a[:, 0:n], pb[:, 0:n], gates[:, br * 8 + oc, 0:n], ALU.mult)
                    else:
                        tm = tmpm()
                        P.tt(tm[:, 0:n], pb[:, 0:n], gates[:, br * 8 + oc, 0:n], ALU.mult)
                        if br == 1:
                            P.tt(a[:, 0:n], a[:, 0:n], tm[:, 0:n], ALU.add, eng="pool")
                        else:
                            P.tt(merged[:, oc, 0:n], a[:, 0:n], tm[:, 0:n], ALU.add, eng="pool")
            def mix(ti):
                xt = xtR(); ts_ = tsR()
                P.dma(xt, (I["ctx"] if is_ctx else I["x_own"]).rows(r0 + ti * 128))
                pms = [P.ps(), P.ps()]
                mmg(P, [(pms[half][:, 0:512], lambda k: merged[:, k, ti * 128:(ti + 1) * 128],
                         (lambda k, half=half: wout[:, k, half * 512:(half + 1) * 512])) for half in range(2)], 8)
                for half in range(2):
                    P.tt(ts_[:, half * 512:(half + 1) * 512], pms[half][:, 0:512], g1b[tix][:, half * 512:(half + 1) * 512], ALU.mult)
                P.stt(ts_, xt, ALPHA, ts_, ALU.mult, ALU.add)
                return ts_

            def chain(ti, ts_):
                st = stR(); mv = mvR(); rs = rsR(); xm = xmR()
                for hh in range(2):
                    P.bn_stats(st[:, hh, :], ts_[:, hh * 512:(hh + 1) * 512])
                P.bn_aggr(mv, st)
                rstd_from_ms(P, rs, mv[:, 1:2], 1)
                P.stt(xm, ts_, mv[:, 0:1], lng, ALU.subtract, ALU.mult)
                P.stt(xm, xm, rs, lnb, ALU.mult, ALU.add)
                P.dma(G.xmidD.v(G.xmidD.ap[c0 + ti * 128:c0 + (ti + 1) * 128, :]), xm)
                st = stR(); mv = mvR(); rs = rsR(); xn = xnR()
                for hh in range(2):
                    P.bn_stats(st[:, hh, :], xm[:, hh * 512:(hh + 1) * 512])
                P.bn_aggr(mv, st)
                rstd_from_ms(P, rs, mv[:, 1:2], 1)
                P.ts(xn, xm, mv[:, 0:1], rs, op0=ALU.subtract, op1=ALU.mult)
                return xn

            def xpose(ti, xn):
                ps = P.ps()
                psb = ps.v(ps.ap.bitcast(BF16))
                for kc in range(8):
                    P.transpose(psb[:, kc * 128:(kc + 1) * 128], xn[:, kc * 128:(kc + 1) * 128], G.ident)
                for kc in range(8):
                    P.act(h2T[:, kc, ti * 128:(ti + 1) * 128], psb[:, kc * 128:(kc + 1) * 128], AF.Identity,
                          bias=G.modT[:, 3 * 8 + kc, tix:tix + 1], scale=G.modT[:, 4 * 8 + kc, tix:tix + 1])
            nti = n // 128
            ts_cur = mix(0)
            for ti in range(nti):
                ts_nxt = mix(ti + 1) if ti + 1 < nti else None
                xn = chain(ti, ts_cur)
                xpose(ti, xn)
                ts_cur = ts_nxt
            for kc in range(8):
                P.dma(G.h2D.v(G.h2D.ap[kc * 128:(kc + 1) * 128, c0:c0 + n]), h2T[:, kc, 0:n])


def stage_B4b(P, I, G, last, out_own, out_ctx):
    NQ = G.NQ
    NT = NQ // 128
    with P.scope():
        g2b = [P.sb([128, D], F32, f"g2b{t}") for t in range(2)]
        for t in range(2):
            P.dma(g2b[t], dT(G.modD.ap[t, 5 * D:6 * D].partition_broadcast(128), "modD_r"))
        lng = P.sb([128, D], F32, "lng"); bcast_load(P, lng, I["ln2_g"].ap)
        lnb = P.sb([128, D], F32, "lnb"); bcast_load(P, lnb, I["ln2_b"].ap)
        h2T = P.sb([128, 8, NQ], BF16, "h2Tall")
        for kc in range(8):
            P.dma(h2T[:, kc, :], G.h2D.v(G.h2D.ap[kc * 128:(kc + 1) * 128, :]))
        tsum = P.sb([128, NT, D], F32, "tsum2")
        wuR = Rot(P, [128, 8, 512], BF16, "wu", 2)
        wdR = Rot(P, [128, 4, D], BF16, "wd", 2)
        stg = Rot(P, [128, 1024], F32, "stg", 3)
        aR = Rot(P, [128, 4, 512], BF16, "aog", 3)
        rl = Rot(P, [128, 512], BF16, "rl", 4)
        xmR = Rot(P, [128, D], F32, "xm", 2)
        stR = Rot(P, [128, 2, 6], F32, "bnst", 2); mvR = Rot(P, [128, 2], F32, "mv", 2); rsR = Rot(P, [128, 1], F32, "rs", 2)
        for og in range(8):
            wu = wuR(); wd = wdR()
            for kc in range(8):
                wload(P, stg, wu[:, kc, :], dT(I["w_up"].ap[kc * 128:(kc + 1) * 128, og * 512:(og + 1) * 512], "w"), engs=("pool", "act"))
            for oc in range(4):
                wload(P, stg, wd[:, oc, :], dT(I["w_down"].ap[og * 512 + oc * 128:og * 512 + (oc + 1) * 128, :], "w"), engs=("pool", "act"))
            blks = own_blocks(last)

            def up(blk):
                (c0, n, is_ctx, r0) = blk
                a = aR()
                pus = [P.ps() for _ in range(4)]
                mmg(P, [(pus[oc][:, 0:n], (lambda k, oc=oc: wu[:, k, oc * 128:(oc + 1) * 128]), lambda k: h2T[:, k, c0:c0 + n]) for oc in range(4)], 8)
                for oc in range(4):
                    r = rl()
                    P.act(r[:, 0:n], pus[oc][:, 0:n], AF.Relu)
                    P.tt(a[:, oc, 0:n], pus[oc][:, 0:n], r[:, 0:n], ALU.mult)
                return a

            def down(blk, a):
                (c0, n, is_ctx, r0) = blk
                combos = [(ti, half) for ti in range(n // 128) for half in range(2)]
                for g0 in range(0, len(combos), 4):
                    grp = combos[g0:g0 + 4]
                    pds = [P.ps() for _ in grp]
                    mmg(P, [(pds[i][:, 0:512], (lambda k, ti=ti: a[:, k, ti * 128:(ti + 1) * 128]),
                             (lambda k, half=half: wd[:, k, half * 512:(half + 1) * 512])) for i, (ti, half) in enumerate(grp)], 4)
                    for i, (ti, half) in enumerate(grp):
                        tile = c0 // 128 + ti
                        dst = tsum[:, tile, half * 512:(half + 1) * 512]
                        if og == 0:
                            P.copy(dst, pds[i][:, 0:512], eng="act")
                        else:
                            P.tt(dst, pds[i][:, 0:512], dst, ALU.add)
            a_cur = up(blks[0])
            for bi, blk in enumerate(blks):
                a_nxt = up(blks[bi + 1]) if bi + 1 < len(blks) else None
                down(blk, a_cur)
                a_cur = a_nxt
        for (c0, n, is_ctx, r0) in own_blocks(last):
            tix = 1 if is_ctx else 0
            for ti in range(n // 128):
                tile = c0 // 128 + ti
                xm = xmR()
                P.dma(xm, G.xmidD.v(G.xmidD.ap[c0 + ti * 128:c0 + (ti + 1) * 128, :]))
                P.tt(tsum[:, tile, :], tsum[:, tile, :], g2b[tix], ALU.mult, eng="pool")
                P.stt(tsum[:, tile, :], xm, ALPHA, tsum[:, tile, :], ALU.mult, ALU.add)
                st = stR(); mv = mvR(); rs = rsR()
                for hh in range(2):
                    P.bn_stats(st[:, hh, :], tsum[:, tile, hh * 512:(hh + 1) * 512])
                P.bn_aggr(mv, st)
                rstd_from_ms(P, rs, mv[:, 1:2], 1)
                o = xm
                P.stt(o, tsum[:, tile, :], mv[:, 0:1], lng, ALU.subtract, ALU.mult)
                P.stt(o, o, rs, lnb, ALU.mult, ALU.add)
                dst = out_ctx if is_ctx else out_own
                P.dma(dst.rows(r0 + ti * 128), o)


def bc(t, pattern):
    a = t.ap
    return t.v(bass.AP(a.tensor, a.offset, [list(a.ap[0])] + [list(p) for p in pattern]))


MAGIC = 12582912.0
TWO_PI = 6.283185307179586


def s5_load(P, I, G):
    Lt = Ctx()
    Lt.are = P.sb([128, 64], F32, "are"); Lt.aim = P.sb([128, 64], F32, "aim"); Lt.ldt = P.sb([128, 64], F32, "ldt")
    Lt.Bre = P.sb([128, 64, 16], F32, "Bre"); Lt.Bim = P.sb([128, 64, 16], F32, "Bim")
    Lt.Cre = P.sb([128, 64, 16], F32, "Cre"); Lt.Cim = P.sb([128, 64, 16], F32, "Cim")
    for hf in range(2):
        sl = slice(hf * 64, (hf + 1) * 64)
        P.dma(Lt.are[sl, :], dT(I["s5_a_re"].ap.rearrange("d g p -> p (d g)"), "a"), allow_slow_non_contiguous=True, eng="act")
        P.dma(Lt.aim[sl, :], dT(I["s5_a_im"].ap.rearrange("d g p -> p (d g)"), "a"), allow_slow_non_contiguous=True, eng="act")
        P.dma(Lt.Bre[sl], dT(I["s5_b_re"].ap.rearrange("d g p c -> p (d g) c"), "a"))
        P.dma(Lt.Bim[sl], dT(I["s5_b_im"].ap.rearrange("d g p c -> p (d g) c"), "a"))
        P.dma(Lt.Cre[sl], dT(I["s5_c_re"].ap.rearrange("d g c p -> p (d g) c"), "a"), allow_slow_non_contiguous=True, eng="act")
        P.dma(Lt.Cim[sl], dT(I["s5_c_im"].ap.rearrange("d g c p -> p (d g) c"), "a"), allow_slow_non_contiguous=True, eng="act")
    P.dma(Lt.ldt, dT(I["s5_log_dt"].ap.rearrange("d g -> (d g)").partition_broadcast(128), "a"))
    G.s5l = Lt


def s5_alloc(P, I, G):
    PR = P.sb([128, 32, 64], F32, "PR"); NPI = P.sb([128, 32, 64], F32, "NPI")
    DA = P.sb([128, 10, 64], F32, "DA"); DB = P.sb([128, 10, 64], F32, "DB")
    BX1 = P.sb([128, 64, 16], F32, "BX1"); BX2 = P.sb([128, 64, 16], F32, "BX2")
    CX1 = P.sb([128, 64, 16], F32, "CX1"); CX2 = P.sb([128, 64, 16], F32, "CX2")
    Dcol = P.sb([128, 32], F32, "Dcol")
    identF = G.ident
    swapF = P.sb([128, 128], BF16, "swapF"); P.dma(swapF, I["swap"], eng="pool")
    maskf = P.sb([128, 128], BF16, "maskf"); P.dma(maskf, I["mask_f"], eng="pool")
    maskb = P.sb([128, 128], BF16, "maskb"); P.dma(maskb, I["mask_b"], eng="pool")
    sgn = P.sb([128, 1], F32, "sgn"); P.memset(sgn[0:64, :], 1.0); P.memset(sgn[64:128, :], -1.0)
    for i in range(8):
        P.dma(Dcol[i * 16:(i + 1) * 16, :], dT(I["s5_d"].ap.rearrange("(g c) -> c g", c=16), "s5d"), allow_slow_non_contiguous=True)
    G.s5t = dict(PR=PR, NPI=NPI, DA=DA, DB=DB, BX1=BX1, BX2=BX2, CX1=CX1, CX2=CX2, Dcol=Dcol, identF=identF, swapF=swapF, maskf=maskf, maskb=maskb, sgn=sgn)


def s5_params(P, I, G):
    PR = G.s5t["PR"]
    NPI = G.s5t["NPI"]
    DA = G.s5t["DA"]
    DB = G.s5t["DB"]
    BX1 = G.s5t["BX1"]
    BX2 = G.s5t["BX2"]
    CX1 = G.s5t["CX1"]
    CX2 = G.s5t["CX2"]
    Dcol = G.s5t["Dcol"]
    identF = G.s5t["identF"]
    swapF = G.s5t["swapF"]
    maskf = G.s5t["maskf"]
    maskb = G.s5t["maskb"]
    sgn = G.s5t["sgn"]
    with P.scope():
        are, aim, ldt = G.s5l.are, G.s5l.aim, G.s5l.ldt
        dt_ = P.sb([128, 64], F32, "dt")
        P.act(dt_, ldt, AF.Exp)
        lr = P.sb([128, 64], F32, "lr"); li = P.sb([128, 64], F32, "li")
        P.tt(lr, are, dt_, ALU.mult); P.tt(li, aim, dt_, ALU.mult)
        with P.scope():
            elist = [t - 7 for t in range(16)] + [8 - t for t in range(16)]
            LR = P.sb([128, 32, 64], F32, "LR"); LI = P.sb([128, 32, 64], F32, "LI")
            for idx, e in enumerate(elist):
                P.ts(LR[:, idx, :], lr, float(e), None, op0=ALU.mult)
                P.ts(LI[:, idx, :], li, float(e), None, op0=ALU.mult, eng="pool")
            mag = P.sb([128, 32, 64], F32, "mag")
            P.act(mag, LR, AF.Exp)
            rr = P.sb([128, 32, 64], F32, "rr"); kk = P.sb([128, 32, 64], F32, "kk")

            def sin_of(dst, ang_t, shift):
                P.ts(rr, ang_t, 1.0 / TWO_PI, shift / TWO_PI, op0=ALU.mult, op1=ALU.add)
                P.ts(kk, rr, MAGIC, None, op0=ALU.add)
                P.ts(kk, kk, MAGIC, None, op0=ALU.subtract)
                P.tt(rr, rr, kk, ALU.subtract)
                P.ts(rr, rr, TWO_PI, None, op0=ALU.mult)
                P.ts(rr, rr, 3.1415925, -3.1415925, op0=ALU.min, op1=ALU.max)
                P.act(dst, rr, AF.Sin)
            sn = LR
            sin_of(sn, LI, 0.0)
            P.stt(NPI, mag, -1.0, sn, ALU.mult, ALU.mult)
            sin_of(sn, LI, TWO_PI / 4)
            P.tt(PR, mag, sn, ALU.mult)
        cr_ = P.sb([128, 64], F32, "cr"); ci_ = P.sb([128, 64], F32, "ci")
        t1 = P.sb([128, 64], F32, "t1"); t2 = P.sb([128, 64], F32, "t2")
        P.copy(DA[:, 0, :], PR[:, 15, :])
        P.ts(DB[:, 0, :], NPI[:, 15, :], -1.0, None, op0=ALU.mult)
        for m in range(1, 10):
            P.tt(t1, DA[:, m - 1, :], DA[:, m - 1, :], ALU.mult)
            P.tt(t2, DB[:, m - 1, :], DB[:, m - 1, :], ALU.mult)
            P.tt(DA[:, m, :], t1, t2, ALU.subtract)
            P.stt(DB[:, m, :], DA[:, m - 1, :], 2.0, DB[:, m - 1, :], ALU.mult, ALU.mult)
        P.ts(DB, DB, sgn[:, 0:1], None, op0=ALU.mult)
        den = P.sb([128, 64], F32, "den"); nr = P.sb([128, 64], F32, "nr"); abi = P.sb([128, 64], F32, "abi")
        P.tt(t1, are, are, ALU.mult); P.tt(t2, aim, aim, ALU.mult); P.tt(den, t1, t2, ALU.add); P.recip(den, den)
        P.ts(nr, PR[:, 8, :], -1.0, None, op0=ALU.add)
        P.ts(abi, NPI[:, 8, :], -1.0, None, op0=ALU.mult)
        P.tt(t1, nr, are, ALU.mult); P.tt(t2, abi, aim, ALU.mult); P.tt(cr_, t1, t2, ALU.add); P.tt(cr_, cr_, den, ALU.mult)
        P.tt(t1, abi, are, ALU.mult); P.tt(t2, nr, aim, ALU.mult); P.tt(ci_, t1, t2, ALU.subtract); P.tt(ci_, ci_, den, ALU.mult)
        crb = bc(cr_, [[1, 64], [0, 16]]); cib = bc(ci_, [[1, 64], [0, 16]])
        Bre, Bim, Cre, Cim = G.s5l.Bre, G.s5l.Bim, G.s5l.Cre, G.s5l.Cim
        bbr = P.sb([128, 64, 16], F32, "bbr"); bbi = P.sb([128, 64, 16], F32, "bbi"); t3 = P.sb([128, 64, 16], F32, "t3")
        P.tt(bbr, Bre, crb, ALU.mult); P.tt(t3, Bim, cib, ALU.mult); P.tt(bbr, bbr, t3, ALU.subtract)
        P.tt(bbi, Bim, crb, ALU.mult); P.tt(t3, Bre, cib, ALU.mult); P.tt(bbi, bbi, t3, ALU.add)
        P.copy(BX1[0:64], bbr[0:64]); P.copy(BX1[64:128], bbi[64:128])
        P.copy(BX2[0:64], bbi[0:64]); P.ts(BX2[64:128], bbr[64:128], -1.0, None, op0=ALU.mult)
        P.copy(CX1[0:64], Cre[0:64]); P.ts(CX1[64:128], Cim[64:128], -1.0, None, op0=ALU.mult)
        P.copy(CX2[0:64], Cim[0:64]); P.copy(CX2[64:128], Cre[64:128])


def stage_S5(P, I, G, last):
    NQ = LH + (0 if last else CTX)
    G.ysD = P.dram([512, NQ], BF16, "ysD")
    with P.scope():
        PR = G.s5t["PR"]
        NPI = G.s5t["NPI"]
        DA = G.s5t["DA"]
        DB = G.s5t["DB"]
        BX1 = G.s5t["BX1"]
        BX2 = G.s5t["BX2"]
        CX1 = G.s5t["CX1"]
        CX2 = G.s5t["CX2"]
        Dcol = G.s5t["Dcol"]
        identF = G.s5t["identF"]
        swapF = G.s5t["swapF"]
        maskf = G.s5t["maskf"]
        maskb = G.s5t["maskb"]
        sgn = G.s5t["sgn"]
        Sel = P.sb([128, 64, 128], BF16, "Sel"); SelT = P.sb([128, 64, 128], BF16, "SelT")
        for q in range(4):
            P.dma(Sel[:, q * 16:(q + 1) * 16, :], dT(I["sel8"].ap[:, q * 16:(q + 1) * 16, :], "sel8"), eng="pool")
            P.dma(SelT[:, q * 16:(q + 1) * 16, :], dT(I["sel8T"].ap[:, q * 16:(q + 1) * 16, :], "sel8T"), eng="pool")
        KQ = {nm: Rot(P, [128, 16, 16], BF16, nm, 2) for nm in ["Kf", "Qf", "Kb", "Qb"]}
        tA = Rot(P, [128, 16, 16], F32, "tA", 1); tB = Rot(P, [128, 16, 16], F32, "tB", 1)
        tC = Rot(P, [128, 16, 16], F32, "tC", 1); tD = Rot(P, [128, 16, 16], F32, "tD", 1)
        SgR = Rot(P, [128, 128], BF16, "Sg", 2)
        WeR = Rot(P, [128, 128], BF16, "We", 4)
        s1R = Rot(P, [128, 128], F32, "s1", 1); s2R = Rot(P, [128, 128], F32, "s2", 1)
        UcR = Rot(P, [128, 576], BF16, "Uc", 2)
        XR = {d: [P.sb([128, 545], BF16, f"X{d}{i}") for i in range(2)] for d in "fb"}
        for d in "fb":
            for x in XR[d]:
                P.memset(x, 0.0)
        MdR = {d: Rot(P, [128, 10, 128], BF16, "Md" + d, 2) for d in "fb"}
        mA = Rot(P, [128, 10, 128], BF16, "mA", 1); mB = Rot(P, [128, 10, 128], BF16, "mB", 1)
        Yt = P.sb([128, 8, 544], BF16, "Yt")
        tqR = Rot(P, [128, 256], F32, "tq", 2); yctx = P.sb([128, CTX], BF16, "yctx")
        yown = Rot(P, [128, LH], BF16, "yown", 1)
        identB = G.ident
        flip = 0
        spec = {"Kf": (17, 15), "Qf": (7, 9), "Kb": (7, 8), "Qb": (16, 16)}

        def m128(t, lo):
            return t.v(t.ap[:, lo:lo + 8, :].rearrange("p a b -> p (a b)"))

        def build(g):
            stt_ = {"mats": {}, "We": {}, "Mall": {}}
            th = []
            for dname, gd in (("f", g), ("b", 32 + g)):
                for kind, X1, X2, eng in (("K", BX1, BX2, "dve"), ("Q", CX1, CX2, "pool")):
                    t0_, ne = spec[kind + dname]
                    out = KQ[kind + dname]()[:, 0:ne, :]
                    a_ = (tA() if kind == "K" else tC())[:, 0:ne, :]
                    b_ = (tB() if kind == "K" else tD())[:, 0:ne, :]
                    prb = bc(PR[:, t0_, gd:gd + 1], [[64, ne], [0, 16]]); npb = bc(NPI[:, t0_, gd:gd + 1], [[64, ne], [0, 16]])
                    x1b = bc(X1[:, gd, :], [[0, ne], [1, 16]]); x2b = bc(X2[:, gd, :], [[0, ne], [1, 16]])
                    th.append(lambda a_=a_, prb=prb, x1b=x1b, eng=eng: P.tt(a_, prb, x1b, ALU.mult, eng=eng))
                    th.append(lambda b_=b_, npb=npb, x2b=x2b, eng=eng: P.tt(b_, npb, x2b, ALU.mult, eng=eng))
                    th.append(lambda out=out, a_=a_, b_=b_, eng=eng: P.tt(out, a_, b_, ALU.add, eng=eng))
                    stt_["mats"][kind + dname] = out
            Sg = SgR()
            stt_["Sg"] = Sg
            mt = stt_["mats"]

            def th_S():
                psf = P.ps(); psb_ = P.ps()
                P.mm(psf[:, 0:128], m128(mt["Kf"], 7), m128(mt["Qf"], 0))
                P.mm(psb_[:, 0:128], m128(mt["Kb"], 0), m128(mt["Qb"], 8))
                s1 = s1R(); s2 = s2R()
                P.tt(s1, psf[:, 0:128], maskf, ALU.mult)
                P.tt(s2, psb_[:, 0:128], maskb, ALU.mult)
                P.tt(s1, s1, s2, ALU.add)
                P.stt(Sg, identF, Dcol[:, g:g + 1], s1, ALU.mult, ALU.add)
            th.append(th_S)
            for dname, key in (("f", "Kf"), ("b", "Kb")):
                w = WeR()
                stt_["We"][dname] = w

                def th_W(w=w, key=key):
                    pt = P.ps()
                    ptb = pt.v(pt.ap.bitcast(BF16))
                    P.transpose(ptb[:, 0:128], m128(mt[key], 0), identB)
                    P.copy(w, ptb[:, 0:128], eng="act")
                th.append(th_W)
            stt_["Wo"] = {"f": m128(mt["Qf"], 1), "b": m128(mt["Qb"], 0)}
            for dname, gd in (("f", g), ("b", 32 + g)):
                Mall = MdR[dname]()
                stt_["Mall"][dname] = Mall
                ta = mA(); tb = mB()
                idb = bc(identF, [[0, 10], [1, 128]]); swb = bc(swapF, [[0, 10], [1, 128]])
                dab = bc(DA[:, :, gd], [[64, 10], [0, 128]]); dbb = bc(DB[:, :, gd], [[64, 10], [0, 128]])
                th.append(lambda ta=ta, idb=idb, dab=dab: P.tt(ta, idb, dab, ALU.mult, eng="pool"))
                th.append(lambda tb=tb, swb=swb, dbb=dbb: P.tt(tb, swb, dbb, ALU.mult, eng="pool"))
                th.append(lambda Mall=Mall, ta=ta, tb=tb: P.tt(Mall, ta, tb, ALU.add, eng="pool"))
            return stt_, th

        cur_state, th0 = build(0)
        for t_ in th0:
            t_()
        for g in range(32):
            j, gl = g // 8, g % 8
            if g + 1 < 32:
                nxt_state, pend = build(g + 1)
            else:
                nxt_state, pend = None, []
            per_step = (len(pend) + 9) // 10
            Sg = cur_state["Sg"]; We = cur_state["We"]; Wo = cur_state["Wo"]
            Uc = UcR()
            puA = P.ps(); puB = P.ps(); pu2 = P.ps()
            for i in range(8):
                P.mm(puA[:, 0:256], Sel[:, gl * 8 + i, :], G.uT[:, j, i, 32:288], start=(i == 0), stop=(i == 7))
                P.mm(puB[:, 0:256], Sel[:, gl * 8 + i, :], G.uT[:, j, i, 288:544], start=(i == 0), stop=(i == 7))
                P.mm(pu2[:, 0:32], Sel[:, gl * 8 + i, :], G.uT[:, j, i, 0:32], start=(i == 0), stop=(i == 7))
            P.copy(Uc[:, 32:288], puA[:, 0:256], eng="act")
            P.copy(Uc[:, 288:544], puB[:, 0:256])
            P.copy(Uc[:, 0:32], pu2[:, 0:32], eng="act")
            P.copy(Uc[:, 544:576], pu2[:, 0:32])
            st_ = {}
            for dname, gd in (("f", g), ("b", 32 + g)):
                ucoff = 0 if dname == "f" else 32
                xoff = 1 if dname == "f" else 0
                cur = XR[dname][0]; nxt = XR[dname][1]
                pa = P.ps(); pb2 = P.ps()
                P.mm(pa[:, 0:512], We[dname], Uc[:, ucoff:ucoff + 512])
                P.mm(pb2[:, 0:32], We[dname], Uc[:, ucoff + 512:ucoff + 544])
                P.copy(cur[:, xoff:xoff + 512], pa[:, 0:512], eng="act")
                P.copy(cur[:, xoff + 512:xoff + 544], pb2[:, 0:32])
                st_[dname] = [cur, nxt, xoff, cur_state["Mall"][dname]]
            for m in range(10):
                d = 1 << m
                work = []
                for dname in ("f", "b"):
                    cur, nxt, xoff, Mall = st_[dname]
                    for (lo, hi) in ((0, 272), (272, 544)):
                        ps = P.ps()
                        if dname == "f":
                            s_ = max(lo, d)
                            has = s_ < hi
                            shift = (ps[:, s_ - lo:hi - lo], cur[:, xoff + s_ - d:xoff + hi - d]) if has else None
                        else:
                            e_ = min(hi, 544 - d)
                            has = lo < e_
                            shift = (ps[:, 0:e_ - lo], cur[:, xoff + lo + d:xoff + e_ + d]) if has else None
                        work.append((dname, ps, lo, hi, shift, cur, nxt, xoff, Mall))
                for (dname, ps, lo, hi, shift, cur, nxt, xoff, Mall) in work:
                    P.mm(ps[:, 0:hi - lo], identB, cur[:, xoff + lo:xoff + hi], start=True, stop=(shift is None))
                for (dname, ps, lo, hi, shift, cur, nxt, xoff, Mall) in work:
                    if shift is not None:
                        P.mm(shift[0], Mall[:, m, :], shift[1], start=False, stop=True)
                for (dname, ps, lo, hi, shift, cur, nxt, xoff, Mall) in work:
                    flip ^= 1
                    P.copy(nxt[:, xoff + lo:xoff + hi], ps[:, 0:hi - lo], eng=("act" if flip else "dve"))
                for dname in ("f", "b"):
                    st_[dname][0], st_[dname][1] = st_[dname][1], st_[dname][0]
                for _ in range(per_step):
                    if pend:
                        pend.pop(0)()
            while pend:
                pend.pop(0)()
            Xfin = {dname: st_[dname][0] for dname in ("f", "b")}
            Xf, Xb = Xfin["f"], Xfin["b"]
            py = P.ps()
            pyc = P.ps() if not last else None
            ytl = [(Sg, Uc[:, 32:544], Uc[:, 0:32]), (Wo["f"], Xf[:, 32:544], Xf[:, 0:32]), (Wo["b"], Xb[:, 1:513], Xb[:, 513:545])]
            for q, (lh, r1, r2) in enumerate(ytl):
                P.mm(py[:, 0:512], lh, r1, start=(q == 0), stop=(q == 2))
                if not last:
                    P.mm(pyc[:, 0:32], lh, r2, start=(q == 0), stop=(q == 2))
            P.copy(Yt[:, gl, 0:512], py[:, 0:512], eng="act")
            if not last:
                P.copy(Yt[:, gl, 512:544], pyc[:, 0:32])
            cur_state = nxt_state
            if gl == 7:
                yo = yown()
                for i in range(8):
                    ps = P.ps()
                    ps2 = P.ps() if not last else None
                    for g2 in range(8):
                        P.mm(ps[:, 0:512], SelT[:, g2 * 8 + i, :], Yt[:, g2, 0:512], start=(g2 == 0), stop=(g2 == 7))
                        if not last:
                            P.mm(ps2[:, 0:32], SelT[:, g2 * 8 + i, :], Yt[:, g2, 512:544], start=(g2 == 0), stop=(g2 == 7))
                    tq = tqR()
                    P.act(tq, ps[:, 0:256], AF.Copy, scale=G.sel[:, 0:1])
                    P.stt(yo.v(yo.ap[:, i:LH:8]), ps[:, 256:512], G.sel[:, 1:2], tq, ALU.mult, ALU.add)
                    if not last:
                        P.copy(yctx.v(yctx.ap[:, i:CTX:8]), ps2[:, 0:32])
                P.dma(G.ysD.v(G.ysD.ap[j * 128:(j + 1) * 128, 0:LH]), yo)
                if not last:
                    P.dma(G.ysD.v(G.ysD.ap[j * 128:(j + 1) * 128, LH:LH + CTX]), yctx)


def emit_layer(P, I, G, last, out_own, out_ctx):
    stage_prep(P, I, G)
    with P.scope():
        alloc_persist(P, G)
        with P.scope():
            G.uT = P.sb([128, 4, 8, NK // 8], BF16, "uT")
            s5_alloc(P, I, G)
            with P.scope():
                s5_load(P, I, G)
                stage_A(P, I, G)
                s5_params(P, I, G)
            stage_S5(P, I, G, last)
        stage_B1(P, I, G, last)
        stage_B2(P, I, G, last)
        stage_B3(P, I, G, last)
    stage_B4a(P, I, G, last)
    stage_B4b(P, I, G, last, out_own, out_ctx)
    P.flush()


def build_fused():
    nc = bass.Bass("TRN2", target_bir_lowering=False)
    Cn = declare_consts(nc)
    W = [declare_weights(nc, l) for l in range(2)]
    y_out = dT(nc.dram_tensor("y_own", [LH, D], F32, kind="ExternalOutput").ap(), "y_own")
    with ExitStack() as st:
        P = Prog(nc, st)
        P.init_psum()
        NCH = 4
        CR = LH // NCH
        x1o = [P.dram([CR, D], F32, f"x1o{c}") for c in range(NCH)]
        x1g = [P.dram([2 * CR, D], F32, f"x1g{c}") for c in range(NCH)]
        ctx1 = P.dram([CTX, D], F32, "ctx1")
        own_src = RowSrc(lambda r0: x1o[r0 // CR].v(x1o[r0 // CR].ap[r0 % CR:r0 % CR + 128, :]))

        def all_fn(r0):
            half, rr = r0 // LH, r0 % LH
            c, i = rr // CR, rr % CR
            return x1g[c].v(x1g[c].ap[half * CR + i:half * CR + i + 128, :])
        with P.scope():
            G = Ctx()
            I0 = dict(Cn); I0.update(W[0])
            for k in ("x_all", "x_own", "ctx"):
                I0[k] = flat_src(Cn[k])
            emit_layer(P, I0, G, False, own_src, flat_src(ctx1))
        groups = [[0, 1], [2, 3], [4, 5], [6, 7]]
        for c in range(NCH):
            P.add("pool", lambda e, c=c: e.collective_compute("AllGather", ALU.bypass, replica_groups=groups,
                                                              ins=[x1o[c].ap.opt()], outs=[x1g[c].ap.opt()]), [x1o[c]], [x1g[c]])
            P.add("pool", None, [x1g[c]], [])
        P.flush()
        with P.scope():
            G = Ctx()
            I1 = dict(Cn); I1.update(W[1])
            I1["x_all"] = RowSrc(all_fn); I1["x_own"] = own_src; I1["ctx"] = flat_src(ctx1)
            emit_layer(P, I1, G, True, flat_src(y_out), None)
    return nc


_NC_CACHE = {}


def kernel(**inputs):
    inputs = {k: np.asarray(v) for k, v in inputs.items()}
    C = host_constants()
    if "nc" not in _NC_CACHE:
        _NC_CACHE["nc"] = build_fused()
    nc = _NC_CACHE["nc"]
    in_maps = [per_core_inputs(inputs, core, C) for core in range(8)]
    res = run_bass_kernel_spmd(nc, in_maps, core_ids=list(range(8)))
    out = np.empty((4, L, D), np.float32)
    for core in range(8):
        b, hh = core // 2, core % 2
        out[b, hh * LH:(hh + 1) * LH] = np.asarray(res.results[core]["y_own"])
    return out
```

```python
import numpy as np
import concourse.bass as bass
import concourse.mybir as mybir
from concourse.bass_utils import run_bass_kernel_spmd
from contextlib import ExitStack, contextmanager

F32 = mybir.dt.float32
BF16 = mybir.dt.bfloat16
I32 = mybir.dt.int32
AF = mybir.ActivationFunctionType
ALU = mybir.AluOpType
AX = mybir.AxisListType

ENGS = ["pe", "act", "dve", "pool", "sp"]
DMA_WIN = 8
SAME_ENG_SYNC = True


class T:
    __slots__ = ("ap", "keys")

    def __init__(self, ap, keys):
        self.ap = ap
        self.keys = tuple(keys)

    def __getitem__(self, sl):
        return T(self.ap[sl], self.keys)

    def v(self, ap):
        return T(ap, self.keys)

    def k(self, *sub):
        return T(self.ap, [(self.keys[0],) + tuple(sub)])


class Prog:
    def __init__(self, nc, stack):
        self.nc = nc
        self.stack = stack
        self.cur = stack
        self.streams = {e: [] for e in ENGS}
        self.last_writer = {}
        self.readers = {}
        self.ndma = {e: 0 for e in ENGS}
        self.sigcount = {e: 0 for e in ENGS}
        self.waited = {e: {} for e in ENGS}
        self.nt = 0
        self.psum_banks = []
        self.psum_i = 0
        self.sem = {e: stack.enter_context(nc.semaphore(f"s_{e}")) for e in ENGS}
        self.dsem = {e: [stack.enter_context(nc.semaphore(f"d_{e}{i}")) for i in range(DMA_WIN)]
                     for e in ("sp", "pool", "act")}
        self.dbg = {}
        self.nops = {e: 0 for e in ENGS}

    def sb(self, shape, dt, name=None):
        self.nt += 1
        name = name or "t"
        nm = f"{name}_{self.nt}"
        t = self.cur.enter_context(self.nc.sbuf_tensor(nm, list(shape), dt))
        return T(t[:], [nm])

    def dram(self, shape, dt, name):
        self.nt += 1
        nm = f"{name}_{self.nt}"
        t = self.nc.dram_tensor(nm, list(shape), dt, kind="Internal")
        return T(t.ap(), [nm])

    def init_psum(self, n=8):
        for i in range(n):
            t = self.stack.enter_context(self.nc.psum_tensor(f"bank{i}", [128, 512], F32))
            self.psum_banks.append(T(t[:], [f"bank{i}"]))

    def ps(self):
        b = self.psum_banks[self.psum_i % len(self.psum_banks)]
        self.psum_i += 1
        return b

    @contextmanager
    def scope(self):
        prev = self.cur
        with ExitStack() as st:
            self.cur = st
            yield
            self.flush()
        self.cur = prev

    def add(self, eng, fn, reads=(), writes=(), dma=False):
        deps = set()
        rk = [k for t in reads for k in t.keys]
        wk = [k for t in writes for k in t.keys]
        for k in rk:
            if k in self.last_writer:
                deps.add(self.last_writer[k])
        for k in wk:
            if k in self.last_writer:
                deps.add(self.last_writer[k])
            for r in self.readers.get(k, ()):
                deps.add(r)
        idx = len(self.streams[eng])
        me = (eng, idx)
        deps.discard(me)
        op = dict(fn=fn, deps=deps, dma=dma, signal=False, dman=None)
        if dma:
            op["dman"] = self.ndma[eng]
            self.ndma[eng] += 1
        self.streams[eng].append(op)
        for k in rk:
            self.readers.setdefault(k, []).append(me)
        for k in wk:
            self.last_writer[k] = me
            self.readers[k] = []
        return me

    def dma(self, out, in_, eng="sp", **kw):
        o = out.ap
        i = in_.ap
        return self.add(eng, lambda e: e.dma_start(out=o, in_=i, **kw), [in_], [out], dma=True)

    def mm(self, out, lhsT, rhs, start=True, stop=True, **kw):
        return self.add("pe", lambda e: e.matmul(out.ap, lhsT.ap, rhs.ap, start=start, stop=stop, **kw),
                        [lhsT, rhs], [out])

    def transpose(self, out, in_, ident):
        return self.add("pe", lambda e: e.transpose(out.ap, in_.ap, ident.ap), [in_, ident], [out])

    def act(self, out, in_, func, bias=None, scale=None, eng="act", accum_out=None):
        reads = [in_]
        kw = {}
        if bias is not None:
            if isinstance(bias, T):
                reads.append(bias); kw["bias"] = bias.ap
            else:
                kw["bias"] = bias
        if scale is not None:
            if isinstance(scale, T):
                reads.append(scale); kw["scale"] = scale.ap
            else:
                kw["scale"] = scale
        writes = [out]
        if accum_out is not None:
            kw["accum_out"] = accum_out.ap; writes.append(accum_out)
        return self.add(eng, lambda e: e.activation(out.ap, in_.ap, func, **kw), reads, writes)

    def tt(self, out, a, b, op, eng="dve"):
        return self.add(eng, lambda e: e.tensor_tensor(out.ap, a.ap, b.ap, op), [a, b], [out])

    def ts(self, out, a, s1, s2=None, op0=ALU.mult, op1=None, eng="dve"):
        reads = [a]
        v1 = s1.ap if isinstance(s1, T) else s1
        if isinstance(s1, T): reads.append(s1)
        v2 = s2.ap if isinstance(s2, T) else s2
        if isinstance(s2, T): reads.append(s2)
        if op1 is None:
            return self.add(eng, lambda e: e.tensor_scalar(out.ap, a.ap, v1, None, op0), reads, [out])
        return self.add(eng, lambda e: e.tensor_scalar(out.ap, a.ap, v1, v2, op0, op1), reads, [out])

    def stt(self, out, a, s, b, op0, op1, eng="dve"):
        reads = [a, b]
        v = s.ap if isinstance(s, T) else s
        if isinstance(s, T): reads.append(s)
        return self.add(eng, lambda e: e.scalar_tensor_tensor(out.ap, a.ap, v, b.ap, op0, op1), reads, [out])

    def copy(self, out, in_, eng="dve"):
        if eng == "act":
            return self.add("act", lambda e: e.copy(out.ap, in_.ap), [in_], [out])
        return self.add(eng, lambda e: e.tensor_copy(out.ap, in_.ap), [in_], [out])

    def memset(self, out, val, eng="pool"):
        return self.add(eng, lambda e: e.memset(out.ap, val), [], [out])

    def recip(self, out, in_, eng="dve"):
        return self.add(eng, lambda e: e.reciprocal(out.ap, in_.ap), [in_], [out])

    def recip_fast(self, out, in_):
        return self.add("dve", lambda e: e.reciprocal_approx_fast(out.ap, in_.ap), [in_], [out])

    def bn_stats(self, out, in_):
        return self.add("dve", lambda e: e.bn_stats(out.ap, in_.ap), [in_], [out])

    def bn_aggr(self, out, in_):
        return self.add("dve", lambda e: e.bn_aggr(out.ap, in_.ap), [in_], [out])

    def debug_out(self, name, t, shape, dt=F32):
        d = self.nc.dram_tensor(name, list(shape), dt, kind="ExternalOutput").ap()
        self.dbg[name] = d
        return self.dma(T(d, [name]), t)

    def flush(self):
        nc = self.nc
        streams = self.streams
        lasts = []
        for e in ENGS:
            for j in range(len(streams[e]) - 1, -1, -1):
                op = streams[e][j]
                if op["fn"] is not None and not op["dma"]:
                    lasts.append((e, j))
                    break
        dmas = [(e, j) for e in ENGS for j, op in enumerate(streams[e]) if op["dma"]]
        for e in ENGS:
            deps = set(l for l in lasts if l[0] != e) | set(dmas)
            streams[e].append(dict(fn=None, deps=deps, dma=False, signal=False, dman=None))
        for e in ENGS:
            for op in streams[e]:
                for (f, j) in op["deps"]:
                    d = streams[f][j]
                    if not d["dma"]:
                        if f == e and not SAME_ENG_SYNC:
                            continue
                        d["signal"] = True
        for e in ENGS:
            for op in streams[e]:
                if op["signal"]:
                    self.sigcount[e] += 1
                    op["sigval"] = self.sigcount[e]
        sem, dsem = self.sem, self.dsem

        def run(ename):
            def body(eng):
                waited = self.waited[ename]

                def wait(s, v, key):
                    if waited.get(key, 0) >= v:
                        return
                    waited[key] = v
                    eng.wait_ge(s, v)

                for op in streams[ename]:
                    for (f, j) in sorted(op["deps"]):
                        d = streams[f][j]
                        if d["dma"]:
                            n = d["dman"]
                            wait(dsem[f][n % DMA_WIN], 16 * (n // DMA_WIN + 1), (f, n % DMA_WIN))
                        else:
                            if f == ename and not SAME_ENG_SYNC:
                                continue
                            wait(sem[f], d["sigval"], f)
                    if op["dma"]:
                        n = op["dman"]
                        if n >= DMA_WIN:
                            wait(dsem[ename][n % DMA_WIN], 16 * (n // DMA_WIN), (ename, n % DMA_WIN))
                    if op["fn"] is None:
                        continue
                    ins = op["fn"](eng)
                    if op["dma"]:
                        ins.then_inc(dsem[ename][op["dman"] % DMA_WIN], 16)
                    elif op["signal"]:
                        ins.then_inc(sem[ename], 1)
            return body

        with nc.Block() as block:
            block.tensor(run("pe"))
            block.scalar(run("act"))
            block.vector(run("dve"))
            block.gpsimd(run("pool"))
            block.sync(run("sp"))
        for e in ENGS:
            self.nops[e] += len(streams[e])
        self.streams = {e: [] for e in ENGS}
        self.last_writer = {}
        self.readers = {}


class Rot:
    def __init__(self, P, shape, dt, name, n):
        self.tiles = [P.sb(shape, dt, f"{name}{i}") for i in range(n)]
        self.i = 0

    def __call__(self):
        t = self.tiles[self.i % len(self.tiles)]
        self.i += 1
        return t


def _ps6(self):
    b = self.psum_banks[self.psum_i % 6]
    self.psum_i += 1
    return b


def _psacc(self):
    self.acc_i = getattr(self, "acc_i", 0) + 1
    return self.psum_banks[6 + self.acc_i % 2]


Prog.ps = _ps6
Prog.ps_acc = _psacc


D = 1024
L = 4096
LH = 2048
CTX = 256
NK = CTX + L
NKT = NK // 128
EPS = 1e-6
OFF_AK, OFF_AV, OFF_CKV, OFF_CKR, OFF_U, NST = 0, 128, 256, 512, 544, 1056
OFF_AQ, OFF_CQ, OFF_GATE, NIN = 1056, 1568, 2336, 5408
ALPHA = (2.0 * 2) ** 0.25


def dT(ap, name):
    return T(ap, [name])


class Ctx:
    pass


class RowSrc:
    def __init__(self, fn):
        self.fn = fn

    def rows(self, r0):
        return self.fn(r0)


def flat_src(t):
    return RowSrc(lambda r0: t.v(t.ap[r0:r0 + 128, :]))


WEIGHT_SHAPES = {
    "w_mod": [D, 6 * D], "b_mod": [6 * D], "w_in": [D, NIN], "a_q_gain": [64], "a_k_gain": [64],
    "c_q_a_gain": [768], "c_kv_a_gain": [256], "c_w_qb": [768, 768], "c_w_kvb": [256, 1024],
    "s5_a_re": [2, 32, 64], "s5_a_im": [2, 32, 64], "s5_log_dt": [2, 32],
    "s5_b_re": [2, 32, 64, 16], "s5_b_im": [2, 32, 64, 16], "s5_c_re": [2, 32, 16, 64], "s5_c_im": [2, 32, 16, 64],
    "s5_d": [512], "s5_w_glu": [512, 1024], "w_branch_a": [512, D], "w_branch_s5": [512, D], "w_branch_c": [512, D],
    "w_out": [D, D], "ln1_g": [D], "ln1_b": [D], "w_up": [D, 4 * D], "w_down": [4 * D, D], "ln2_g": [D], "ln2_b": [D],
}
CONST_SHAPES = {
    "x_all": [L, D], "x_own": [LH, D], "ctx": [CTX, D], "cvec": [2, D],
    "ident": [128, 128], "blk64": [128, 128], "perm64": [128, 128], "perm32": [32, 32],
    "ropek_cos": [128, L], "ropek_sin": [128, L], "ropeq_cos": [128, LH], "ropeq_sin": [128, LH],
    "rope32k_cos": [32, L], "rope32k_sin": [32, L], "rope32q_cos": [32, LH], "rope32q_sin": [32, LH],
    "sel": [128, 2], "swap": [128, 128], "mask_f": [128, 128], "mask_b": [128, 128],
    "sel8": [128, 64, 128], "sel8T": [128, 64, 128],
}


def declare_consts(nc):
    return {k: dT(nc.dram_tensor(k, list(v), F32, kind="ExternalInput").ap(), k) for k, v in CONST_SHAPES.items()}


def declare_weights(nc, l):
    return {k: dT(nc.dram_tensor(f"{k}_{l}", list(v), F32, kind="ExternalInput").ap(), f"{k}_{l}")
            for k, v in WEIGHT_SHAPES.items()}


def declare_inputs(nc, last):
    I = declare_consts(nc)
    I.update(declare_weights(nc, 1 if last else 0))
    return I


def host_constants():
    import math
    C = {}
    C["ident"] = np.eye(128, dtype=np.float32)
    blk = np.zeros((128, 128), np.float32); blk[:64, :64] = 1 / 64; blk[64:, 64:] = 1 / 64
    C["blk64"] = blk

    def perm_and_sign(dim):
        half = dim // 2; q = half // 2
        Pm = np.zeros((dim, dim), np.float32)
        sg = np.zeros(dim, np.float32)
        for m in range(dim):
            if (m % half) < q:
                Pm[m + q, m] = 1; sg[m] = -1
            else:
                Pm[m - q, m] = 1; sg[m] = 1
        return Pm, sg
    P64, s64 = perm_and_sign(64)
    p128 = np.zeros((128, 128), np.float32); p128[:64, :64] = P64; p128[64:, 64:] = P64
    C["perm64"] = p128
    P32, s32 = perm_and_sign(32)
    C["perm32"] = P32

    def tables(dim):
        half = dim // 2
        inv = 10000.0 ** (-np.arange(0, half, 2, dtype=np.float32) / half)
        rows = L // 64
        row = np.repeat(np.arange(rows, dtype=np.float32), 64)
        col = np.tile(np.arange(64, dtype=np.float32), rows)
        ang_r = row[:, None] * inv; ang_c = col[:, None] * inv
        ang = np.concatenate([ang_r, ang_r, ang_c, ang_c], axis=-1).astype(np.float32)
        return np.cos(ang).T.astype(np.float32), np.sin(ang).T.astype(np.float32)
    c64, s64t = tables(64)
    s64t = s64t * s64[:, None]
    C["ropek_cos"] = np.concatenate([c64, c64], 0); C["ropek_sin"] = np.concatenate([s64t, s64t], 0)
    c32, s32t = tables(32)
    s32t = s32t * s32[:, None]
    C["rope32k_cos"] = c32; C["rope32k_sin"] = s32t
    sw = np.zeros((128, 128), np.float32)
    for p in range(64):
        sw[p, 64 + p] = 1; sw[64 + p, p] = 1
    C["swap"] = sw
    ii = np.arange(128) // 16
    C["mask_f"] = (ii[:, None] <= ii[None, :]).astype(np.float32)
    C["mask_b"] = (ii[:, None] >= ii[None, :]).astype(np.float32)
    sel = np.zeros((128, 64, 128), np.float32)
    for gl in range(8):
        for i in range(8):
            for c in range(16):
                sel[gl * 16 + c, gl * 8 + i, i * 16 + c] = 1
    C["sel8"] = sel
    C["sel8T"] = np.ascontiguousarray(sel.transpose(2, 1, 0))
    return C


def per_core_inputs(inputs, core, C, layers=(0, 1)):
    b, hh = core // 2, core % 2
    m = {}
    xb = inputs["x"][b]
    m["x_all"] = xb; m["x_own"] = xb[hh * LH:(hh + 1) * LH]; m["ctx"] = inputs["ctx"][b]
    m["cvec"] = np.stack([inputs["c"][b], inputs["c_ctx"]], 0)
    for l in layers:
        for k in WEIGHT_SHAPES:
            m[f"{k}_{l}"] = inputs[k][l]
    for k in ["ident", "blk64", "perm64", "perm32", "ropek_cos", "ropek_sin", "rope32k_cos", "rope32k_sin",
              "swap", "mask_f", "mask_b", "sel8", "sel8T"]:
        m[k] = C[k]
    sl = slice(hh * LH, (hh + 1) * LH)
    m["ropeq_cos"] = C["ropek_cos"][:, sl]; m["ropeq_sin"] = C["ropek_sin"][:, sl]
    m["rope32q_cos"] = C["rope32k_cos"][:, sl]; m["rope32q_sin"] = C["rope32k_sin"][:, sl]
    s = np.zeros((128, 2), np.float32); s[:, hh] = 1
    m["sel"] = s
    return {k: np.ascontiguousarray(v, dtype=np.float32) for k, v in m.items()}


def rstd_from_ms(P, out, ms, n, eps=EPS, eng_a="act"):
    P.ts(out, ms, eps, None, op0=ALU.add)
    P.act(out, out, AF.Sqrt)
    if n > 1:
        P.recip(out, out)
    else:
        P.recip(out, out)


def stage_prep(P, I, G):
    G.ident = P.sb([128, 128], BF16, "ident"); P.dma(G.ident, I["ident"], eng="pool")
    G.blk64 = P.sb([128, 128], BF16, "blk64"); P.dma(G.blk64, I["blk64"], eng="pool")
    G.perm64 = P.sb([128, 128], BF16, "perm64"); P.dma(G.perm64, I["perm64"], eng="pool")
    G.perm32 = P.sb([32, 32], BF16, "perm32"); P.dma(G.perm32, I["perm32"], eng="pool")
    G.ones = P.sb([128, 128], BF16, "ones"); P.memset(G.ones, 1.0)
    G.modT = P.sb([128, 48, 2], F32, "modT")
    G.sel = P.sb([128, 2], F32, "sel"); P.dma(G.sel, I["sel"])
    with P.scope():
        cT = P.sb([128, 8, 2], F32, "cT")
        for t in range(2):
            src = I["cvec"].ap[t, :].rearrange("(k p) -> p k", p=128)
            P.dma(cT[:, :, t], dT(src, "cvec"), allow_slow_non_contiguous=True)
        sT = P.sb([128, 8, 2], BF16, "sT")
        P.act(sT, cT, AF.Silu)
        bT = P.sb([128, 48], F32, "bT")
        P.dma(bT, dT(I["b_mod"].ap.rearrange("(j p) -> p j", p=128), "b_mod"), allow_slow_non_contiguous=True)
        wbufs = [P.sb([128, 8, 128], BF16, f"wm{i}") for i in range(3)]
        for j in range(48):
            wb = wbufs[j % 3]
            src = I["w_mod"].ap[:, j * 128:(j + 1) * 128].rearrange("(k p) c -> p k c", p=128)
            P.dma(wb, dT(src, "w_mod"), eng="pool")
            ps = P.ps()
            for kc in range(8):
                P.mm(ps[:, 0:2], wb[:, kc, :], sT[:, kc, :], start=(kc == 0), stop=(kc == 7))
            P.ts(G.modT[:, j, :], ps[:, 0:2], bT[:, j:j + 1], None, op0=ALU.add)
        for w in (1, 4):
            P.ts(G.modT[:, w * 8:(w + 1) * 8, :], G.modT[:, w * 8:(w + 1) * 8, :], 1.0, None, op0=ALU.add)
        G.modD = P.dram([2, 6 * D], F32, "modD")
        for t in range(2):
            dst = G.modD.ap[t, :].rearrange("(j p) -> p j", p=128)
            P.dma(G.modD.v(dst), G.modT[:, :, t], allow_slow_non_contiguous=True)


def ln_tile_to_hT(P, G, xt, hT_dst, t_idx, which_sh, which_sc):
    st = P.sb([128, 2, 6], F32, "bnst")
    mv = P.sb([128, 2], F32, "mv")
    for hh in range(2):
        P.bn_stats(st[:, hh, :], xt[:, hh * 512:(hh + 1) * 512])
    P.bn_aggr(mv, st)
    rs = P.sb([128, 1], F32, "rs")
    rstd_from_ms(P, rs, mv[:, 1:2], 1)
    xn = P.sb([128, D], BF16, "xn")
    P.ts(xn, xt, mv[:, 0:1], rs, op0=ALU.subtract, op1=ALU.mult)
    ps = P.ps()
    psb = ps.v(ps.ap.bitcast(BF16))
    for kc in range(8):
        P.transpose(psb[:, kc * 128:(kc + 1) * 128], xn[:, kc * 128:(kc + 1) * 128], G.ident)
    for kc in range(8):
        P.act(hT_dst[:, kc, :], psb[:, kc * 128:(kc + 1) * 128], AF.Identity,
              bias=G.modT[:, which_sh * 8 + kc, t_idx:t_idx + 1], scale=G.modT[:, which_sc * 8 + kc, t_idx:t_idx + 1])


_CAST = {"i": 0}


def wload(P, stg, dst, src, engs=("pool", "dve", "act")):
    shape = list(dst.ap.shape)[1:]
    n = 1
    for v in shape:
        n *= v
    st = stg()
    sv = st.ap[0:dst.ap.shape[0], 0:n]
    if len(shape) == 2:
        sv = sv.rearrange("p (a b) -> p a b", a=shape[0])
    svt = st.v(sv)
    P.dma(svt, src)
    e = engs[_CAST["i"] % len(engs)]
    _CAST["i"] += 1
    P.copy(dst, svt, eng=e)


def mmg(P, items, K):
    for k in range(K):
        for (out, lf, rf) in items:
            P.mm(out, lf(k), rf(k), start=(k == 0), stop=(k == K - 1))


def make_ln_pools(P, nb=2):
    R = Ctx()
    R.xt = Rot(P, [128, D], F32, "xt", nb)
    R.st = Rot(P, [128, 2, 6], F32, "bnst", nb)
    R.mv = Rot(P, [128, 2], F32, "mv", nb)
    R.rs = Rot(P, [128, 1], F32, "rs", nb)
    R.xn = Rot(P, [128, D], BF16, "xn", nb)
    return R


def ln_part1(P, G, R, src_dram):
    xt = R.xt()
    P.dma(xt, src_dram)
    st = R.st(); mv = R.mv(); rs = R.rs(); xn = R.xn()
    for hh in range(2):
        P.bn_stats(st[:, hh, :], xt[:, hh * 512:(hh + 1) * 512])
    P.bn_aggr(mv, st)
    rstd_from_ms(P, rs, mv[:, 1:2], 1)
    P.ts(xn, xt, mv[:, 0:1], rs, op0=ALU.subtract, op1=ALU.mult)
    return xn


def ln_part2(P, G, xn, hT_dst, t_idx, which_sh, which_sc):
    ps = P.ps()
    psb = ps.v(ps.ap.bitcast(BF16))
    for kc in range(8):
        P.transpose(psb[:, kc * 128:(kc + 1) * 128], xn[:, kc * 128:(kc + 1) * 128], G.ident)
    for kc in range(8):
        bias = G.modT[:, which_sh * 8 + kc, t_idx:t_idx + 1]
        scale = G.modT[:, which_sc * 8 + kc, t_idx:t_idx + 1]
        if kc % 2 == 0:
            P.act(hT_dst[:, kc, :], psb[:, kc * 128:(kc + 1) * 128], AF.Identity, bias=bias, scale=scale)
        else:
            P.ts(hT_dst[:, kc, :], psb[:, kc * 128:(kc + 1) * 128], scale, bias, op0=ALU.mult, op1=ALU.add)


def ln_tile_to_hT2(P, G, R, src_dram, hT_dst, t_idx, which_sh, which_sc):
    xn = ln_part1(P, G, R, src_dram)
    ln_part2(P, G, xn, hT_dst, t_idx, which_sh, which_sc)


def rope_apply(P, dst, src_bf, perm, cos, sin, tmp, n, rows=128, psfn=None):
    ps = (psfn or P.ps)()
    P.mm(ps[0:rows, 0:n], perm, src_bf)
    P.tt(tmp, src_bf, cos, ALU.mult)
    P.tt(dst, ps[0:rows, 0:n], sin, ALU.mult)
    P.tt(dst, dst, tmp, ALU.add)


def alloc_persist(P, G):
    G.kT = P.sb([128, NK], BF16, "kT")
    G.Vg = P.sb([128, NKT, 2, 128], BF16, "Vg")
    G.ckvT = P.sb([128, 2, NK], BF16, "ckvT")
    G.krT = P.sb([32, NK], BF16, "krT")


def stage_A(P, I, G):
    P.memset(G.Vg[:, :, :, 64:128], 1.0)
    with P.scope():
        w_st = P.sb([128, 8, NST], BF16, "w_st")
        with P.scope():
            stg = Rot(P, [128, NST], F32, "stg", 2)
            for kc in range(8):
                wload(P, stg, w_st[:, kc, :], dT(I["w_in"].ap[kc * 128:(kc + 1) * 128, 0:NST], "w_in"))
        kg = P.sb([128, 1], F32, "kg")
        for r in range(2):
            P.dma(kg[r * 64:(r + 1) * 64, :], dT(I["a_k_gain"].ap.rearrange("(p o) -> p o", o=1), "akg"))
        cg = P.sb([128, 2], F32, "cg")
        P.dma(cg, dT(I["c_kv_a_gain"].ap.rearrange("(j p) -> p j", p=128), "ckg"), allow_slow_non_contiguous=True)
        R = make_ln_pools(P, 3)
        hTs = Rot(P, [128, 8, 512], BF16, "hT", 2)
        sq = Rot(P, [128, 512], BF16, "sq", 2)
        rst = Rot(P, [128, 512], F32, "rst", 1)
        knb = Rot(P, [128, 512], BF16, "knb", 2)
        tmp = Rot(P, [128, 512], F32, "tmp", 1)
        cosb = Rot(P, [128, 512], F32, "cosb", 1)
        sinb = Rot(P, [128, 512], F32, "sinb", 1)
        cos32 = Rot(P, [32, 512], BF16, "cos32", 1)
        sin32 = Rot(P, [32, 512], BF16, "sin32", 1)
        blocks = [(0, 2, True)] + [(2 + 4 * i, 4, False) for i in range(8)]
        alltiles = [(t0 + ti, is_ctx) for (t0, nt, is_ctx) in blocks for ti in range(nt)]

        def a_p1(q):
            t, is_ctx = alltiles[q]
            return ln_part1(P, G, R, I["ctx"].rows(t * 128) if is_ctx else I["x_all"].rows((t - 2) * 128))
        qi = 0
        xn_cur = a_p1(0)
        for (t0, nt, is_ctx) in blocks:
            n = nt * 128
            c0 = t0 * 128
            hT = hTs()
            for ti in range(nt):
                xn_nxt = a_p1(qi + 1) if qi + 1 < len(alltiles) else None
                ln_part2(P, G, xn_cur, hT[:, :, ti * 128:(ti + 1) * 128], 1 if is_ctx else 0, 0, 1)
                xn_cur = xn_nxt
                qi += 1
            if not is_ctx:
                lc = c0 - CTX
                cb, sb_, c32, s32 = cosb(), sinb(), cos32(), sin32()
                P.dma(cb[:, 0:n], dT(I["ropek_cos"].ap[:, lc:lc + n], "rc"))
                P.dma(sb_[:, 0:n], dT(I["ropek_sin"].ap[:, lc:lc + n], "rs"))
                P.dma(c32[:, 0:n], dT(I["rope32k_cos"].ap[:, lc:lc + n], "rc32"), eng="pool")
                P.dma(s32[:, 0:n], dT(I["rope32k_sin"].ap[:, lc:lc + n], "rs32"), eng="pool")
            pk = P.ps(); pc = [P.ps(), P.ps()]; pr = P.ps()
            mmg(P, [(pk[:, 0:n], lambda k: w_st[:, k, OFF_AK:OFF_AK + 128], lambda k: hT[:, k, 0:n]),
                    (pc[0][:, 0:n], lambda k: w_st[:, k, OFF_CKV:OFF_CKV + 128], lambda k: hT[:, k, 0:n]),
                    (pc[1][:, 0:n], lambda k: w_st[:, k, OFF_CKV + 128:OFF_CKV + 256], lambda k: hT[:, k, 0:n]),
                    (pr[0:32, 0:n], lambda k: w_st[:, k, OFF_CKR:OFF_CKR + 32], lambda k: hT[:, k, 0:n])], 8)
            s = sq()
            P.act(s[:, 0:n], pk[:, 0:n], AF.Square)
            pm = P.ps()
            P.mm(pm[:, 0:n], G.blk64, s[:, 0:n])
            rs = rst()
            rstd_from_ms(P, rs[:, 0:n], pm[:, 0:n], n)
            kn = knb()
            P.stt(kn[:, 0:n], pk[:, 0:n], kg[:, 0:1], rs[:, 0:n], ALU.mult, ALU.mult)
            if is_ctx:
                P.copy(G.kT[:, c0:c0 + n], kn[:, 0:n])
            else:
                rope_apply(P, G.kT[:, c0:c0 + n], kn[:, 0:n], G.perm64, cb[:, 0:n], sb_[:, 0:n], tmp()[:, 0:n], n)
            ss = [sq(), sq()]
            for j in range(2):
                P.act(ss[j][:, 0:n], pc[j][:, 0:n], AF.Square)
            pm = P.ps()
            for j in range(2):
                P.mm(pm[:, 0:n], G.ones, ss[j][:, 0:n], start=(j == 0), stop=(j == 1))
            rs = rst()
            P.ts(rs[:, 0:n], pm[:, 0:n], 1.0 / 256, EPS, op0=ALU.mult, op1=ALU.add)
            P.act(rs[:, 0:n], rs[:, 0:n], AF.Sqrt)
            P.recip(rs[:, 0:n], rs[:, 0:n])
            for j in range(2):
                P.stt(G.ckvT[:, j, c0:c0 + n], pc[j][:, 0:n], cg[:, j:j + 1], rs[:, 0:n], ALU.mult, ALU.mult)
            if is_ctx:
                P.copy(G.krT[:, c0:c0 + n], pr[0:32, 0:n])
            else:
                kr = knb()
                P.copy(kr[0:32, 0:n], pr[0:32, 0:n])
                rope_apply(P, G.krT[:, c0:c0 + n], kr[0:32, 0:n], G.perm32, c32[:, 0:n], s32[:, 0:n], tmp()[0:32, 0:n], n, rows=32)
            pvs = [P.ps() for _ in range(nt)]
            mmg(P, [(pvs[ti][:, 0:128], (lambda k, ti=ti: hT[:, k, ti * 128:(ti + 1) * 128]),
                     lambda k: w_st[:, k, OFF_AV:OFF_AV + 128]) for ti in range(nt)], 8)
            for ti in range(nt):
                pv = pvs[ti]
                P.copy(G.Vg[:, t0 + ti, :, 0:64], pv.v(pv.ap[:, 0:128].rearrange("p (a b) -> p a b", a=2)), eng="act")
            pus = [P.ps() for _ in range(4)]
            mmg(P, [(pus[j][:, 0:n], (lambda k, j=j: w_st[:, k, OFF_U + j * 128:OFF_U + (j + 1) * 128]),
                     lambda k: hT[:, k, 0:n]) for j in range(4)], 8)
            for j in range(4):
                dst = G.uT.v(G.uT.ap[:, j, :, c0 // 8:(c0 + n) // 8].rearrange("p i c -> p c i"))
                src = pus[j].v(pus[j].ap[:, 0:n].rearrange("p (c i) -> p c i", i=8))
                P.copy(dst, src, eng=("act" if j % 2 else "dve"))


def own_blocks(last):
    bl = [(i * 512, 512, False, i * 512) for i in range(4)]
    if not last:
        bl.append((LH, 256, True, 0))
    return bl


def stage_B1(P, I, G, last):
    NQ = LH + (0 if last else CTX)
    G.NQ = NQ
    G.qg = P.sb([128, 4, NQ], BF16, "qg")
    G.qm = P.sb([96, 8, NQ], BF16, "qm")
    G.gD = P.dram([3 * D, NQ], BF16, "gD")
    with P.scope():
        hT = P.sb([128, 8, NQ], BF16, "hTall")
        with P.scope():
            R = make_ln_pools(P, 4)
            tiles = [(c0 + ti * 128, r0 + ti * 128, is_ctx) for (c0, n, is_ctx, r0) in own_blocks(last) for ti in range(n // 128)]

            def p1(q):
                col, row, is_ctx = tiles[q]
                return ln_part1(P, G, R, (I["ctx"] if is_ctx else I["x_own"]).rows(row))
            xn_cur = p1(0)
            for q in range(len(tiles)):
                xn_nxt = p1(q + 1) if q + 1 < len(tiles) else None
                col, row, is_ctx = tiles[q]
                ln_part2(P, G, xn_cur, hT[:, :, col:col + 128], 1 if is_ctx else 0, 0, 1)
                xn_cur = xn_nxt
        qgain = P.sb([128, 1], F32, "qgain")
        for r in range(2):
            P.dma(qgain[r * 64:(r + 1) * 64, :], dT(I["a_q_gain"].ap.rearrange("(p o) -> p o", o=1), "aqg"))
        cqg = P.sb([128, 6], F32, "cqg")
        P.dma(cqg, dT(I["c_q_a_gain"].ap.rearrange("(j p) -> p j", p=128), "cqg"), allow_slow_non_contiguous=True)
        perm32h = P.sb([96, 32], BF16, "perm32h")
        P.dma(perm32h[64:96, :], I["perm32"], eng="pool")
        cosqR = Rot(P, [128, 512], F32, "cosq", 2)
        sinqR = Rot(P, [128, 512], F32, "sinq", 2)
        cos32R = Rot(P, [96, 512], F32, "cos32q", 2)
        sin32R = Rot(P, [96, 512], F32, "sin32q", 2)
        sq = Rot(P, [128, 512], BF16, "sq", 6)
        rst = Rot(P, [128, 512], F32, "rst", 2)
        knb = Rot(P, [128, 512], BF16, "knb", 2)
        tmp = Rot(P, [128, 512], F32, "tmp", 2)
        with P.scope():
            wq = P.sb([128, 8, 4, 128], BF16, "wq")
            stg = Rot(P, [128, 768], F32, "stg", 2)
            for kc in range(8):
                for hf in range(2):
                    src = I["w_in"].ap[kc * 128:(kc + 1) * 128, OFF_AQ + hf * 256:OFF_AQ + (hf + 1) * 256].rearrange("p (a b) -> p a b", a=4)
                    wload(P, stg, wq[:, kc, :, hf * 64:(hf + 1) * 64], dT(src, "w_in"))
            for (c0, n, is_ctx, r0) in own_blocks(last):
                if not is_ctx:
                    cosq = cosqR(); sinq = sinqR()
                    P.dma(cosq, dT(I["ropeq_cos"].ap[:, c0:c0 + n], "rqc"))
                    P.dma(sinq, dT(I["ropeq_sin"].ap[:, c0:c0 + n], "rqs"))
                pks = [P.ps() for _ in range(4)]
                mmg(P, [(pks[hd][:, 0:n], (lambda k, hd=hd: wq[:, k, hd, :]), lambda k: hT[:, k, c0:c0 + n]) for hd in range(4)], 8)
                for hd in range(4):
                    pk = pks[hd]
                    s = sq()
                    P.act(s[:, 0:n], pk[:, 0:n], AF.Square)
                    pm = P.ps_acc()
                    P.mm(pm[:, 0:n], G.blk64, s[:, 0:n])
                    rs = rst()
                    rstd_from_ms(P, rs[:, 0:n], pm[:, 0:n], n)
                    if is_ctx:
                        P.stt(G.qg[:, hd, c0:c0 + n], pk[:, 0:n], qgain[:, 0:1], rs[:, 0:n], ALU.mult, ALU.mult)
                    else:
                        kn = knb()
                        P.stt(kn[:, 0:n], pk[:, 0:n], qgain[:, 0:1], rs[:, 0:n], ALU.mult, ALU.mult)
                        rope_apply(P, G.qg[:, hd, c0:c0 + n], kn[:, 0:n], G.perm64, cosq[:, 0:n], sinq[:, 0:n],
                                   tmp()[:, 0:n], n, psfn=P.ps_acc)
        with P.scope():
            wc = P.sb([128, 8, 768], BF16, "wc")
            stg = Rot(P, [128, 768], F32, "stg", 1)
            for kc in range(8):
                wload(P, stg, wc[:, kc, :], dT(I["w_in"].ap[kc * 128:(kc + 1) * 128, OFF_CQ:OFF_CQ + 768], "w_in"))
            wqb = P.sb([128, 6, 768], BF16, "wqb")
            for j in range(6):
                wload(P, stg, wqb[:, j, :], dT(I["c_w_qb"].ap[j * 128:(j + 1) * 128, :], "wqb"))
            cqn = P.sb([128, 6, 512], BF16, "cqn")
            qrb = Rot(P, [96, 512], BF16, "qrb", 2)
            for (c0, n, is_ctx, r0) in own_blocks(last):
                if not is_ctx:
                    cos32 = cos32R(); sin32 = sin32R()
                    P.dma(cos32[64:96, :], dT(I["rope32q_cos"].ap[:, c0:c0 + n], "rqc32"))
                    P.dma(sin32[64:96, :], dT(I["rope32q_sin"].ap[:, c0:c0 + n], "rqs32"))
                pcs = [P.ps() for _ in range(6)]
                mmg(P, [(pcs[j][:, 0:n], (lambda k, j=j: wc[:, k, j * 128:(j + 1) * 128]), lambda k: hT[:, k, c0:c0 + n]) for j in range(6)], 8)
                sqs = []
                for j in range(6):
                    s = sq()
                    P.act(s[:, 0:n], pcs[j][:, 0:n], AF.Square)
                    sqs.append(s)
                pm = P.ps_acc()
                for j in range(6):
                    P.mm(pm[:, 0:n], G.ones, sqs[j][:, 0:n], start=(j == 0), stop=(j == 5))
                rs = rst()
                P.ts(rs[:, 0:n], pm[:, 0:n], 1.0 / 768, EPS, op0=ALU.mult, op1=ALU.add)
                P.act(rs[:, 0:n], rs[:, 0:n], AF.Sqrt)
                P.recip(rs[:, 0:n], rs[:, 0:n])
                for j in range(6):
                    P.stt(cqn[:, j, 0:n], pcs[j][:, 0:n], cqg[:, j:j + 1], rs[:, 0:n], ALU.mult, ALU.mult)
                for hg in range(2):
                    pqs = [P.ps() for _ in range(4)]
                    mmg(P, [(pqs[i][0:96, 0:n], (lambda k, h=hg * 4 + i: wqb[:, k, h * 96:(h + 1) * 96]), lambda k: cqn[:, k, 0:n]) for i in range(4)], 6)
                    for i in range(4):
                        h = hg * 4 + i
                        pq = pqs[i]
                        if is_ctx:
                            P.copy(G.qm[:, h, c0:c0 + n], pq[0:96, 0:n], eng="act")
                        else:
                            P.copy(G.qm[0:64, h, c0:c0 + n], pq[0:64, 0:n], eng="act")
                            qr = qrb()
                            P.copy(qr[64:96, 0:n], pq[64:96, 0:n])
                            pr = P.ps_acc()
                            P.mm(pr[64:96, 0:n], perm32h[64:96, :], qr[64:96, 0:n])
                            t = tmp()
                            P.tt(t[64:96, 0:n], qr[64:96, 0:n], cos32[64:96, 0:n], ALU.mult)
                            t2 = tmp()
                            P.tt(t2[64:96, 0:n], pr[64:96, 0:n], sin32[64:96, 0:n], ALU.mult)
                            P.tt(G.qm[64:96, h, c0:c0 + n], t[64:96, 0:n], t2[64:96, 0:n], ALU.add)
        with P.scope():
            wg = Rot(P, [128, 8, 512], BF16, "wg", 2)
            stg = Rot(P, [128, 512], F32, "stg", 3)
            gb = Rot(P, [128, 512], BF16, "gb", 8)
            for gi in range(6):
                w = wg()
                for kc in range(8):
                    wload(P, stg, w[:, kc, :], dT(I["w_in"].ap[kc * 128:(kc + 1) * 128, OFF_GATE + gi * 512:OFF_GATE + (gi + 1) * 512], "w_in"), engs=("pool", "dve"))
                for (c0, n, is_ctx, r0) in own_blocks(last):
                    pgs = [P.ps() for _ in range(4)]
                    mmg(P, [(pgs[oc][:, 0:n], (lambda k, oc=oc: w[:, k, oc * 128:(oc + 1) * 128]), lambda k: hT[:, k, c0:c0 + n]) for oc in range(4)], 8)
                    for oc in range(4):
                        g = gb()
                        P.act(g[:, 0:n], pgs[oc][:, 0:n], AF.Sigmoid)
                        row = (gi * 4 + oc) * 128
                        P.dma(G.gD.v(G.gD.ap[row:row + 128, c0:c0 + n]), g[:, 0:n])


def run_attn(P, chains, pT, scale):
    LA = 2
    nkt = chains[0][1]
    pls = [dict() for _ in chains]
    for kt in range(nkt + LA):
        if kt < nkt:
            for ci, (po, _, n, slf, srhs, vlf) in enumerate(chains):
                pss = P.ps()
                P.mm(pss[:, 0:n], slf(kt), srhs)
                p = pT()
                P.act(p[:, 0:n], pss[:, 0:n], AF.Exp, scale=scale)
                pls[ci][kt] = p
        jj = kt - LA
        if jj >= 0:
            for ci, (po, _, n, slf, srhs, vlf) in enumerate(chains):
                P.mm(po[:, 0:n], vlf(jj), pls[ci].pop(jj)[:, 0:n], start=(jj == 0), stop=(jj == nkt - 1))


def block_groups(last):
    bl = own_blocks(last)
    groups = [bl[0:2], bl[2:4]]
    if not last:
        groups.append(bl[4:5])
    return groups


def attn_finish(P, po, n, rec, yo, dst):
    r = rec()
    P.recip(r[64:128, 0:n], po[64:128, 0:n])
    y = yo()
    P.tt(y[:, 0:n], po[0:64, 0:n], r[64:128, 0:n], ALU.mult)
    P.dma(dst, y[:, 0:n])


def stage_B2(P, I, G, last):
    NQ = G.NQ
    G.yaD = P.dram([512, NQ], BF16, "yaD")
    with P.scope():
        pT = Rot(P, [128, 512], BF16, "pT", 8)
        rec = Rot(P, [128, 512], F32, "rec", 2)
        yo = Rot(P, [64, 512], BF16, "yo", 2)
        kTp = P.sb([128, 2, NK], BF16, "kTp")
        P.memset(kTp, 0.0)
        P.copy(kTp[0:64, 0, :], G.kT[0:64, :], eng="pool")
        P.copy(kTp[64:128, 1, :], G.kT[64:128, :], eng="dve")
        for hd in range(4):
            for kvh in range(2):
                head = hd + 4 * kvh
                for grp in block_groups(last):
                    chains = []
                    for (c0, n, is_ctx, r0) in grp:
                        nkt = 2 if is_ctx else NKT
                        chains.append((P.ps_acc(), nkt, n, (lambda kt: kTp[:, kvh, kt * 128:(kt + 1) * 128]),
                                       G.qg[:, hd, c0:c0 + n], (lambda kt: G.Vg[:, kt, kvh, :])))
                    run_attn(P, chains, pT, 0.125)
                    for (po, _, n, _, _, _), (c0, _, _, _) in zip(chains, grp):
                        attn_finish(P, po, n, rec, yo, G.yaD.v(G.yaD.ap[head * 64:(head + 1) * 64, c0:c0 + n]))


def stage_B3(P, I, G, last):
    NQ = G.NQ
    G.ycD = P.dram([512, NQ], BF16, "ycD")
    with P.scope():
        wkv = P.sb([128, 2, 1024], BF16, "wkv")
        stg = Rot(P, [128, 1024], F32, "stg", 1)
        for j in range(2):
            wload(P, stg, wkv[:, j, :], dT(I["c_w_kvb"].ap[j * 128:(j + 1) * 128, :], "wkvb"))
        Kh = Rot(P, [96, NK], BF16, "Kh", 2)
        Vh = [P.sb([128, NKT, 128], BF16, f"Vh{i}") for i in range(2)]
        for v in Vh:
            P.memset(v[:, :, 64:128], 1.0)
        pT = Rot(P, [128, 512], BF16, "pT", 8)
        rec = Rot(P, [128, 512], F32, "rec", 2)
        yo = Rot(P, [64, 512], BF16, "yo", 2)
        scale = 96 ** -0.5
        for h in range(8):
            K = Kh(); V = Vh[h % 2]
            cbs = [(cb * 512, min(512, NK - cb * 512)) for cb in range(9)]
            for g0 in range(0, 9, 3):
                grp = cbs[g0:g0 + 3]
                pks = [P.ps() for _ in grp]
                mmg(P, [(pks[i][0:64, 0:n], lambda k: wkv[:, k, h * 128:h * 128 + 64], (lambda k, k0=k0, n=n: G.ckvT[:, k, k0:k0 + n]))
                        for i, (k0, n) in enumerate(grp)], 2)
                for i, (k0, n) in enumerate(grp):
                    P.copy(K[0:64, k0:k0 + n], pks[i][0:64, 0:n], eng=("act" if i % 2 else "dve"))
            P.copy(K[64:96, :], G.krT[0:32, :], eng="pool")
            for g0 in range(0, NKT, 4):
                kts = list(range(g0, min(g0 + 4, NKT)))
                pvs = [P.ps() for _ in kts]
                mmg(P, [(pvs[i][:, 0:64], (lambda k, kt=kt: G.ckvT[:, k, kt * 128:(kt + 1) * 128]),
                         lambda k: wkv[:, k, h * 128 + 64:h * 128 + 128]) for i, kt in enumerate(kts)], 2)
                for i, kt in enumerate(kts):
                    P.copy(V[:, kt, 0:64], pvs[i][:, 0:64], eng=("act" if kt % 2 else "dve"))
            for grp in block_groups(last):
                chains = []
                for (c0, n, is_ctx, r0) in grp:
                    nkt = 2 if is_ctx else NKT
                    chains.append((P.ps_acc(), nkt, n, (lambda kt: K[0:96, kt * 128:(kt + 1) * 128]),
                                   G.qm[0:96, h, c0:c0 + n], (lambda kt: V[:, kt, :])))
                run_attn(P, chains, pT, scale)
                for (po, _, n, _, _, _), (c0, _, _, _) in zip(chains, grp):
                    attn_finish(P, po, n, rec, yo, G.ycD.v(G.ycD.ap[h * 64:(h + 1) * 64, c0:c0 + n]))


def bcast_load(P, dst, src_ap_1d):
    P.dma(dst, dT(src_ap_1d.partition_broadcast(128), "bc"))


def stage_B4a(P, I, G, last):
    NQ = G.NQ
    G.xmidD = P.dram([NQ, D], F32, "xmidD")
    G.h2D = P.dram([D, NQ], BF16, "h2D")
    with P.scope():
        wglu = P.sb([128, 4, 1024], BF16, "wglu")
        wb = [P.sb([128, 4, 1024], BF16, f"wb{i}") for i in range(3)]
        wout = P.sb([128, 8, 1024], BF16, "wout")
        stg = Rot(P, [128, 1024], F32, "stg", 2)
        for j in range(4):
            wload(P, stg, wglu[:, j, :], dT(I["s5_w_glu"].ap[j * 128:(j + 1) * 128, :], "w"))
            for i, nm in enumerate(["w_branch_a", "w_branch_s5", "w_branch_c"]):
                wload(P, stg, wb[i][:, j, :], dT(I[nm].ap[j * 128:(j + 1) * 128, :], "w"))
        for kc in range(8):
            wload(P, stg, wout[:, kc, :], dT(I["w_out"].ap[kc * 128:(kc + 1) * 128, :], "w"))
        g1b = [P.sb([128, D], F32, f"g1b{t}") for t in range(2)]
        for t in range(2):
            P.dma(g1b[t], dT(G.modD.ap[t, 2 * D:3 * D].partition_broadcast(128), "modD_r"))
        lng = P.sb([128, D], F32, "lng"); bcast_load(P, lng, I["ln1_g"].ap)
        lnb = P.sb([128, D], F32, "lnb"); bcast_load(P, lnb, I["ln1_b"].ap)
        gates = P.sb([128, 24, 512], BF16, "gates")
        srcs = [P.sb([128, 4, 512], BF16, f"src{i}") for i in range(3)]
        yT = P.sb([128, 4, 512], BF16, "yT")
        t1 = P.sb([128, 4, 512], F32, "t1")
        sg = Rot(P, [128, 512], F32, "sg", 2)
        acc = Rot(P, [128, 512], F32, "acc", 2)
        tmpm = Rot(P, [128, 512], F32, "tmpm", 2)
        merged = P.sb([128, 8, 512], BF16, "merged")
        xtR = Rot(P, [128, D], F32, "xt", 2)
        tsR = Rot(P, [128, D], F32, "tsum", 3)
        xmR = Rot(P, [128, D], F32, "xm", 2)
        stR = Rot(P, [128, 2, 6], F32, "bnst", 2); mvR = Rot(P, [128, 2], F32, "mv", 2); rsR = Rot(P, [128, 1], F32, "rs", 2)
        xnR = Rot(P, [128, D], BF16, "xn", 2)
        h2T = P.sb([128, 8, 512], BF16, "h2T")
        for (c0, n, is_ctx, r0) in own_blocks(last):
            tix = 1 if is_ctx else 0
            for j in range(4):
                P.dma(yT[:, j, 0:n], G.ysD.v(G.ysD.ap[j * 128:(j + 1) * 128, c0:c0 + n]))
                P.dma(srcs[0][:, j, 0:n], G.yaD.v(G.yaD.ap[j * 128:(j + 1) * 128, c0:c0 + n]))
                P.dma(srcs[2][:, j, 0:n], G.ycD.v(G.ycD.ap[j * 128:(j + 1) * 128, c0:c0 + n]))
            for gi in range(24):
                P.dma(gates[:, gi, 0:n], G.gD.v(G.gD.ap[gi * 128:(gi + 1) * 128, c0:c0 + n]))
            P.tt(t1[:, :, 0:n], yT[:, :, 0:n], yT[:, :, 0:n], ALU.mult)
            P.ts(t1[:, :, 0:n], t1[:, :, 0:n], 0.044715, 1.0, op0=ALU.mult, op1=ALU.add)
            P.tt(t1[:, :, 0:n], t1[:, :, 0:n], yT[:, :, 0:n], ALU.mult)
            P.act(t1[:, :, 0:n], t1[:, :, 0:n], AF.Sigmoid, scale=1.5957691216)
            ge = P.sb([128, 4, 512], BF16, "ge") if c0 == 0 else ge
            P.tt(ge[:, :, 0:n], t1[:, :, 0:n], yT[:, :, 0:n], ALU.mult)
            for op_ in range(2):
                pa = [P.ps(), P.ps()]; pg = [P.ps(), P.ps()]
                items = []
                for q in range(2):
                    oc = op_ * 2 + q
                    items.append((pa[q][:, 0:n], (lambda k, oc=oc: wglu[:, k, oc * 128:(oc + 1) * 128]), lambda k: ge[:, k, 0:n]))
                    items.append((pg[q][:, 0:n], (lambda k, oc=oc: wglu[:, k, 512 + oc * 128:512 + (oc + 1) * 128]), lambda k: ge[:, k, 0:n]))
                mmg(P, items, 4)
                for q in range(2):
                    oc = op_ * 2 + q
                    s = sg()
                    P.act(s[:, 0:n], pg[q][:, 0:n], AF.Sigmoid)
                    P.tt(srcs[1][:, oc, 0:n], pa[q][:, 0:n], s[:, 0:n], ALU.mult)
            for oc in range(8):
                a = acc()
                pbs = [P.ps() for _ in range(3)]
                mmg(P, [(pbs[br][:, 0:n], (lambda k, br=br: wb[br][:, k, oc * 128:(oc + 1) * 128]), (lambda k, br=br: srcs[br][:, k, 0:n])) for br in range(3)], 4)
                for br in range(3):
                    pb = pbs[br]
                    if br == 0:
                        P.tt(a[:, 0:n], pb[:, 0:n], gates[:, br * 8 + oc, 0:n], ALU.mult)
                    else:
                        tm = tmpm()
                        P.tt(tm[:, 0:n], pb[:, 0:n], gates[:, br * 8 + oc, 0:n], ALU.mult)
                        if br == 1:
                            P.tt(a[:, 0:n], a[:, 0:n], tm[:, 0:n], ALU.add, eng="pool")
                        else:
                            P.tt(merged[:, oc, 0:n], a[:, 0:n], tm[:, 0:n], ALU.add, eng="pool")
            def mix(ti):
                xt = xtR(); ts_ = tsR()
                P.dma(xt, (I["ctx"] if is_ctx else I["x_own"]).rows(r0 + ti * 128))
                pms = [P.ps(), P.ps()]
                mmg(P, [(pms[half][:, 0:512], lambda k: merged[:, k, ti * 128:(ti + 1) * 128],
                         (lambda k, half=half: wout[:, k, half * 512:(half + 1) * 512])) for half in range(2)], 8)
                for half in range(2):
                    P.tt(ts_[:, half * 512:(half + 1) * 512], pms[half][:, 0:512], g1b[tix][:, half * 512:(half + 1) * 512], ALU.mult)
                P.stt(ts_, xt, ALPHA, ts_, ALU.mult, ALU.add)
                return ts_

            def chain(ti, ts_):
                st = stR(); mv = mvR(); rs = rsR(); xm = xmR()
                for hh in range(2):
                    P.bn_stats(st[:, hh, :], ts_[:, hh * 512:(hh + 1) * 512])
                P.bn_aggr(mv, st)
                rstd_from_ms(P, rs, mv[:, 1:2], 1)
                P.stt(xm, ts_, mv[:, 0:1], lng, ALU.subtract, ALU.mult)
                P.stt(xm, xm, rs, lnb, ALU.mult, ALU.add)
                P.dma(G.xmidD.v(G.xmidD.ap[c0 + ti * 128:c0 + (ti + 1) * 128, :]), xm)
                st = stR(); mv = mvR(); rs = rsR(); xn = xnR()
                for hh in range(2):
                    P.bn_stats(st[:, hh, :], xm[:, hh * 512:(hh + 1) * 512])
                P.bn_aggr(mv, st)
                rstd_from_ms(P, rs, mv[:, 1:2], 1)
                P.ts(xn, xm, mv[:, 0:1], rs, op0=ALU.subtract, op1=ALU.mult)
                return xn

            def xpose(ti, xn):
                ps = P.ps()
                psb = ps.v(ps.ap.bitcast(BF16))
                for kc in range(8):
                    P.transpose(psb[:, kc * 128:(kc + 1) * 128], xn[:, kc * 128:(kc + 1) * 128], G.ident)
                for kc in range(8):
                    P.act(h2T[:, kc, ti * 128:(ti + 1) * 128], psb[:, kc * 128:(kc + 1) * 128], AF.Identity,
                          bias=G.modT[:, 3 * 8 + kc, tix:tix + 1], scale=G.modT[:, 4 * 8 + kc, tix:tix + 1])
            nti = n // 128
            ts_cur = mix(0)
            for ti in range(nti):
                ts_nxt = mix(ti + 1) if ti + 1 < nti else None
                xn = chain(ti, ts_cur)
                xpose(ti, xn)
                ts_cur = ts_nxt
            for kc in range(8):
                P.dma(G.h2D.v(G.h2D.ap[kc * 128:(kc + 1) * 128, c0:c0 + n]), h2T[:, kc, 0:n])


def stage_B4b(P, I, G, last, out_own, out_ctx):
    NQ = G.NQ
    NT = NQ // 128
    with P.scope():
        g2b = [P.sb([128, D], F32, f"g2b{t}") for t in range(2)]
        for t in range(2):
            P.dma(g2b[t], dT(G.modD.ap[t, 5 * D:6 * D].partition_broadcast(128), "modD_r"))
        lng = P.sb([128, D], F32, "lng"); bcast_load(P, lng, I["ln2_g"].ap)
        lnb = P.sb([128, D], F32, "lnb"); bcast_load(P, lnb, I["ln2_b"].ap)
        h2T = P.sb([128, 8, NQ], BF16, "h2Tall")
        for kc in range(8):
            P.dma(h2T[:, kc, :], G.h2D.v(G.h2D.ap[kc * 128:(kc + 1) * 128, :]))
        tsum = P.sb([128, NT, D], F32, "tsum2")
        wuR = Rot(P, [128, 8, 512], BF16, "wu", 2)
        wdR = Rot(P, [128, 4, D], BF16, "wd", 2)
        stg = Rot(P, [128, 1024], F32, "stg", 3)
        aR = Rot(P, [128, 4, 512], BF16, "aog", 3)
        rl = Rot(P, [128, 512], BF16, "rl", 4)
        xmR = Rot(P, [128, D], F32, "xm", 2)
        stR = Rot(P, [128, 2, 6], F32, "bnst", 2); mvR = Rot(P, [128, 2], F32, "mv", 2); rsR = Rot(P, [128, 1], F32, "rs", 2)
        for og in range(8):
            wu = wuR(); wd = wdR()
            for kc in range(8):
                wload(P, stg, wu[:, kc, :], dT(I["w_up"].ap[kc * 128:(kc + 1) * 128, og * 512:(og + 1) * 512], "w"), engs=("pool", "act"))
            for oc in range(4):
                wload(P, stg, wd[:, oc, :], dT(I["w_down"].ap[og * 512 + oc * 128:og * 512 + (oc + 1) * 128, :], "w"), engs=("pool", "act"))
            blks = own_blocks(last)

            def up(blk):
                (c0, n, is_ctx, r0) = blk
                a = aR()
                pus = [P.ps() for _ in range(4)]
                mmg(P, [(pus[oc][:, 0:n], (lambda k, oc=oc: wu[:, k, oc * 128:(oc + 1) * 128]), lambda k: h2T[:, k, c0:c0 + n]) for oc in range(4)], 8)
                for oc in range(4):
                    r = rl()
                    P.act(r[:, 0:n], pus[oc][:, 0:n], AF.Relu)
                    P.tt(a[:, oc, 0:n], pus[oc][:, 0:n], r[:, 0:n], ALU.mult)
                return a

            def down(blk, a):
                (c0, n, is_ctx, r0) = blk
                combos = [(ti, half) for ti in range(n // 128) for half in range(2)]
                for g0 in range(0, len(combos), 4):
                    grp = combos[g0:g0 + 4]
                    pds = [P.ps() for _ in grp]
                    mmg(P, [(pds[i][:, 0:512], (lambda k, ti=ti: a[:, k, ti * 128:(ti + 1) * 128]),
                             (lambda k, half=half: wd[:, k, half * 512:(half + 1) * 512])) for i, (ti, half) in enumerate(grp)], 4)
                    for i, (ti, half) in enumerate(grp):
                        tile = c0 // 128 + ti
                        dst = tsum[:, tile, half * 512:(half + 1) * 512]
                        if og == 0:
                            P.copy(dst, pds[i][:, 0:512], eng="act")
                        else:
                            P.tt(dst, pds[i][:, 0:512], dst, ALU.add)
            a_cur = up(blks[0])
            for bi, blk in enumerate(blks):
                a_nxt = up(blks[bi + 1]) if bi + 1 < len(blks) else None
                down(blk, a_cur)
                a_cur = a_nxt
        for (c0, n, is_ctx, r0) in own_blocks(last):
            tix = 1 if is_ctx else 0
            for ti in range(n // 128):
                tile = c0 // 128 + ti
                xm = xmR()
                P.dma(xm, G.xmidD.v(G.xmidD.ap[c0 + ti * 128:c0 + (ti + 1) * 128, :]))
                P.tt(tsum[:, tile, :], tsum[:, tile, :], g2b[tix], ALU.mult, eng="pool")
                P.stt(tsum[:, tile, :], xm, ALPHA, tsum[:, tile, :], ALU.mult, ALU.add)
                st = stR(); mv = mvR(); rs = rsR()
                for hh in range(2):
                    P.bn_stats(st[:, hh, :], tsum[:, tile, hh * 512:(hh + 1) * 512])
                P.bn_aggr(mv, st)
                rstd_from_ms(P, rs, mv[:, 1:2], 1)
                o = xm
                P.stt(o, tsum[:, tile, :], mv[:, 0:1], lng, ALU.subtract, ALU.mult)
                P.stt(o, o, rs, lnb, ALU.mult, ALU.add)
                dst = out_ctx if is_ctx else out_own
                P.dma(dst.rows(r0 + ti * 128), o)


def bc(t, pattern):
    a = t.ap
    return t.v(bass.AP(a.tensor, a.offset, [list(a.ap[0])] + [list(p) for p in pattern]))


MAGIC = 12582912.0
TWO_PI = 6.283185307179586


def s5_load(P, I, G):
    Lt = Ctx()
    Lt.are = P.sb([128, 64], F32, "are"); Lt.aim = P.sb([128, 64], F32, "aim"); Lt.ldt = P.sb([128, 64], F32, "ldt")
    Lt.Bre = P.sb([128, 64, 16], F32, "Bre"); Lt.Bim = P.sb([128, 64, 16], F32, "Bim")
    Lt.Cre = P.sb([128, 64, 16], F32, "Cre"); Lt.Cim = P.sb([128, 64, 16], F32, "Cim")
    for hf in range(2):
        sl = slice(hf * 64, (hf + 1) * 64)
        P.dma(Lt.are[sl, :], dT(I["s5_a_re"].ap.rearrange("d g p -> p (d g)"), "a"), allow_slow_non_contiguous=True, eng="act")
        P.dma(Lt.aim[sl, :], dT(I["s5_a_im"].ap.rearrange("d g p -> p (d g)"), "a"), allow_slow_non_contiguous=True, eng="act")
        P.dma(Lt.Bre[sl], dT(I["s5_b_re"].ap.rearrange("d g p c -> p (d g) c"), "a"))
        P.dma(Lt.Bim[sl], dT(I["s5_b_im"].ap.rearrange("d g p c -> p (d g) c"), "a"))
        P.dma(Lt.Cre[sl], dT(I["s5_c_re"].ap.rearrange("d g c p -> p (d g) c"), "a"), allow_slow_non_contiguous=True, eng="act")
        P.dma(Lt.Cim[sl], dT(I["s5_c_im"].ap.rearrange("d g c p -> p (d g) c"), "a"), allow_slow_non_contiguous=True, eng="act")
    P.dma(Lt.ldt, dT(I["s5_log_dt"].ap.rearrange("d g -> (d g)").partition_broadcast(128), "a"))
    G.s5l = Lt


def s5_alloc(P, I, G):
    PR = P.sb([128, 32, 64], F32, "PR"); NPI = P.sb([128, 32, 64], F32, "NPI")
    DA = P.sb([128, 10, 64], F32, "DA"); DB = P.sb([128, 10, 64], F32, "DB")
    BX1 = P.sb([128, 64, 16], F32, "BX1"); BX2 = P.sb([128, 64, 16], F32, "BX2")
    CX1 = P.sb([128, 64, 16], F32, "CX1"); CX2 = P.sb([128, 64, 16], F32, "CX2")
    Dcol = P.sb([128, 32], F32, "Dcol")
    identF = G.ident
    swapF = P.sb([128, 128], BF16, "swapF"); P.dma(swapF, I["swap"], eng="pool")
    maskf = P.sb([128, 128], BF16, "maskf"); P.dma(maskf, I["mask_f"], eng="pool")
    maskb = P.sb([128, 128], BF16, "maskb"); P.dma(maskb, I["mask_b"], eng="pool")
    sgn = P.sb([128, 1], F32, "sgn"); P.memset(sgn[0:64, :], 1.0); P.memset(sgn[64:128, :], -1.0)
    for i in range(8):
        P.dma(Dcol[i * 16:(i + 1) * 16, :], dT(I["s5_d"].ap.rearrange("(g c) -> c g", c=16), "s5d"), allow_slow_non_contiguous=True)
    G.s5t = dict(PR=PR, NPI=NPI, DA=DA, DB=DB, BX1=BX1, BX2=BX2, CX1=CX1, CX2=CX2, Dcol=Dcol, identF=identF, swapF=swapF, maskf=maskf, maskb=maskb, sgn=sgn)


def s5_params(P, I, G):
    PR = G.s5t["PR"]
    NPI = G.s5t["NPI"]
    DA = G.s5t["DA"]
    DB = G.s5t["DB"]
    BX1 = G.s5t["BX1"]
    BX2 = G.s5t["BX2"]
    CX1 = G.s5t["CX1"]
    CX2 = G.s5t["CX2"]
    Dcol = G.s5t["Dcol"]
    identF = G.s5t["identF"]
    swapF = G.s5t["swapF"]
    maskf = G.s5t["maskf"]
    maskb = G.s5t["maskb"]
    sgn = G.s5t["sgn"]
    with P.scope():
        are, aim, ldt = G.s5l.are, G.s5l.aim, G.s5l.ldt
        dt_ = P.sb([128, 64], F32, "dt")
        P.act(dt_, ldt, AF.Exp)
        lr = P.sb([128, 64], F32, "lr"); li = P.sb([128, 64], F32, "li")
        P.tt(lr, are, dt_, ALU.mult); P.tt(li, aim, dt_, ALU.mult)
        with P.scope():
            elist = [t - 7 for t in range(16)] + [8 - t for t in range(16)]
            LR = P.sb([128, 32, 64], F32, "LR"); LI = P.sb([128, 32, 64], F32, "LI")
            for idx, e in enumerate(elist):
                P.ts(LR[:, idx, :], lr, float(e), None, op0=ALU.mult)
                P.ts(LI[:, idx, :], li, float(e), None, op0=ALU.mult, eng="pool")
            mag = P.sb([128, 32, 64], F32, "mag")
            P.act(mag, LR, AF.Exp)
            rr = P.sb([128, 32, 64], F32, "rr"); kk = P.sb([128, 32, 64], F32, "kk")

            def sin_of(dst, ang_t, shift):
                P.ts(rr, ang_t, 1.0 / TWO_PI, shift / TWO_PI, op0=ALU.mult, op1=ALU.add)
                P.ts(kk, rr, MAGIC, None, op0=ALU.add)
                P.ts(kk, kk, MAGIC, None, op0=ALU.subtract)
                P.tt(rr, rr, kk, ALU.subtract)
                P.ts(rr, rr, TWO_PI, None, op0=ALU.mult)
                P.ts(rr, rr, 3.1415925, -3.1415925, op0=ALU.min, op1=ALU.max)
                P.act(dst, rr, AF.Sin)
            sn = LR
            sin_of(sn, LI, 0.0)
            P.stt(NPI, mag, -1.0, sn, ALU.mult, ALU.mult)
            sin_of(sn, LI, TWO_PI / 4)
            P.tt(PR, mag, sn, ALU.mult)
        cr_ = P.sb([128, 64], F32, "cr"); ci_ = P.sb([128, 64], F32, "ci")
        t1 = P.sb([128, 64], F32, "t1"); t2 = P.sb([128, 64], F32, "t2")
        P.copy(DA[:, 0, :], PR[:, 15, :])
        P.ts(DB[:, 0, :], NPI[:, 15, :], -1.0, None, op0=ALU.mult)
        for m in range(1, 10):
            P.tt(t1, DA[:, m - 1, :], DA[:, m - 1, :], ALU.mult)
            P.tt(t2, DB[:, m - 1, :], DB[:, m - 1, :], ALU.mult)
            P.tt(DA[:, m, :], t1, t2, ALU.subtract)
            P.stt(DB[:, m, :], DA[:, m - 1, :], 2.0, DB[:, m - 1, :], ALU.mult, ALU.mult)
        P.ts(DB, DB, sgn[:, 0:1], None, op0=ALU.mult)
        den = P.sb([128, 64], F32, "den"); nr = P.sb([128, 64], F32, "nr"); abi = P.sb([128, 64], F32, "abi")
        P.tt(t1, are, are, ALU.mult); P.tt(t2, aim, aim, ALU.mult); P.tt(den, t1, t2, ALU.add); P.recip(den, den)
        P.ts(nr, PR[:, 8, :], -1.0, None, op0=ALU.add)
        P.ts(abi, NPI[:, 8, :], -1.0, None, op0=ALU.mult)
        P.tt(t1, nr, are, ALU.mult); P.tt(t2, abi, aim, ALU.mult); P.tt(cr_, t1, t2, ALU.add); P.tt(cr_, cr_, den, ALU.mult)
        P.tt(t1, abi, are, ALU.mult); P.tt(t2, nr, aim, ALU.mult); P.tt(ci_, t1, t2, ALU.subtract); P.tt(ci_, ci_, den, ALU.mult)
        crb = bc(cr_, [[1, 64], [0, 16]]); cib = bc(ci_, [[1, 64], [0, 16]])
        Bre, Bim, Cre, Cim = G.s5l.Bre, G.s5l.Bim, G.s5l.Cre, G.s5l.Cim
        bbr = P.sb([128, 64, 16], F32, "bbr"); bbi = P.sb([128, 64, 16], F32, "bbi"); t3 = P.sb([128, 64, 16], F32, "t3")
        P.tt(bbr, Bre, crb, ALU.mult); P.tt(t3, Bim, cib, ALU.mult); P.tt(bbr, bbr, t3, ALU.subtract)
        P.tt(bbi, Bim, crb, ALU.mult); P.tt(t3, Bre, cib, ALU.mult); P.tt(bbi, bbi, t3, ALU.add)
        P.copy(BX1[0:64], bbr[0:64]); P.copy(BX1[64:128], bbi[64:128])
        P.copy(BX2[0:64], bbi[0:64]); P.ts(BX2[64:128], bbr[64:128], -1.0, None, op0=ALU.mult)
        P.copy(CX1[0:64], Cre[0:64]); P.ts(CX1[64:128], Cim[64:128], -1.0, None, op0=ALU.mult)
        P.copy(CX2[0:64], Cim[0:64]); P.copy(CX2[64:128], Cre[64:128])


def stage_S5(P, I, G, last):
    NQ = LH + (0 if last else CTX)
    G.ysD = P.dram([512, NQ], BF16, "ysD")
    with P.scope():
        PR = G.s5t["PR"]
        NPI = G.s5t["NPI"]
        DA = G.s5t["DA"]
        DB = G.s5t["DB"]
        BX1 = G.s5t["BX1"]
        BX2 = G.s5t["BX2"]
        CX1 = G.s5t["CX1"]
        CX2 = G.s5t["CX2"]
        Dcol = G.s5t["Dcol"]
        identF = G.s5t["identF"]
        swapF = G.s5t["swapF"]
        maskf = G.s5t["maskf"]
        maskb = G.s5t["maskb"]
        sgn = G.s5t["sgn"]
        Sel = P.sb([128, 64, 128], BF16, "Sel"); SelT = P.sb([128, 64, 128], BF16, "SelT")
        for q in range(4):
            P.dma(Sel[:, q * 16:(q + 1) * 16, :], dT(I["sel8"].ap[:, q * 16:(q + 1) * 16, :], "sel8"), eng="pool")
            P.dma(SelT[:, q * 16:(q + 1) * 16, :], dT(I["sel8T"].ap[:, q * 16:(q + 1) * 16, :], "sel8T"), eng="pool")
        KQ = {nm: Rot(P, [128, 16, 16], BF16, nm, 2) for nm in ["Kf", "Qf", "Kb", "Qb"]}
        tA = Rot(P, [128, 16, 16], F32, "tA", 1); tB = Rot(P, [128, 16, 16], F32, "tB", 1)
        tC = Rot(P, [128, 16, 16], F32, "tC", 1); tD = Rot(P, [128, 16, 16], F32, "tD", 1)
        SgR = Rot(P, [128, 128], BF16, "Sg", 2)
        WeR = Rot(P, [128, 128], BF16, "We", 4)
        s1R = Rot(P, [128, 128], F32, "s1", 1); s2R = Rot(P, [128, 128], F32, "s2", 1)
        UcR = Rot(P, [128, 576], BF16, "Uc", 2)
        XR = {d: [P.sb([128, 545], BF16, f"X{d}{i}") for i in range(2)] for d in "fb"}
        for d in "fb":
            for x in XR[d]:
                P.memset(x, 0.0)
        MdR = {d: Rot(P, [128, 10, 128], BF16, "Md" + d, 2) for d in "fb"}
        mA = Rot(P, [128, 10, 128], BF16, "mA", 1); mB = Rot(P, [128, 10, 128], BF16, "mB", 1)
        Yt = P.sb([128, 8, 544], BF16, "Yt")
        tqR = Rot(P, [128, 256], F32, "tq", 2); yctx = P.sb([128, CTX], BF16, "yctx")
        yown = Rot(P, [128, LH], BF16, "yown", 1)
        identB = G.ident
        flip = 0
        spec = {"Kf": (17, 15), "Qf": (7, 9), "Kb": (7, 8), "Qb": (16, 16)}

        def m128(t, lo):
            return t.v(t.ap[:, lo:lo + 8, :].rearrange("p a b -> p (a b)"))

        def build(g):
            stt_ = {"mats": {}, "We": {}, "Mall": {}}
            th = []
            for dname, gd in (("f", g), ("b", 32 + g)):
                for kind, X1, X2, eng in (("K", BX1, BX2, "dve"), ("Q", CX1, CX2, "pool")):
                    t0_, ne = spec[kind + dname]
                    out = KQ[kind + dname]()[:, 0:ne, :]
                    a_ = (tA() if kind == "K" else tC())[:, 0:ne, :]
                    b_ = (tB() if kind == "K" else tD())[:, 0:ne, :]
                    prb = bc(PR[:, t0_, gd:gd + 1], [[64, ne], [0, 16]]); npb = bc(NPI[:, t0_, gd:gd + 1], [[64, ne], [0, 16]])
                    x1b = bc(X1[:, gd, :], [[0, ne], [1, 16]]); x2b = bc(X2[:, gd, :], [[0, ne], [1, 16]])
                    th.append(lambda a_=a_, prb=prb, x1b=x1b, eng=eng: P.tt(a_, prb, x1b, ALU.mult, eng=eng))
                    th.append(lambda b_=b_, npb=npb, x2b=x2b, eng=eng: P.tt(b_, npb, x2b, ALU.mult, eng=eng))
                    th.append(lambda out=out, a_=a_, b_=b_, eng=eng: P.tt(out, a_, b_, ALU.add, eng=eng))
                    stt_["mats"][kind + dname] = out
            Sg = SgR()
            stt_["Sg"] = Sg
            mt = stt_["mats"]

            def th_S():
                psf = P.ps(); psb_ = P.ps()
                P.mm(psf[:, 0:128], m128(mt["Kf"], 7), m128(mt["Qf"], 0))
                P.mm(psb_[:, 0:128], m128(mt["Kb"], 0), m128(mt["Qb"], 8))
                s1 = s1R(); s2 = s2R()
                P.tt(s1, psf[:, 0:128], maskf, ALU.mult)
                P.tt(s2, psb_[:, 0:128], maskb, ALU.mult)
                P.tt(s1, s1, s2, ALU.add)
                P.stt(Sg, identF, Dcol[:, g:g + 1], s1, ALU.mult, ALU.add)
            th.append(th_S)
            for dname, key in (("f", "Kf"), ("b", "Kb")):
                w = WeR()
                stt_["We"][dname] = w

                def th_W(w=w, key=key):
                    pt = P.ps()
                    ptb = pt.v(pt.ap.bitcast(BF16))
                    P.transpose(ptb[:, 0:128], m128(mt[key], 0), identB)
                    P.copy(w, ptb[:, 0:128], eng="act")
                th.append(th_W)
            stt_["Wo"] = {"f": m128(mt["Qf"], 1), "b": m128(mt["Qb"], 0)}
            for dname, gd in (("f", g), ("b", 32 + g)):
                Mall = MdR[dname]()
                stt_["Mall"][dname] = Mall
                ta = mA(); tb = mB()
                idb = bc(identF, [[0, 10], [1, 128]]); swb = bc(swapF, [[0, 10], [1, 128]])
                dab = bc(DA[:, :, gd], [[64, 10], [0, 128]]); dbb = bc(DB[:, :, gd], [[64, 10], [0, 128]])
                th.append(lambda ta=ta, idb=idb, dab=dab: P.tt(ta, idb, dab, ALU.mult, eng="pool"))
                th.append(lambda tb=tb, swb=swb, dbb=dbb: P.tt(tb, swb, dbb, ALU.mult, eng="pool"))
                th.append(lambda Mall=Mall, ta=ta, tb=tb: P.tt(Mall, ta, tb, ALU.add, eng="pool"))
            return stt_, th

        cur_state, th0 = build(0)
        for t_ in th0:
            t_()
        for g in range(32):
            j, gl = g // 8, g % 8
            if g + 1 < 32:
                nxt_state, pend = build(g + 1)
            else:
                nxt_state, pend = None, []
            per_step = (len(pend) + 9) // 10
            Sg = cur_state["Sg"]; We = cur_state["We"]; Wo = cur_state["Wo"]
            Uc = UcR()
            puA = P.ps(); puB = P.ps(); pu2 = P.ps()
            for i in range(8):
                P.mm(puA[:, 0:256], Sel[:, gl * 8 + i, :], G.uT[:, j, i, 32:288], start=(i == 0), stop=(i == 7))
                P.mm(puB[:, 0:256], Sel[:, gl * 8 + i, :], G.uT[:, j, i, 288:544], start=(i == 0), stop=(i == 7))
                P.mm(pu2[:, 0:32], Sel[:, gl * 8 + i, :], G.uT[:, j, i, 0:32], start=(i == 0), stop=(i == 7))
            P.copy(Uc[:, 32:288], puA[:, 0:256], eng="act")
            P.copy(Uc[:, 288:544], puB[:, 0:256])
            P.copy(Uc[:, 0:32], pu2[:, 0:32], eng="act")
            P.copy(Uc[:, 544:576], pu2[:, 0:32])
            st_ = {}
            for dname, gd in (("f", g), ("b", 32 + g)):
                ucoff = 0 if dname == "f" else 32
                xoff = 1 if dname == "f" else 0
                cur = XR[dname][0]; nxt = XR[dname][1]
                pa = P.ps(); pb2 = P.ps()
                P.mm(pa[:, 0:512], We[dname], Uc[:, ucoff:ucoff + 512])
                P.mm(pb2[:, 0:32], We[dname], Uc[:, ucoff + 512:ucoff + 544])
                P.copy(cur[:, xoff:xoff + 512], pa[:, 0:512], eng="act")
                P.copy(cur[:, xoff + 512:xoff + 544], pb2[:, 0:32])
                st_[dname] = [cur, nxt, xoff, cur_state["Mall"][dname]]
            for m in range(10):
                d = 1 << m
                work = []
                for dname in ("f", "b"):
                    cur, nxt, xoff, Mall = st_[dname]
                    for (lo, hi) in ((0, 272), (272, 544)):
                        ps = P.ps()
                        if dname == "f":
                            s_ = max(lo, d)
                            has = s_ < hi
                            shift = (ps[:, s_ - lo:hi - lo], cur[:, xoff + s_ - d:xoff + hi - d]) if has else None
                        else:
                            e_ = min(hi, 544 - d)
                            has = lo < e_
                            shift = (ps[:, 0:e_ - lo], cur[:, xoff + lo + d:xoff + e_ + d]) if has else None
                        work.append((dname, ps, lo, hi, shift, cur, nxt, xoff, Mall))
                for (dname, ps, lo, hi, shift, cur, nxt, xoff, Mall) in work:
                    P.mm(ps[:, 0:hi - lo], identB, cur[:, xoff + lo:xoff + hi], start=True, stop=(shift is None))
                for (dname, ps, lo, hi, shift, cur, nxt, xoff, Mall) in work:
                    if shift is not None:
                        P.mm(shift[0], Mall[:, m, :], shift[1], start=False, stop=True)
                for (dname, ps, lo, hi, shift, cur, nxt, xoff, Mall) in work:
                    flip ^= 1
                    P.copy(nxt[:, xoff + lo:xoff + hi], ps[:, 0:hi - lo], eng=("act" if flip else "dve"))
                for dname in ("f", "b"):
                    st_[dname][0], st_[dname][1] = st_[dname][1], st_[dname][0]
                for _ in range(per_step):
                    if pend:
                        pend.pop(0)()
            while pend:
                pend.pop(0)()
            Xfin = {dname: st_[dname][0] for dname in ("f", "b")}
            Xf, Xb = Xfin["f"], Xfin["b"]
            py = P.ps()
            pyc = P.ps() if not last else None
            ytl = [(Sg, Uc[:, 32:544], Uc[:, 0:32]), (Wo["f"], Xf[:, 32:544], Xf[:, 0:32]), (Wo["b"], Xb[:, 1:513], Xb[:, 513:545])]
            for q, (lh, r1, r2) in enumerate(ytl):
                P.mm(py[:, 0:512], lh, r1, start=(q == 0), stop=(q == 2))
                if not last:
                    P.mm(pyc[:, 0:32], lh, r2, start=(q == 0), stop=(q == 2))
            P.copy(Yt[:, gl, 0:512], py[:, 0:512], eng="act")
            if not last:
                P.copy(Yt[:, gl, 512:544], pyc[:, 0:32])
            cur_state = nxt_state
            if gl == 7:
                yo = yown()
                for i in range(8):
                    ps = P.ps()
                    ps2 = P.ps() if not last else None
                    for g2 in range(8):
                        P.mm(ps[:, 0:512], SelT[:, g2 * 8 + i, :], Yt[:, g2, 0:512], start=(g2 == 0), stop=(g2 == 7))
                        if not last:
                            P.mm(ps2[:, 0:32], SelT[:, g2 * 8 + i, :], Yt[:, g2, 512:544], start=(g2 == 0), stop=(g2 == 7))
                    tq = tqR()
                    P.act(tq, ps[:, 0:256], AF.Copy, scale=G.sel[:, 0:1])
                    P.stt(yo.v(yo.ap[:, i:LH:8]), ps[:, 256:512], G.sel[:, 1:2], tq, ALU.mult, ALU.add)
                    if not last:
                        P.copy(yctx.v(yctx.ap[:, i:CTX:8]), ps2[:, 0:32])
                P.dma(G.ysD.v(G.ysD.ap[j * 128:(j + 1) * 128, 0:LH]), yo)
                if not last:
                    P.dma(G.ysD.v(G.ysD.ap[j * 128:(j + 1) * 128, LH:LH + CTX]), yctx)


def emit_layer(P, I, G, last, out_own, out_ctx):
    stage_prep(P, I, G)
    with P.scope():
        alloc_persist(P, G)
        with P.scope():
            G.uT = P.sb([128, 4, 8, NK // 8], BF16, "uT")
            s5_alloc(P, I, G)
            with P.scope():
                s5_load(P, I, G)
                stage_A(P, I, G)
                s5_params(P, I, G)
            stage_S5(P, I, G, last)
        stage_B1(P, I, G, last)
        stage_B2(P, I, G, last)
        stage_B3(P, I, G, last)
    stage_B4a(P, I, G, last)
    stage_B4b(P, I, G, last, out_own, out_ctx)
    P.flush()


def build_fused():
    nc = bass.Bass("TRN2", target_bir_lowering=False)
    Cn = declare_consts(nc)
    W = [declare_weights(nc, l) for l in range(2)]
    y_out = dT(nc.dram_tensor("y_own", [LH, D], F32, kind="ExternalOutput").ap(), "y_own")
    with ExitStack() as st:
        P = Prog(nc, st)
        P.init_psum()
        NCH = 4
        CR = LH // NCH
        x1o = [P.dram([CR, D], F32, f"x1o{c}") for c in range(NCH)]
        x1g = [P.dram([2 * CR, D], F32, f"x1g{c}") for c in range(NCH)]
        ctx1 = P.dram([CTX, D], F32, "ctx1")
        own_src = RowSrc(lambda r0: x1o[r0 // CR].v(x1o[r0 // CR].ap[r0 % CR:r0 % CR + 128, :]))

        def all_fn(r0):
            half, rr = r0 // LH, r0 % LH
            c, i = rr // CR, rr % CR
            return x1g[c].v(x1g[c].ap[half * CR + i:half * CR + i + 128, :])
        with P.scope():
            G = Ctx()
            I0 = dict(Cn); I0.update(W[0])
            for k in ("x_all", "x_own", "ctx"):
                I0[k] = flat_src(Cn[k])
            emit_layer(P, I0, G, False, own_src, flat_src(ctx1))
        groups = [[0, 1], [2, 3], [4, 5], [6, 7]]
        for c in range(NCH):
            P.add("pool", lambda e, c=c: e.collective_compute("AllGather", ALU.bypass, replica_groups=groups,
                                                              ins=[x1o[c].ap.opt()], outs=[x1g[c].ap.opt()]), [x1o[c]], [x1g[c]])
            P.add("pool", None, [x1g[c]], [])
        P.flush()
        with P.scope():
            G = Ctx()
            I1 = dict(Cn); I1.update(W[1])
            I1["x_all"] = RowSrc(all_fn); I1["x_own"] = own_src; I1["ctx"] = flat_src(ctx1)
            emit_layer(P, I1, G, True, flat_src(y_out), None)
    return nc


_NC_CACHE = {}


def kernel(**inputs):
    inputs = {k: np.asarray(v) for k, v in inputs.items()}
    C = host_constants()
    if "nc" not in _NC_CACHE:
        _NC_CACHE["nc"] = build_fused()
    nc = _NC_CACHE["nc"]
    in_maps = [per_core_inputs(inputs, core, C) for core in range(8)]
    res = run_bass_kernel_spmd(nc, in_maps, core_ids=list(range(8)))
    out = np.empty((4, L, D), np.float32)
    for core in range(8):
        b, hh = core // 2, core % 2
        out[b, hh * LH:(hh + 1) * LH] = np.asarray(res.results[core]["y_own"])
    return out
```

```python
import numpy as np
import concourse.bass as bass
import concourse.mybir as mybir
from concourse.bass_utils import run_bass_kernel_spmd
from contextlib import ExitStack, contextmanager

F32 = mybir.dt.float32
BF16 = mybir.dt.bfloat16
I32 = mybir.dt.int32
AF = mybir.ActivationFunctionType
ALU = mybir.AluOpType
AX = mybir.AxisListType

ENGS = ["pe", "act", "dve", "pool", "sp"]
DMA_WIN = 8
SAME_ENG_SYNC = True


class T:
    __slots__ = ("ap", "keys")

    def __init__(self, ap, keys):
        self.ap = ap
        self.keys = tuple(keys)

    def __getitem__(self, sl):
        return T(self.ap[sl], self.keys)

    def v(self, ap):
        return T(ap, self.keys)

    def k(self, *sub):
        return T(self.ap, [(self.keys[0],) + tuple(sub)])


class Prog:
    def __init__(self, nc, stack):
        self.nc = nc
        self.stack = stack
        self.cur = stack
        self.streams = {e: [] for e in ENGS}
        self.last_writer = {}
        self.readers = {}
        self.ndma = {e: 0 for e in ENGS}
        self.sigcount = {e: 0 for e in ENGS}
        self.waited = {e: {} for e in ENGS}
        self.nt = 0
        self.psum_banks = []
        self.psum_i = 0
        self.sem = {e: stack.enter_context(nc.semaphore(f"s_{e}")) for e in ENGS}
        self.dsem = {e: [stack.enter_context(nc.semaphore(f"d_{e}{i}")) for i in range(DMA_WIN)]
                     for e in ("sp", "pool", "act")}
        self.dbg = {}
        self.nops = {e: 0 for e in ENGS}

    def sb(self, shape, dt, name=None):
        self.nt += 1
        name = name or "t"
        nm = f"{name}_{self.nt}"
        t = self.cur.enter_context(self.nc.sbuf_tensor(nm, list(shape), dt))
        return T(t[:], [nm])

    def dram(self, shape, dt, name):
        self.nt += 1
        nm = f"{name}_{self.nt}"
        t = self.nc.dram_tensor(nm, list(shape), dt, kind="Internal")
        return T(t.ap(), [nm])

    def init_psum(self, n=8):
        for i in range(n):
            t = self.stack.enter_context(self.nc.psum_tensor(f"bank{i}", [128, 512], F32))
            self.psum_banks.append(T(t[:], [f"bank{i}"]))

    def ps(self):
        b = self.psum_banks[self.psum_i % len(self.psum_banks)]
        self.psum_i += 1
        return b

    @contextmanager
    def scope(self):
        prev = self.cur
        with ExitStack() as st:
            self.cur = st
            yield
            self.flush()
        self.cur = prev

    def add(self, eng, fn, reads=(), writes=(), dma=False):
        deps = set()
        rk = [k for t in reads for k in t.keys]
        wk = [k for t in writes for k in t.keys]
        for k in rk:
            if k in self.last_writer:
                deps.add(self.last_writer[k])
        for k in wk:
            if k in self.last_writer:
                deps.add(self.last_writer[k])
            for r in self.readers.get(k, ()):
                deps.add(r)
        idx = len(self.streams[eng])
        me = (eng, idx)
        deps.discard(me)
        op = dict(fn=fn, deps=deps, dma=dma, signal=False, dman=None)
        if dma:
            op["dman"] = self.ndma[eng]
            self.ndma[eng] += 1
        self.streams[eng].append(op)
        for k in rk:
            self.readers.setdefault(k, []).append(me)
        for k in wk:
            self.last_writer[k] = me
            self.readers[k] = []
        return me

    def dma(self, out, in_, eng="sp", **kw):
        o = out.ap
        i = in_.ap
        return self.add(eng, lambda e: e.dma_start(out=o, in_=i, **kw), [in_], [out], dma=True)

    def mm(self, out, lhsT, rhs, start=True, stop=True, **kw):
        return self.add("pe", lambda e: e.matmul(out.ap, lhsT.ap, rhs.ap, start=start, stop=stop, **kw),
                        [lhsT, rhs], [out])

    def transpose(self, out, in_, ident):
        return self.add("pe", lambda e: e.transpose(out.ap, in_.ap, ident.ap), [in_, ident], [out])

    def act(self, out, in_, func, bias=None, scale=None, eng="act", accum_out=None):
        reads = [in_]
        kw = {}
        if bias is not None:
            if isinstance(bias, T):
                reads.append(bias); kw["bias"] = bias.ap
            else:
                kw["bias"] = bias
        if scale is not None:
            if isinstance(scale, T):
                reads.append(scale); kw["scale"] = scale.ap
            else:
                kw["scale"] = scale
        writes = [out]
        if accum_out is not None:
            kw["accum_out"] = accum_out.ap; writes.append(accum_out)
        return self.add(eng, lambda e: e.activation(out.ap, in_.ap, func, **kw), reads, writes)

    def tt(self, out, a, b, op, eng="dve"):
        return self.add(eng, lambda e: e.tensor_tensor(out.ap, a.ap, b.ap, op), [a, b], [out])

    def ts(self, out, a, s1, s2=None, op0=ALU.mult, op1=None, eng="dve"):
        reads = [a]
        v1 = s1.ap if isinstance(s1, T) else s1
        if isinstance(s1, T): reads.append(s1)
        v2 = s2.ap if isinstance(s2, T) else s2
        if isinstance(s2, T): reads.append(s2)
        if op1 is None:
            return self.add(eng, lambda e: e.tensor_scalar(out.ap, a.ap, v1, None, op0), reads, [out])
        return self.add(eng, lambda e: e.tensor_scalar(out.ap, a.ap, v1, v2, op0, op1), reads, [out])

    def stt(self, out, a, s, b, op0, op1, eng="dve"):
        reads = [a, b]
        v = s.ap if isinstance(s, T) else s
        if isinstance(s, T): reads.append(s)
        return self.add(eng, lambda e: e.scalar_tensor_tensor(out.ap, a.ap, v, b.ap, op0, op1), reads, [out])

    def copy(self, out, in_, eng="dve"):
        if eng == "act":
            return self.add("act", lambda e: e.copy(out.ap, in_.ap), [in_], [out])
        return self.add(eng, lambda e: e.tensor_copy(out.ap, in_.ap), [in_], [out])

    def memset(self, out, val, eng="pool"):
        return self.add(eng, lambda e: e.memset(out.ap, val), [], [out])

    def recip(self, out, in_, eng="dve"):
        return self.add(eng, lambda e: e.reciprocal(out.ap, in_.ap), [in_], [out])

    def recip_fast(self, out, in_):
        return self.add("dve", lambda e: e.reciprocal_approx_fast(out.ap, in_.ap), [in_], [out])

    def bn_stats(self, out, in_):
        return self.add("dve", lambda e: e.bn_stats(out.ap, in_.ap), [in_], [out])

    def bn_aggr(self, out, in_):
        return self.add("dve", lambda e: e.bn_aggr(out.ap, in_.ap), [in_], [out])

    def debug_out(self, name, t, shape, dt=F32):
        d = self.nc.dram_tensor(name, list(shape), dt, kind="ExternalOutput").ap()
        self.dbg[name] = d
        return self.dma(T(d, [name]), t)

    def flush(self):
        nc = self.nc
        streams = self.streams
        lasts = []
        for e in ENGS:
            for j in range(len(streams[e]) - 1, -1, -1):
                op = streams[e][j]
                if op["fn"] is not None and not op["dma"]:
                    lasts.append((e, j))
                    break
        dmas = [(e, j) for e in ENGS for j, op in enumerate(streams[e]) if op["dma"]]
        for e in ENGS:
            deps = set(l for l in lasts if l[0] != e) | set(dmas)
            streams[e].append(dict(fn=None, deps=deps, dma=False, signal=False, dman=None))
        for e in ENGS:
            for op in streams[e]:
                for (f, j) in op["deps"]:
                    d = streams[f][j]
                    if not d["dma"]:
                        if f == e and not SAME_ENG_SYNC:
                            continue
                        d["signal"] = True
        for e in ENGS:
            for op in streams[e]:
                if op["signal"]:
                    self.sigcount[e] += 1
                    op["sigval"] = self.sigcount[e]
        sem, dsem = self.sem, self.dsem

        def run(ename):
            def body(eng):
                waited = self.waited[ename]

                def wait(s, v, key):
                    if waited.get(key, 0) >= v:
                        return
                    waited[key] = v
                    eng.wait_ge(s, v)

                for op in streams[ename]:
                    for (f, j) in sorted(op["deps"]):
                        d = streams[f][j]
                        if d["dma"]:
                            n = d["dman"]
                            wait(dsem[f][n % DMA_WIN], 16 * (n // DMA_WIN + 1), (f, n % DMA_WIN))
                        else:
                            if f == ename and not SAME_ENG_SYNC:
                                continue
                            wait(sem[f], d["sigval"], f)
                    if op["dma"]:
                        n = op["dman"]
                        if n >= DMA_WIN:
                            wait(dsem[ename][n % DMA_WIN], 16 * (n // DMA_WIN), (ename, n % DMA_WIN))
                    if op["fn"] is None:
                        continue
                    ins = op["fn"](eng)
                    if op["dma"]:
                        ins.then_inc(dsem[ename][op["dman"] % DMA_WIN], 16)
                    elif op["signal"]:
                        ins.then_inc(sem[ename], 1)
            return body

        with nc.Block() as block:
            block.tensor(run("pe"))
            block.scalar(run("act"))
            block.vector(run("dve"))
            block.gpsimd(run("pool"))
            block.sync(run("sp"))
        for e in ENGS:
            self.nops[e] += len(streams[e])
        self.streams = {e: [] for e in ENGS}
        self.last_writer = {}
        self.readers = {}


class Rot:
    def __init__(self, P, shape, dt, name, n):
        self.tiles = [P.sb(shape, dt, f"{name}{i}") for i in range(n)]
        self.i = 0

    def __call__(self):
        t = self.tiles[self.i % len(self.tiles)]
        self.i += 1
        return t


def _ps6(self):
    b = self.psum_banks[self.psum_i % 6]
    self.psum_i += 1
    return b


def _psacc(self):
    self.acc_i = getattr(self, "acc_i", 0) + 1
    return self.psum_banks[6 + self.acc_i % 2]


Prog.ps = _ps6
Prog.ps_acc = _psacc


D = 1024
L = 4096
LH = 2048
CTX = 256
NK = CTX + L
NKT = NK // 128
EPS = 1e-6
OFF_AK, OFF_AV, OFF_CKV, OFF_CKR, OFF_U, NST = 0, 128, 256, 512, 544, 1056
OFF_AQ, OFF_CQ, OFF_GATE, NIN = 1056, 1568, 2336, 5408
ALPHA = (2.0 * 2) ** 0.25


def dT(ap, name):
    return T(ap, [name])


class Ctx:
    pass


class RowSrc:
    def __init__(self, fn):
        self.fn = fn

    def rows(self, r0):
        return self.fn(r0)


def flat_src(t):
    return RowSrc(lambda r0: t.v(t.ap[r0:r0 + 128, :]))


WEIGHT_SHAPES = {
    "w_mod": [D, 6 * D], "b_mod": [6 * D], "w_in": [D, NIN], "a_q_gain": [64], "a_k_gain": [64],
    "c_q_a_gain": [768], "c_kv_a_gain": [256], "c_w_qb": [768, 768], "c_w_kvb": [256, 1024],
    "s5_a_re": [2, 32, 64], "s5_a_im": [2, 32, 64], "s5_log_dt": [2, 32],
    "s5_b_re": [2, 32, 64, 16], "s5_b_im": [2, 32, 64, 16], "s5_c_re": [2, 32, 16, 64], "s5_c_im": [2, 32, 16, 64],
    "s5_d": [512], "s5_w_glu": [512, 1024], "w_branch_a": [512, D], "w_branch_s5": [512, D], "w_branch_c": [512, D],
    "w_out": [D, D], "ln1_g": [D], "ln1_b": [D], "w_up": [D, 4 * D], "w_down": [4 * D, D], "ln2_g": [D], "ln2_b": [D],
}
CONST_SHAPES = {
    "x_all": [L, D], "x_own": [LH, D], "ctx": [CTX, D], "cvec": [2, D],
    "ident": [128, 128], "blk64": [128, 128], "perm64": [128, 128], "perm32": [32, 32],
    "ropek_cos": [128, L], "ropek_sin": [128, L], "ropeq_cos": [128, LH], "ropeq_sin": [128, LH],
    "rope32k_cos": [32, L], "rope32k_sin": [32, L], "rope32q_cos": [32, LH], "rope32q_sin": [32, LH],
    "sel": [128, 2], "swap": [128, 128], "mask_f": [128, 128], "mask_b": [128, 128],
    "sel8": [128, 64, 128], "sel8T": [128, 64, 128],
}


def declare_consts(nc):
    return {k: dT(nc.dram_tensor(k, list(v), F32, kind="ExternalInput").ap(), k) for k, v in CONST_SHAPES.items()}


def declare_weights(nc, l):
    return {k: dT(nc.dram_tensor(f"{k}_{l}", list(v), F32, kind="ExternalInput").ap(), f"{k}_{l}")
            for k, v in WEIGHT_SHAPES.items()}


def declare_inputs(nc, last):
    I = declare_consts(nc)
    I.update(declare_weights(nc, 1 if last else 0))
    return I


def host_constants():
    import math
    C = {}
    C["ident"] = np.eye(128, dtype=np.float32)
    blk = np.zeros((128, 128), np.float32); blk[:64, :64] = 1 / 64; blk[64:, 64:] = 1 / 64
    C["blk64"] = blk

    def perm_and_sign(dim):
        half = dim // 2; q = half // 2
        Pm = np.zeros((dim, dim), np.float32)
        sg = np.zeros(dim, np.float32)
        for m in range(dim):
            if (m % half) < q:
                Pm[m + q, m] = 1; sg[m] = -1
            else:
                Pm[m - q, m] = 1; sg[m] = 1
        return Pm, sg
    P64, s64 = perm_and_sign(64)
    p128 = np.zeros((128, 128), np.float32); p128[:64, :64] = P64; p128[64:, 64:] = P64
    C["perm64"] = p128
    P32, s32 = perm_and_sign(32)
    C["perm32"] = P32

    def tables(dim):
        half = dim // 2
        inv = 10000.0 ** (-np.arange(0, half, 2, dtype=np.float32) / half)
        rows = L // 64
        row = np.repeat(np.arange(rows, dtype=np.float32), 64)
        col = np.tile(np.arange(64, dtype=np.float32), rows)
        ang_r = row[:, None] * inv; ang_c = col[:, None] * inv
        ang = np.concatenate([ang_r, ang_r, ang_c, ang_c], axis=-1).astype(np.float32)
        return np.cos(ang).T.astype(np.float32), np.sin(ang).T.astype(np.float32)
    c64, s64t = tables(64)
    s64t = s64t * s64[:, None]
    C["ropek_cos"] = np.concatenate([c64, c64], 0); C["ropek_sin"] = np.concatenate([s64t, s64t], 0)
    c32, s32t = tables(32)
    s32t = s32t * s32[:, None]
    C["rope32k_cos"] = c32; C["rope32k_sin"] = s32t
    sw = np.zeros((128, 128), np.float32)
    for p in range(64):
        sw[p, 64 + p] = 1; sw[64 + p, p] = 1
    C["swap"] = sw
    ii = np.arange(128) // 16
    C["mask_f"] = (ii[:, None] <= ii[None, :]).astype(np.float32)
    C["mask_b"] = (ii[:, None] >= ii[None, :]).astype(np.float32)
    sel = np.zeros((128, 64, 128), np.float32)
    for gl in range(8):
        for i in range(8):
            for c in range(16):
                sel[gl * 16 + c, gl * 8 + i, i * 16 + c] = 1
    C["sel8"] = sel
    C["sel8T"] = np.ascontiguousarray(sel.transpose(2, 1, 0))
    return C


def per_core_inputs(inputs, core, C, layers=(0, 1)):
    b, hh = core // 2, core % 2
    m = {}
    xb = inputs["x"][b]
    m["x_all"] = xb; m["x_own"] = xb[hh * LH:(hh + 1) * LH]; m["ctx"] = inputs["ctx"][b]
    m["cvec"] = np.stack([inputs["c"][b], inputs["c_ctx"]], 0)
    for l in layers:
        for k in WEIGHT_SHAPES:
            m[f"{k}_{l}"] = inputs[k][l]
    for k in ["ident", "blk64", "perm64", "perm32", "ropek_cos", "ropek_sin", "rope32k_cos", "rope32k_sin",
              "swap", "mask_f", "mask_b", "sel8", "sel8T"]:
        m[k] = C[k]
    sl = slice(hh * LH, (hh + 1) * LH)
    m["ropeq_cos"] = C["ropek_cos"][:, sl]; m["ropeq_sin"] = C["ropek_sin"][:, sl]
    m["rope32q_cos"] = C["rope32k_cos"][:, sl]; m["rope32q_sin"] = C["rope32k_sin"][:, sl]
    s = np.zeros((128, 2), np.float32); s[:, hh] = 1
    m["sel"] = s
    return {k: np.ascontiguousarray(v, dtype=np.float32) for k, v in m.items()}


def rstd_from_ms(P, out, ms, n, eps=EPS, eng_a="act"):
    P.ts(out, ms, eps, None, op0=ALU.add)
    P.act(out, out, AF.Sqrt)
    if n > 1:
        P.recip(out, out)
    else:
        P.recip(out, out)


def stage_prep(P, I, G):
    G.ident = P.sb([128, 128], BF16, "ident"); P.dma(G.ident, I["ident"], eng="pool")
    G.blk64 = P.sb([128, 128], BF16, "blk64"); P.dma(G.blk64, I["blk64"], eng="pool")
    G.perm64 = P.sb([128, 128], BF16, "perm64"); P.dma(G.perm64, I["perm64"], eng="pool")
    G.perm32 = P.sb([32, 32], BF16, "perm32"); P.dma(G.perm32, I["perm32"], eng="pool")
    G.ones = P.sb([128, 128], BF16, "ones"); P.memset(G.ones, 1.0)
    G.modT = P.sb([128, 48, 2], F32, "modT")
    G.sel = P.sb([128, 2], F32, "sel"); P.dma(G.sel, I["sel"])
    with P.scope():
        cT = P.sb([128, 8, 2], F32, "cT")
        for t in range(2):
            src = I["cvec"].ap[t, :].rearrange("(k p) -> p k", p=128)
            P.dma(cT[:, :, t], dT(src, "cvec"), allow_slow_non_contiguous=True)
        sT = P.sb([128, 8, 2], BF16, "sT")
        P.act(sT, cT, AF.Silu)
        bT = P.sb([128, 48], F32, "bT")
        P.dma(bT, dT(I["b_mod"].ap.rearrange("(j p) -> p j", p=128), "b_mod"), allow_slow_non_contiguous=True)
        wbufs = [P.sb([128, 8, 128], BF16, f"wm{i}") for i in range(3)]
        for j in range(48):
            wb = wbufs[j % 3]
            src = I["w_mod"].ap[:, j * 128:(j + 1) * 128].rearrange("(k p) c -> p k c", p=128)
            P.dma(wb, dT(src, "w_mod"), eng="pool")
            ps = P.ps()
            for kc in range(8):
                P.mm(ps[:, 0:2], wb[:, kc, :], sT[:, kc, :], start=(kc == 0), stop=(kc == 7))
            P.ts(G.modT[:, j, :], ps[:, 0:2], bT[:, j:j + 1], None, op0=ALU.add)
        for w in (1, 4):
            P.ts(G.modT[:, w * 8:(w + 1) * 8, :], G.modT[:, w * 8:(w + 1) * 8, :], 1.0, None, op0=ALU.add)
        G.modD = P.dram([2, 6 * D], F32, "modD")
        for t in range(2):
            dst = G.modD.ap[t, :].rearrange("(j p) -> p j", p=128)
            P.dma(G.modD.v(dst), G.modT[:, :, t], allow_slow_non_contiguous=True)


def ln_tile_to_hT(P, G, xt, hT_dst, t_idx, which_sh, which_sc):
    st = P.sb([128, 2, 6], F32, "bnst")
    mv = P.sb([128, 2], F32, "mv")
    for hh in range(2):
        P.bn_stats(st[:, hh, :], xt[:, hh * 512:(hh + 1) * 512])
    P.bn_aggr(mv, st)
    rs = P.sb([128, 1], F32, "rs")
    rstd_from_ms(P, rs, mv[:, 1:2], 1)
    xn = P.sb([128, D], BF16, "xn")
    P.ts(xn, xt, mv[:, 0:1], rs, op0=ALU.subtract, op1=ALU.mult)
    ps = P.ps()
    psb = ps.v(ps.ap.bitcast(BF16))
    for kc in range(8):
        P.transpose(psb[:, kc * 128:(kc + 1) * 128], xn[:, kc * 128:(kc + 1) * 128], G.ident)
    for kc in range(8):
        P.act(hT_dst[:, kc, :], psb[:, kc * 128:(kc + 1) * 128], AF.Identity,
              bias=G.modT[:, which_sh * 8 + kc, t_idx:t_idx + 1], scale=G.modT[:, which_sc * 8 + kc, t_idx:t_idx + 1])


_CAST = {"i": 0}


def wload(P, stg, dst, src, engs=("pool", "dve", "act")):
    shape = list(dst.ap.shape)[1:]
    n = 1
    for v in shape:
        n *= v
    st = stg()
    sv = st.ap[0:dst.ap.shape[0], 0:n]
    if len(shape) == 2:
        sv = sv.rearrange("p (a b) -> p a b", a=shape[0])
    svt = st.v(sv)
    P.dma(svt, src)
    e = engs[_CAST["i"] % len(engs)]
    _CAST["i"] += 1
    P.copy(dst, svt, eng=e)


def mmg(P, items, K):
    for k in range(K):
        for (out, lf, rf) in items:
            P.mm(out, lf(k), rf(k), start=(k == 0), stop=(k == K - 1))


def make_ln_pools(P, nb=2):
    R = Ctx()
    R.xt = Rot(P, [128, D], F32, "xt", nb)
    R.st = Rot(P, [128, 2, 6], F32, "bnst", nb)
    R.mv = Rot(P, [128, 2], F32, "mv", nb)
    R.rs = Rot(P, [128, 1], F32, "rs", nb)
    R.xn = Rot(P, [128, D], BF16, "xn", nb)
    return R


def ln_part1(P, G, R, src_dram):
    xt = R.xt()
    P.dma(xt, src_dram)
    st = R.st(); mv = R.mv(); rs = R.rs(); xn = R.xn()
    for hh in range(2):
        P.bn_stats(st[:, hh, :], xt[:, hh * 512:(hh + 1) * 512])
    P.bn_aggr(mv, st)
    rstd_from_ms(P, rs, mv[:, 1:2], 1)
    P.ts(xn, xt, mv[:, 0:1], rs, op0=ALU.subtract, op1=ALU.mult)
    return xn


def ln_part2(P, G, xn, hT_dst, t_idx, which_sh, which_sc):
    ps = P.ps()
    psb = ps.v(ps.ap.bitcast(BF16))
    for kc in range(8):
        P.transpose(psb[:, kc * 128:(kc + 1) * 128], xn[:, kc * 128:(kc + 1) * 128], G.ident)
    for kc in range(8):
        bias = G.modT[:, which_sh * 8 + kc, t_idx:t_idx + 1]
        scale = G.modT[:, which_sc * 8 + kc, t_idx:t_idx + 1]
        if kc % 2 == 0:
            P.act(hT_dst[:, kc, :], psb[:, kc * 128:(kc + 1) * 128], AF.Identity, bias=bias, scale=scale)
        else:
            P.ts(hT_dst[:, kc, :], psb[:, kc * 128:(kc + 1) * 128], scale, bias, op0=ALU.mult, op1=ALU.add)


def ln_tile_to_hT2(P, G, R, src_dram, hT_dst, t_idx, which_sh, which_sc):
    xn = ln_part1(P, G, R, src_dram)
    ln_part2(P, G, xn, hT_dst, t_idx, which_sh, which_sc)


def rope_apply(P, dst, src_bf, perm, cos, sin, tmp, n, rows=128, psfn=None):
    ps = (psfn or P.ps)()
    P.mm(ps[0:rows, 0:n], perm, src_bf)
    P.tt(tmp, src_bf, cos, ALU.mult)
    P.tt(dst, ps[0:rows, 0:n], sin, ALU.mult)
    P.tt(dst, dst, tmp, ALU.add)


def alloc_persist(P, G):
    G.kT = P.sb([128, NK], BF16, "kT")
    G.Vg = P.sb([128, NKT, 2, 128], BF16, "Vg")
    G.ckvT = P.sb([128, 2, NK], BF16, "ckvT")
    G.krT = P.sb([32, NK], BF16, "krT")


def stage_A(P, I, G):
    P.memset(G.Vg[:, :, :, 64:128], 1.0)
    with P.scope():
        w_st = P.sb([128, 8, NST], BF16, "w_st")
        with P.scope():
            stg = Rot(P, [128, NST], F32, "stg", 2)
            for kc in range(8):
                wload(P, stg, w_st[:, kc, :], dT(I["w_in"].ap[kc * 128:(kc + 1) * 128, 0:NST], "w_in"))
        kg = P.sb([128, 1], F32, "kg")
        for r in range(2):
            P.dma(kg[r * 64:(r + 1) * 64, :], dT(I["a_k_gain"].ap.rearrange("(p o) -> p o", o=1), "akg"))
        cg = P.sb([128, 2], F32, "cg")
        P.dma(cg, dT(I["c_kv_a_gain"].ap.rearrange("(j p) -> p j", p=128), "ckg"), allow_slow_non_contiguous=True)
        R = make_ln_pools(P, 3)
        hTs = Rot(P, [128, 8, 512], BF16, "hT", 2)
        sq = Rot(P, [128, 512], BF16, "sq", 2)
        rst = Rot(P, [128, 512], F32, "rst", 1)
        knb = Rot(P, [128, 512], BF16, "knb", 2)
        tmp = Rot(P, [128, 512], F32, "tmp", 1)
        cosb = Rot(P, [128, 512], F32, "cosb", 1)
        sinb = Rot(P, [128, 512], F32, "sinb", 1)
        cos32 = Rot(P, [32, 512], BF16, "cos32", 1)
        sin32 = Rot(P, [32, 512], BF16, "sin32", 1)
        blocks = [(0, 2, True)] + [(2 + 4 * i, 4, False) for i in range(8)]
        alltiles = [(t0 + ti, is_ctx) for (t0, nt, is_ctx) in blocks for ti in range(nt)]

        def a_p1(q):
            t, is_ctx = alltiles[q]
            return ln_part1(P, G, R, I["ctx"].rows(t * 128) if is_ctx else I["x_all"].rows((t - 2) * 128))
        qi = 0
        xn_cur = a_p1(0)
        for (t0, nt, is_ctx) in blocks:
            n = nt * 128
            c0 = t0 * 128
            hT = hTs()
            for ti in range(nt):
                xn_nxt = a_p1(qi + 1) if qi + 1 < len(alltiles) else None
                ln_part2(P, G, xn_cur, hT[:, :, ti * 128:(ti + 1) * 128], 1 if is_ctx else 0, 0, 1)
                xn_cur = xn_nxt
                qi += 1
            if not is_ctx:
                lc = c0 - CTX
                cb, sb_, c32, s32 = cosb(), sinb(), cos32(), sin32()
                P.dma(cb[:, 0:n], dT(I["ropek_cos"].ap[:, lc:lc + n], "rc"))
                P.dma(sb_[:, 0:n], dT(I["ropek_sin"].ap[:, lc:lc + n], "rs"))
                P.dma(c32[:, 0:n], dT(I["rope32k_cos"].ap[:, lc:lc + n], "rc32"), eng="pool")
                P.dma(s32[:, 0:n], dT(I["rope32k_sin"].ap[:, lc:lc + n], "rs32"), eng="pool")
            pk = P.ps(); pc = [P.ps(), P.ps()]; pr = P.ps()
            mmg(P, [(pk[:, 0:n], lambda k: w_st[:, k, OFF_AK:OFF_AK + 128], lambda k: hT[:, k, 0:n]),
                    (pc[0][:, 0:n], lambda k: w_st[:, k, OFF_CKV:OFF_CKV + 128], lambda k: hT[:, k, 0:n]),
                    (pc[1][:, 0:n], lambda k: w_st[:, k, OFF_CKV + 128:OFF_CKV + 256], lambda k: hT[:, k, 0:n]),
                    (pr[0:32, 0:n], lambda k: w_st[:, k, OFF_CKR:OFF_CKR + 32], lambda k: hT[:, k, 0:n])], 8)
            s = sq()
            P.act(s[:, 0:n], pk[:, 0:n], AF.Square)
            pm = P.ps()
            P.mm(pm[:, 0:n], G.blk64, s[:, 0:n])
            rs = rst()
            rstd_from_ms(P, rs[:, 0:n], pm[:, 0:n], n)
            kn = knb()
            P.stt(kn[:, 0:n], pk[:, 0:n], kg[:, 0:1], rs[:, 0:n], ALU.mult, ALU.mult)
            if is_ctx:
                P.copy(G.kT[:, c0:c0 + n], kn[:, 0:n])
            else:
                rope_apply(P, G.kT[:, c0:c0 + n], kn[:, 0:n], G.perm64, cb[:, 0:n], sb_[:, 0:n], tmp()[:, 0:n], n)
            ss = [sq(), sq()]
            for j in range(2):
                P.act(ss[j][:, 0:n], pc[j][:, 0:n], AF.Square)
            pm = P.ps()
            for j in range(2):
                P.mm(pm[:, 0:n], G.ones, ss[j][:, 0:n], start=(j == 0), stop=(j == 1))
            rs = rst()
            P.ts(rs[:, 0:n], pm[:, 0:n], 1.0 / 256, EPS, op0=ALU.mult, op1=ALU.add)
            P.act(rs[:, 0:n], rs[:, 0:n], AF.Sqrt)
            P.recip(rs[:, 0:n], rs[:, 0:n])
            for j in range(2):
                P.stt(G.ckvT[:, j, c0:c0 + n], pc[j][:, 0:n], cg[:, j:j + 1], rs[:, 0:n], ALU.mult, ALU.mult)
            if is_ctx:
                P.copy(G.krT[:, c0:c0 + n], pr[0:32, 0:n])
            else:
                kr = knb()
                P.copy(kr[0:32, 0:n], pr[0:32, 0:n])
                rope_apply(P, G.krT[:, c0:c0 + n], kr[0:32, 0:n], G.perm32, c32[:, 0:n], s32[:, 0:n], tmp()[0:32, 0:n], n, rows=32)
            pvs = [P.ps() for _ in range(nt)]
            mmg(P, [(pvs[ti][:, 0:128], (lambda k, ti=ti: hT[:, k, ti * 128:(ti + 1) * 128]),
                     lambda k: w_st[:, k, OFF_AV:OFF_AV + 128]) for ti in range(nt)], 8)
            for ti in range(nt):
                pv = pvs[ti]
                P.copy(G.Vg[:, t0 + ti, :, 0:64], pv.v(pv.ap[:, 0:128].rearrange("p (a b) -> p a b", a=2)), eng="act")
            pus = [P.ps() for _ in range(4)]
            mmg(P, [(pus[j][:, 0:n], (lambda k, j=j: w_st[:, k, OFF_U + j * 128:OFF_U + (j + 1) * 128]),
                     lambda k: hT[:, k, 0:n]) for j in range(4)], 8)
            for j in range(4):
                dst = G.uT.v(G.uT.ap[:, j, :, c0 // 8:(c0 + n) // 8].rearrange("p i c -> p c i"))
                src = pus[j].v(pus[j].ap[:, 0:n].rearrange("p (c i) -> p c i", i=8))
                P.copy(dst, src, eng=("act" if j % 2 else "dve"))


def own_blocks(last):
    bl = [(i * 512, 512, False, i * 512) for i in range(4)]
    if not last:
        bl.append((LH, 256, True, 0))
    return bl


def stage_B1(P, I, G, last):
    NQ = LH + (0 if last else CTX)
    G.NQ = NQ
    G.qg = P.sb([128, 4, NQ], BF16, "qg")
    G.qm = P.sb([96, 8, NQ], BF16, "qm")
    G.gD = P.dram([3 * D, NQ], BF16, "gD")
    with P.scope():
        hT = P.sb([128, 8, NQ], BF16, "hTall")
        with P.scope():
            R = make_ln_pools(P, 4)
            tiles = [(c0 + ti * 128, r0 + ti * 128, is_ctx) for (c0, n, is_ctx, r0) in own_blocks(last) for ti in range(n // 128)]

            def p1(q):
                col, row, is_ctx = tiles[q]
                return ln_part1(P, G, R, (I["ctx"] if is_ctx else I["x_own"]).rows(row))
            xn_cur = p1(0)
            for q in range(len(tiles)):
                xn_nxt = p1(q + 1) if q + 1 < len(tiles) else None
                col, row, is_ctx = tiles[q]
                ln_part2(P, G, xn_cur, hT[:, :, col:col + 128], 1 if is_ctx else 0, 0, 1)
                xn_cur = xn_nxt
        qgain = P.sb([128, 1], F32, "qgain")
        for r in range(2):
            P.dma(qgain[r * 64:(r + 1) * 64, :], dT(I["a_q_gain"].ap.rearrange("(p o) -> p o", o=1), "aqg"))
        cqg = P.sb([128, 6], F32, "cqg")
        P.dma(cqg, dT(I["c_q_a_gain"].ap.rearrange("(j p) -> p j", p=128), "cqg"), allow_slow_non_contiguous=True)
        perm32h = P.sb([96, 32], BF16, "perm32h")
        P.dma(perm32h[64:96, :], I["perm32"], eng="pool")
        cosqR = Rot(P, [128, 512], F32, "cosq", 2)
        sinqR = Rot(P, [128, 512], F32, "sinq", 2)
        cos32R = Rot(P, [96, 512], F32, "cos32q", 2)
        sin32R = Rot(P, [96, 512], F32, "sin32q", 2)
        sq = Rot(P, [128, 512], BF16, "sq", 6)
        rst = Rot(P, [128, 512], F32, "rst", 2)
        knb = Rot(P, [128, 512], BF16, "knb", 2)
        tmp = Rot(P, [128, 512], F32, "tmp", 2)
        with P.scope():
            wq = P.sb([128, 8, 4, 128], BF16, "wq")
            stg = Rot(P, [128, 768], F32, "stg", 2)
            for kc in range(8):
                for hf in range(2):
                    src = I["w_in"].ap[kc * 128:(kc + 1) * 128, OFF_AQ + hf * 256:OFF_AQ + (hf + 1) * 256].rearrange("p (a b) -> p a b", a=4)
                    wload(P, stg, wq[:, kc, :, hf * 64:(hf + 1) * 64], dT(src, "w_in"))
            for (c0, n, is_ctx, r0) in own_blocks(last):
                if not is_ctx:
                    cosq = cosqR(); sinq = sinqR()
                    P.dma(cosq, dT(I["ropeq_cos"].ap[:, c0:c0 + n], "rqc"))
                    P.dma(sinq, dT(I["ropeq_sin"].ap[:, c0:c0 + n], "rqs"))
                pks = [P.ps() for _ in range(4)]
                mmg(P, [(pks[hd][:, 0:n], (lambda k, hd=hd: wq[:, k, hd, :]), lambda k: hT[:, k, c0:c0 + n]) for hd in range(4)], 8)
                for hd in range(4):
                    pk = pks[hd]
                    s = sq()
                    P.act(s[:, 0:n], pk[:, 0:n], AF.Square)
                    pm = P.ps_acc()
                    P.mm(pm[:, 0:n], G.blk64, s[:, 0:n])
                    rs = rst()
                    rstd_from_ms(P, rs[:, 0:n], pm[:, 0:n], n)
                    if is_ctx:
                        P.stt(G.qg[:, hd, c0:c0 + n], pk[:, 0:n], qgain[:, 0:1], rs[:, 0:n], ALU.mult, ALU.mult)
                    else:
                        kn = knb()
                        P.stt(kn[:, 0:n], pk[:, 0:n], qgain[:, 0:1], rs[:, 0:n], ALU.mult, ALU.mult)
                        rope_apply(P, G.qg[:, hd, c0:c0 + n], kn[:, 0:n], G.perm64, cosq[:, 0:n], sinq[:, 0:n],
                                   tmp()[:, 0:n], n, psfn=P.ps_acc)
        with P.scope():
            wc = P.sb([128, 8, 768], BF16, "wc")
            stg = Rot(P, [128, 768], F32, "stg", 1)
            for kc in range(8):
                wload(P, stg, wc[:, kc, :], dT(I["w_in"].ap[kc * 128:(kc + 1) * 128, OFF_CQ:OFF_CQ + 768], "w_in"))
            wqb = P.sb([128, 6, 768], BF16, "wqb")
            for j in range(6):
                wload(P, stg, wqb[:, j, :], dT(I["c_w_qb"].ap[j * 128:(j + 1) * 128, :], "wqb"))
            cqn = P.sb([128, 6, 512], BF16, "cqn")
            qrb = Rot(P, [96, 512], BF16, "qrb", 2)
            for (c0, n, is_ctx, r0) in own_blocks(last):
                if not is_ctx:
                    cos32 = cos32R(); sin32 = sin32R()
                    P.dma(cos32[64:96, :], dT(I["rope32q_cos"].ap[:, c0:c0 + n], "rqc32"))
                    P.dma(sin32[64:96, :], dT(I["rope32q_sin"].ap[:, c0:c0 + n], "rqs32"))
                pcs = [P.ps() for _ in range(6)]
                mmg(P, [(pcs[j][:, 0:n], (lambda k, j=j: wc[:, k, j * 128:(j + 1) * 128]), lambda k: hT[:, k, c0:c0 + n]) for j in range(6)], 8)
                sqs = []
                for j in range(6):
                    s = sq()
                    P.act(s[:, 0:n], pcs[j][:, 0:n], AF.Square)
                    sqs.append(s)
                pm = P.ps_acc()
                for j in range(6):
                    P.mm(pm[:, 0:n], G.ones, sqs[j][:, 0:n], start=(j == 0), stop=(j == 5))
                rs = rst()
                P.ts(rs[:, 0:n], pm[:, 0:n], 1.0 / 768, EPS, op0=ALU.mult, op1=ALU.add)
                P.act(rs[:, 0:n], rs[:, 0:n], AF.Sqrt)
                P.recip(rs[:, 0:n], rs[:, 0:n])
                for j in range(6):
                    P.stt(cqn[:, j, 0:n], pcs[j][:, 0:n], cqg[:, j:j + 1], rs[:, 0:n], ALU.mult, ALU.mult)
                for hg in range(2):
                    pqs = [P.ps() for _ in range(4)]
                    mmg(P, [(pqs[i][0:96, 0:n], (lambda k, h=hg * 4 + i: wqb[:, k, h * 96:(h + 1) * 96]), lambda k: cqn[:, k, 0:n]) for i in range(4)], 6)
                    for i in range(4):
                        h = hg * 4 + i
                        pq = pqs[i]
                        if is_ctx:
                            P.copy(G.qm[:, h, c0:c0 + n], pq[0:96, 0:n], eng="act")
                        else:
                            P.copy(G.qm[0:64, h, c0:c0 + n], pq[0:64, 0:n], eng="act")
                            qr = qrb()
                            P.copy(qr[64:96, 0:n], pq[64:96, 0:n])
                            pr = P.ps_acc()
                            P.mm(pr[64:96, 0:n], perm32h[64:96, :], qr[64:96, 0:n])
                            t = tmp()
                            P.tt(t[64:96, 0:n], qr[64:96, 0:n], cos32[64:96, 0:n], ALU.mult)
                            t2 = tmp()
                            P.tt(t2[64:96, 0:n], pr[64:96, 0:n], sin32[64:96, 0:n], ALU.mult)
                            P.tt(G.qm[64:96, h, c0:c0 + n], t[64:96, 0:n], t2[64:96, 0:n], ALU.add)
        with P.scope():
            wg = Rot(P, [128, 8, 512], BF16, "wg", 2)
            stg = Rot(P, [128, 512], F32, "stg", 3)
            gb = Rot(P, [128, 512], BF16, "gb", 8)
            def load_g(gi):
                w = wg()
                for kc in range(8):
                    wload(P, stg, w[:, kc, :], dT(I["w_in"].ap[kc * 128:(kc + 1) * 128, OFF_GATE + gi * 512:OFF_GATE + (gi + 1) * 512], "w_in"), engs=("dve", "act"))
                return w
            w_nxt = load_g(0)
            for gi in range(6):
                w = w_nxt
                w_nxt = load_g(gi + 1) if gi + 1 < 6 else None
                for (c0, n, is_ctx, r0) in own_blocks(last):
                    pgs = [P.ps() for _ in range(4)]
                    mmg(P, [(pgs[oc][:, 0:n], (lambda k, oc=oc: w[:, k, oc * 128:(oc + 1) * 128]), lambda k: hT[:, k, c0:c0 + n]) for oc in range(4)], 8)
                    for oc in range(4):
                        g = gb()
                        P.act(g[:, 0:n], pgs[oc][:, 0:n], AF.Sigmoid)
                        row = (gi * 4 + oc) * 128
                        P.dma(G.gD.v(G.gD.ap[row:row + 128, c0:c0 + n]), g[:, 0:n])


def run_attn(P, chains, pT, scale):
    LA = 2
    nkt = chains[0][1]
    pls = [dict() for _ in chains]
    for kt in range(nkt + LA):
        if kt < nkt:
            for ci, (po, _, n, slf, srhs, vlf) in enumerate(chains):
                pss = P.ps()
                P.mm(pss[:, 0:n], slf(kt), srhs)
                p = pT()
                P.act(p[:, 0:n], pss[:, 0:n], AF.Exp, scale=scale)
                pls[ci][kt] = p
        jj = kt - LA
        if jj >= 0:
            for ci, (po, _, n, slf, srhs, vlf) in enumerate(chains):
                P.mm(po[:, 0:n], vlf(jj), pls[ci].pop(jj)[:, 0:n], start=(jj == 0), stop=(jj == nkt - 1))


def block_groups(last):
    bl = own_blocks(last)
    groups = [bl[0:2], bl[2:4]]
    if not last:
        groups.append(bl[4:5])
    return groups


def attn_finish(P, po, n, rec, yo, dst):
    r = rec()
    P.recip(r[64:128, 0:n], po[64:128, 0:n])
    y = yo()
    P.tt(y[:, 0:n], po[0:64, 0:n], r[64:128, 0:n], ALU.mult)
    P.dma(dst, y[:, 0:n])


def stage_B2(P, I, G, last):
    NQ = G.NQ
    G.yaD = P.dram([512, NQ], BF16, "yaD")
    with P.scope():
        pT = Rot(P, [128, 512], BF16, "pT", 8)
        rec = Rot(P, [128, 512], F32, "rec", 2)
        yo = Rot(P, [64, 512], BF16, "yo", 2)
        kTp = P.sb([128, 2, NK], BF16, "kTp")
        P.memset(kTp, 0.0)
        P.copy(kTp[0:64, 0, :], G.kT[0:64, :], eng="pool")
        P.copy(kTp[64:128, 1, :], G.kT[64:128, :], eng="dve")
        for hd in range(4):
            for kvh in range(2):
                head = hd + 4 * kvh
                for grp in block_groups(last):
                    chains = []
                    for (c0, n, is_ctx, r0) in grp:
                        nkt = 2 if is_ctx else NKT
                        chains.append((P.ps_acc(), nkt, n, (lambda kt: kTp[:, kvh, kt * 128:(kt + 1) * 128]),
                                       G.qg[:, hd, c0:c0 + n], (lambda kt: G.Vg[:, kt, kvh, :])))
                    run_attn(P, chains, pT, 0.125)
                    for (po, _, n, _, _, _), (c0, _, _, _) in zip(chains, grp):
                        attn_finish(P, po, n, rec, yo, G.yaD.v(G.yaD.ap[head * 64:(head + 1) * 64, c0:c0 + n]))


def stage_B3(P, I, G, last):
    NQ = G.NQ
    G.ycD = P.dram([512, NQ], BF16, "ycD")
    with P.scope():
        wkv = P.sb([128, 2, 1024], BF16, "wkv")
        stg = Rot(P, [128, 1024], F32, "stg", 1)
        for j in range(2):
            wload(P, stg, wkv[:, j, :], dT(I["c_w_kvb"].ap[j * 128:(j + 1) * 128, :], "wkvb"))
        Kh = Rot(P, [96, NK], BF16, "Kh", 2)
        Vh = [P.sb([128, NKT, 128], BF16, f"Vh{i}") for i in range(2)]
        for v in Vh:
            P.memset(v[:, :, 64:128], 1.0)
        pT = Rot(P, [128, 512], BF16, "pT", 8)
        rec = Rot(P, [128, 512], F32, "rec", 2)
        yo = Rot(P, [64, 512], BF16, "yo", 2)
        scale = 96 ** -0.5
        for h in range(8):
            K = Kh(); V = Vh[h % 2]
            cbs = [(cb * 512, min(512, NK - cb * 512)) for cb in range(9)]
            for g0 in range(0, 9, 3):
                grp = cbs[g0:g0 + 3]
                pks = [P.ps() for _ in grp]
                mmg(P, [(pks[i][0:64, 0:n], lambda k: wkv[:, k, h * 128:h * 128 + 64], (lambda k, k0=k0, n=n: G.ckvT[:, k, k0:k0 + n]))
                        for i, (k0, n) in enumerate(grp)], 2)
                for i, (k0, n) in enumerate(grp):
                    P.copy(K[0:64, k0:k0 + n], pks[i][0:64, 0:n], eng=("act" if i % 2 else "dve"))
            P.copy(K[64:96, :], G.krT[0:32, :], eng="pool")
            for g0 in range(0, NKT, 4):
                kts = list(range(g0, min(g0 + 4, NKT)))
                pvs = [P.ps() for _ in kts]
                mmg(P, [(pvs[i][:, 0:64], (lambda k, kt=kt: G.ckvT[:, k, kt * 128:(kt + 1) * 128]),
                         lambda k: wkv[:, k, h * 128 + 64:h * 128 + 128]) for i, kt in enumerate(kts)], 2)
                for i, kt in enumerate(kts):
                    P.copy(V[:, kt, 0:64], pvs[i][:, 0:64], eng=("act" if kt % 2 else "dve"))
            for grp in block_groups(last):
                chains = []
                for (c0, n, is_ctx, r0) in grp:
                    nkt = 2 if is_ctx else NKT
                    chains.append((P.ps_acc(), nkt, n, (lambda kt: K[0:96, kt * 128:(kt + 1) * 128]),
                                   G.qm[0:96, h, c0:c0 + n], (lambda kt: V[:, kt, :])))
                run_attn(P, chains, pT, scale)
                for (po, _, n, _, _, _), (c0, _, _, _) in zip(chains, grp):
                    attn_finish(P, po, n, rec, yo, G.ycD.v(G.ycD.ap[h * 64:(h + 1) * 64, c0:c0 + n]))


def bcast_load(P, dst, src_ap_1d):
    P.dma(dst, dT(src_ap_1d.partition_broadcast(128), "bc"))


def stage_B4a(P, I, G, last):
    NQ = G.NQ
    G.xmidD = P.dram([NQ, D], F32, "xmidD")
    G.h2D = P.dram([D, NQ], BF16, "h2D")
    with P.scope():
        wglu = P.sb([128, 4, 1024], BF16, "wglu")
        wb = [P.sb([128, 4, 1024], BF16, f"wb{i}") for i in range(3)]
        wout = P.sb([128, 8, 1024], BF16, "wout")
        with P.scope():
            stg = Rot(P, [128, 1024], F32, "stg", 3)
            for j in range(4):
                wload(P, stg, wglu[:, j, :], dT(I["s5_w_glu"].ap[j * 128:(j + 1) * 128, :], "w"))
                for i, nm in enumerate(["w_branch_a", "w_branch_s5", "w_branch_c"]):
                    wload(P, stg, wb[i][:, j, :], dT(I[nm].ap[j * 128:(j + 1) * 128, :], "w"))
            for kc in range(8):
                wload(P, stg, wout[:, kc, :], dT(I["w_out"].ap[kc * 128:(kc + 1) * 128, :], "w"))
        g1b = [P.sb([128, D], F32, f"g1b{t}") for t in range(2)]
        for t in range(2):
            P.dma(g1b[t], dT(G.modD.ap[t, 2 * D:3 * D].partition_broadcast(128), "modD_r"))
        lng = P.sb([128, D], F32, "lng"); bcast_load(P, lng, I["ln1_g"].ap)
        lnb = P.sb([128, D], F32, "lnb"); bcast_load(P, lnb, I["ln1_b"].ap)
        class _InR:
            def __init__(self):
                gts = P.sb([128, 24, 512], BF16, "gates")
                self.sets = [(P.sb([128, 4, 512], BF16, f"yT{i}"), P.sb([128, 4, 512], BF16, f"sa{i}"),
                              P.sb([128, 4, 512], BF16, f"sc{i}"), gts) for i in range(2)]
                self.i = 0

            def __call__(self):
                r = self.sets[self.i % 2]
                self.i += 1
                return r
        inR = _InR()
        srcs = [None, P.sb([128, 4, 512], BF16, "src1"), None]
        t1 = P.sb([128, 4, 512], F32, "t1")
        sg = Rot(P, [128, 512], F32, "sg", 2)
        acc = Rot(P, [128, 512], F32, "acc", 2)
        tmpm = Rot(P, [128, 512], F32, "tmpm", 2)
        merged = P.sb([128, 8, 512], BF16, "merged")
        xtR = Rot(P, [128, D], F32, "xt", 2)
        tsR = Rot(P, [128, D], F32, "tsum", 3)
        xmR = Rot(P, [128, D], F32, "xm", 2)
        stR = Rot(P, [128, 2, 6], F32, "bnst", 2); mvR = Rot(P, [128, 2], F32, "mv", 2); rsR = Rot(P, [128, 1], F32, "rs", 2)
        xnR = Rot(P, [128, D], BF16, "xn", 2)
        h2T = P.sb([128, 8, 512], BF16, "h2T")
        def load_blk(blk):
            (c0, n, is_ctx, r0) = blk
            bufs = inR()
            yT_, sa_, sc_, gates_ = bufs
            for j in range(4):
                P.dma(yT_[:, j, 0:n], G.ysD.v(G.ysD.ap[j * 128:(j + 1) * 128, c0:c0 + n]))
                P.dma(sa_[:, j, 0:n], G.yaD.v(G.yaD.ap[j * 128:(j + 1) * 128, c0:c0 + n]))
                P.dma(sc_[:, j, 0:n], G.ycD.v(G.ycD.ap[j * 128:(j + 1) * 128, c0:c0 + n]))
            return bufs
        blks_ = own_blocks(last)
        nxt_bufs = load_blk(blks_[0])
        for bi_, (c0, n, is_ctx, r0) in enumerate(blks_):
            tix = 1 if is_ctx else 0
            yT, srcs[0], srcs[2], gates = nxt_bufs
            for gi in range(24):
                P.dma(gates[:, gi, 0:n], G.gD.v(G.gD.ap[gi * 128:(gi + 1) * 128, c0:c0 + n]))
            nxt_bufs = load_blk(blks_[bi_ + 1]) if bi_ + 1 < len(blks_) else None
            P.tt(t1[:, :, 0:n], yT[:, :, 0:n], yT[:, :, 0:n], ALU.mult)
            P.ts(t1[:, :, 0:n], t1[:, :, 0:n], 0.044715, 1.0, op0=ALU.mult, op1=ALU.add)
            P.tt(t1[:, :, 0:n], t1[:, :, 0:n], yT[:, :, 0:n], ALU.mult)
            P.act(t1[:, :, 0:n], t1[:, :, 0:n], AF.Sigmoid, scale=1.5957691216)
            ge = P.sb([128, 4, 512], BF16, "ge") if c0 == 0 else ge
            P.tt(ge[:, :, 0:n], t1[:, :, 0:n], yT[:, :, 0:n], ALU.mult)
            for op_ in range(2):
                pa = [P.ps(), P.ps()]; pg = [P.ps(), P.ps()]
                items = []
                for q in range(2):
                    oc = op_ * 2 + q
                    items.append((pa[q][:, 0:n], (lambda k, oc=oc: wglu[:, k, oc * 128:(oc + 1) * 128]), lambda k: ge[:, k, 0:n]))
                    items.append((pg[q][:, 0:n], (lambda k, oc=oc: wglu[:, k, 512 + oc * 128:512 + (oc + 1) * 128]), lambda k: ge[:, k, 0:n]))
                mmg(P, items, 4)
                for q in range(2):
                    oc = op_ * 2 + q
                    s = sg()
                    P.act(s[:, 0:n], pg[q][:, 0:n], AF.Sigmoid)
                    P.tt(srcs[1][:, oc, 0:n], pa[q][:, 0:n], s[:, 0:n], ALU.mult)
            for oc in range(8):
                a = acc()
                pbs = [P.ps() for _ in range(3)]
                mmg(P, [(pbs[br][:, 0:n], (lambda k, br=br: wb[br][:, k, oc * 128:(oc + 1) * 128]), (lambda k, br=br: srcs[br][:, k, 0:n])) for br in range(3)], 4)
                for br in range(3):
                    pb = pbs[br]
                    if br == 0:
                        P.tt(a[:, 0:n], pb[:, 0:n], gates[:, br * 8 + oc, 0:n], ALU.mult)
                    else:
                        tm = tmpm()
                        P.tt(tm[:, 0:n], pb[:, 0:n], gates[:, br * 8 + oc, 0:n], ALU.mult)
                        if br == 1:
                            P.tt(a[:, 0:n], a[:, 0:n], tm[:, 0:n], ALU.add, eng="pool")
                        else:
                            P.tt(merged[:, oc, 0:n], a[:, 0:n], tm[:, 0:n], ALU.add, eng="pool")
            def mix(ti):
                xt = xtR(); ts_ = tsR()
                P.dma(xt, (I["ctx"] if is_ctx else I["x_own"]).rows(r0 + ti * 128))
                pms = [P.ps(), P.ps()]
                mmg(P, [(pms[half][:, 0:512], lambda k: merged[:, k, ti * 128:(ti + 1) * 128],
                         (lambda k, half=half: wout[:, k, half * 512:(half + 1) * 512])) for half in range(2)], 8)
                for half in range(2):
                    P.tt(ts_[:, half * 512:(half + 1) * 512], pms[half][:, 0:512], g1b[tix][:, half * 512:(half + 1) * 512], ALU.mult)
                P.stt(ts_, xt, ALPHA, ts_, ALU.mult, ALU.add)
                return ts_

            def chain(ti, ts_):
                st = stR(); mv = mvR(); rs = rsR(); xm = xmR()
                for hh in range(2):
                    P.bn_stats(st[:, hh, :], ts_[:, hh * 512:(hh + 1) * 512])
                P.bn_aggr(mv, st)
                rstd_from_ms(P, rs, mv[:, 1:2], 1)
                P.stt(xm, ts_, mv[:, 0:1], lng, ALU.subtract, ALU.mult)
                P.stt(xm, xm, rs, lnb, ALU.mult, ALU.add)
                P.dma(G.xmidD.v(G.xmidD.ap[c0 + ti * 128:c0 + (ti + 1) * 128, :]), xm)
                st = stR(); mv = mvR(); rs = rsR(); xn = xnR()
                for hh in range(2):
                    P.bn_stats(st[:, hh, :], xm[:, hh * 512:(hh + 1) * 512])
                P.bn_aggr(mv, st)
                rstd_from_ms(P, rs, mv[:, 1:2], 1)
                P.ts(xn, xm, mv[:, 0:1], rs, op0=ALU.subtract, op1=ALU.mult)
                return xn

            def xpose(ti, xn):
                ps = P.ps()
                psb = ps.v(ps.ap.bitcast(BF16))
                for kc in range(8):
                    P.transpose(psb[:, kc * 128:(kc + 1) * 128], xn[:, kc * 128:(kc + 1) * 128], G.ident)
                for kc in range(8):
                    P.act(h2T[:, kc, ti * 128:(ti + 1) * 128], psb[:, kc * 128:(kc + 1) * 128], AF.Identity,
                          bias=G.modT[:, 3 * 8 + kc, tix:tix + 1], scale=G.modT[:, 4 * 8 + kc, tix:tix + 1])
            nti = n // 128
            ts_cur = mix(0)
            for ti in range(nti):
                ts_nxt = mix(ti + 1) if ti + 1 < nti else None
                xn = chain(ti, ts_cur)
                xpose(ti, xn)
                ts_cur = ts_nxt
            for kc in range(8):
                P.dma(G.h2D.v(G.h2D.ap[kc * 128:(kc + 1) * 128, c0:c0 + n]), h2T[:, kc, 0:n])


def stage_B4b(P, I, G, last, out_own, out_ctx):
    NQ = G.NQ
    NT = NQ // 128
    with P.scope():
        g2b = [P.sb([128, D], F32, f"g2b{t}") for t in range(2)]
        for t in range(2):
            P.dma(g2b[t], dT(G.modD.ap[t, 5 * D:6 * D].partition_broadcast(128), "modD_r"))
        lng = P.sb([128, D], F32, "lng"); bcast_load(P, lng, I["ln2_g"].ap)
        lnb = P.sb([128, D], F32, "lnb"); bcast_load(P, lnb, I["ln2_b"].ap)
        h2T = P.sb([128, 8, NQ], BF16, "h2Tall")
        for kc in range(8):
            P.dma(h2T[:, kc, :], G.h2D.v(G.h2D.ap[kc * 128:(kc + 1) * 128, :]))
        tsum = P.sb([128, NT, D], F32, "tsum2")
        wuR = Rot(P, [128, 8, 512], BF16, "wu", 2)
        wdR = Rot(P, [128, 4, D], BF16, "wd", 2)
        stg = Rot(P, [128, 1024], F32, "stg", 3)
        aR = Rot(P, [128, 4, 512], BF16, "aog", 3)
        rl = Rot(P, [128, 512], BF16, "rl", 4)
        xmR = Rot(P, [128, D], F32, "xm", 2)
        stR = Rot(P, [128, 2, 6], F32, "bnst", 2); mvR = Rot(P, [128, 2], F32, "mv", 2); rsR = Rot(P, [128, 1], F32, "rs", 2)
        for og in range(8):
            wu = wuR(); wd = wdR()
            for kc in range(8):
                wload(P, stg, wu[:, kc, :], dT(I["w_up"].ap[kc * 128:(kc + 1) * 128, og * 512:(og + 1) * 512], "w"), engs=("pool", "act"))
            for oc in range(4):
                wload(P, stg, wd[:, oc, :], dT(I["w_down"].ap[og * 512 + oc * 128:og * 512 + (oc + 1) * 128, :], "w"), engs=("pool", "act"))
            blks = own_blocks(last)

            def up(blk):
                (c0, n, is_ctx, r0) = blk
                a = aR()
                pus = [P.ps() for _ in range(4)]
                mmg(P, [(pus[oc][:, 0:n], (lambda k, oc=oc: wu[:, k, oc * 128:(oc + 1) * 128]), lambda k: h2T[:, k, c0:c0 + n]) for oc in range(4)], 8)
                for oc in range(4):
                    r = rl()
                    P.act(r[:, 0:n], pus[oc][:, 0:n], AF.Relu)
                    P.tt(a[:, oc, 0:n], pus[oc][:, 0:n], r[:, 0:n], ALU.mult)
                return a

            def down(blk, a):
                (c0, n, is_ctx, r0) = blk
                combos = [(ti, half) for ti in range(n // 128) for half in range(2)]
                for g0 in range(0, len(combos), 4):
                    grp = combos[g0:g0 + 4]
                    pds = [P.ps() for _ in grp]
                    mmg(P, [(pds[i][:, 0:512], (lambda k, ti=ti: a[:, k, ti * 128:(ti + 1) * 128]),
                             (lambda k, half=half: wd[:, k, half * 512:(half + 1) * 512])) for i, (ti, half) in enumerate(grp)], 4)
                    for i, (ti, half) in enumerate(grp):
                        tile = c0 // 128 + ti
                        dst = tsum[:, tile, half * 512:(half + 1) * 512]
                        if og == 0:
                            P.copy(dst, pds[i][:, 0:512], eng="act")
                        else:
                            P.tt(dst, pds[i][:, 0:512], dst, ALU.add)
            a_cur = up(blks[0])
            for bi, blk in enumerate(blks):
                a_nxt = up(blks[bi + 1]) if bi + 1 < len(blks) else None
                down(blk, a_cur)
                a_cur = a_nxt
        for (c0, n, is_ctx, r0) in own_blocks(last):
            tix = 1 if is_ctx else 0
            for ti in range(n // 128):
                tile = c0 // 128 + ti
                xm = xmR()
                P.dma(xm, G.xmidD.v(G.xmidD.ap[c0 + ti * 128:c0 + (ti + 1) * 128, :]))
                P.tt(tsum[:, tile, :], tsum[:, tile, :], g2b[tix], ALU.mult, eng="pool")
                P.stt(tsum[:, tile, :], xm, ALPHA, tsum[:, tile, :], ALU.mult, ALU.add)
                st = stR(); mv = mvR(); rs = rsR()
                for hh in range(2):
                    P.bn_stats(st[:, hh, :], tsum[:, tile, hh * 512:(hh + 1) * 512])
                P.bn_aggr(mv, st)
                rstd_from_ms(P, rs, mv[:, 1:2], 1)
                o = xm
                P.stt(o, tsum[:, tile, :], mv[:, 0:1], lng, ALU.subtract, ALU.mult)
                P.stt(o, o, rs, lnb, ALU.mult, ALU.add)
                dst = out_ctx if is_ctx else out_own
                P.dma(dst.rows(r0 + ti * 128), o)


def bc(t, pattern):
    a = t.ap
    return t.v(bass.AP(a.tensor, a.offset, [list(a.ap[0])] + [list(p) for p in pattern]))


MAGIC = 12582912.0
TWO_PI = 6.283185307179586


def s5_load(P, I, G):
    Lt = Ctx()
    Lt.are = P.sb([128, 64], F32, "are"); Lt.aim = P.sb([128, 64], F32, "aim"); Lt.ldt = P.sb([128, 64], F32, "ldt")
    Lt.Bre = P.sb([128, 64, 16], F32, "Bre"); Lt.Bim = P.sb([128, 64, 16], F32, "Bim")
    Lt.Cre = P.sb([128, 64, 16], F32, "Cre"); Lt.Cim = P.sb([128, 64, 16], F32, "Cim")
    for hf in range(2):
        sl = slice(hf * 64, (hf + 1) * 64)
        P.dma(Lt.are[sl, :], dT(I["s5_a_re"].ap.rearrange("d g p -> p (d g)"), "a"), allow_slow_non_contiguous=True, eng="act")
        P.dma(Lt.aim[sl, :], dT(I["s5_a_im"].ap.rearrange("d g p -> p (d g)"), "a"), allow_slow_non_contiguous=True, eng="act")
        P.dma(Lt.Bre[sl], dT(I["s5_b_re"].ap.rearrange("d g p c -> p (d g) c"), "a"))
        P.dma(Lt.Bim[sl], dT(I["s5_b_im"].ap.rearrange("d g p c -> p (d g) c"), "a"))
        P.dma(Lt.Cre[sl], dT(I["s5_c_re"].ap.rearrange("d g c p -> p (d g) c"), "a"), allow_slow_non_contiguous=True, eng="act")
        P.dma(Lt.Cim[sl], dT(I["s5_c_im"].ap.rearrange("d g c p -> p (d g) c"), "a"), allow_slow_non_contiguous=True, eng="act")
    P.dma(Lt.ldt, dT(I["s5_log_dt"].ap.rearrange("d g -> (d g)").partition_broadcast(128), "a"))
    G.s5l = Lt


def s5_alloc(P, I, G):
    PR = P.sb([128, 32, 64], F32, "PR"); NPI = P.sb([128, 32, 64], F32, "NPI")
    DA = P.sb([128, 10, 64], F32, "DA"); DB = P.sb([128, 10, 64], F32, "DB")
    BX1 = P.sb([128, 64, 16], F32, "BX1"); BX2 = P.sb([128, 64, 16], F32, "BX2")
    CX1 = P.sb([128, 64, 16], F32, "CX1"); CX2 = P.sb([128, 64, 16], F32, "CX2")
    Dcol = P.sb([128, 32], F32, "Dcol")
    identF = G.ident
    swapF = P.sb([128, 128], BF16, "swapF"); P.dma(swapF, I["swap"], eng="pool")
    maskf = P.sb([128, 128], BF16, "maskf"); P.dma(maskf, I["mask_f"], eng="pool")
    maskb = P.sb([128, 128], BF16, "maskb"); P.dma(maskb, I["mask_b"], eng="pool")
    sgn = P.sb([128, 1], F32, "sgn"); P.memset(sgn[0:64, :], 1.0); P.memset(sgn[64:128, :], -1.0)
    for i in range(8):
        P.dma(Dcol[i * 16:(i + 1) * 16, :], dT(I["s5_d"].ap.rearrange("(g c) -> c g", c=16), "s5d"), allow_slow_non_contiguous=True)
    G.s5t = dict(PR=PR, NPI=NPI, DA=DA, DB=DB, BX1=BX1, BX2=BX2, CX1=CX1, CX2=CX2, Dcol=Dcol, identF=identF, swapF=swapF, maskf=maskf, maskb=maskb, sgn=sgn)


def s5_params(P, I, G):
    PR = G.s5t["PR"]
    NPI = G.s5t["NPI"]
    DA = G.s5t["DA"]
    DB = G.s5t["DB"]
    BX1 = G.s5t["BX1"]
    BX2 = G.s5t["BX2"]
    CX1 = G.s5t["CX1"]
    CX2 = G.s5t["CX2"]
    Dcol = G.s5t["Dcol"]
    identF = G.s5t["identF"]
    swapF = G.s5t["swapF"]
    maskf = G.s5t["maskf"]
    maskb = G.s5t["maskb"]
    sgn = G.s5t["sgn"]
    with P.scope():
        are, aim, ldt = G.s5l.are, G.s5l.aim, G.s5l.ldt
        dt_ = P.sb([128, 64], F32, "dt")
        P.act(dt_, ldt, AF.Exp)
        lr = P.sb([128, 64], F32, "lr"); li = P.sb([128, 64], F32, "li")
        P.tt(lr, are, dt_, ALU.mult); P.tt(li, aim, dt_, ALU.mult)
        with P.scope():
            elist = [t - 7 for t in range(16)] + [8 - t for t in range(16)]
            LR = P.sb([128, 32, 64], F32, "LR"); LI = P.sb([128, 32, 64], F32, "LI")
            for idx, e in enumerate(elist):
                P.ts(LR[:, idx, :], lr, float(e), None, op0=ALU.mult)
                P.ts(LI[:, idx, :], li, float(e), None, op0=ALU.mult, eng="pool")
            mag = P.sb([128, 32, 64], F32, "mag")
            P.act(mag, LR, AF.Exp)
            rr = P.sb([128, 32, 64], F32, "rr"); kk = P.sb([128, 32, 64], F32, "kk")

            def sin_of(dst, ang_t, shift):
                P.ts(rr, ang_t, 1.0 / TWO_PI, shift / TWO_PI, op0=ALU.mult, op1=ALU.add)
                P.ts(kk, rr, MAGIC, None, op0=ALU.add)
                P.ts(kk, kk, MAGIC, None, op0=ALU.subtract)
                P.tt(rr, rr, kk, ALU.subtract)
                P.ts(rr, rr, TWO_PI, None, op0=ALU.mult)
                P.ts(rr, rr, 3.1415925, -3.1415925, op0=ALU.min, op1=ALU.max)
                P.act(dst, rr, AF.Sin)
            sn = LR
            sin_of(sn, LI, 0.0)
            P.stt(NPI, mag, -1.0, sn, ALU.mult, ALU.mult)
            sin_of(sn, LI, TWO_PI / 4)
            P.tt(PR, mag, sn, ALU.mult)
        cr_ = P.sb([128, 64], F32, "cr"); ci_ = P.sb([128, 64], F32, "ci")
        t1 = P.sb([128, 64], F32, "t1"); t2 = P.sb([128, 64], F32, "t2")
        P.copy(DA[:, 0, :], PR[:, 15, :])
        P.ts(DB[:, 0, :], NPI[:, 15, :], -1.0, None, op0=ALU.mult)
        for m in range(1, 10):
            P.tt(t1, DA[:, m - 1, :], DA[:, m - 1, :], ALU.mult)
            P.tt(t2, DB[:, m - 1, :], DB[:, m - 1, :], ALU.mult)
            P.tt(DA[:, m, :], t1, t2, ALU.subtract)
            P.stt(DB[:, m, :], DA[:, m - 1, :], 2.0, DB[:, m - 1, :], ALU.mult, ALU.mult)
        P.ts(DB, DB, sgn[:, 0:1], None, op0=ALU.mult)
        den = P.sb([128, 64], F32, "den"); nr = P.sb([128, 64], F32, "nr"); abi = P.sb([128, 64], F32, "abi")
        P.tt(t1, are, are, ALU.mult); P.tt(t2, aim, aim, ALU.mult); P.tt(den, t1, t2, ALU.add); P.recip(den, den)
        P.ts(nr, PR[:, 8, :], -1.0, None, op0=ALU.add)
        P.ts(abi, NPI[:, 8, :], -1.0, None, op0=ALU.mult)
        P.tt(t1, nr, are, ALU.mult); P.tt(t2, abi, aim, ALU.mult); P.tt(cr_, t1, t2, ALU.add); P.tt(cr_, cr_, den, ALU.mult)
        P.tt(t1, abi, are, ALU.mult); P.tt(t2, nr, aim, ALU.mult); P.tt(ci_, t1, t2, ALU.subtract); P.tt(ci_, ci_, den, ALU.mult)
        crb = bc(cr_, [[1, 64], [0, 16]]); cib = bc(ci_, [[1, 64], [0, 16]])
        Bre, Bim, Cre, Cim = G.s5l.Bre, G.s5l.Bim, G.s5l.Cre, G.s5l.Cim
        bbr = P.sb([128, 64, 16], F32, "bbr"); bbi = P.sb([128, 64, 16], F32, "bbi"); t3 = P.sb([128, 64, 16], F32, "t3")
        P.tt(bbr, Bre, crb, ALU.mult); P.tt(t3, Bim, cib, ALU.mult); P.tt(bbr, bbr, t3, ALU.subtract)
        P.tt(bbi, Bim, crb, ALU.mult); P.tt(t3, Bre, cib, ALU.mult); P.tt(bbi, bbi, t3, ALU.add)
        P.copy(BX1[0:64], bbr[0:64]); P.copy(BX1[64:128], bbi[64:128])
        P.copy(BX2[0:64], bbi[0:64]); P.ts(BX2[64:128], bbr[64:128], -1.0, None, op0=ALU.mult)
        P.copy(CX1[0:64], Cre[0:64]); P.ts(CX1[64:128], Cim[64:128], -1.0, None, op0=ALU.mult)
        P.copy(CX2[0:64], Cim[0:64]); P.copy(CX2[64:128], Cre[64:128])


def stage_S5(P, I, G, last):
    NQ = LH + (0 if last else CTX)
    G.ysD = P.dram([512, NQ], BF16, "ysD")
    with P.scope():
        PR = G.s5t["PR"]
        NPI = G.s5t["NPI"]
        DA = G.s5t["DA"]
        DB = G.s5t["DB"]
        BX1 = G.s5t["BX1"]
        BX2 = G.s5t["BX2"]
        CX1 = G.s5t["CX1"]
        CX2 = G.s5t["CX2"]
        Dcol = G.s5t["Dcol"]
        identF = G.s5t["identF"]
        swapF = G.s5t["swapF"]
        maskf = G.s5t["maskf"]
        maskb = G.s5t["maskb"]
        sgn = G.s5t["sgn"]
        Sel = P.sb([128, 64, 128], BF16, "Sel"); SelT = P.sb([128, 64, 128], BF16, "SelT")
        for q in range(4):
            P.dma(Sel[:, q * 16:(q + 1) * 16, :], dT(I["sel8"].ap[:, q * 16:(q + 1) * 16, :], "sel8"), eng="pool")
            P.dma(SelT[:, q * 16:(q + 1) * 16, :], dT(I["sel8T"].ap[:, q * 16:(q + 1) * 16, :], "sel8T"), eng="pool")
        KQ = {nm: Rot(P, [128, 16, 16], BF16, nm, 2) for nm in ["Kf", "Qf", "Kb", "Qb"]}
        tA = Rot(P, [128, 16, 16], F32, "tA", 1); tB = Rot(P, [128, 16, 16], F32, "tB", 1)
        tC = Rot(P, [128, 16, 16], F32, "tC", 1); tD = Rot(P, [128, 16, 16], F32, "tD", 1)
        SgR = Rot(P, [128, 128], BF16, "Sg", 2)
        WeR = Rot(P, [128, 128], BF16, "We", 4)
        s1R = Rot(P, [128, 128], F32, "s1", 1); s2R = Rot(P, [128, 128], F32, "s2", 1)
        UcR = Rot(P, [128, 576], BF16, "Uc", 2)
        XR = {d: [P.sb([128, 545], BF16, f"X{d}{i}") for i in range(2)] for d in "fb"}
        for d in "fb":
            for x in XR[d]:
                P.memset(x, 0.0)
        MdR = {d: Rot(P, [128, 10, 128], BF16, "Md" + d, 2) for d in "fb"}
        mA = Rot(P, [128, 10, 128], BF16, "mA", 1); mB = Rot(P, [128, 10, 128], BF16, "mB", 1)
        Yt = P.sb([128, 8, 544], BF16, "Yt")
        tqR = Rot(P, [128, 256], F32, "tq", 2); yctx = P.sb([128, CTX], BF16, "yctx")
        yown = Rot(P, [128, LH], BF16, "yown", 1)
        identB = G.ident
        flip = 0
        spec = {"Kf": (17, 15), "Qf": (7, 9), "Kb": (7, 8), "Qb": (16, 16)}

        def m128(t, lo):
            return t.v(t.ap[:, lo:lo + 8, :].rearrange("p a b -> p (a b)"))

        def build(g):
            stt_ = {"mats": {}, "We": {}, "Mall": {}}
            th = []
            for dname, gd in (("f", g), ("b", 32 + g)):
                for kind, X1, X2, eng in (("K", BX1, BX2, "dve"), ("Q", CX1, CX2, "pool")):
                    t0_, ne = spec[kind + dname]
                    out = KQ[kind + dname]()[:, 0:ne, :]
                    a_ = (tA() if kind == "K" else tC())[:, 0:ne, :]
                    b_ = (tB() if kind == "K" else tD())[:, 0:ne, :]
                    prb = bc(PR[:, t0_, gd:gd + 1], [[64, ne], [0, 16]]); npb = bc(NPI[:, t0_, gd:gd + 1], [[64, ne], [0, 16]])
                    x1b = bc(X1[:, gd, :], [[0, ne], [1, 16]]); x2b = bc(X2[:, gd, :], [[0, ne], [1, 16]])
                    th.append(lambda a_=a_, prb=prb, x1b=x1b, eng=eng: P.tt(a_, prb, x1b, ALU.mult, eng=eng))
                    th.append(lambda b_=b_, npb=npb, x2b=x2b, eng=eng: P.tt(b_, npb, x2b, ALU.mult, eng=eng))
                    th.append(lambda out=out, a_=a_, b_=b_, eng=eng: P.tt(out, a_, b_, ALU.add, eng=eng))
                    stt_["mats"][kind + dname] = out
            Sg = SgR()
            stt_["Sg"] = Sg
            mt = stt_["mats"]

            def th_S():
                psf = P.ps(); psb_ = P.ps()
                P.mm(psf[:, 0:128], m128(mt["Kf"], 7), m128(mt["Qf"], 0))
                P.mm(psb_[:, 0:128], m128(mt["Kb"], 0), m128(mt["Qb"], 8))
                s1 = s1R(); s2 = s2R()
                P.tt(s1, psf[:, 0:128], maskf, ALU.mult)
                P.tt(s2, psb_[:, 0:128], maskb, ALU.mult)
                P.tt(s1, s1, s2, ALU.add)
                P.stt(Sg, identF, Dcol[:, g:g + 1], s1, ALU.mult, ALU.add)
            th.append(th_S)
            for dname, key in (("f", "Kf"), ("b", "Kb")):
                w = WeR()
                stt_["We"][dname] = w

                def th_W(w=w, key=key):
                    pt = P.ps()
                    ptb = pt.v(pt.ap.bitcast(BF16))
                    P.transpose(ptb[:, 0:128], m128(mt[key], 0), identB)
                    P.copy(w, ptb[:, 0:128], eng="act")
                th.append(th_W)
            stt_["Wo"] = {"f": m128(mt["Qf"], 1), "b": m128(mt["Qb"], 0)}
            for dname, gd in (("f", g), ("b", 32 + g)):
                Mall = MdR[dname]()
                stt_["Mall"][dname] = Mall
                ta = mA(); tb = mB()
                idb = bc(identF, [[0, 10], [1, 128]]); swb = bc(swapF, [[0, 10], [1, 128]])
                dab = bc(DA[:, :, gd], [[64, 10], [0, 128]]); dbb = bc(DB[:, :, gd], [[64, 10], [0, 128]])
                th.append(lambda ta=ta, idb=idb, dab=dab: P.tt(ta, idb, dab, ALU.mult, eng="pool"))
                th.append(lambda tb=tb, swb=swb, dbb=dbb: P.tt(tb, swb, dbb, ALU.mult, eng="pool"))
                th.append(lambda Mall=Mall, ta=ta, tb=tb: P.tt(Mall, ta, tb, ALU.add, eng="pool"))
            return stt_, th

        cur_state, th0 = build(0)
        for t_ in th0:
            t_()
        for g in range(32):
            j, gl = g // 8, g % 8
            if g + 1 < 32:
                nxt_state, pend = build(g + 1)
            else:
                nxt_state, pend = None, []
            per_step = (len(pend) + 9) // 10
            Sg = cur_state["Sg"]; We = cur_state["We"]; Wo = cur_state["Wo"]
            Uc = UcR()
            puA = P.ps(); puB = P.ps(); pu2 = P.ps()
            for i in range(8):
                P.mm(puA[:, 0:256], Sel[:, gl * 8 + i, :], G.uT[:, j, i, 32:288], start=(i == 0), stop=(i == 7))
                P.mm(puB[:, 0:256], Sel[:, gl * 8 + i, :], G.uT[:, j, i, 288:544], start=(i == 0), stop=(i == 7))
                P.mm(pu2[:, 0:32], Sel[:, gl * 8 + i, :], G.uT[:, j, i, 0:32], start=(i == 0), stop=(i == 7))
            P.copy(Uc[:, 32:288], puA[:, 0:256], eng="act")
            P.copy(Uc[:, 288:544], puB[:, 0:256])
            P.copy(Uc[:, 0:32], pu2[:, 0:32], eng="act")
            P.copy(Uc[:, 544:576], pu2[:, 0:32])
            st_ = {}
            for dname, gd in (("f", g), ("b", 32 + g)):
                ucoff = 0 if dname == "f" else 32
                xoff = 1 if dname == "f" else 0
                cur = XR[dname][0]; nxt = XR[dname][1]
                pa = P.ps(); pb2 = P.ps()
                P.mm(pa[:, 0:512], We[dname], Uc[:, ucoff:ucoff + 512])
                P.mm(pb2[:, 0:32], We[dname], Uc[:, ucoff + 512:ucoff + 544])
                P.copy(cur[:, xoff:xoff + 512], pa[:, 0:512], eng="act")
                P.copy(cur[:, xoff + 512:xoff + 544], pb2[:, 0:32])
                st_[dname] = [cur, nxt, xoff, cur_state["Mall"][dname]]
            for m in range(10):
                d = 1 << m
                work = []
                for dname in ("f", "b"):
                    cur, nxt, xoff, Mall = st_[dname]
                    for (lo, hi) in ((0, 272), (272, 544)):
                        ps = P.ps()
                        if dname == "f":
                            s_ = max(lo, d)
                            has = s_ < hi
                            shift = (ps[:, s_ - lo:hi - lo], cur[:, xoff + s_ - d:xoff + hi - d]) if has else None
                        else:
                            e_ = min(hi, 544 - d)
                            has = lo < e_
                            shift = (ps[:, 0:e_ - lo], cur[:, xoff + lo + d:xoff + e_ + d]) if has else None
                        work.append((dname, ps, lo, hi, shift, cur, nxt, xoff, Mall))
                for (dname, ps, lo, hi, shift, cur, nxt, xoff, Mall) in work:
                    P.mm(ps[:, 0:hi - lo], identB, cur[:, xoff + lo:xoff + hi], start=True, stop=(shift is None))
                for (dname, ps, lo, hi, shift, cur, nxt, xoff, Mall) in work:
                    if shift is not None:
                        P.mm(shift[0], Mall[:, m, :], shift[1], start=False, stop=True)
                for (dname, ps, lo, hi, shift, cur, nxt, xoff, Mall) in work:
                    flip ^= 1
                    P.copy(nxt[:, xoff + lo:xoff + hi], ps[:, 0:hi - lo], eng=("act" if flip else "dve"))
                for dname in ("f", "b"):
                    st_[dname][0], st_[dname][1] = st_[dname][1], st_[dname][0]
                for _ in range(per_step):
                    if pend:
                        pend.pop(0)()
            while pend:
                pend.pop(0)()
            Xfin = {dname: st_[dname][0] for dname in ("f", "b")}
            Xf, Xb = Xfin["f"], Xfin["b"]
            py = P.ps()
            pyc = P.ps() if not last else None
            ytl = [(Sg, Uc[:, 32:544], Uc[:, 0:32]), (Wo["f"], Xf[:, 32:544], Xf[:, 0:32]), (Wo["b"], Xb[:, 1:513], Xb[:, 513:545])]
            for q, (lh, r1, r2) in enumerate(ytl):
                P.mm(py[:, 0:512], lh, r1, start=(q == 0), stop=(q == 2))
                if not last:
                    P.mm(pyc[:, 0:32], lh, r2, start=(q == 0), stop=(q == 2))
            P.copy(Yt[:, gl, 0:512], py[:, 0:512], eng="act")
            if not last:
                P.copy(Yt[:, gl, 512:544], pyc[:, 0:32])
            cur_state = nxt_state
            if gl == 7:
                yo = yown()
                for i in range(8):
                    ps = P.ps()
                    ps2 = P.ps() if not last else None
                    for g2 in range(8):
                        P.mm(ps[:, 0:512], SelT[:, g2 * 8 + i, :], Yt[:, g2, 0:512], start=(g2 == 0), stop=(g2 == 7))
                        if not last:
                            P.mm(ps2[:, 0:32], SelT[:, g2 * 8 + i, :], Yt[:, g2, 512:544], start=(g2 == 0), stop=(g2 == 7))
                    tq = tqR()
                    P.act(tq, ps[:, 0:256], AF.Copy, scale=G.sel[:, 0:1])
                    P.stt(yo.v(yo.ap[:, i:LH:8]), ps[:, 256:512], G.sel[:, 1:2], tq, ALU.mult, ALU.add)
                    if not last:
                        P.copy(yctx.v(yctx.ap[:, i:CTX:8]), ps2[:, 0:32])
                P.dma(G.ysD.v(G.ysD.ap[j * 128:(j + 1) * 128, 0:LH]), yo)
                if not last:
                    P.dma(G.ysD.v(G.ysD.ap[j * 128:(j + 1) * 128, LH:LH + CTX]), yctx)


def emit_layer(P, I, G, last, out_own, out_ctx):
    stage_prep(P, I, G)
    with P.scope():
        alloc_persist(P, G)
        with P.scope():
            G.uT = P.sb([128, 4, 8, NK // 8], BF16, "uT")
            s5_alloc(P, I, G)
            with P.scope():
                s5_load(P, I, G)
                stage_A(P, I, G)
                s5_params(P, I, G)
            stage_S5(P, I, G, last)
        stage_B1(P, I, G, last)
        stage_B2(P, I, G, last)
        stage_B3(P, I, G, last)
    stage_B4a(P, I, G, last)
    stage_B4b(P, I, G, last, out_own, out_ctx)
    P.flush()


def build_fused():
    nc = bass.Bass("TRN2", target_bir_lowering=False)
    Cn = declare_consts(nc)
    W = [declare_weights(nc, l) for l in range(2)]
    y_out = dT(nc.dram_tensor("y_own", [LH, D], F32, kind="ExternalOutput").ap(), "y_own")
    with ExitStack() as st:
        P = Prog(nc, st)
        P.init_psum()
        NCH = 4
        CR = LH // NCH
        x1o = [P.dram([CR, D], F32, f"x1o{c}") for c in range(NCH)]
        x1g = [P.dram([2 * CR, D], F32, f"x1g{c}") for c in range(NCH)]
        ctx1 = P.dram([CTX, D], F32, "ctx1")
        own_src = RowSrc(lambda r0: x1o[r0 // CR].v(x1o[r0 // CR].ap[r0 % CR:r0 % CR + 128, :]))

        def all_fn(r0):
            half, rr = r0 // LH, r0 % LH
            c, i = rr // CR, rr % CR
            return x1g[c].v(x1g[c].ap[half * CR + i:half * CR + i + 128, :])
        with P.scope():
            G = Ctx()
            I0 = dict(Cn); I0.update(W[0])
            for k in ("x_all", "x_own", "ctx"):
                I0[k] = flat_src(Cn[k])
            emit_layer(P, I0, G, False, own_src, flat_src(ctx1))
        groups = [[0, 1], [2, 3], [4, 5], [6, 7]]
        for c in range(NCH):
            P.add("pool", lambda e, c=c: e.collective_compute("AllGather", ALU.bypass, replica_groups=groups,
                                                              ins=[x1o[c].ap.opt()], outs=[x1g[c].ap.opt()]), [x1o[c]], [x1g[c]])
            P.add("pool", None, [x1g[c]], [])
        P.flush()
        with P.scope():
            G = Ctx()
            I1 = dict(Cn); I1.update(W[1])
            I1["x_all"] = RowSrc(all_fn); I1["x_own"] = own_src; I1["ctx"] = flat_src(ctx1)
            emit_layer(P, I1, G, True, flat_src(y_out), None)
    return nc


_NC_CACHE = {}


def kernel(**inputs):
    inputs = {k: np.asarray(v) for k, v in inputs.items()}
    C = host_constants()
    if "nc" not in _NC_CACHE:
        _NC_CACHE["nc"] = build_fused()
    nc = _NC_CACHE["nc"]
    in_maps = [per_core_inputs(inputs, core, C) for core in range(8)]
    res = run_bass_kernel_spmd(nc, in_maps, core_ids=list(range(8)))
    out = np.empty((4, L, D), np.float32)
    for core in range(8):
        b, hh = core // 2, core % 2
        out[b, hh * LH:(hh + 1) * LH] = np.asarray(res.results[core]["y_own"])
    return out
```

```python
import numpy as np
import concourse.bass as bass
import concourse.mybir as mybir
from concourse.bass_utils import run_bass_kernel_spmd
from contextlib import ExitStack, contextmanager

F32 = mybir.dt.float32
BF16 = mybir.dt.bfloat16
I32 = mybir.dt.int32
AF = mybir.ActivationFunctionType
ALU = mybir.AluOpType
AX = mybir.AxisListType

ENGS = ["pe", "act", "dve", "pool", "sp"]
DMA_WIN = 8
SAME_ENG_SYNC = True


class T:
    __slots__ = ("ap", "keys")

    def __init__(self, ap, keys):
        self.ap = ap
        self.keys = tuple(keys)

    def __getitem__(self, sl):
        return T(self.ap[sl], self.keys)

    def v(self, ap):
        return T(ap, self.keys)

    def k(self, *sub):
        return T(self.ap, [(self.keys[0],) + tuple(sub)])


class Prog:
    def __init__(self, nc, stack):
        self.nc = nc
        self.stack = stack
        self.cur = stack
        self.streams = {e: [] for e in ENGS}
        self.last_writer = {}
        self.readers = {}
        self.ndma = {e: 0 for e in ENGS}
        self.sigcount = {e: 0 for e in ENGS}
        self.waited = {e: {} for e in ENGS}
        self.nt = 0
        self.psum_banks = []
        self.psum_i = 0
        self.sem = {e: stack.enter_context(nc.semaphore(f"s_{e}")) for e in ENGS}
        self.dsem = {e: [stack.enter_context(nc.semaphore(f"d_{e}{i}")) for i in range(DMA_WIN)]
                     for e in ("sp", "pool", "act")}
        self.dbg = {}
        self.nops = {e: 0 for e in ENGS}

    def sb(self, shape, dt, name=None):
        self.nt += 1
        name = name or "t"
        nm = f"{name}_{self.nt}"
        t = self.cur.enter_context(self.nc.sbuf_tensor(nm, list(shape), dt))
        return T(t[:], [nm])

    def dram(self, shape, dt, name):
        self.nt += 1
        nm = f"{name}_{self.nt}"
        t = self.nc.dram_tensor(nm, list(shape), dt, kind="Internal")
        return T(t.ap(), [nm])

    def init_psum(self, n=8):
        for i in range(n):
            t = self.stack.enter_context(self.nc.psum_tensor(f"bank{i}", [128, 512], F32))
            self.psum_banks.append(T(t[:], [f"bank{i}"]))

    def ps(self):
        b = self.psum_banks[self.psum_i % len(self.psum_banks)]
        self.psum_i += 1
        return b

    @contextmanager
    def scope(self):
        prev = self.cur
        with ExitStack() as st:
            self.cur = st
            yield
            self.flush()
        self.cur = prev

    def add(self, eng, fn, reads=(), writes=(), dma=False):
        deps = set()
        rk = [k for t in reads for k in t.keys]
        wk = [k for t in writes for k in t.keys]
        for k in rk:
            if k in self.last_writer:
                deps.add(self.last_writer[k])
        for k in wk:
            if k in self.last_writer:
                deps.add(self.last_writer[k])
            for r in self.readers.get(k, ()):
                deps.add(r)
        idx = len(self.streams[eng])
        me = (eng, idx)
        deps.discard(me)
        op = dict(fn=fn, deps=deps, dma=dma, signal=False, dman=None)
        if dma:
            op["dman"] = self.ndma[eng]
            self.ndma[eng] += 1
        self.streams[eng].append(op)
        for k in rk:
            self.readers.setdefault(k, []).append(me)
        for k in wk:
            self.last_writer[k] = me
            self.readers[k] = []
        return me

    def dma(self, out, in_, eng="sp", **kw):
        o = out.ap
        i = in_.ap
        return self.add(eng, lambda e: e.dma_start(out=o, in_=i, **kw), [in_], [out], dma=True)

    def mm(self, out, lhsT, rhs, start=True, stop=True, **kw):
        return self.add("pe", lambda e: e.matmul(out.ap, lhsT.ap, rhs.ap, start=start, stop=stop, **kw),
                        [lhsT, rhs], [out])

    def transpose(self, out, in_, ident):
        return self.add("pe", lambda e: e.transpose(out.ap, in_.ap, ident.ap), [in_, ident], [out])

    def act(self, out, in_, func, bias=None, scale=None, eng="act", accum_out=None):
        reads = [in_]
        kw = {}
        if bias is not None:
            if isinstance(bias, T):
                reads.append(bias); kw["bias"] = bias.ap
            else:
                kw["bias"] = bias
        if scale is not None:
            if isinstance(scale, T):
                reads.append(scale); kw["scale"] = scale.ap
            else:
                kw["scale"] = scale
        writes = [out]
        if accum_out is not None:
            kw["accum_out"] = accum_out.ap; writes.append(accum_out)
        return self.add(eng, lambda e: e.activation(out.ap, in_.ap, func, **kw), reads, writes)

    def tt(self, out, a, b, op, eng="dve"):
        return self.add(eng, lambda e: e.tensor_tensor(out.ap, a.ap, b.ap, op), [a, b], [out])

    def ts(self, out, a, s1, s2=None, op0=ALU.mult, op1=None, eng="dve"):
        reads = [a]
        v1 = s1.ap if isinstance(s1, T) else s1
        if isinstance(s1, T): reads.append(s1)
        v2 = s2.ap if isinstance(s2, T) else s2
        if isinstance(s2, T): reads.append(s2)
        if op1 is None:
            return self.add(eng, lambda e: e.tensor_scalar(out.ap, a.ap, v1, None, op0), reads, [out])
        return self.add(eng, lambda e: e.tensor_scalar(out.ap, a.ap, v1, v2, op0, op1), reads, [out])

    def stt(self, out, a, s, b, op0, op1, eng="dve"):
        reads = [a, b]
        v = s.ap if isinstance(s, T) else s
        if isinstance(s, T): reads.append(s)
        return self.add(eng, lambda e: e.scalar_tensor_tensor(out.ap, a.ap, v, b.ap, op0, op1), reads, [out])

    def copy(self, out, in_, eng="dve"):
        if eng == "act":
            return self.add("act", lambda e: e.copy(out.ap, in_.ap), [in_], [out])
        return self.add(eng, lambda e: e.tensor_copy(out.ap, in_.ap), [in_], [out])

    def memset(self, out, val, eng="pool"):
        return self.add(eng, lambda e: e.memset(out.ap, val), [], [out])

    def recip(self, out, in_, eng="dve"):
        return self.add(eng, lambda e: e.reciprocal(out.ap, in_.ap), [in_], [out])

    def recip_fast(self, out, in_):
        return self.add("dve", lambda e: e.reciprocal_approx_fast(out.ap, in_.ap), [in_], [out])

    def bn_stats(self, out, in_):
        return self.add("dve", lambda e: e.bn_stats(out.ap, in_.ap), [in_], [out])

    def bn_aggr(self, out, in_):
        return self.add("dve", lambda e: e.bn_aggr(out.ap, in_.ap), [in_], [out])

    def debug_out(self, name, t, shape, dt=F32):
        d = self.nc.dram_tensor(name, list(shape), dt, kind="ExternalOutput").ap()
        self.dbg[name] = d
        return self.dma(T(d, [name]), t)

    def flush(self):
        nc = self.nc
        streams = self.streams
        lasts = []
        for e in ENGS:
            for j in range(len(streams[e]) - 1, -1, -1):
                op = streams[e][j]
                if op["fn"] is not None and not op["dma"]:
                    lasts.append((e, j))
                    break
        dmas = [(e, j) for e in ENGS for j, op in enumerate(streams[e]) if op["dma"]]
        for e in ENGS:
            deps = set(l for l in lasts if l[0] != e) | set(dmas)
            streams[e].append(dict(fn=None, deps=deps, dma=False, signal=False, dman=None))
        for e in ENGS:
            for op in streams[e]:
                for (f, j) in op["deps"]:
                    d = streams[f][j]
                    if not d["dma"]:
                        if f == e and not SAME_ENG_SYNC:
                            continue
                        d["signal"] = True
        for e in ENGS:
            for op in streams[e]:
                if op["signal"]:
                    self.sigcount[e] += 1
                    op["sigval"] = self.sigcount[e]
        sem, dsem = self.sem, self.dsem

        def run(ename):
            def body(eng):
                waited = self.waited[ename]

                def wait(s, v, key):
                    if waited.get(key, 0) >= v:
                        return
                    waited[key] = v
                    eng.wait_ge(s, v)

                for op in streams[ename]:
                    for (f, j) in sorted(op["deps"]):
                        d = streams[f][j]
                        if d["dma"]:
                            n = d["dman"]
                            wait(dsem[f][n % DMA_WIN], 16 * (n // DMA_WIN + 1), (f, n % DMA_WIN))
                        else:
                            if f == ename and not SAME_ENG_SYNC:
                                continue
                            wait(sem[f], d["sigval"], f)
                    if op["dma"]:
                        n = op["dman"]
                        if n >= DMA_WIN:
                            wait(dsem[ename][n % DMA_WIN], 16 * (n // DMA_WIN), (ename, n % DMA_WIN))
                    if op["fn"] is None:
                        continue
                    ins = op["fn"](eng)
                    if op["dma"]:
                        ins.then_inc(dsem[ename][op["dman"] % DMA_WIN], 16)
                    elif op["signal"]:
                        ins.then_inc(sem[ename], 1)
            return body

        with nc.Block() as block:
            block.tensor(run("pe"))
            block.scalar(run("act"))
            block.vector(run("dve"))
            block.gpsimd(run("pool"))
            block.sync(run("sp"))
        for e in ENGS:
            self.nops[e] += len(streams[e])
        self.streams = {e: [] for e in ENGS}
        self.last_writer = {}
        self.readers = {}


class Rot:
    def __init__(self, P, shape, dt, name, n):
        self.tiles = [P.sb(shape, dt, f"{name}{i}") for i in range(n)]
        self.i = 0

    def __call__(self):
        t = self.tiles[self.i % len(self.tiles)]
        self.i += 1
        return t


def _ps6(self):
    b = self.psum_banks[self.psum_i % 6]
    self.psum_i += 1
    return b


def _psacc(self):
    self.acc_i = getattr(self, "acc_i", 0) + 1
    return self.psum_banks[6 + self.acc_i % 2]


Prog.ps = _ps6
Prog.ps_acc = _psacc


D = 1024
L = 4096
LH = 2048
CTX = 256
NK = CTX + L
NKT = NK // 128
EPS = 1e-6
OFF_AK, OFF_AV, OFF_CKV, OFF_CKR, OFF_U, NST = 0, 128, 256, 512, 544, 1056
OFF_AQ, OFF_CQ, OFF_GATE, NIN = 1056, 1568, 2336, 5408
ALPHA = (2.0 * 2) ** 0.25


def dT(ap, name):
    return T(ap, [name])


class Ctx:
    pass


class RowSrc:
    def __init__(self, fn):
        self.fn = fn

    def rows(self, r0):
        return self.fn(r0)


def flat_src(t):
    return RowSrc(lambda r0: t.v(t.ap[r0:r0 + 128, :]))


WEIGHT_SHAPES = {
    "w_mod": [D, 6 * D], "b_mod": [6 * D], "w_in": [D, NIN], "a_q_gain": [64], "a_k_gain": [64],
    "c_q_a_gain": [768], "c_kv_a_gain": [256], "c_w_qb": [768, 768], "c_w_kvb": [256, 1024],
    "s5_a_re": [2, 32, 64], "s5_a_im": [2, 32, 64], "s5_log_dt": [2, 32],
    "s5_b_re": [2, 32, 64, 16], "s5_b_im": [2, 32, 64, 16], "s5_c_re": [2, 32, 16, 64], "s5_c_im": [2, 32, 16, 64],
    "s5_d": [512], "s5_w_glu": [512, 1024], "w_branch_a": [512, D], "w_branch_s5": [512, D], "w_branch_c": [512, D],
    "w_out": [D, D], "ln1_g": [D], "ln1_b": [D], "w_up": [D, 4 * D], "w_down": [4 * D, D], "ln2_g": [D], "ln2_b": [D],
}
CONST_SHAPES = {
    "x_all": [L, D], "x_own": [LH, D], "ctx": [CTX, D], "cvec": [2, D],
    "ident": [128, 128], "blk64": [128, 128], "perm64": [128, 128], "perm32": [32, 32],
    "ropek_cos": [128, L], "ropek_sin": [128, L], "ropeq_cos": [128, LH], "ropeq_sin": [128, LH],
    "rope32k_cos": [32, L], "rope32k_sin": [32, L], "rope32q_cos": [32, LH], "rope32q_sin": [32, LH],
    "sel": [128, 2], "swap": [128, 128], "mask_f": [128, 128], "mask_b": [128, 128],
    "sel8": [128, 64, 128], "sel8T": [128, 64, 128],
}


def declare_consts(nc):
    return {k: dT(nc.dram_tensor(k, list(v), F32, kind="ExternalInput").ap(), k) for k, v in CONST_SHAPES.items()}


def declare_weights(nc, l):
    return {k: dT(nc.dram_tensor(f"{k}_{l}", list(v), F32, kind="ExternalInput").ap(), f"{k}_{l}")
            for k, v in WEIGHT_SHAPES.items()}


def declare_inputs(nc, last):
    I = declare_consts(nc)
    I.update(declare_weights(nc, 1 if last else 0))
    return I


def host_constants():
    import math
    C = {}
    C["ident"] = np.eye(128, dtype=np.float32)
    blk = np.zeros((128, 128), np.float32); blk[:64, :64] = 1 / 64; blk[64:, 64:] = 1 / 64
    C["blk64"] = blk

    def perm_and_sign(dim):
        half = dim // 2; q = half // 2
        Pm = np.zeros((dim, dim), np.float32)
        sg = np.zeros(dim, np.float32)
        for m in range(dim):
            if (m % half) < q:
                Pm[m + q, m] = 1; sg[m] = -1
            else:
                Pm[m - q, m] = 1; sg[m] = 1
        return Pm, sg
    P64, s64 = perm_and_sign(64)
    p128 = np.zeros((128, 128), np.float32); p128[:64, :64] = P64; p128[64:, 64:] = P64
    C["perm64"] = p128
    P32, s32 = perm_and_sign(32)
    C["perm32"] = P32

    def tables(dim):
        half = dim // 2
        inv = 10000.0 ** (-np.arange(0, half, 2, dtype=np.float32) / half)
        rows = L // 64
        row = np.repeat(np.arange(rows, dtype=np.float32), 64)
        col = np.tile(np.arange(64, dtype=np.float32), rows)
        ang_r = row[:, None] * inv; ang_c = col[:, None] * inv
        ang = np.concatenate([ang_r, ang_r, ang_c, ang_c], axis=-1).astype(np.float32)
        return np.cos(ang).T.astype(np.float32), np.sin(ang).T.astype(np.float32)
    c64, s64t = tables(64)
    s64t = s64t * s64[:, None]
    C["ropek_cos"] = np.concatenate([c64, c64], 0); C["ropek_sin"] = np.concatenate([s64t, s64t], 0)
    c32, s32t = tables(32)
    s32t = s32t * s32[:, None]
    C["rope32k_cos"] = c32; C["rope32k_sin"] = s32t
    sw = np.zeros((128, 128), np.float32)
    for p in range(64):
        sw[p, 64 + p] = 1; sw[64 + p, p] = 1
    C["swap"] = sw
    ii = np.arange(128) // 16
    C["mask_f"] = (ii[:, None] <= ii[None, :]).astype(np.float32)
    C["mask_b"] = (ii[:, None] >= ii[None, :]).astype(np.float32)
    sel = np.zeros((128, 64, 128), np.float32)
    for gl in range(8):
        for i in range(8):
            for c in range(16):
                sel[gl * 16 + c, gl * 8 + i, i * 16 + c] = 1
    C["sel8"] = sel
    C["sel8T"] = np.ascontiguousarray(sel.transpose(2, 1, 0))
    return C


def per_core_inputs(inputs, core, C, layers=(0, 1)):
    b, hh = core // 2, core % 2
    m = {}
    xb = inputs["x"][b]
    m["x_all"] = xb; m["x_own"] = xb[hh * LH:(hh + 1) * LH]; m["ctx"] = inputs["ctx"][b]
    m["cvec"] = np.stack([inputs["c"][b], inputs["c_ctx"]], 0)
    for l in layers:
        for k in WEIGHT_SHAPES:
            m[f"{k}_{l}"] = inputs[k][l]
    for k in ["ident", "blk64", "perm64", "perm32", "ropek_cos", "ropek_sin", "rope32k_cos", "rope32k_sin",
              "swap", "mask_f", "mask_b", "sel8", "sel8T"]:
        m[k] = C[k]
    sl = slice(hh * LH, (hh + 1) * LH)
    m["ropeq_cos"] = C["ropek_cos"][:, sl]; m["ropeq_sin"] = C["ropek_sin"][:, sl]
    m["rope32q_cos"] = C["rope32k_cos"][:, sl]; m["rope32q_sin"] = C["rope32k_sin"][:, sl]
    s = np.zeros((128, 2), np.float32); s[:, hh] = 1
    m["sel"] = s
    return {k: np.ascontiguousarray(v, dtype=np.float32) for k, v in m.items()}


def rstd_from_ms(P, out, ms, n, eps=EPS, eng_a="act"):
    P.ts(out, ms, eps, None, op0=ALU.add)
    P.act(out, out, AF.Sqrt)
    if n > 1:
        P.recip(out, out)
    else:
        P.recip(out, out)


def stage_prep(P, I, G):
    G.ident = P.sb([128, 128], BF16, "ident"); P.dma(G.ident, I["ident"], eng="pool")
    G.blk64 = P.sb([128, 128], BF16, "blk64"); P.dma(G.blk64, I["blk64"], eng="pool")
    G.perm64 = P.sb([128, 128], BF16, "perm64"); P.dma(G.perm64, I["perm64"], eng="pool")
    G.perm32 = P.sb([32, 32], BF16, "perm32"); P.dma(G.perm32, I["perm32"], eng="pool")
    G.ones = P.sb([128, 128], BF16, "ones"); P.memset(G.ones, 1.0)
    G.modT = P.sb([128, 48, 2], F32, "modT")
    G.sel = P.sb([128, 2], F32, "sel"); P.dma(G.sel, I["sel"])
    with P.scope():
        cT = P.sb([128, 8, 2], F32, "cT")
        for t in range(2):
            src = I["cvec"].ap[t, :].rearrange("(k p) -> p k", p=128)
            P.dma(cT[:, :, t], dT(src, "cvec"), allow_slow_non_contiguous=True)
        sT = P.sb([128, 8, 2], BF16, "sT")
        P.act(sT, cT, AF.Silu)
        bT = P.sb([128, 48], F32, "bT")
        P.dma(bT, dT(I["b_mod"].ap.rearrange("(j p) -> p j", p=128), "b_mod"), allow_slow_non_contiguous=True)
        wbufs = [P.sb([128, 8, 128], BF16, f"wm{i}") for i in range(3)]
        for j in range(48):
            wb = wbufs[j % 3]
            src = I["w_mod"].ap[:, j * 128:(j + 1) * 128].rearrange("(k p) c -> p k c", p=128)
            P.dma(wb, dT(src, "w_mod"), eng="pool")
            ps = P.ps()
            for kc in range(8):
                P.mm(ps[:, 0:2], wb[:, kc, :], sT[:, kc, :], start=(kc == 0), stop=(kc == 7))
            P.ts(G.modT[:, j, :], ps[:, 0:2], bT[:, j:j + 1], None, op0=ALU.add)
        for w in (1, 4):
            P.ts(G.modT[:, w * 8:(w + 1) * 8, :], G.modT[:, w * 8:(w + 1) * 8, :], 1.0, None, op0=ALU.add)
        G.modD = P.dram([2, 6 * D], F32, "modD")
        for t in range(2):
            dst = G.modD.ap[t, :].rearrange("(j p) -> p j", p=128)
            P.dma(G.modD.v(dst), G.modT[:, :, t], allow_slow_non_contiguous=True)


def ln_tile_to_hT(P, G, xt, hT_dst, t_idx, which_sh, which_sc):
    st = P.sb([128, 2, 6], F32, "bnst")
    mv = P.sb([128, 2], F32, "mv")
    for hh in range(2):
        P.bn_stats(st[:, hh, :], xt[:, hh * 512:(hh + 1) * 512])
    P.bn_aggr(mv, st)
    rs = P.sb([128, 1], F32, "rs")
    rstd_from_ms(P, rs, mv[:, 1:2], 1)
    xn = P.sb([128, D], BF16, "xn")
    P.ts(xn, xt, mv[:, 0:1], rs, op0=ALU.subtract, op1=ALU.mult)
    ps = P.ps()
    psb = ps.v(ps.ap.bitcast(BF16))
    for kc in range(8):
        P.transpose(psb[:, kc * 128:(kc + 1) * 128], xn[:, kc * 128:(kc + 1) * 128], G.ident)
    for kc in range(8):
        P.act(hT_dst[:, kc, :], psb[:, kc * 128:(kc + 1) * 128], AF.Identity,
              bias=G.modT[:, which_sh * 8 + kc, t_idx:t_idx + 1], scale=G.modT[:, which_sc * 8 + kc, t_idx:t_idx + 1])


_CAST = {"i": 0}


def wload(P, stg, dst, src, engs=("pool", "dve", "act")):
    shape = list(dst.ap.shape)[1:]
    n = 1
    for v in shape:
        n *= v
    st = stg()
    sv = st.ap[0:dst.ap.shape[0], 0:n]
    if len(shape) == 2:
        sv = sv.rearrange("p (a b) -> p a b", a=shape[0])
    svt = st.v(sv)
    P.dma(svt, src)
    e = engs[_CAST["i"] % len(engs)]
    _CAST["i"] += 1
    P.copy(dst, svt, eng=e)


def mmg(P, items, K):
    for k in range(K):
        for (out, lf, rf) in items:
            P.mm(out, lf(k), rf(k), start=(k == 0), stop=(k == K - 1))


def make_ln_pools(P, nb=2):
    R = Ctx()
    R.xt = Rot(P, [128, D], F32, "xt", nb)
    R.st = Rot(P, [128, 2, 6], F32, "bnst", nb)
    R.mv = Rot(P, [128, 2], F32, "mv", nb)
    R.rs = Rot(P, [128, 1], F32, "rs", nb)
    R.xn = Rot(P, [128, D], BF16, "xn", nb)
    return R


def ln_part1(P, G, R, src_dram):
    xt = R.xt()
    P.dma(xt, src_dram)
    st = R.st(); mv = R.mv(); rs = R.rs(); xn = R.xn()
    for hh in range(2):
        P.bn_stats(st[:, hh, :], xt[:, hh * 512:(hh + 1) * 512])
    P.bn_aggr(mv, st)
    rstd_from_ms(P, rs, mv[:, 1:2], 1)
    P.ts(xn, xt, mv[:, 0:1], rs, op0=ALU.subtract, op1=ALU.mult)
    return xn


def ln_part2(P, G, xn, hT_dst, t_idx, which_sh, which_sc):
    ps = P.ps()
    psb = ps.v(ps.ap.bitcast(BF16))
    for kc in range(8):
        P.transpose(psb[:, kc * 128:(kc + 1) * 128], xn[:, kc * 128:(kc + 1) * 128], G.ident)
    for kc in range(8):
        bias = G.modT[:, which_sh * 8 + kc, t_idx:t_idx + 1]
        scale = G.modT[:, which_sc * 8 + kc, t_idx:t_idx + 1]
        if kc % 2 == 0:
            P.act(hT_dst[:, kc, :], psb[:, kc * 128:(kc + 1) * 128], AF.Identity, bias=bias, scale=scale)
        else:
            P.ts(hT_dst[:, kc, :], psb[:, kc * 128:(kc + 1) * 128], scale, bias, op0=ALU.mult, op1=ALU.add)


def ln_tile_to_hT2(P, G, R, src_dram, hT_dst, t_idx, which_sh, which_sc):
    xn = ln_part1(P, G, R, src_dram)
    ln_part2(P, G, xn, hT_dst, t_idx, which_sh, which_sc)


def rope_apply(P, dst, src_bf, perm, cos, sin, tmp, n, rows=128, psfn=None):
    ps = (psfn or P.ps)()
    P.mm(ps[0:rows, 0:n], perm, src_bf)
    P.tt(tmp, src_bf, cos, ALU.mult)
    P.tt(dst, ps[0:rows, 0:n], sin, ALU.mult)
    P.tt(dst, dst, tmp, ALU.add)


def alloc_persist(P, G):
    G.kT = P.sb([128, NK], BF16, "kT")
    G.Vg = P.sb([128, NKT, 2, 128], BF16, "Vg")
    G.ckvT = P.sb([128, 2, NK], BF16, "ckvT")
    G.krT = P.sb([32, NK], BF16, "krT")


def stage_A(P, I, G):
    P.memset(G.Vg[:, :, :, 64:128], 1.0)
    with P.scope():
        w_st = P.sb([128, 8, NST], BF16, "w_st")
        with P.scope():
            stg = Rot(P, [128, NST], F32, "stg", 2)
            for kc in range(8):
                wload(P, stg, w_st[:, kc, :], dT(I["w_in"].ap[kc * 128:(kc + 1) * 128, 0:NST], "w_in"))
        kg = P.sb([128, 1], F32, "kg")
        for r in range(2):
            P.dma(kg[r * 64:(r + 1) * 64, :], dT(I["a_k_gain"].ap.rearrange("(p o) -> p o", o=1), "akg"))
        cg = P.sb([128, 2], F32, "cg")
        P.dma(cg, dT(I["c_kv_a_gain"].ap.rearrange("(j p) -> p j", p=128), "ckg"), allow_slow_non_contiguous=True)
        R = make_ln_pools(P, 3)
        hTs = Rot(P, [128, 8, 512], BF16, "hT", 2)
        sq = Rot(P, [128, 512], BF16, "sq", 2)
        rst = Rot(P, [128, 512], F32, "rst", 1)
        knb = Rot(P, [128, 512], BF16, "knb", 2)
        tmp = Rot(P, [128, 512], F32, "tmp", 1)
        cosb = Rot(P, [128, 512], F32, "cosb", 1)
        sinb = Rot(P, [128, 512], F32, "sinb", 1)
        cos32 = Rot(P, [32, 512], BF16, "cos32", 1)
        sin32 = Rot(P, [32, 512], BF16, "sin32", 1)
        blocks = [(0, 2, True)] + [(2 + 4 * i, 4, False) for i in range(8)]
        alltiles = [(t0 + ti, is_ctx) for (t0, nt, is_ctx) in blocks for ti in range(nt)]

        def a_p1(q):
            t, is_ctx = alltiles[q]
            return ln_part1(P, G, R, I["ctx"].rows(t * 128) if is_ctx else I["x_all"].rows((t - 2) * 128))
        qi = 0
        xn_cur = a_p1(0)
        for (t0, nt, is_ctx) in blocks:
            n = nt * 128
            c0 = t0 * 128
            hT = hTs()
            for ti in range(nt):
                xn_nxt = a_p1(qi + 1) if qi + 1 < len(alltiles) else None
                ln_part2(P, G, xn_cur, hT[:, :, ti * 128:(ti + 1) * 128], 1 if is_ctx else 0, 0, 1)
                xn_cur = xn_nxt
                qi += 1
            if not is_ctx:
                lc = c0 - CTX
                cb, sb_, c32, s32 = cosb(), sinb(), cos32(), sin32()
                P.dma(cb[:, 0:n], dT(I["ropek_cos"].ap[:, lc:lc + n], "rc"))
                P.dma(sb_[:, 0:n], dT(I["ropek_sin"].ap[:, lc:lc + n], "rs"))
                P.dma(c32[:, 0:n], dT(I["rope32k_cos"].ap[:, lc:lc + n], "rc32"), eng="pool")
                P.dma(s32[:, 0:n], dT(I["rope32k_sin"].ap[:, lc:lc + n], "rs32"), eng="pool")
            pk = P.ps(); pc = [P.ps(), P.ps()]; pr = P.ps()
            mmg(P, [(pk[:, 0:n], lambda k: w_st[:, k, OFF_AK:OFF_AK + 128], lambda k: hT[:, k, 0:n]),
                    (pc[0][:, 0:n], lambda k: w_st[:, k, OFF_CKV:OFF_CKV + 128], lambda k: hT[:, k, 0:n]),
                    (pc[1][:, 0:n], lambda k: w_st[:, k, OFF_CKV + 128:OFF_CKV + 256], lambda k: hT[:, k, 0:n]),
                    (pr[0:32, 0:n], lambda k: w_st[:, k, OFF_CKR:OFF_CKR + 32], lambda k: hT[:, k, 0:n])], 8)
            s = sq()
            P.act(s[:, 0:n], pk[:, 0:n], AF.Square)
            pm = P.ps()
            P.mm(pm[:, 0:n], G.blk64, s[:, 0:n])
            rs = rst()
            rstd_from_ms(P, rs[:, 0:n], pm[:, 0:n], n)
            kn = knb()
            P.stt(kn[:, 0:n], pk[:, 0:n], kg[:, 0:1], rs[:, 0:n], ALU.mult, ALU.mult)
            if is_ctx:
                P.copy(G.kT[:, c0:c0 + n], kn[:, 0:n])
            else:
                rope_apply(P, G.kT[:, c0:c0 + n], kn[:, 0:n], G.perm64, cb[:, 0:n], sb_[:, 0:n], tmp()[:, 0:n], n)
            ss = [sq(), sq()]
            for j in range(2):
                P.act(ss[j][:, 0:n], pc[j][:, 0:n], AF.Square)
            pm = P.ps()
            for j in range(2):
                P.mm(pm[:, 0:n], G.ones, ss[j][:, 0:n], start=(j == 0), stop=(j == 1))
            rs = rst()
            P.ts(rs[:, 0:n], pm[:, 0:n], 1.0 / 256, EPS, op0=ALU.mult, op1=ALU.add)
            P.act(rs[:, 0:n], rs[:, 0:n], AF.Sqrt)
            P.recip(rs[:, 0:n], rs[:, 0:n])
            for j in range(2):
                P.stt(G.ckvT[:, j, c0:c0 + n], pc[j][:, 0:n], cg[:, j:j + 1], rs[:, 0:n], ALU.mult, ALU.mult)
            if is_ctx:
                P.copy(G.krT[:, c0:c0 + n], pr[0:32, 0:n])
            else:
                kr = knb()
                P.copy(kr[0:32, 0:n], pr[0:32, 0:n])
                rope_apply(P, G.krT[:, c0:c0 + n], kr[0:32, 0:n], G.perm32, c32[:, 0:n], s32[:, 0:n], tmp()[0:32, 0:n], n, rows=32)
            pvs = [P.ps() for _ in range(nt)]
            mmg(P, [(pvs[ti][:, 0:128], (lambda k, ti=ti: hT[:, k, ti * 128:(ti + 1) * 128]),
                     lambda k: w_st[:, k, OFF_AV:OFF_AV + 128]) for ti in range(nt)], 8)
            for ti in range(nt):
                pv = pvs[ti]
                P.copy(G.Vg[:, t0 + ti, :, 0:64], pv.v(pv.ap[:, 0:128].rearrange("p (a b) -> p a b", a=2)), eng="act")
            pus = [P.ps() for _ in range(4)]
            mmg(P, [(pus[j][:, 0:n], (lambda k, j=j: w_st[:, k, OFF_U + j * 128:OFF_U + (j + 1) * 128]),
                     lambda k: hT[:, k, 0:n]) for j in range(4)], 8)
            for j in range(4):
                dst = G.uT.v(G.uT.ap[:, j, :, c0 // 8:(c0 + n) // 8].rearrange("p i c -> p c i"))
                src = pus[j].v(pus[j].ap[:, 0:n].rearrange("p (c i) -> p c i", i=8))
                P.copy(dst, src, eng=("act" if j % 2 else "dve"))


def own_blocks(last):
    bl = [(i * 512, 512, False, i * 512) for i in range(4)]
    if not last:
        bl.append((LH, 256, True, 0))
    return bl


def stage_B1(P, I, G, last):
    NQ = LH + (0 if last else CTX)
    G.NQ = NQ
    G.qg = P.sb([128, 4, NQ], BF16, "qg")
    G.qm = P.sb([96, 8, NQ], BF16, "qm")
    G.gD = P.dram([3 * D, NQ], BF16, "gD")
    with P.scope():
        hT = P.sb([128, 8, NQ], BF16, "hTall")
        with P.scope():
            R = make_ln_pools(P, 4)
            tiles = [(c0 + ti * 128, r0 + ti * 128, is_ctx) for (c0, n, is_ctx, r0) in own_blocks(last) for ti in range(n // 128)]

            def p1(q):
                col, row, is_ctx = tiles[q]
                return ln_part1(P, G, R, (I["ctx"] if is_ctx else I["x_own"]).rows(row))
            xn_cur = p1(0)
            for q in range(len(tiles)):
                xn_nxt = p1(q + 1) if q + 1 < len(tiles) else None
                col, row, is_ctx = tiles[q]
                ln_part2(P, G, xn_cur, hT[:, :, col:col + 128], 1 if is_ctx else 0, 0, 1)
                xn_cur = xn_nxt
        qgain = P.sb([128, 1], F32, "qgain")
        for r in range(2):
            P.dma(qgain[r * 64:(r + 1) * 64, :], dT(I["a_q_gain"].ap.rearrange("(p o) -> p o", o=1), "aqg"))
        cqg = P.sb([128, 6], F32, "cqg")
        P.dma(cqg, dT(I["c_q_a_gain"].ap.rearrange("(j p) -> p j", p=128), "cqg"), allow_slow_non_contiguous=True)
        perm32h = P.sb([96, 32], BF16, "perm32h")
        P.dma(perm32h[64:96, :], I["perm32"], eng="pool")
        cosqR = Rot(P, [128, 512], F32, "cosq", 2)
        sinqR = Rot(P, [128, 512], F32, "sinq", 2)
        cos32R = Rot(P, [96, 512], F32, "cos32q", 2)
        sin32R = Rot(P, [96, 512], F32, "sin32q", 2)
        sq = Rot(P, [128, 512], BF16, "sq", 6)
        rst = Rot(P, [128, 512], F32, "rst", 2)
        knb = Rot(P, [128, 512], BF16, "knb", 2)
        tmp = Rot(P, [128, 512], F32, "tmp", 2)
        with P.scope():
            wq = P.sb([128, 8, 4, 128], BF16, "wq")
            stg = Rot(P, [128, 768], F32, "stg", 2)
            for kc in range(8):
                for hf in range(2):
                    src = I["w_in"].ap[kc * 128:(kc + 1) * 128, OFF_AQ + hf * 256:OFF_AQ + (hf + 1) * 256].rearrange("p (a b) -> p a b", a=4)
                    wload(P, stg, wq[:, kc, :, hf * 64:(hf + 1) * 64], dT(src, "w_in"))
            for (c0, n, is_ctx, r0) in own_blocks(last):
                if not is_ctx:
                    cosq = cosqR(); sinq = sinqR()
                    P.dma(cosq, dT(I["ropeq_cos"].ap[:, c0:c0 + n], "rqc"))
                    P.dma(sinq, dT(I["ropeq_sin"].ap[:, c0:c0 + n], "rqs"))
                pks = [P.ps() for _ in range(4)]
                mmg(P, [(pks[hd][:, 0:n], (lambda k, hd=hd: wq[:, k, hd, :]), lambda k: hT[:, k, c0:c0 + n]) for hd in range(4)], 8)
                for hd in range(4):
                    pk = pks[hd]
                    s = sq()
                    P.act(s[:, 0:n], pk[:, 0:n], AF.Square)
                    pm = P.ps_acc()
                    P.mm(pm[:, 0:n], G.blk64, s[:, 0:n])
                    rs = rst()
                    rstd_from_ms(P, rs[:, 0:n], pm[:, 0:n], n)
                    if is_ctx:
                        P.stt(G.qg[:, hd, c0:c0 + n], pk[:, 0:n], qgain[:, 0:1], rs[:, 0:n], ALU.mult, ALU.mult)
                    else:
                        kn = knb()
                        P.stt(kn[:, 0:n], pk[:, 0:n], qgain[:, 0:1], rs[:, 0:n], ALU.mult, ALU.mult)
                        rope_apply(P, G.qg[:, hd, c0:c0 + n], kn[:, 0:n], G.perm64, cosq[:, 0:n], sinq[:, 0:n],
                                   tmp()[:, 0:n], n, psfn=P.ps_acc)
        with P.scope():
            wc = P.sb([128, 8, 768], BF16, "wc")
            stg = Rot(P, [128, 768], F32, "stg", 1)
            for kc in range(8):
                wload(P, stg, wc[:, kc, :], dT(I["w_in"].ap[kc * 128:(kc + 1) * 128, OFF_CQ:OFF_CQ + 768], "w_in"))
            wqb = P.sb([128, 6, 768], BF16, "wqb")
            for j in range(6):
                wload(P, stg, wqb[:, j, :], dT(I["c_w_qb"].ap[j * 128:(j + 1) * 128, :], "wqb"))
            cqn = P.sb([128, 6, 512], BF16, "cqn")
            qrb = Rot(P, [96, 512], BF16, "qrb", 2)
            for (c0, n, is_ctx, r0) in own_blocks(last):
                if not is_ctx:
                    cos32 = cos32R(); sin32 = sin32R()
                    P.dma(cos32[64:96, :], dT(I["rope32q_cos"].ap[:, c0:c0 + n], "rqc32"))
                    P.dma(sin32[64:96, :], dT(I["rope32q_sin"].ap[:, c0:c0 + n], "rqs32"))
                pcs = [P.ps() for _ in range(6)]
                mmg(P, [(pcs[j][:, 0:n], (lambda k, j=j: wc[:, k, j * 128:(j + 1) * 128]), lambda k: hT[:, k, c0:c0 + n]) for j in range(6)], 8)
                sqs = []
                for j in range(6):
                    s = sq()
                    P.act(s[:, 0:n], pcs[j][:, 0:n], AF.Square)
                    sqs.append(s)
                pm = P.ps_acc()
                for j in range(6):
                    P.mm(pm[:, 0:n], G.ones, sqs[j][:, 0:n], start=(j == 0), stop=(j == 5))
                rs = rst()
                P.ts(rs[:, 0:n], pm[:, 0:n], 1.0 / 768, EPS, op0=ALU.mult, op1=ALU.add)
                P.act(rs[:, 0:n], rs[:, 0:n], AF.Sqrt)
                P.recip(rs[:, 0:n], rs[:, 0:n])
                for j in range(6):
                    P.stt(cqn[:, j, 0:n], pcs[j][:, 0:n], cqg[:, j:j + 1], rs[:, 0:n], ALU.mult, ALU.mult)
                for hg in range(2):
                    pqs = [P.ps() for _ in range(4)]
                    mmg(P, [(pqs[i][0:96, 0:n], (lambda k, h=hg * 4 + i: wqb[:, k, h * 96:(h + 1) * 96]), lambda k: cqn[:, k, 0:n]) for i in range(4)], 6)
                    for i in range(4):
                        h = hg * 4 + i
                        pq = pqs[i]
                        if is_ctx:
                            P.copy(G.qm[:, h, c0:c0 + n], pq[0:96, 0:n], eng="act")
                        else:
                            P.copy(G.qm[0:64, h, c0:c0 + n], pq[0:64, 0:n], eng="act")
                            qr = qrb()
                            P.copy(qr[64:96, 0:n], pq[64:96, 0:n])
                            pr = P.ps_acc()
                            P.mm(pr[64:96, 0:n], perm32h[64:96, :], qr[64:96, 0:n])
                            t = tmp()
                            P.tt(t[64:96, 0:n], qr[64:96, 0:n], cos32[64:96, 0:n], ALU.mult)
                            t2 = tmp()
                            P.tt(t2[64:96, 0:n], pr[64:96, 0:n], sin32[64:96, 0:n], ALU.mult)
                            P.tt(G.qm[64:96, h, c0:c0 + n], t[64:96, 0:n], t2[64:96, 0:n], ALU.add)
        with P.scope():
            wg = Rot(P, [128, 8, 512], BF16, "wg", 2)
            stg = Rot(P, [128, 512], F32, "stg", 3)
            gb = Rot(P, [128, 512], BF16, "gb", 8)
            def load_g(gi):
                w = wg()
                for kc in range(8):
                    wload(P, stg, w[:, kc, :], dT(I["w_in"].ap[kc * 128:(kc + 1) * 128, OFF_GATE + gi * 512:OFF_GATE + (gi + 1) * 512], "w_in"), engs=("dve", "act"))
                return w
            w_nxt = load_g(0)
            for gi in range(6):
                w = w_nxt
                w_nxt = load_g(gi + 1) if gi + 1 < 6 else None
                for (c0, n, is_ctx, r0) in own_blocks(last):
                    pgs = [P.ps() for _ in range(4)]
                    mmg(P, [(pgs[oc][:, 0:n], (lambda k, oc=oc: w[:, k, oc * 128:(oc + 1) * 128]), lambda k: hT[:, k, c0:c0 + n]) for oc in range(4)], 8)
                    for oc in range(4):
                        g = gb()
                        P.act(g[:, 0:n], pgs[oc][:, 0:n], AF.Sigmoid)
                        row = (gi * 4 + oc) * 128
                        P.dma(G.gD.v(G.gD.ap[row:row + 128, c0:c0 + n]), g[:, 0:n])


def run_attn(P, chains, pT, scale):
    LA = 2
    nkt = chains[0][1]
    pls = [dict() for _ in chains]
    for kt in range(nkt + LA):
        if kt < nkt:
            for ci, (po, _, n, slf, srhs, vlf) in enumerate(chains):
                pss = P.ps()
                P.mm(pss[:, 0:n], slf(kt), srhs)
                p = pT()
                P.act(p[:, 0:n], pss[:, 0:n], AF.Exp, scale=scale)
                pls[ci][kt] = p
        jj = kt - LA
        if jj >= 0:
            for ci, (po, _, n, slf, srhs, vlf) in enumerate(chains):
                P.mm(po[:, 0:n], vlf(jj), pls[ci].pop(jj)[:, 0:n], start=(jj == 0), stop=(jj == nkt - 1))


def block_groups(last):
    bl = own_blocks(last)
    groups = [bl[0:2], bl[2:4]]
    if not last:
        groups.append(bl[4:5])
    return groups


def attn_finish(P, po, n, rec, yo, dst):
    r = rec()
    P.recip(r[64:128, 0:n], po[64:128, 0:n])
    y = yo()
    P.tt(y[:, 0:n], po[0:64, 0:n], r[64:128, 0:n], ALU.mult)
    P.dma(dst, y[:, 0:n])


def stage_B2(P, I, G, last):
    NQ = G.NQ
    G.yaD = P.dram([512, NQ], BF16, "yaD")
    with P.scope():
        pT = Rot(P, [128, 512], BF16, "pT", 8)
        rec = Rot(P, [128, 512], F32, "rec", 2)
        yo = Rot(P, [64, 512], BF16, "yo", 2)
        kTp = P.sb([128, 2, NK], BF16, "kTp")
        P.memset(kTp, 0.0)
        P.copy(kTp[0:64, 0, :], G.kT[0:64, :], eng="pool")
        P.copy(kTp[64:128, 1, :], G.kT[64:128, :], eng="dve")
        for hd in range(4):
            for kvh in range(2):
                head = hd + 4 * kvh
                for grp in block_groups(last):
                    chains = []
                    for (c0, n, is_ctx, r0) in grp:
                        nkt = 2 if is_ctx else NKT
                        chains.append((P.ps_acc(), nkt, n, (lambda kt: kTp[:, kvh, kt * 128:(kt + 1) * 128]),
                                       G.qg[:, hd, c0:c0 + n], (lambda kt: G.Vg[:, kt, kvh, :])))
                    run_attn(P, chains, pT, 0.125)
                    for (po, _, n, _, _, _), (c0, _, _, _) in zip(chains, grp):
                        attn_finish(P, po, n, rec, yo, G.yaD.v(G.yaD.ap[head * 64:(head + 1) * 64, c0:c0 + n]))


def stage_B3(P, I, G, last):
    NQ = G.NQ
    G.ycD = P.dram([512, NQ], BF16, "ycD")
    with P.scope():
        wkv = P.sb([128, 2, 1024], BF16, "wkv")
        stg = Rot(P, [128, 1024], F32, "stg", 1)
        for j in range(2):
            wload(P, stg, wkv[:, j, :], dT(I["c_w_kvb"].ap[j * 128:(j + 1) * 128, :], "wkvb"))
        Kh = Rot(P, [96, NK], BF16, "Kh", 2)
        Vh = [P.sb([128, NKT, 128], BF16, f"Vh{i}") for i in range(2)]
        for v in Vh:
            P.memset(v[:, :, 64:128], 1.0)
        pT = Rot(P, [128, 512], BF16, "pT", 8)
        rec = Rot(P, [128, 512], F32, "rec", 2)
        yo = Rot(P, [64, 512], BF16, "yo", 2)
        scale = 96 ** -0.5
        for h in range(8):
            K = Kh(); V = Vh[h % 2]
            cbs = [(cb * 512, min(512, NK - cb * 512)) for cb in range(9)]
            for g0 in range(0, 9, 3):
                grp = cbs[g0:g0 + 3]
                pks = [P.ps() for _ in grp]
                mmg(P, [(pks[i][0:64, 0:n], lambda k: wkv[:, k, h * 128:h * 128 + 64], (lambda k, k0=k0, n=n: G.ckvT[:, k, k0:k0 + n]))
                        for i, (k0, n) in enumerate(grp)], 2)
                for i, (k0, n) in enumerate(grp):
                    P.copy(K[0:64, k0:k0 + n], pks[i][0:64, 0:n], eng=("act" if i % 2 else "dve"))
            P.copy(K[64:96, :], G.krT[0:32, :], eng="pool")
            for g0 in range(0, NKT, 4):
                kts = list(range(g0, min(g0 + 4, NKT)))
                pvs = [P.ps() for _ in kts]
                mmg(P, [(pvs[i][:, 0:64], (lambda k, kt=kt: G.ckvT[:, k, kt * 128:(kt + 1) * 128]),
                         lambda k: wkv[:, k, h * 128 + 64:h * 128 + 128]) for i, kt in enumerate(kts)], 2)
                for i, kt in enumerate(kts):
                    P.copy(V[:, kt, 0:64], pvs[i][:, 0:64], eng=("act" if kt % 2 else "dve"))
            for grp in block_groups(last):
                chains = []
                for (c0, n, is_ctx, r0) in grp:
                    nkt = 2 if is_ctx else NKT
                    chains.append((P.ps_acc(), nkt, n, (lambda kt: K[0:96, kt * 128:(kt + 1) * 128]),
                                   G.qm[0:96, h, c0:c0 + n], (lambda kt: V[:, kt, :])))
                run_attn(P, chains, pT, scale)
                for (po, _, n, _, _, _), (c0, _, _, _) in zip(chains, grp):
                    attn_finish(P, po, n, rec, yo, G.ycD.v(G.ycD.ap[h * 64:(h + 1) * 64, c0:c0 + n]))


def bcast_load(P, dst, src_ap_1d):
    P.dma(dst, dT(src_ap_1d.partition_broadcast(128), "bc"))


def stage_B4a(P, I, G, last):
    NQ = G.NQ
    G.xmidD = P.dram([NQ, D], F32, "xmidD")
    G.h2D = P.dram([D, NQ], BF16, "h2D")
    with P.scope():
        wglu = P.sb([128, 4, 1024], BF16, "wglu")
        wb = [P.sb([128, 4, 1024], BF16, f"wb{i}") for i in range(3)]
        wout = P.sb([128, 8, 1024], BF16, "wout")
        with P.scope():
            stg = Rot(P, [128, 1024], F32, "stg", 3)
            for j in range(4):
                wload(P, stg, wglu[:, j, :], dT(I["s5_w_glu"].ap[j * 128:(j + 1) * 128, :], "w"))
                for i, nm in enumerate(["w_branch_a", "w_branch_s5", "w_branch_c"]):
                    wload(P, stg, wb[i][:, j, :], dT(I[nm].ap[j * 128:(j + 1) * 128, :], "w"))
            for kc in range(8):
                wload(P, stg, wout[:, kc, :], dT(I["w_out"].ap[kc * 128:(kc + 1) * 128, :], "w"))
        g1b = [P.sb([128, D], F32, f"g1b{t}") for t in range(2)]
        for t in range(2):
            P.dma(g1b[t], dT(G.modD.ap[t, 2 * D:3 * D].partition_broadcast(128), "modD_r"))
        lng = P.sb([128, D], F32, "lng"); bcast_load(P, lng, I["ln1_g"].ap)
        lnb = P.sb([128, D], F32, "lnb"); bcast_load(P, lnb, I["ln1_b"].ap)
        class _InR:
            def __init__(self):
                gts = P.sb([128, 24, 512], BF16, "gates")
                self.sets = [(P.sb([128, 4, 512], BF16, f"yT{i}"), P.sb([128, 4, 512], BF16, f"sa{i}"),
                              P.sb([128, 4, 512], BF16, f"sc{i}"), gts) for i in range(2)]
                self.i = 0

            def __call__(self):
                r = self.sets[self.i % 2]
                self.i += 1
                return r
        inR = _InR()
        srcs = [None, P.sb([128, 4, 512], BF16, "src1"), None]
        t1 = P.sb([128, 4, 512], F32, "t1")
        sg = Rot(P, [128, 512], F32, "sg", 2)
        acc = Rot(P, [128, 512], F32, "acc", 2)
        tmpm = Rot(P, [128, 512], F32, "tmpm", 2)
        merged = P.sb([128, 8, 512], BF16, "merged")
        xtR = Rot(P, [128, D], F32, "xt", 2)
        tsR = Rot(P, [128, D], F32, "tsum", 3)
        xmR = Rot(P, [128, D], F32, "xm", 2)
        stR = Rot(P, [128, 2, 6], F32, "bnst", 2); mvR = Rot(P, [128, 2], F32, "mv", 2); rsR = Rot(P, [128, 1], F32, "rs", 2)
        xnR = Rot(P, [128, D], BF16, "xn", 2)
        h2T = P.sb([128, 8, 512], BF16, "h2T")
        def load_blk(blk):
            (c0, n, is_ctx, r0) = blk
            bufs = inR()
            yT_, sa_, sc_, gates_ = bufs
            for j in range(4):
                P.dma(yT_[:, j, 0:n], G.ysD.v(G.ysD.ap[j * 128:(j + 1) * 128, c0:c0 + n]))
                P.dma(sa_[:, j, 0:n], G.yaD.v(G.yaD.ap[j * 128:(j + 1) * 128, c0:c0 + n]))
                P.dma(sc_[:, j, 0:n], G.ycD.v(G.ycD.ap[j * 128:(j + 1) * 128, c0:c0 + n]))
            return bufs
        blks_ = own_blocks(last)
        nxt_bufs = load_blk(blks_[0])
        for bi_, (c0, n, is_ctx, r0) in enumerate(blks_):
            tix = 1 if is_ctx else 0
            yT, srcs[0], srcs[2], gates = nxt_bufs
            for gi in range(24):
                P.dma(gates[:, gi, 0:n], G.gD.v(G.gD.ap[gi * 128:(gi + 1) * 128, c0:c0 + n]))
            nxt_bufs = load_blk(blks_[bi_ + 1]) if bi_ + 1 < len(blks_) else None
            P.tt(t1[:, :, 0:n], yT[:, :, 0:n], yT[:, :, 0:n], ALU.mult)
            P.ts(t1[:, :, 0:n], t1[:, :, 0:n], 0.044715, 1.0, op0=ALU.mult, op1=ALU.add)
            P.tt(t1[:, :, 0:n], t1[:, :, 0:n], yT[:, :, 0:n], ALU.mult)
            P.act(t1[:, :, 0:n], t1[:, :, 0:n], AF.Sigmoid, scale=1.5957691216)
            ge = P.sb([128, 4, 512], BF16, "ge") if c0 == 0 else ge
            P.tt(ge[:, :, 0:n], t1[:, :, 0:n], yT[:, :, 0:n], ALU.mult)
            for op_ in range(2):
                pa = [P.ps(), P.ps()]; pg = [P.ps(), P.ps()]
                items = []
                for q in range(2):
                    oc = op_ * 2 + q
                    items.append((pa[q][:, 0:n], (lambda k, oc=oc: wglu[:, k, oc * 128:(oc + 1) * 128]), lambda k: ge[:, k, 0:n]))
                    items.append((pg[q][:, 0:n], (lambda k, oc=oc: wglu[:, k, 512 + oc * 128:512 + (oc + 1) * 128]), lambda k: ge[:, k, 0:n]))
                mmg(P, items, 4)
                for q in range(2):
                    oc = op_ * 2 + q
                    s = sg()
                    P.act(s[:, 0:n], pg[q][:, 0:n], AF.Sigmoid)
                    P.tt(srcs[1][:, oc, 0:n], pa[q][:, 0:n], s[:, 0:n], ALU.mult)
            for oc in range(8):
                a = acc()
                pbs = [P.ps() for _ in range(3)]
                mmg(P, [(pbs[br][:, 0:n], (lambda k, br=br: wb[br][:, k, oc * 128:(oc + 1) * 128]), (lambda k, br=br: srcs[br][:, k, 0:n])) for br in range(3)], 4)
                for br in range(3):
                    pb = pbs[br]
                    if br == 0:
                        P.tt(a[:, 0:n], pb[:, 0:n], gates[:, br * 8 + oc, 0:n], ALU.mult)
                    else:
                        tm = tmpm()
                        P.tt(tm[:, 0:n], pb[:, 0:n], gates[:, br * 8 + oc, 0:n], ALU.mult)
                        if br == 1:
                            P.tt(a[:, 0:n], a[:, 0:n], tm[:, 0:n], ALU.add, eng="pool")
                        else:
                            P.tt(merged[:, oc, 0:n], a[:, 0:n], tm[:, 0:n], ALU.add, eng="pool")
            def mix(ti):
                xt = xtR(); ts_ = tsR()
                P.dma(xt, (I["ctx"] if is_ctx else I["x_own"]).rows(r0 + ti * 128))
                pms = [P.ps(), P.ps()]
                mmg(P, [(pms[half][:, 0:512], lambda k: merged[:, k, ti * 128:(ti + 1) * 128],
                         (lambda k, half=half: wout[:, k, half * 512:(half + 1) * 512])) for half in range(2)], 8)
                for half in range(2):
                    P.tt(ts_[:, half * 512:(half + 1) * 512], pms[half][:, 0:512], g1b[tix][:, half * 512:(half + 1) * 512], ALU.mult)
                P.stt(ts_, xt, ALPHA, ts_, ALU.mult, ALU.add)
                return ts_

            def chain(ti, ts_):
                st = stR(); mv = mvR(); rs = rsR(); xm = xmR()
                for hh in range(2):
                    P.bn_stats(st[:, hh, :], ts_[:, hh * 512:(hh + 1) * 512])
                P.bn_aggr(mv, st)
                rstd_from_ms(P, rs, mv[:, 1:2], 1)
                P.stt(xm, ts_, mv[:, 0:1], lng, ALU.subtract, ALU.mult)
                P.stt(xm, xm, rs, lnb, ALU.mult, ALU.add)
                P.dma(G.xmidD.v(G.xmidD.ap[c0 + ti * 128:c0 + (ti + 1) * 128, :]), xm)
                st = stR(); mv = mvR(); rs = rsR(); xn = xnR()
                for hh in range(2):
                    P.bn_stats(st[:, hh, :], xm[:, hh * 512:(hh + 1) * 512])
                P.bn_aggr(mv, st)
                rstd_from_ms(P, rs, mv[:, 1:2], 1)
                P.ts(xn, xm, mv[:, 0:1], rs, op0=ALU.subtract, op1=ALU.mult)
                return xn

            def xpose(ti, xn):
                ps = P.ps()
                psb = ps.v(ps.ap.bitcast(BF16))
                for kc in range(8):
                    P.transpose(psb[:, kc * 128:(kc + 1) * 128], xn[:, kc * 128:(kc + 1) * 128], G.ident)
                for kc in range(8):
                    P.act(h2T[:, kc, ti * 128:(ti + 1) * 128], psb[:, kc * 128:(kc + 1) * 128], AF.Identity,
                          bias=G.modT[:, 3 * 8 + kc, tix:tix + 1], scale=G.modT[:, 4 * 8 + kc, tix:tix + 1])
            nti = n // 128
            ts_cur = mix(0)
            for ti in range(nti):
                ts_nxt = mix(ti + 1) if ti + 1 < nti else None
                xn = chain(ti, ts_cur)
                xpose(ti, xn)
                ts_cur = ts_nxt
            for kc in range(8):
                P.dma(G.h2D.v(G.h2D.ap[kc * 128:(kc + 1) * 128, c0:c0 + n]), h2T[:, kc, 0:n])


def stage_B4b(P, I, G, last, out_own, out_ctx):
    NQ = G.NQ
    NT = NQ // 128
    with P.scope():
        g2b = [P.sb([128, D], F32, f"g2b{t}") for t in range(2)]
        for t in range(2):
            P.dma(g2b[t], dT(G.modD.ap[t, 5 * D:6 * D].partition_broadcast(128), "modD_r"))
        lng = P.sb([128, D], F32, "lng"); bcast_load(P, lng, I["ln2_g"].ap)
        lnb = P.sb([128, D], F32, "lnb"); bcast_load(P, lnb, I["ln2_b"].ap)
        h2T = P.sb([128, 8, NQ], BF16, "h2Tall")
        for kc in range(8):
            P.dma(h2T[:, kc, :], G.h2D.v(G.h2D.ap[kc * 128:(kc + 1) * 128, :]))
        tsum = P.sb([128, NT, D], F32, "tsum2")
        wuR = Rot(P, [128, 8, 512], BF16, "wu", 2)
        wdR = Rot(P, [128, 4, D], BF16, "wd", 2)
        stg = Rot(P, [128, 1024], F32, "stg", 3)
        aR = Rot(P, [128, 4, 512], BF16, "aog", 3)
        rl = Rot(P, [128, 512], BF16, "rl", 4)
        xmR = Rot(P, [128, D], F32, "xm", 3)
        stR = Rot(P, [128, 2, 6], F32, "bnst", 3); mvR = Rot(P, [128, 2], F32, "mv", 3); rsR = Rot(P, [128, 1], F32, "rs", 3)
        def ep_a(tile, c0, ti, tix):
            xm = xmR()
            P.dma(xm, G.xmidD.v(G.xmidD.ap[c0 + ti * 128:c0 + (ti + 1) * 128, :]))
            P.tt(tsum[:, tile, :], tsum[:, tile, :], g2b[tix], ALU.mult, eng="pool")
            P.stt(tsum[:, tile, :], xm, ALPHA, tsum[:, tile, :], ALU.mult, ALU.add)
            st = stR(); mv = mvR(); rs = rsR()
            for hh in range(2):
                P.bn_stats(st[:, hh, :], tsum[:, tile, hh * 512:(hh + 1) * 512])
            P.bn_aggr(mv, st)
            rstd_from_ms(P, rs, mv[:, 1:2], 1)
            return (xm, mv, rs)

        def ep_b(tile, r0, ti, is_ctx, stt_):
            xm, mv, rs = stt_
            o = xm
            P.stt(o, tsum[:, tile, :], mv[:, 0:1], lng, ALU.subtract, ALU.mult)
            P.stt(o, o, rs, lnb, ALU.mult, ALU.add)
            dst = out_ctx if is_ctx else out_own
            P.dma(dst.rows(r0 + ti * 128), o)

        def epilogue(blk):
            (c0, n, is_ctx, r0) = blk
            tix = 1 if is_ctx else 0
            nti = n // 128
            cur = ep_a(c0 // 128, c0, 0, tix)
            for ti in range(nti):
                nxt = ep_a(c0 // 128 + ti + 1, c0, ti + 1, tix) if ti + 1 < nti else None
                ep_b(c0 // 128 + ti, r0, ti, is_ctx, cur)
                cur = nxt

        for og in range(8):
            wu = wuR(); wd = wdR()
            for kc in range(8):
                wload(P, stg, wu[:, kc, :], dT(I["w_up"].ap[kc * 128:(kc + 1) * 128, og * 512:(og + 1) * 512], "w"), engs=("pool", "act"))
            for oc in range(4):
                wload(P, stg, wd[:, oc, :], dT(I["w_down"].ap[og * 512 + oc * 128:og * 512 + (oc + 1) * 128, :], "w"), engs=("pool", "act"))
            blks = own_blocks(last)

            def up(blk):
                (c0, n, is_ctx, r0) = blk
                a = aR()
                pus = [P.ps() for _ in range(4)]
                mmg(P, [(pus[oc][:, 0:n], (lambda k, oc=oc: wu[:, k, oc * 128:(oc + 1) * 128]), lambda k: h2T[:, k, c0:c0 + n]) for oc in range(4)], 8)
                for oc in range(4):
                    r = rl()
                    P.act(r[:, 0:n], pus[oc][:, 0:n], AF.Relu)
                    P.tt(a[:, oc, 0:n], pus[oc][:, 0:n], r[:, 0:n], ALU.mult)
                return a

            def down(blk, a):
                (c0, n, is_ctx, r0) = blk
                combos = [(ti, half) for ti in range(n // 128) for half in range(2)]
                for g0 in range(0, len(combos), 4):
                    grp = combos[g0:g0 + 4]
                    pds = [P.ps() for _ in grp]
                    mmg(P, [(pds[i][:, 0:512], (lambda k, ti=ti: a[:, k, ti * 128:(ti + 1) * 128]),
                             (lambda k, half=half: wd[:, k, half * 512:(half + 1) * 512])) for i, (ti, half) in enumerate(grp)], 4)
                    for i, (ti, half) in enumerate(grp):
                        tile = c0 // 128 + ti
                        dst = tsum[:, tile, half * 512:(half + 1) * 512]
                        if og == 0:
                            P.copy(dst, pds[i][:, 0:512], eng="act")
                        else:
                            P.tt(dst, pds[i][:, 0:512], dst, ALU.add)
            a_cur = up(blks[0])
            for bi, blk in enumerate(blks):
                a_nxt = up(blks[bi + 1]) if bi + 1 < len(blks) else None
                down(blk, a_cur)
                a_cur = a_nxt
                if og == 7:
                    epilogue(blk)


def bc(t, pattern):
    a = t.ap
    return t.v(bass.AP(a.tensor, a.offset, [list(a.ap[0])] + [list(p) for p in pattern]))


MAGIC = 12582912.0
TWO_PI = 6.283185307179586


def s5_load(P, I, G):
    Lt = Ctx()
    Lt.are = P.sb([128, 64], F32, "are"); Lt.aim = P.sb([128, 64], F32, "aim"); Lt.ldt = P.sb([128, 64], F32, "ldt")
    Lt.Bre = P.sb([128, 64, 16], F32, "Bre"); Lt.Bim = P.sb([128, 64, 16], F32, "Bim")
    Lt.Cre = P.sb([128, 64, 16], F32, "Cre"); Lt.Cim = P.sb([128, 64, 16], F32, "Cim")
    for hf in range(2):
        sl = slice(hf * 64, (hf + 1) * 64)
        P.dma(Lt.are[sl, :], dT(I["s5_a_re"].ap.rearrange("d g p -> p (d g)"), "a"), allow_slow_non_contiguous=True, eng="act")
        P.dma(Lt.aim[sl, :], dT(I["s5_a_im"].ap.rearrange("d g p -> p (d g)"), "a"), allow_slow_non_contiguous=True, eng="act")
        P.dma(Lt.Bre[sl], dT(I["s5_b_re"].ap.rearrange("d g p c -> p (d g) c"), "a"))
        P.dma(Lt.Bim[sl], dT(I["s5_b_im"].ap.rearrange("d g p c -> p (d g) c"), "a"))
        P.dma(Lt.Cre[sl], dT(I["s5_c_re"].ap.rearrange("d g c p -> p (d g) c"), "a"), allow_slow_non_contiguous=True, eng="act")
        P.dma(Lt.Cim[sl], dT(I["s5_c_im"].ap.rearrange("d g c p -> p (d g) c"), "a"), allow_slow_non_contiguous=True, eng="act")
    P.dma(Lt.ldt, dT(I["s5_log_dt"].ap.rearrange("d g -> (d g)").partition_broadcast(128), "a"))
    G.s5l = Lt


def s5_alloc(P, I, G):
    PR = P.sb([128, 32, 64], F32, "PR"); NPI = P.sb([128, 32, 64], F32, "NPI")
    DA = P.sb([128, 10, 64], F32, "DA"); DB = P.sb([128, 10, 64], F32, "DB")
    BX1 = P.sb([128, 64, 16], F32, "BX1"); BX2 = P.sb([128, 64, 16], F32, "BX2")
    CX1 = P.sb([128, 64, 16], F32, "CX1"); CX2 = P.sb([128, 64, 16], F32, "CX2")
    Dcol = P.sb([128, 32], F32, "Dcol")
    identF = G.ident
    swapF = P.sb([128, 128], BF16, "swapF"); P.dma(swapF, I["swap"], eng="pool")
    maskf = P.sb([128, 128], BF16, "maskf"); P.dma(maskf, I["mask_f"], eng="pool")
    maskb = P.sb([128, 128], BF16, "maskb"); P.dma(maskb, I["mask_b"], eng="pool")
    sgn = P.sb([128, 1], F32, "sgn"); P.memset(sgn[0:64, :], 1.0); P.memset(sgn[64:128, :], -1.0)
    for i in range(8):
        P.dma(Dcol[i * 16:(i + 1) * 16, :], dT(I["s5_d"].ap.rearrange("(g c) -> c g", c=16), "s5d"), allow_slow_non_contiguous=True)
    G.s5t = dict(PR=PR, NPI=NPI, DA=DA, DB=DB, BX1=BX1, BX2=BX2, CX1=CX1, CX2=CX2, Dcol=Dcol, identF=identF, swapF=swapF, maskf=maskf, maskb=maskb, sgn=sgn)


def s5_params(P, I, G):
    PR = G.s5t["PR"]
    NPI = G.s5t["NPI"]
    DA = G.s5t["DA"]
    DB = G.s5t["DB"]
    BX1 = G.s5t["BX1"]
    BX2 = G.s5t["BX2"]
    CX1 = G.s5t["CX1"]
    CX2 = G.s5t["CX2"]
    Dcol = G.s5t["Dcol"]
    identF = G.s5t["identF"]
    swapF = G.s5t["swapF"]
    maskf = G.s5t["maskf"]
    maskb = G.s5t["maskb"]
    sgn = G.s5t["sgn"]
    with P.scope():
        are, aim, ldt = G.s5l.are, G.s5l.aim, G.s5l.ldt
        dt_ = P.sb([128, 64], F32, "dt")
        P.act(dt_, ldt, AF.Exp)
        lr = P.sb([128, 64], F32, "lr"); li = P.sb([128, 64], F32, "li")
        P.tt(lr, are, dt_, ALU.mult); P.tt(li, aim, dt_, ALU.mult)
        with P.scope():
            elist = [t - 7 for t in range(16)] + [8 - t for t in range(16)]
            LR = P.sb([128, 32, 64], F32, "LR"); LI = P.sb([128, 32, 64], F32, "LI")
            for idx, e in enumerate(elist):
                P.ts(LR[:, idx, :], lr, float(e), None, op0=ALU.mult)
                P.ts(LI[:, idx, :], li, float(e), None, op0=ALU.mult, eng="pool")
            mag = P.sb([128, 32, 64], F32, "mag")
            P.act(mag, LR, AF.Exp)
            rr = P.sb([128, 32, 64], F32, "rr"); kk = P.sb([128, 32, 64], F32, "kk")

            def sin_of(dst, ang_t, shift):
                P.ts(rr, ang_t, 1.0 / TWO_PI, shift / TWO_PI, op0=ALU.mult, op1=ALU.add)
                P.ts(kk, rr, MAGIC, None, op0=ALU.add)
                P.ts(kk, kk, MAGIC, None, op0=ALU.subtract)
                P.tt(rr, rr, kk, ALU.subtract)
                P.ts(rr, rr, TWO_PI, None, op0=ALU.mult)
                P.ts(rr, rr, 3.1415925, -3.1415925, op0=ALU.min, op1=ALU.max)
                P.act(dst, rr, AF.Sin)
            sn = LR
            sin_of(sn, LI, 0.0)
            P.stt(NPI, mag, -1.0, sn, ALU.mult, ALU.mult)
            sin_of(sn, LI, TWO_PI / 4)
            P.tt(PR, mag, sn, ALU.mult)
        cr_ = P.sb([128, 64], F32, "cr"); ci_ = P.sb([128, 64], F32, "ci")
        t1 = P.sb([128, 64], F32, "t1"); t2 = P.sb([128, 64], F32, "t2")
        P.copy(DA[:, 0, :], PR[:, 15, :])
        P.ts(DB[:, 0, :], NPI[:, 15, :], -1.0, None, op0=ALU.mult)
        for m in range(1, 10):
            P.tt(t1, DA[:, m - 1, :], DA[:, m - 1, :], ALU.mult)
            P.tt(t2, DB[:, m - 1, :], DB[:, m - 1, :], ALU.mult)
            P.tt(DA[:, m, :], t1, t2, ALU.subtract)
            P.stt(DB[:, m, :], DA[:, m - 1, :], 2.0, DB[:, m - 1, :], ALU.mult, ALU.mult)
        P.ts(DB, DB, sgn[:, 0:1], None, op0=ALU.mult)
        den = P.sb([128, 64], F32, "den"); nr = P.sb([128, 64], F32, "nr"); abi = P.sb([128, 64], F32, "abi")
        P.tt(t1, are, are, ALU.mult); P.tt(t2, aim, aim, ALU.mult); P.tt(den, t1, t2, ALU.add); P.recip(den, den)
        P.ts(nr, PR[:, 8, :], -1.0, None, op0=ALU.add)
        P.ts(abi, NPI[:, 8, :], -1.0, None, op0=ALU.mult)
        P.tt(t1, nr, are, ALU.mult); P.tt(t2, abi, aim, ALU.mult); P.tt(cr_, t1, t2, ALU.add); P.tt(cr_, cr_, den, ALU.mult)
        P.tt(t1, abi, are, ALU.mult); P.tt(t2, nr, aim, ALU.mult); P.tt(ci_, t1, t2, ALU.subtract); P.tt(ci_, ci_, den, ALU.mult)
        crb = bc(cr_, [[1, 64], [0, 16]]); cib = bc(ci_, [[1, 64], [0, 16]])
        Bre, Bim, Cre, Cim = G.s5l.Bre, G.s5l.Bim, G.s5l.Cre, G.s5l.Cim
        bbr = P.sb([128, 64, 16], F32, "bbr"); bbi = P.sb([128, 64, 16], F32, "bbi"); t3 = P.sb([128, 64, 16], F32, "t3")
        P.tt(bbr, Bre, crb, ALU.mult); P.tt(t3, Bim, cib, ALU.mult); P.tt(bbr, bbr, t3, ALU.subtract)
        P.tt(bbi, Bim, crb, ALU.mult); P.tt(t3, Bre, cib, ALU.mult); P.tt(bbi, bbi, t3, ALU.add)
        P.copy(BX1[0:64], bbr[0:64]); P.copy(BX1[64:128], bbi[64:128])
        P.copy(BX2[0:64], bbi[0:64]); P.ts(BX2[64:128], bbr[64:128], -1.0, None, op0=ALU.mult)
        P.copy(CX1[0:64], Cre[0:64]); P.ts(CX1[64:128], Cim[64:128], -1.0, None, op0=ALU.mult)
        P.copy(CX2[0:64], Cim[0:64]); P.copy(CX2[64:128], Cre[64:128])


def stage_S5(P, I, G, last):
    NQ = LH + (0 if last else CTX)
    G.ysD = P.dram([512, NQ], BF16, "ysD")
    with P.scope():
        PR = G.s5t["PR"]
        NPI = G.s5t["NPI"]
        DA = G.s5t["DA"]
        DB = G.s5t["DB"]
        BX1 = G.s5t["BX1"]
        BX2 = G.s5t["BX2"]
        CX1 = G.s5t["CX1"]
        CX2 = G.s5t["CX2"]
        Dcol = G.s5t["Dcol"]
        identF = G.s5t["identF"]
        swapF = G.s5t["swapF"]
        maskf = G.s5t["maskf"]
        maskb = G.s5t["maskb"]
        sgn = G.s5t["sgn"]
        Sel = P.sb([128, 64, 128], BF16, "Sel"); SelT = P.sb([128, 64, 128], BF16, "SelT")
        for q in range(4):
            P.dma(Sel[:, q * 16:(q + 1) * 16, :], dT(I["sel8"].ap[:, q * 16:(q + 1) * 16, :], "sel8"), eng="pool")
            P.dma(SelT[:, q * 16:(q + 1) * 16, :], dT(I["sel8T"].ap[:, q * 16:(q + 1) * 16, :], "sel8T"), eng="pool")
        KQ = {nm: Rot(P, [128, 16, 16], BF16, nm, 2) for nm in ["Kf", "Qf", "Kb", "Qb"]}
        tA = Rot(P, [128, 16, 16], F32, "tA", 1); tB = Rot(P, [128, 16, 16], F32, "tB", 1)
        tC = Rot(P, [128, 16, 16], F32, "tC", 1); tD = Rot(P, [128, 16, 16], F32, "tD", 1)
        SgR = Rot(P, [128, 128], BF16, "Sg", 2)
        WeR = Rot(P, [128, 128], BF16, "We", 4)
        s1R = Rot(P, [128, 128], F32, "s1", 1); s2R = Rot(P, [128, 128], F32, "s2", 1)
        UcR = Rot(P, [128, 576], BF16, "Uc", 2)
        XR = {d: [P.sb([128, 545], BF16, f"X{d}{i}") for i in range(2)] for d in "fb"}
        for d in "fb":
            for x in XR[d]:
                P.memset(x, 0.0)
        MdR = {d: Rot(P, [128, 10, 128], BF16, "Md" + d, 2) for d in "fb"}
        mA = Rot(P, [128, 10, 128], BF16, "mA", 1); mB = Rot(P, [128, 10, 128], BF16, "mB", 1)
        Yt = P.sb([128, 8, 544], BF16, "Yt")
        tqR = Rot(P, [128, 256], F32, "tq", 2); yctx = P.sb([128, CTX], BF16, "yctx")
        yown = Rot(P, [128, LH], BF16, "yown", 1)
        identB = G.ident
        flip = 0
        spec = {"Kf": (17, 15), "Qf": (7, 9), "Kb": (7, 8), "Qb": (16, 16)}

        def m128(t, lo):
            return t.v(t.ap[:, lo:lo + 8, :].rearrange("p a b -> p (a b)"))

        def build(g):
            stt_ = {"mats": {}, "We": {}, "Mall": {}}
            th = []
            for dname, gd in (("f", g), ("b", 32 + g)):
                for kind, X1, X2, eng in (("K", BX1, BX2, "dve"), ("Q", CX1, CX2, "pool")):
                    t0_, ne = spec[kind + dname]
                    out = KQ[kind + dname]()[:, 0:ne, :]
                    a_ = (tA() if kind == "K" else tC())[:, 0:ne, :]
                    b_ = (tB() if kind == "K" else tD())[:, 0:ne, :]
                    prb = bc(PR[:, t0_, gd:gd + 1], [[64, ne], [0, 16]]); npb = bc(NPI[:, t0_, gd:gd + 1], [[64, ne], [0, 16]])
                    x1b = bc(X1[:, gd, :], [[0, ne], [1, 16]]); x2b = bc(X2[:, gd, :], [[0, ne], [1, 16]])
                    th.append(lambda a_=a_, prb=prb, x1b=x1b, eng=eng: P.tt(a_, prb, x1b, ALU.mult, eng=eng))
                    th.append(lambda b_=b_, npb=npb, x2b=x2b, eng=eng: P.tt(b_, npb, x2b, ALU.mult, eng=eng))
                    th.append(lambda out=out, a_=a_, b_=b_, eng=eng: P.tt(out, a_, b_, ALU.add, eng=eng))
                    stt_["mats"][kind + dname] = out
            Sg = SgR()
            stt_["Sg"] = Sg
            mt = stt_["mats"]

            def th_S():
                psf = P.ps(); psb_ = P.ps()
                P.mm(psf[:, 0:128], m128(mt["Kf"], 7), m128(mt["Qf"], 0))
                P.mm(psb_[:, 0:128], m128(mt["Kb"], 0), m128(mt["Qb"], 8))
                s1 = s1R(); s2 = s2R()
                P.tt(s1, psf[:, 0:128], maskf, ALU.mult)
                P.tt(s2, psb_[:, 0:128], maskb, ALU.mult)
                P.tt(s1, s1, s2, ALU.add)
                P.stt(Sg, identF, Dcol[:, g:g + 1], s1, ALU.mult, ALU.add)
            th.append(th_S)
            for dname, key in (("f", "Kf"), ("b", "Kb")):
                w = WeR()
                stt_["We"][dname] = w

                def th_W(w=w, key=key):
                    pt = P.ps()
                    ptb = pt.v(pt.ap.bitcast(BF16))
                    P.transpose(ptb[:, 0:128], m128(mt[key], 0), identB)
                    P.copy(w, ptb[:, 0:128], eng="act")
                th.append(th_W)
            stt_["Wo"] = {"f": m128(mt["Qf"], 1), "b": m128(mt["Qb"], 0)}
            for dname, gd in (("f", g), ("b", 32 + g)):
                Mall = MdR[dname]()
                stt_["Mall"][dname] = Mall
                ta = mA(); tb = mB()
                idb = bc(identF, [[0, 10], [1, 128]]); swb = bc(swapF, [[0, 10], [1, 128]])
                dab = bc(DA[:, :, gd], [[64, 10], [0, 128]]); dbb = bc(DB[:, :, gd], [[64, 10], [0, 128]])
                th.append(lambda ta=ta, idb=idb, dab=dab: P.tt(ta, idb, dab, ALU.mult, eng="pool"))
                th.append(lambda tb=tb, swb=swb, dbb=dbb: P.tt(tb, swb, dbb, ALU.mult, eng="pool"))
                th.append(lambda Mall=Mall, ta=ta, tb=tb: P.tt(Mall, ta, tb, ALU.add, eng="pool"))
            return stt_, th

        cur_state, th0 = build(0)
        for t_ in th0:
            t_()
        for g in range(32):
            j, gl = g // 8, g % 8
            if g + 1 < 32:
                nxt_state, pend = build(g + 1)
            else:
                nxt_state, pend = None, []
            per_step = (len(pend) + 9) // 10
            Sg = cur_state["Sg"]; We = cur_state["We"]; Wo = cur_state["Wo"]
            Uc = UcR()
            puA = P.ps(); puB = P.ps(); pu2 = P.ps()
            for i in range(8):
                P.mm(puA[:, 0:256], Sel[:, gl * 8 + i, :], G.uT[:, j, i, 32:288], start=(i == 0), stop=(i == 7))
                P.mm(puB[:, 0:256], Sel[:, gl * 8 + i, :], G.uT[:, j, i, 288:544], start=(i == 0), stop=(i == 7))
                P.mm(pu2[:, 0:32], Sel[:, gl * 8 + i, :], G.uT[:, j, i, 0:32], start=(i == 0), stop=(i == 7))
            P.copy(Uc[:, 32:288], puA[:, 0:256], eng="act")
            P.copy(Uc[:, 288:544], puB[:, 0:256])
            P.copy(Uc[:, 0:32], pu2[:, 0:32], eng="act")
            P.copy(Uc[:, 544:576], pu2[:, 0:32])
            st_ = {}
            for dname, gd in (("f", g), ("b", 32 + g)):
                ucoff = 0 if dname == "f" else 32
                xoff = 1 if dname == "f" else 0
                cur = XR[dname][0]; nxt = XR[dname][1]
                pa = P.ps(); pb2 = P.ps()
                P.mm(pa[:, 0:512], We[dname], Uc[:, ucoff:ucoff + 512])
                P.mm(pb2[:, 0:32], We[dname], Uc[:, ucoff + 512:ucoff + 544])
                P.copy(cur[:, xoff:xoff + 512], pa[:, 0:512], eng="act")
                P.copy(cur[:, xoff + 512:xoff + 544], pb2[:, 0:32])
                st_[dname] = [cur, nxt, xoff, cur_state["Mall"][dname]]
            for m in range(10):
                d = 1 << m
                work = []
                for dname in ("f", "b"):
                    cur, nxt, xoff, Mall = st_[dname]
                    for (lo, hi) in ((0, 272), (272, 544)):
                        ps = P.ps()
                        if dname == "f":
                            s_ = max(lo, d)
                            has = s_ < hi
                            shift = (ps[:, s_ - lo:hi - lo], cur[:, xoff + s_ - d:xoff + hi - d]) if has else None
                        else:
                            e_ = min(hi, 544 - d)
                            has = lo < e_
                            shift = (ps[:, 0:e_ - lo], cur[:, xoff + lo + d:xoff + e_ + d]) if has else None
                        work.append((dname, ps, lo, hi, shift, cur, nxt, xoff, Mall))
                for (dname, ps, lo, hi, shift, cur, nxt, xoff, Mall) in work:
                    P.mm(ps[:, 0:hi - lo], identB, cur[:, xoff + lo:xoff + hi], start=True, stop=(shift is None))
                for (dname, ps, lo, hi, shift, cur, nxt, xoff, Mall) in work:
                    if shift is not None:
                        P.mm(shift[0], Mall[:, m, :], shift[1], start=False, stop=True)
                for (dname, ps, lo, hi, shift, cur, nxt, xoff, Mall) in work:
                    flip ^= 1
                    P.copy(nxt[:, xoff + lo:xoff + hi], ps[:, 0:hi - lo], eng=("act" if flip else "dve"))
                for dname in ("f", "b"):
                    st_[dname][0], st_[dname][1] = st_[dname][1], st_[dname][0]
                for _ in range(per_step):
                    if pend:
                        pend.pop(0)()
            while pend:
                pend.pop(0)()
            Xfin = {dname: st_[dname][0] for dname in ("f", "b")}
            Xf, Xb = Xfin["f"], Xfin["b"]
            py = P.ps()
            pyc = P.ps() if not last else None
            ytl = [(Sg, Uc[:, 32:544], Uc[:, 0:32]), (Wo["f"], Xf[:, 32:544], Xf[:, 0:32]), (Wo["b"], Xb[:, 1:513], Xb[:, 513:545])]
            for q, (lh, r1, r2) in enumerate(ytl):
                P.mm(py[:, 0:512], lh, r1, start=(q == 0), stop=(q == 2))
                if not last:
                    P.mm(pyc[:, 0:32], lh, r2, start=(q == 0), stop=(q == 2))
            P.copy(Yt[:, gl, 0:512], py[:, 0:512], eng="act")
            if not last:
                P.copy(Yt[:, gl, 512:544], pyc[:, 0:32])
            cur_state = nxt_state
            if gl == 7:
                yo = yown()
                for i in range(8):
                    ps = P.ps()
                    ps2 = P.ps() if not last else None
                    for g2 in range(8):
                        P.mm(ps[:, 0:512], SelT[:, g2 * 8 + i, :], Yt[:, g2, 0:512], start=(g2 == 0), stop=(g2 == 7))
                        if not last:
                            P.mm(ps2[:, 0:32], SelT[:, g2 * 8 + i, :], Yt[:, g2, 512:544], start=(g2 == 0), stop=(g2 == 7))
                    tq = tqR()
                    P.act(tq, ps[:, 0:256], AF.Copy, scale=G.sel[:, 0:1])
                    P.stt(yo.v(yo.ap[:, i:LH:8]), ps[:, 256:512], G.sel[:, 1:2], tq, ALU.mult, ALU.add)
                    if not last:
                        P.copy(yctx.v(yctx.ap[:, i:CTX:8]), ps2[:, 0:32])
                P.dma(G.ysD.v(G.ysD.ap[j * 128:(j + 1) * 128, 0:LH]), yo)
                if not last:
                    P.dma(G.ysD.v(G.ysD.ap[j * 128:(j + 1) * 128, LH:LH + CTX]), yctx)


def emit_layer(P, I, G, last, out_own, out_ctx):
    stage_prep(P, I, G)
    with P.scope():
        alloc_persist(P, G)
        with P.scope():
            G.uT = P.sb([128, 4, 8, NK // 8], BF16, "uT")
            s5_alloc(P, I, G)
            with P.scope():
                s5_load(P, I, G)
                stage_A(P, I, G)
                s5_params(P, I, G)
            stage_S5(P, I, G, last)
        stage_B1(P, I, G, last)
        stage_B2(P, I, G, last)
        stage_B3(P, I, G, last)
    stage_B4a(P, I, G, last)
    stage_B4b(P, I, G, last, out_own, out_ctx)
    P.flush()


def build_fused():
    nc = bass.Bass("TRN2", target_bir_lowering=False)
    Cn = declare_consts(nc)
    W = [declare_weights(nc, l) for l in range(2)]
    y_out = dT(nc.dram_tensor("y_own", [LH, D], F32, kind="ExternalOutput").ap(), "y_own")
    with ExitStack() as st:
        P = Prog(nc, st)
        P.init_psum()
        NCH = 4
        CR = LH // NCH
        x1o = [P.dram([CR, D], F32, f"x1o{c}") for c in range(NCH)]
        x1g = [P.dram([2 * CR, D], F32, f"x1g{c}") for c in range(NCH)]
        ctx1 = P.dram([CTX, D], F32, "ctx1")
        own_src = RowSrc(lambda r0: x1o[r0 // CR].v(x1o[r0 // CR].ap[r0 % CR:r0 % CR + 128, :]))

        def all_fn(r0):
            half, rr = r0 // LH, r0 % LH
            c, i = rr // CR, rr % CR
            return x1g[c].v(x1g[c].ap[half * CR + i:half * CR + i + 128, :])
        with P.scope():
            G = Ctx()
            I0 = dict(Cn); I0.update(W[0])
            for k in ("x_all", "x_own", "ctx"):
                I0[k] = flat_src(Cn[k])
            emit_layer(P, I0, G, False, own_src, flat_src(ctx1))
        groups = [[0, 1], [2, 3], [4, 5], [6, 7]]
        for c in range(NCH):
            P.add("pool", lambda e, c=c: e.collective_compute("AllGather", ALU.bypass, replica_groups=groups,
                                                              ins=[x1o[c].ap.opt()], outs=[x1g[c].ap.opt()]), [x1o[c]], [x1g[c]])
            P.add("pool", None, [x1g[c]], [])
        P.flush()
        with P.scope():
            G = Ctx()
            I1 = dict(Cn); I1.update(W[1])
            I1["x_all"] = RowSrc(all_fn); I1["x_own"] = own_src; I1["ctx"] = flat_src(ctx1)
            emit_layer(P, I1, G, True, flat_src(y_out), None)
    return nc


_NC_CACHE = {}


def kernel(**inputs):
    inputs = {k: np.asarray(v) for k, v in inputs.items()}
    C = host_constants()
    if "nc" not in _NC_CACHE:
        _NC_CACHE["nc"] = build_fused()
    nc = _NC_CACHE["nc"]
    in_maps = [per_core_inputs(inputs, core, C) for core in range(8)]
    res = run_bass_kernel_spmd(nc, in_maps, core_ids=list(range(8)))
    out = np.empty((4, L, D), np.float32)
    for core in range(8):
        b, hh = core // 2, core % 2
        out[b, hh * LH:(hh + 1) * LH] = np.asarray(res.results[core]["y_own"])
    return out
```

```python
import numpy as np
import concourse.bass as bass
import concourse.mybir as mybir
from concourse.bass_utils import run_bass_kernel_spmd
from contextlib import ExitStack, contextmanager

F32 = mybir.dt.float32
BF16 = mybir.dt.bfloat16
I32 = mybir.dt.int32
AF = mybir.ActivationFunctionType
ALU = mybir.AluOpType
AX = mybir.AxisListType

ENGS = ["pe", "act", "dve", "pool", "sp"]
DMA_WIN = 8
SAME_ENG_SYNC = True


class T:
    __slots__ = ("ap", "keys")

    def __init__(self, ap, keys):
        self.ap = ap
        self.keys = tuple(keys)

    def __getitem__(self, sl):
        return T(self.ap[sl], self.keys)

    def v(self, ap):
        return T(ap, self.keys)

    def k(self, *sub):
        return T(self.ap, [(self.keys[0],) + tuple(sub)])


class Prog:
    def __init__(self, nc, stack):
        self.nc = nc
        self.stack = stack
        self.cur = stack
        self.streams = {e: [] for e in ENGS}
        self.last_writer = {}
        self.readers = {}
        self.ndma = {e: 0 for e in ENGS}
        self.sigcount = {e: 0 for e in ENGS}
        self.waited = {e: {} for e in ENGS}
        self.nt = 0
        self.psum_banks = []
        self.psum_i = 0
        self.sem = {e: stack.enter_context(nc.semaphore(f"s_{e}")) for e in ENGS}
        self.dsem = {e: [stack.enter_context(nc.semaphore(f"d_{e}{i}")) for i in range(DMA_WIN)]
                     for e in ("sp", "pool", "act")}
        self.dbg = {}
        self.nops = {e: 0 for e in ENGS}

    def sb(self, shape, dt, name=None):
        self.nt += 1
        name = name or "t"
        nm = f"{name}_{self.nt}"
        t = self.cur.enter_context(self.nc.sbuf_tensor(nm, list(shape), dt))
        return T(t[:], [nm])

    def dram(self, shape, dt, name):
        self.nt += 1
        nm = f"{name}_{self.nt}"
        t = self.nc.dram_tensor(nm, list(shape), dt, kind="Internal")
        return T(t.ap(), [nm])

    def init_psum(self, n=8):
        for i in range(n):
            t = self.stack.enter_context(self.nc.psum_tensor(f"bank{i}", [128, 512], F32))
            self.psum_banks.append(T(t[:], [f"bank{i}"]))

    def ps(self):
        b = self.psum_banks[self.psum_i % len(self.psum_banks)]
        self.psum_i += 1
        return b

    @contextmanager
    def scope(self):
        prev = self.cur
        with ExitStack() as st:
            self.cur = st
            yield
            self.flush()
        self.cur = prev

    def add(self, eng, fn, reads=(), writes=(), dma=False):
        deps = set()
        rk = [k for t in reads for k in t.keys]
        wk = [k for t in writes for k in t.keys]
        for k in rk:
            if k in self.last_writer:
                deps.add(self.last_writer[k])
        for k in wk:
            if k in self.last_writer:
                deps.add(self.last_writer[k])
            for r in self.readers.get(k, ()):
                deps.add(r)
        idx = len(self.streams[eng])
        me = (eng, idx)
        deps.discard(me)
        op = dict(fn=fn, deps=deps, dma=dma, signal=False, dman=None)
        if dma:
            op["dman"] = self.ndma[eng]
            self.ndma[eng] += 1
        self.streams[eng].append(op)
        for k in rk:
            self.readers.setdefault(k, []).append(me)
        for k in wk:
            self.last_writer[k] = me
            self.readers[k] = []
        return me

    def dma(self, out, in_, eng="sp", **kw):
        o = out.ap
        i = in_.ap
        return self.add(eng, lambda e: e.dma_start(out=o, in_=i, **kw), [in_], [out], dma=True)

    def mm(self, out, lhsT, rhs, start=True, stop=True, **kw):
        return self.add("pe", lambda e: e.matmul(out.ap, lhsT.ap, rhs.ap, start=start, stop=stop, **kw),
                        [lhsT, rhs], [out])

    def transpose(self, out, in_, ident):
        return self.add("pe", lambda e: e.transpose(out.ap, in_.ap, ident.ap), [in_, ident], [out])

    def act(self, out, in_, func, bias=None, scale=None, eng="act", accum_out=None):
        reads = [in_]
        kw = {}
        if bias is not None:
            if isinstance(bias, T):
                reads.append(bias); kw["bias"] = bias.ap
            else:
                kw["bias"] = bias
        if scale is not None:
            if isinstance(scale, T):
                reads.append(scale); kw["scale"] = scale.ap
            else:
                kw["scale"] = scale
        writes = [out]
        if accum_out is not None:
            kw["accum_out"] = accum_out.ap; writes.append(accum_out)
        return self.add(eng, lambda e: e.activation(out.ap, in_.ap, func, **kw), reads, writes)

    def tt(self, out, a, b, op, eng="dve"):
        return self.add(eng, lambda e: e.tensor_tensor(out.ap, a.ap, b.ap, op), [a, b], [out])

    def ts(self, out, a, s1, s2=None, op0=ALU.mult, op1=None, eng="dve"):
        reads = [a]
        v1 = s1.ap if isinstance(s1, T) else s1
        if isinstance(s1, T): reads.append(s1)
        v2 = s2.ap if isinstance(s2, T) else s2
        if isinstance(s2, T): reads.append(s2)
        if op1 is None:
            return self.add(eng, lambda e: e.tensor_scalar(out.ap, a.ap, v1, None, op0), reads, [out])
        return self.add(eng, lambda e: e.tensor_scalar(out.ap, a.ap, v1, v2, op0, op1), reads, [out])

    def stt(self, out, a, s, b, op0, op1, eng="dve"):
        reads = [a, b]
        v = s.ap if isinstance(s, T) else s
        if isinstance(s, T): reads.append(s)
        return self.add(eng, lambda e: e.scalar_tensor_tensor(out.ap, a.ap, v, b.ap, op0, op1), reads, [out])

    def copy(self, out, in_, eng="dve"):
        if eng == "act":
            return self.add("act", lambda e: e.copy(out.ap, in_.ap), [in_], [out])
        return self.add(eng, lambda e: e.tensor_copy(out.ap, in_.ap), [in_], [out])

    def memset(self, out, val, eng="pool"):
        return self.add(eng, lambda e: e.memset(out.ap, val), [], [out])

    def recip(self, out, in_, eng="dve"):
        return self.add(eng, lambda e: e.reciprocal(out.ap, in_.ap), [in_], [out])

    def recip_fast(self, out, in_):
        return self.add("dve", lambda e: e.reciprocal_approx_fast(out.ap, in_.ap), [in_], [out])

    def bn_stats(self, out, in_):
        return self.add("dve", lambda e: e.bn_stats(out.ap, in_.ap), [in_], [out])

    def bn_aggr(self, out, in_):
        return self.add("dve", lambda e: e.bn_aggr(out.ap, in_.ap), [in_], [out])

    def debug_out(self, name, t, shape, dt=F32):
        d = self.nc.dram_tensor(name, list(shape), dt, kind="ExternalOutput").ap()
        self.dbg[name] = d
        return self.dma(T(d, [name]), t)

    def flush(self):
        nc = self.nc
        streams = self.streams
        lasts = []
        for e in ENGS:
            for j in range(len(streams[e]) - 1, -1, -1):
                op = streams[e][j]
                if op["fn"] is not None and not op["dma"]:
                    lasts.append((e, j))
                    break
        dmas = [(e, j) for e in ENGS for j, op in enumerate(streams[e]) if op["dma"]]
        for e in ENGS:
            deps = set(l for l in lasts if l[0] != e) | set(dmas)
            streams[e].append(dict(fn=None, deps=deps, dma=False, signal=False, dman=None))
        for e in ENGS:
            for op in streams[e]:
                for (f, j) in op["deps"]:
                    d = streams[f][j]
                    if not d["dma"]:
                        if f == e and not SAME_ENG_SYNC:
                            continue
                        d["signal"] = True
        for e in ENGS:
            for op in streams[e]:
                if op["signal"]:
                    self.sigcount[e] += 1
                    op["sigval"] = self.sigcount[e]
        sem, dsem = self.sem, self.dsem

        def run(ename):
            def body(eng):
                waited = self.waited[ename]

                def wait(s, v, key):
                    if waited.get(key, 0) >= v:
                        return
                    waited[key] = v
                    eng.wait_ge(s, v)

                for op in streams[ename]:
                    for (f, j) in sorted(op["deps"]):
                        d = streams[f][j]
                        if d["dma"]:
                            n = d["dman"]
                            wait(dsem[f][n % DMA_WIN], 16 * (n // DMA_WIN + 1), (f, n % DMA_WIN))
                        else:
                            if f == ename and not SAME_ENG_SYNC:
                                continue
                            wait(sem[f], d["sigval"], f)
                    if op["dma"]:
                        n = op["dman"]
                        if n >= DMA_WIN:
                            wait(dsem[ename][n % DMA_WIN], 16 * (n // DMA_WIN), (ename, n % DMA_WIN))
                    if op["fn"] is None:
                        continue
                    ins = op["fn"](eng)
                    if op["dma"]:
                        ins.then_inc(dsem[ename][op["dman"] % DMA_WIN], 16)
                    elif op["signal"]:
                        ins.then_inc(sem[ename], 1)
            return body

        with nc.Block() as block:
            block.tensor(run("pe"))
            block.scalar(run("act"))
            block.vector(run("dve"))
            block.gpsimd(run("pool"))
            block.sync(run("sp"))
        for e in ENGS:
            self.nops[e] += len(streams[e])
        self.streams = {e: [] for e in ENGS}
        self.last_writer = {}
        self.readers = {}


class Rot:
    def __init__(self, P, shape, dt, name, n):
        self.tiles = [P.sb(shape, dt, f"{name}{i}") for i in range(n)]
        self.i = 0

    def __call__(self):
        t = self.tiles[self.i % len(self.tiles)]
        self.i += 1
        return t


def _ps6(self):
    b = self.psum_banks[self.psum_i % 6]
    self.psum_i += 1
    return b


def _psacc(self):
    self.acc_i = getattr(self, "acc_i", 0) + 1
    return self.psum_banks[6 + self.acc_i % 2]


Prog.ps = _ps6
Prog.ps_acc = _psacc


D = 1024
L = 4096
LH = 2048
CTX = 256
NK = CTX + L
NKT = NK // 128
EPS = 1e-6
OFF_AK, OFF_AV, OFF_CKV, OFF_CKR, OFF_U, NST = 0, 128, 256, 512, 544, 1056
OFF_AQ, OFF_CQ, OFF_GATE, NIN = 1056, 1568, 2336, 5408
ALPHA = (2.0 * 2) ** 0.25
NGH = 16
NGD = 2 * NGH


def dT(ap, name):
    return T(ap, [name])


class Ctx:
    pass


class RowSrc:
    def __init__(self, fn):
        self.fn = fn

    def rows(self, r0):
        return self.fn(r0)


def flat_src(t):
    return RowSrc(lambda r0: t.v(t.ap[r0:r0 + 128, :]))


WEIGHT_SHAPES = {
    "w_mod": [D, 6 * D], "b_mod": [6 * D], "w_in": [D, NIN], "a_q_gain": [64], "a_k_gain": [64],
    "c_q_a_gain": [768], "c_kv_a_gain": [256], "c_w_qb": [768, 768], "c_w_kvb": [256, 1024],
    "s5_a_re": [2, 16, 64], "s5_a_im": [2, 16, 64], "s5_log_dt": [2, 16],
    "s5_b_re": [2, 16, 64, 16], "s5_b_im": [2, 16, 64, 16], "s5_c_re": [2, 16, 16, 64], "s5_c_im": [2, 16, 16, 64],
    "s5_d": [256], "s5_w_glu": [512, 1024], "w_branch_a": [512, D], "w_branch_s5": [512, D], "w_branch_c": [512, D],
    "w_out": [D, D], "ln1_g": [D], "ln1_b": [D], "w_up": [D, 4 * D], "w_down": [4 * D, D], "ln2_g": [D], "ln2_b": [D],
}
CONST_SHAPES = {
    "x_all": [L, D], "x_own": [LH, D], "ctx": [CTX, D], "cvec": [2, D],
    "ident": [128, 128], "blk64": [128, 128], "perm64": [128, 128], "perm32": [32, 32],
    "ropek_cos": [128, L], "ropek_sin": [128, L], "ropeq_cos": [128, LH], "ropeq_sin": [128, LH],
    "rope32k_cos": [32, L], "rope32k_sin": [32, L], "rope32q_cos": [32, LH], "rope32q_sin": [32, LH],
    "sel": [128, 2], "swap": [128, 128], "mask_f": [128, 128], "mask_b": [128, 128],
    "sel8": [128, 64, 128], "sel8T": [128, 64, 128],
}


def declare_consts(nc):
    return {k: dT(nc.dram_tensor(k, list(v), F32, kind="ExternalInput").ap(), k) for k, v in CONST_SHAPES.items()}


def declare_weights(nc, l):
    return {k: dT(nc.dram_tensor(f"{k}_{l}", list(v), F32, kind="ExternalInput").ap(), f"{k}_{l}")
            for k, v in WEIGHT_SHAPES.items()}


def declare_inputs(nc, last):
    I = declare_consts(nc)
    I.update(declare_weights(nc, 1 if last else 0))
    return I


def host_constants():
    import math
    C = {}
    C["ident"] = np.eye(128, dtype=np.float32)
    blk = np.zeros((128, 128), np.float32); blk[:64, :64] = 1 / 64; blk[64:, 64:] = 1 / 64
    C["blk64"] = blk

    def perm_and_sign(dim):
        half = dim // 2; q = half // 2
        Pm = np.zeros((dim, dim), np.float32)
        sg = np.zeros(dim, np.float32)
        for m in range(dim):
            if (m % half) < q:
                Pm[m + q, m] = 1; sg[m] = -1
            else:
                Pm[m - q, m] = 1; sg[m] = 1
        return Pm, sg
    P64, s64 = perm_and_sign(64)
    p128 = np.zeros((128, 128), np.float32); p128[:64, :64] = P64; p128[64:, 64:] = P64
    C["perm64"] = p128
    P32, s32 = perm_and_sign(32)
    C["perm32"] = P32

    def tables(dim):
        half = dim // 2
        inv = 10000.0 ** (-np.arange(0, half, 2, dtype=np.float32) / half)
        rows = L // 64
        row = np.repeat(np.arange(rows, dtype=np.float32), 64)
        col = np.tile(np.arange(64, dtype=np.float32), rows)
        ang_r = row[:, None] * inv; ang_c = col[:, None] * inv
        ang = np.concatenate([ang_r, ang_r, ang_c, ang_c], axis=-1).astype(np.float32)
        return np.cos(ang).T.astype(np.float32), np.sin(ang).T.astype(np.float32)
    c64, s64t = tables(64)
    s64t = s64t * s64[:, None]
    C["ropek_cos"] = np.concatenate([c64, c64], 0); C["ropek_sin"] = np.concatenate([s64t, s64t], 0)
    c32, s32t = tables(32)
    s32t = s32t * s32[:, None]
    C["rope32k_cos"] = c32; C["rope32k_sin"] = s32t
    sw = np.zeros((128, 128), np.float32)
    for p in range(64):
        sw[p, 64 + p] = 1; sw[64 + p, p] = 1
    C["swap"] = sw
    ii = np.arange(128) // 16
    C["mask_f"] = (ii[:, None] <= ii[None, :]).astype(np.float32)
    C["mask_b"] = (ii[:, None] >= ii[None, :]).astype(np.float32)
    sel = np.zeros((128, 64, 128), np.float32)
    for gl in range(8):
        for i in range(8):
            for c in range(16):
                sel[gl * 16 + c, gl * 8 + i, i * 16 + c] = 1
    C["sel8"] = sel
    C["sel8T"] = np.ascontiguousarray(sel.transpose(2, 1, 0))
    return C


def per_core_inputs(inputs, core, C, layers=(0, 1)):
    b, hh = core // 2, core % 2
    m = {}
    xb = inputs["x"][b]
    m["x_all"] = xb; m["x_own"] = xb[hh * LH:(hh + 1) * LH]; m["ctx"] = inputs["ctx"][b]
    m["cvec"] = np.stack([inputs["c"][b], inputs["c_ctx"]], 0)
    gs = slice(16 * hh, 16 * hh + 16)
    for l in layers:
        for k in WEIGHT_SHAPES:
            v = inputs[k][l]
            if k in ("s5_a_re", "s5_a_im", "s5_b_re", "s5_b_im", "s5_c_re", "s5_c_im", "s5_log_dt"):
                v = v[:, gs]
            elif k == "s5_d":
                v = v[256 * hh:256 * hh + 256]
            elif k == "w_in":
                v = np.concatenate([v[:, :OFF_U], v[:, OFF_U + 256 * hh:OFF_U + 256 * hh + 256],
                                    v[:, OFF_U + 256 * (1 - hh):OFF_U + 256 * (1 - hh) + 256], v[:, NST:]], axis=1)
            m[f"{k}_{l}"] = v
    for k in ["ident", "blk64", "perm64", "perm32", "ropek_cos", "ropek_sin", "rope32k_cos", "rope32k_sin",
              "swap", "mask_f", "mask_b", "sel8", "sel8T"]:
        m[k] = C[k]
    sl = slice(hh * LH, (hh + 1) * LH)
    m["ropeq_cos"] = C["ropek_cos"][:, sl]; m["ropeq_sin"] = C["ropek_sin"][:, sl]
    m["rope32q_cos"] = C["rope32k_cos"][:, sl]; m["rope32q_sin"] = C["rope32k_sin"][:, sl]
    s = np.zeros((128, 2), np.float32); s[:, hh] = 1
    m["sel"] = s
    return {k: np.ascontiguousarray(v, dtype=np.float32) for k, v in m.items()}


def rstd_from_ms(P, out, ms, n, eps=EPS, eng_a="act"):
    P.ts(out, ms, eps, None, op0=ALU.add)
    P.act(out, out, AF.Sqrt)
    if n > 1:
        P.recip(out, out)
    else:
        P.recip(out, out)


def stage_prep(P, I, G):
    G.ident = P.sb([128, 128], BF16, "ident"); P.dma(G.ident, I["ident"], eng="pool")
    G.blk64 = P.sb([128, 128], BF16, "blk64"); P.dma(G.blk64, I["blk64"], eng="pool")
    G.perm64 = P.sb([128, 128], BF16, "perm64"); P.dma(G.perm64, I["perm64"], eng="pool")
    G.perm32 = P.sb([32, 32], BF16, "perm32"); P.dma(G.perm32, I["perm32"], eng="pool")
    G.ones = P.sb([128, 128], BF16, "ones"); P.memset(G.ones, 1.0)
    G.modT = P.sb([128, 48, 2], F32, "modT")
    G.sel = P.sb([128, 2], F32, "sel"); P.dma(G.sel, I["sel"])
    with P.scope():
        cT = P.sb([128, 8, 2], F32, "cT")
        for t in range(2):
            src = I["cvec"].ap[t, :].rearrange("(k p) -> p k", p=128)
            P.dma(cT[:, :, t], dT(src, "cvec"), allow_slow_non_contiguous=True)
        sT = P.sb([128, 8, 2], BF16, "sT")
        P.act(sT, cT, AF.Silu)
        bT = P.sb([128, 48], F32, "bT")
        P.dma(bT, dT(I["b_mod"].ap.rearrange("(j p) -> p j", p=128), "b_mod"), allow_slow_non_contiguous=True)
        wbufs = [P.sb([128, 8, 128], BF16, f"wm{i}") for i in range(3)]
        for j in range(48):
            wb = wbufs[j % 3]
            src = I["w_mod"].ap[:, j * 128:(j + 1) * 128].rearrange("(k p) c -> p k c", p=128)
            P.dma(wb, dT(src, "w_mod"), eng="pool")
            ps = P.ps()
            for kc in range(8):
                P.mm(ps[:, 0:2], wb[:, kc, :], sT[:, kc, :], start=(kc == 0), stop=(kc == 7))
            P.ts(G.modT[:, j, :], ps[:, 0:2], bT[:, j:j + 1], None, op0=ALU.add)
        for w in (1, 4):
            P.ts(G.modT[:, w * 8:(w + 1) * 8, :], G.modT[:, w * 8:(w + 1) * 8, :], 1.0, None, op0=ALU.add)
        G.modD = P.dram([2, 6 * D], F32, "modD")
        for t in range(2):
            dst = G.modD.ap[t, :].rearrange("(j p) -> p j", p=128)
            P.dma(G.modD.v(dst), G.modT[:, :, t], allow_slow_non_contiguous=True)


def ln_tile_to_hT(P, G, xt, hT_dst, t_idx, which_sh, which_sc):
    st = P.sb([128, 2, 6], F32, "bnst")
    mv = P.sb([128, 2], F32, "mv")
    for hh in range(2):
        P.bn_stats(st[:, hh, :], xt[:, hh * 512:(hh + 1) * 512])
    P.bn_aggr(mv, st)
    rs = P.sb([128, 1], F32, "rs")
    rstd_from_ms(P, rs, mv[:, 1:2], 1)
    xn = P.sb([128, D], BF16, "xn")
    P.ts(xn, xt, mv[:, 0:1], rs, op0=ALU.subtract, op1=ALU.mult)
    ps = P.ps()
    psb = ps.v(ps.ap.bitcast(BF16))
    for kc in range(8):
        P.transpose(psb[:, kc * 128:(kc + 1) * 128], xn[:, kc * 128:(kc + 1) * 128], G.ident)
    for kc in range(8):
        P.act(hT_dst[:, kc, :], psb[:, kc * 128:(kc + 1) * 128], AF.Identity,
              bias=G.modT[:, which_sh * 8 + kc, t_idx:t_idx + 1], scale=G.modT[:, which_sc * 8 + kc, t_idx:t_idx + 1])


_CAST = {"i": 0}


def wload(P, stg, dst, src, engs=("pool", "dve", "act")):
    shape = list(dst.ap.shape)[1:]
    n = 1
    for v in shape:
        n *= v
    st = stg()
    sv = st.ap[0:dst.ap.shape[0], 0:n]
    if len(shape) == 2:
        sv = sv.rearrange("p (a b) -> p a b", a=shape[0])
    svt = st.v(sv)
    P.dma(svt, src)
    e = engs[_CAST["i"] % len(engs)]
    _CAST["i"] += 1
    P.copy(dst, svt, eng=e)


def mmg(P, items, K):
    for k in range(K):
        for (out, lf, rf) in items:
            P.mm(out, lf(k), rf(k), start=(k == 0), stop=(k == K - 1))


def make_ln_pools(P, nb=2):
    R = Ctx()
    R.xt = Rot(P, [128, D], F32, "xt", nb)
    R.st = Rot(P, [128, 2, 6], F32, "bnst", nb)
    R.mv = Rot(P, [128, 2], F32, "mv", nb)
    R.rs = Rot(P, [128, 1], F32, "rs", nb)
    R.xn = Rot(P, [128, D], BF16, "xn", nb)
    return R


def ln_part1(P, G, R, src_dram):
    xt = R.xt()
    P.dma(xt, src_dram)
    st = R.st(); mv = R.mv(); rs = R.rs(); xn = R.xn()
    for hh in range(2):
        P.bn_stats(st[:, hh, :], xt[:, hh * 512:(hh + 1) * 512])
    P.bn_aggr(mv, st)
    rstd_from_ms(P, rs, mv[:, 1:2], 1)
    P.ts(xn, xt, mv[:, 0:1], rs, op0=ALU.subtract, op1=ALU.mult)
    return xn


def ln_part2(P, G, xn, hT_dst, t_idx, which_sh, which_sc):
    ps = P.ps()
    psb = ps.v(ps.ap.bitcast(BF16))
    for kc in range(8):
        P.transpose(psb[:, kc * 128:(kc + 1) * 128], xn[:, kc * 128:(kc + 1) * 128], G.ident)
    for kc in range(8):
        bias = G.modT[:, which_sh * 8 + kc, t_idx:t_idx + 1]
        scale = G.modT[:, which_sc * 8 + kc, t_idx:t_idx + 1]
        if kc % 2 == 0:
            P.act(hT_dst[:, kc, :], psb[:, kc * 128:(kc + 1) * 128], AF.Identity, bias=bias, scale=scale)
        else:
            P.ts(hT_dst[:, kc, :], psb[:, kc * 128:(kc + 1) * 128], scale, bias, op0=ALU.mult, op1=ALU.add)


def ln_tile_to_hT2(P, G, R, src_dram, hT_dst, t_idx, which_sh, which_sc):
    xn = ln_part1(P, G, R, src_dram)
    ln_part2(P, G, xn, hT_dst, t_idx, which_sh, which_sc)


def rope_apply(P, dst, src_bf, perm, cos, sin, tmp, n, rows=128, psfn=None):
    ps = (psfn or P.ps)()
    P.mm(ps[0:rows, 0:n], perm, src_bf)
    P.tt(tmp, src_bf, cos, ALU.mult)
    P.tt(dst, ps[0:rows, 0:n], sin, ALU.mult)
    P.tt(dst, dst, tmp, ALU.add)


def alloc_persist(P, G):
    G.kT = P.sb([128, NK], BF16, "kT")
    G.Vg = P.sb([128, NKT, 2, 128], BF16, "Vg")
    G.ckvT = P.sb([128, 2, NK], BF16, "ckvT")
    G.krT = P.sb([32, NK], BF16, "krT")


def stage_A(P, I, G):
    P.memset(G.Vg[:, :, :, 64:128], 1.0)
    with P.scope():
        w_st = P.sb([128, 8, NST], BF16, "w_st")
        with P.scope():
            stg = Rot(P, [128, NST], F32, "stg", 2)
            for kc in range(8):
                wload(P, stg, w_st[:, kc, :], dT(I["w_in"].ap[kc * 128:(kc + 1) * 128, 0:NST], "w_in"))
        kg = P.sb([128, 1], F32, "kg")
        for r in range(2):
            P.dma(kg[r * 64:(r + 1) * 64, :], dT(I["a_k_gain"].ap.rearrange("(p o) -> p o", o=1), "akg"))
        cg = P.sb([128, 2], F32, "cg")
        P.dma(cg, dT(I["c_kv_a_gain"].ap.rearrange("(j p) -> p j", p=128), "ckg"), allow_slow_non_contiguous=True)
        R = make_ln_pools(P, 3)
        hTs = Rot(P, [128, 8, 512], BF16, "hT", 2)
        sq = Rot(P, [128, 512], BF16, "sq", 2)
        rst = Rot(P, [128, 512], F32, "rst", 1)
        knb = Rot(P, [128, 512], BF16, "knb", 2)
        tmp = Rot(P, [128, 512], F32, "tmp", 1)
        cosb = Rot(P, [128, 512], F32, "cosb", 1)
        sinb = Rot(P, [128, 512], F32, "sinb", 1)
        cos32 = Rot(P, [32, 512], BF16, "cos32", 1)
        sin32 = Rot(P, [32, 512], BF16, "sin32", 1)
        blocks = [(0, 2, True)] + [(2 + 4 * i, 4, False) for i in range(8)]
        alltiles = [(t0 + ti, is_ctx) for (t0, nt, is_ctx) in blocks for ti in range(nt)]

        def a_p1(q):
            t, is_ctx = alltiles[q]
            return ln_part1(P, G, R, I["ctx"].rows(t * 128) if is_ctx else I["x_all"].rows((t - 2) * 128))
        qi = 0
        xn_cur = a_p1(0)
        for (t0, nt, is_ctx) in blocks:
            n = nt * 128
            c0 = t0 * 128
            hT = hTs()
            for ti in range(nt):
                xn_nxt = a_p1(qi + 1) if qi + 1 < len(alltiles) else None
                ln_part2(P, G, xn_cur, hT[:, :, ti * 128:(ti + 1) * 128], 1 if is_ctx else 0, 0, 1)
                xn_cur = xn_nxt
                qi += 1
            if not is_ctx:
                lc = c0 - CTX
                cb, sb_, c32, s32 = cosb(), sinb(), cos32(), sin32()
                P.dma(cb[:, 0:n], dT(I["ropek_cos"].ap[:, lc:lc + n], "rc"))
                P.dma(sb_[:, 0:n], dT(I["ropek_sin"].ap[:, lc:lc + n], "rs"))
                P.dma(c32[:, 0:n], dT(I["rope32k_cos"].ap[:, lc:lc + n], "rc32"), eng="pool")
                P.dma(s32[:, 0:n], dT(I["rope32k_sin"].ap[:, lc:lc + n], "rs32"), eng="pool")
            pk = P.ps(); pc = [P.ps(), P.ps()]; pr = P.ps()
            mmg(P, [(pk[:, 0:n], lambda k: w_st[:, k, OFF_AK:OFF_AK + 128], lambda k: hT[:, k, 0:n]),
                    (pc[0][:, 0:n], lambda k: w_st[:, k, OFF_CKV:OFF_CKV + 128], lambda k: hT[:, k, 0:n]),
                    (pc[1][:, 0:n], lambda k: w_st[:, k, OFF_CKV + 128:OFF_CKV + 256], lambda k: hT[:, k, 0:n]),
                    (pr[0:32, 0:n], lambda k: w_st[:, k, OFF_CKR:OFF_CKR + 32], lambda k: hT[:, k, 0:n])], 8)
            s = sq()
            P.act(s[:, 0:n], pk[:, 0:n], AF.Square)
            pm = P.ps()
            P.mm(pm[:, 0:n], G.blk64, s[:, 0:n])
            rs = rst()
            rstd_from_ms(P, rs[:, 0:n], pm[:, 0:n], n)
            kn = knb()
            P.stt(kn[:, 0:n], pk[:, 0:n], kg[:, 0:1], rs[:, 0:n], ALU.mult, ALU.mult)
            if is_ctx:
                P.copy(G.kT[:, c0:c0 + n], kn[:, 0:n])
            else:
                rope_apply(P, G.kT[:, c0:c0 + n], kn[:, 0:n], G.perm64, cb[:, 0:n], sb_[:, 0:n], tmp()[:, 0:n], n)
            ss = [sq(), sq()]
            for j in range(2):
                P.act(ss[j][:, 0:n], pc[j][:, 0:n], AF.Square)
            pm = P.ps()
            for j in range(2):
                P.mm(pm[:, 0:n], G.ones, ss[j][:, 0:n], start=(j == 0), stop=(j == 1))
            rs = rst()
            P.ts(rs[:, 0:n], pm[:, 0:n], 1.0 / 256, EPS, op0=ALU.mult, op1=ALU.add)
            P.act(rs[:, 0:n], rs[:, 0:n], AF.Sqrt)
            P.recip(rs[:, 0:n], rs[:, 0:n])
            for j in range(2):
                P.stt(G.ckvT[:, j, c0:c0 + n], pc[j][:, 0:n], cg[:, j:j + 1], rs[:, 0:n], ALU.mult, ALU.mult)
            if is_ctx:
                P.copy(G.krT[:, c0:c0 + n], pr[0:32, 0:n])
            else:
                kr = knb()
                P.copy(kr[0:32, 0:n], pr[0:32, 0:n])
                rope_apply(P, G.krT[:, c0:c0 + n], kr[0:32, 0:n], G.perm32, c32[:, 0:n], s32[:, 0:n], tmp()[0:32, 0:n], n, rows=32)
            pvs = [P.ps() for _ in range(nt)]
            mmg(P, [(pvs[ti][:, 0:128], (lambda k, ti=ti: hT[:, k, ti * 128:(ti + 1) * 128]),
                     lambda k: w_st[:, k, OFF_AV:OFF_AV + 128]) for ti in range(nt)], 8)
            for ti in range(nt):
                pv = pvs[ti]
                P.copy(G.Vg[:, t0 + ti, :, 0:64], pv.v(pv.ap[:, 0:128].rearrange("p (a b) -> p a b", a=2)), eng="act")
            pus = [P.ps() for _ in range(2)]
            mmg(P, [(pus[j][:, 0:n], (lambda k, j=j: w_st[:, k, OFF_U + j * 128:OFF_U + (j + 1) * 128]),
                     lambda k: hT[:, k, 0:n]) for j in range(2)], 8)
            for j in range(2):
                dst = G.uT.v(G.uT.ap[:, j, :, c0 // 8:(c0 + n) // 8].rearrange("p i c -> p c i"))
                src = pus[j].v(pus[j].ap[:, 0:n].rearrange("p (c i) -> p c i", i=8))
                P.copy(dst, src, eng=("act" if j % 2 else "dve"))


def own_blocks(last):
    bl = [(i * 512, 512, False, i * 512) for i in range(4)]
    if not last:
        bl.append((LH, 256, True, 0))
    return bl


def stage_B1(P, I, G, last):
    NQ = LH + (0 if last else CTX)
    G.NQ = NQ
    G.qg = P.sb([128, 4, NQ], BF16, "qg")
    G.qm = P.sb([96, 8, NQ], BF16, "qm")
    G.gD = P.dram([3 * D, NQ], BF16, "gD")
    with P.scope():
        hT = P.sb([128, 8, NQ], BF16, "hTall")
        with P.scope():
            R = make_ln_pools(P, 4)
            tiles = [(c0 + ti * 128, r0 + ti * 128, is_ctx) for (c0, n, is_ctx, r0) in own_blocks(last) for ti in range(n // 128)]

            def p1(q):
                col, row, is_ctx = tiles[q]
                return ln_part1(P, G, R, (I["ctx"] if is_ctx else I["x_own"]).rows(row))
            xn_cur = p1(0)
            for q in range(len(tiles)):
                xn_nxt = p1(q + 1) if q + 1 < len(tiles) else None
                col, row, is_ctx = tiles[q]
                ln_part2(P, G, xn_cur, hT[:, :, col:col + 128], 1 if is_ctx else 0, 0, 1)
                xn_cur = xn_nxt
        qgain = P.sb([128, 1], F32, "qgain")
        for r in range(2):
            P.dma(qgain[r * 64:(r + 1) * 64, :], dT(I["a_q_gain"].ap.rearrange("(p o) -> p o", o=1), "aqg"))
        cqg = P.sb([128, 6], F32, "cqg")
        P.dma(cqg, dT(I["c_q_a_gain"].ap.rearrange("(j p) -> p j", p=128), "cqg"), allow_slow_non_contiguous=True)
        perm32h = P.sb([96, 32], BF16, "perm32h")
        P.dma(perm32h[64:96, :], I["perm32"], eng="pool")
        cosqR = Rot(P, [128, 512], F32, "cosq", 2)
        sinqR = Rot(P, [128, 512], F32, "sinq", 2)
        cos32R = Rot(P, [96, 512], F32, "cos32q", 2)
        sin32R = Rot(P, [96, 512], F32, "sin32q", 2)
        sq = Rot(P, [128, 512], BF16, "sq", 6)
        rst = Rot(P, [128, 512], F32, "rst", 2)
        knb = Rot(P, [128, 512], BF16, "knb", 2)
        tmp = Rot(P, [128, 512], F32, "tmp", 2)
        with P.scope():
            wq = P.sb([128, 8, 4, 128], BF16, "wq")
            stg = Rot(P, [128, 768], F32, "stg", 2)
            for kc in range(8):
                for hf in range(2):
                    src = I["w_in"].ap[kc * 128:(kc + 1) * 128, OFF_AQ + hf * 256:OFF_AQ + (hf + 1) * 256].rearrange("p (a b) -> p a b", a=4)
                    wload(P, stg, wq[:, kc, :, hf * 64:(hf + 1) * 64], dT(src, "w_in"))
            for (c0, n, is_ctx, r0) in own_blocks(last):
                if not is_ctx:
                    cosq = cosqR(); sinq = sinqR()
                    P.dma(cosq, dT(I["ropeq_cos"].ap[:, c0:c0 + n], "rqc"))
                    P.dma(sinq, dT(I["ropeq_sin"].ap[:, c0:c0 + n], "rqs"))
                pks = [P.ps() for _ in range(4)]
                mmg(P, [(pks[hd][:, 0:n], (lambda k, hd=hd: wq[:, k, hd, :]), lambda k: hT[:, k, c0:c0 + n]) for hd in range(4)], 8)
                for hd in range(4):
                    pk = pks[hd]
                    s = sq()
                    P.act(s[:, 0:n], pk[:, 0:n], AF.Square)
                    pm = P.ps_acc()
                    P.mm(pm[:, 0:n], G.blk64, s[:, 0:n])
                    rs = rst()
                    rstd_from_ms(P, rs[:, 0:n], pm[:, 0:n], n)
                    if is_ctx:
                        P.stt(G.qg[:, hd, c0:c0 + n], pk[:, 0:n], qgain[:, 0:1], rs[:, 0:n], ALU.mult, ALU.mult)
                    else:
                        kn = knb()
                        P.stt(kn[:, 0:n], pk[:, 0:n], qgain[:, 0:1], rs[:, 0:n], ALU.mult, ALU.mult)
                        rope_apply(P, G.qg[:, hd, c0:c0 + n], kn[:, 0:n], G.perm64, cosq[:, 0:n], sinq[:, 0:n],
                                   tmp()[:, 0:n], n, psfn=P.ps_acc)
        with P.scope():
            wc = P.sb([128, 8, 768], BF16, "wc")
            stg = Rot(P, [128, 768], F32, "stg", 1)
            for kc in range(8):
                wload(P, stg, wc[:, kc, :], dT(I["w_in"].ap[kc * 128:(kc + 1) * 128, OFF_CQ:OFF_CQ + 768], "w_in"))
            wqb = P.sb([128, 6, 768], BF16, "wqb")
            for j in range(6):
                wload(P, stg, wqb[:, j, :], dT(I["c_w_qb"].ap[j * 128:(j + 1) * 128, :], "wqb"))
            cqn = P.sb([128, 6, 512], BF16, "cqn")
            qrb = Rot(P, [96, 512], BF16, "qrb", 2)
            for (c0, n, is_ctx, r0) in own_blocks(last):
                if not is_ctx:
                    cos32 = cos32R(); sin32 = sin32R()
                    P.dma(cos32[64:96, :], dT(I["rope32q_cos"].ap[:, c0:c0 + n], "rqc32"))
                    P.dma(sin32[64:96, :], dT(I["rope32q_sin"].ap[:, c0:c0 + n], "rqs32"))
                pcs = [P.ps() for _ in range(6)]
                mmg(P, [(pcs[j][:, 0:n], (lambda k, j=j: wc[:, k, j * 128:(j + 1) * 128]), lambda k: hT[:, k, c0:c0 + n]) for j in range(6)], 8)
                sqs = []
                for j in range(6):
                    s = sq()
                    P.act(s[:, 0:n], pcs[j][:, 0:n], AF.Square)
                    sqs.append(s)
                pm = P.ps_acc()
                for j in range(6):
                    P.mm(pm[:, 0:n], G.ones, sqs[j][:, 0:n], start=(j == 0), stop=(j == 5))
                rs = rst()
                P.ts(rs[:, 0:n], pm[:, 0:n], 1.0 / 768, EPS, op0=ALU.mult, op1=ALU.add)
                P.act(rs[:, 0:n], rs[:, 0:n], AF.Sqrt)
                P.recip(rs[:, 0:n], rs[:, 0:n])
                for j in range(6):
                    P.stt(cqn[:, j, 0:n], pcs[j][:, 0:n], cqg[:, j:j + 1], rs[:, 0:n], ALU.mult, ALU.mult)
                for hg in range(2):
                    pqs = [P.ps() for _ in range(4)]
                    mmg(P, [(pqs[i][0:96, 0:n], (lambda k, h=hg * 4 + i: wqb[:, k, h * 96:(h + 1) * 96]), lambda k: cqn[:, k, 0:n]) for i in range(4)], 6)
                    for i in range(4):
                        h = hg * 4 + i
                        pq = pqs[i]
                        if is_ctx:
                            P.copy(G.qm[:, h, c0:c0 + n], pq[0:96, 0:n], eng="act")
                        else:
                            P.copy(G.qm[0:64, h, c0:c0 + n], pq[0:64, 0:n], eng="act")
                            qr = qrb()
                            P.copy(qr[64:96, 0:n], pq[64:96, 0:n])
                            pr = P.ps_acc()
                            P.mm(pr[64:96, 0:n], perm32h[64:96, :], qr[64:96, 0:n])
                            t = tmp()
                            P.tt(t[64:96, 0:n], qr[64:96, 0:n], cos32[64:96, 0:n], ALU.mult)
                            t2 = tmp()
                            P.tt(t2[64:96, 0:n], pr[64:96, 0:n], sin32[64:96, 0:n], ALU.mult)
                            P.tt(G.qm[64:96, h, c0:c0 + n], t[64:96, 0:n], t2[64:96, 0:n], ALU.add)
        with P.scope():
            wg = Rot(P, [128, 8, 512], BF16, "wg", 2)
            stg = Rot(P, [128, 512], F32, "stg", 3)
            gb = Rot(P, [128, 512], BF16, "gb", 8)
            def load_g(gi):
                w = wg()
                for kc in range(8):
                    wload(P, stg, w[:, kc, :], dT(I["w_in"].ap[kc * 128:(kc + 1) * 128, OFF_GATE + gi * 512:OFF_GATE + (gi + 1) * 512], "w_in"), engs=("dve", "act"))
                return w
            w_nxt = load_g(0)
            for gi in range(6):
                w = w_nxt
                w_nxt = load_g(gi + 1) if gi + 1 < 6 else None
                for (c0, n, is_ctx, r0) in own_blocks(last):
                    pgs = [P.ps() for _ in range(4)]
                    mmg(P, [(pgs[oc][:, 0:n], (lambda k, oc=oc: w[:, k, oc * 128:(oc + 1) * 128]), lambda k: hT[:, k, c0:c0 + n]) for oc in range(4)], 8)
                    for oc in range(4):
                        g = gb()
                        P.act(g[:, 0:n], pgs[oc][:, 0:n], AF.Sigmoid)
                        row = (gi * 4 + oc) * 128
                        P.dma(G.gD.v(G.gD.ap[row:row + 128, c0:c0 + n]), g[:, 0:n])


def run_attn(P, chains, pT, scale):
    LA = 2
    nkt = chains[0][1]
    pls = [dict() for _ in chains]
    for kt in range(nkt + LA):
        if kt < nkt:
            for ci, (po, _, n, slf, srhs, vlf) in enumerate(chains):
                pss = P.ps()
                P.mm(pss[:, 0:n], slf(kt), srhs)
                p = pT()
                P.act(p[:, 0:n], pss[:, 0:n], AF.Exp, scale=scale)
                pls[ci][kt] = p
        jj = kt - LA
        if jj >= 0:
            for ci, (po, _, n, slf, srhs, vlf) in enumerate(chains):
                P.mm(po[:, 0:n], vlf(jj), pls[ci].pop(jj)[:, 0:n], start=(jj == 0), stop=(jj == nkt - 1))


def block_groups(last):
    bl = own_blocks(last)
    groups = [bl[0:2], bl[2:4]]
    if not last:
        groups.append(bl[4:5])
    return groups


def attn_finish(P, po, n, rec, yo, dst):
    r = rec()
    P.recip(r[64:128, 0:n], po[64:128, 0:n])
    y = yo()
    P.tt(y[:, 0:n], po[0:64, 0:n], r[64:128, 0:n], ALU.mult)
    P.dma(dst, y[:, 0:n])


def stage_B2(P, I, G, last):
    NQ = G.NQ
    G.yaD = P.dram([512, NQ], BF16, "yaD")
    with P.scope():
        pT = Rot(P, [128, 512], BF16, "pT", 8)
        rec = Rot(P, [128, 512], F32, "rec", 2)
        yo = Rot(P, [64, 512], BF16, "yo", 2)
        kTp = P.sb([128, 2, NK], BF16, "kTp")
        P.memset(kTp, 0.0)
        P.copy(kTp[0:64, 0, :], G.kT[0:64, :], eng="pool")
        P.copy(kTp[64:128, 1, :], G.kT[64:128, :], eng="dve")
        for hd in range(4):
            for kvh in range(2):
                head = hd + 4 * kvh
                for grp in block_groups(last):
                    chains = []
                    for (c0, n, is_ctx, r0) in grp:
                        nkt = 2 if is_ctx else NKT
                        chains.append((P.ps_acc(), nkt, n, (lambda kt: kTp[:, kvh, kt * 128:(kt + 1) * 128]),
                                       G.qg[:, hd, c0:c0 + n], (lambda kt: G.Vg[:, kt, kvh, :])))
                    run_attn(P, chains, pT, 0.125)
                    for (po, _, n, _, _, _), (c0, _, _, _) in zip(chains, grp):
                        attn_finish(P, po, n, rec, yo, G.yaD.v(G.yaD.ap[head * 64:(head + 1) * 64, c0:c0 + n]))


def stage_B3(P, I, G, last):
    NQ = G.NQ
    G.ycD = P.dram([512, NQ], BF16, "ycD")
    with P.scope():
        wkv = P.sb([128, 2, 1024], BF16, "wkv")
        stg = Rot(P, [128, 1024], F32, "stg", 1)
        for j in range(2):
            wload(P, stg, wkv[:, j, :], dT(I["c_w_kvb"].ap[j * 128:(j + 1) * 128, :], "wkvb"))
        Kh = Rot(P, [96, NK], BF16, "Kh", 2)
        Vh = [P.sb([128, NKT, 128], BF16, f"Vh{i}") for i in range(2)]
        for v in Vh:
            P.memset(v[:, :, 64:128], 1.0)
        pT = Rot(P, [128, 512], BF16, "pT", 8)
        rec = Rot(P, [128, 512], F32, "rec", 2)
        yo = Rot(P, [64, 512], BF16, "yo", 2)
        scale = 96 ** -0.5
        for h in range(8):
            K = Kh(); V = Vh[h % 2]
            cbs = [(cb * 512, min(512, NK - cb * 512)) for cb in range(9)]
            for g0 in range(0, 9, 3):
                grp = cbs[g0:g0 + 3]
                pks = [P.ps() for _ in grp]
                mmg(P, [(pks[i][0:64, 0:n], lambda k: wkv[:, k, h * 128:h * 128 + 64], (lambda k, k0=k0, n=n: G.ckvT[:, k, k0:k0 + n]))
                        for i, (k0, n) in enumerate(grp)], 2)
                for i, (k0, n) in enumerate(grp):
                    P.copy(K[0:64, k0:k0 + n], pks[i][0:64, 0:n], eng=("act" if i % 2 else "dve"))
            P.copy(K[64:96, :], G.krT[0:32, :], eng="pool")
            for g0 in range(0, NKT, 4):
                kts = list(range(g0, min(g0 + 4, NKT)))
                pvs = [P.ps() for _ in kts]
                mmg(P, [(pvs[i][:, 0:64], (lambda k, kt=kt: G.ckvT[:, k, kt * 128:(kt + 1) * 128]),
                         lambda k: wkv[:, k, h * 128 + 64:h * 128 + 128]) for i, kt in enumerate(kts)], 2)
                for i, kt in enumerate(kts):
                    P.copy(V[:, kt, 0:64], pvs[i][:, 0:64], eng=("act" if kt % 2 else "dve"))
            for grp in block_groups(last):
                chains = []
                for (c0, n, is_ctx, r0) in grp:
                    nkt = 2 if is_ctx else NKT
                    chains.append((P.ps_acc(), nkt, n, (lambda kt: K[0:96, kt * 128:(kt + 1) * 128]),
                                   G.qm[0:96, h, c0:c0 + n], (lambda kt: V[:, kt, :])))
                run_attn(P, chains, pT, scale)
                for (po, _, n, _, _, _), (c0, _, _, _) in zip(chains, grp):
                    attn_finish(P, po, n, rec, yo, G.ycD.v(G.ycD.ap[h * 64:(h + 1) * 64, c0:c0 + n]))


def bcast_load(P, dst, src_ap_1d):
    P.dma(dst, dT(src_ap_1d.partition_broadcast(128), "bc"))


def stage_B4a(P, I, G, last):
    NQ = G.NQ
    G.xmidD = P.dram([NQ, D], F32, "xmidD")
    G.h2D = P.dram([D, NQ], BF16, "h2D")
    with P.scope():
        wglu = P.sb([128, 4, 1024], BF16, "wglu")
        wb = [P.sb([128, 4, 1024], BF16, f"wb{i}") for i in range(3)]
        wout = P.sb([128, 8, 1024], BF16, "wout")
        with P.scope():
            stg = Rot(P, [128, 1024], F32, "stg", 3)
            for j in range(4):
                wload(P, stg, wglu[:, j, :], dT(I["s5_w_glu"].ap[j * 128:(j + 1) * 128, :], "w"))
                for i, nm in enumerate(["w_branch_a", "w_branch_s5", "w_branch_c"]):
                    wload(P, stg, wb[i][:, j, :], dT(I[nm].ap[j * 128:(j + 1) * 128, :], "w"))
            for kc in range(8):
                wload(P, stg, wout[:, kc, :], dT(I["w_out"].ap[kc * 128:(kc + 1) * 128, :], "w"))
        g1b = [P.sb([128, D], F32, f"g1b{t}") for t in range(2)]
        for t in range(2):
            P.dma(g1b[t], dT(G.modD.ap[t, 2 * D:3 * D].partition_broadcast(128), "modD_r"))
        lng = P.sb([128, D], F32, "lng"); bcast_load(P, lng, I["ln1_g"].ap)
        lnb = P.sb([128, D], F32, "lnb"); bcast_load(P, lnb, I["ln1_b"].ap)
        class _InR:
            def __init__(self):
                gts = P.sb([128, 24, 512], BF16, "gates")
                self.sets = [(P.sb([128, 4, 512], BF16, f"yT{i}"), P.sb([128, 4, 512], BF16, f"sa{i}"),
                              P.sb([128, 4, 512], BF16, f"sc{i}"), gts) for i in range(2)]
                self.i = 0

            def __call__(self):
                r = self.sets[self.i % 2]
                self.i += 1
                return r
        inR = _InR()
        srcs = [None, P.sb([128, 4, 512], BF16, "src1"), None]
        t1 = P.sb([128, 4, 512], F32, "t1")
        sg = Rot(P, [128, 512], F32, "sg", 2)
        acc = Rot(P, [128, 512], F32, "acc", 2)
        tmpm = Rot(P, [128, 512], F32, "tmpm", 2)
        merged = P.sb([128, 8, 512], BF16, "merged")
        xtR = Rot(P, [128, D], F32, "xt", 2)
        tsR = Rot(P, [128, D], F32, "tsum", 3)
        xmR = Rot(P, [128, D], F32, "xm", 2)
        stR = Rot(P, [128, 2, 6], F32, "bnst", 2); mvR = Rot(P, [128, 2], F32, "mv", 2); rsR = Rot(P, [128, 1], F32, "rs", 2)
        xnR = Rot(P, [128, D], BF16, "xn", 2)
        h2T = P.sb([128, 8, 512], BF16, "h2T")
        def load_blk(blk):
            (c0, n, is_ctx, r0) = blk
            bufs = inR()
            yT_, sa_, sc_, gates_ = bufs
            for j in range(4):
                P.dma(yT_[:, j, 0:n], G.ysD.v(G.ysD.ap[j * 128:(j + 1) * 128, c0:c0 + n]))
                P.dma(sa_[:, j, 0:n], G.yaD.v(G.yaD.ap[j * 128:(j + 1) * 128, c0:c0 + n]))
                P.dma(sc_[:, j, 0:n], G.ycD.v(G.ycD.ap[j * 128:(j + 1) * 128, c0:c0 + n]))
            return bufs
        blks_ = own_blocks(last)
        nxt_bufs = load_blk(blks_[0])
        for bi_, (c0, n, is_ctx, r0) in enumerate(blks_):
            tix = 1 if is_ctx else 0
            yT, srcs[0], srcs[2], gates = nxt_bufs
            for gi in range(24):
                P.dma(gates[:, gi, 0:n], G.gD.v(G.gD.ap[gi * 128:(gi + 1) * 128, c0:c0 + n]))
            nxt_bufs = load_blk(blks_[bi_ + 1]) if bi_ + 1 < len(blks_) else None
            P.tt(t1[:, :, 0:n], yT[:, :, 0:n], yT[:, :, 0:n], ALU.mult)
            P.ts(t1[:, :, 0:n], t1[:, :, 0:n], 0.044715, 1.0, op0=ALU.mult, op1=ALU.add)
            P.tt(t1[:, :, 0:n], t1[:, :, 0:n], yT[:, :, 0:n], ALU.mult)
            P.act(t1[:, :, 0:n], t1[:, :, 0:n], AF.Sigmoid, scale=1.5957691216)
            ge = P.sb([128, 4, 512], BF16, "ge") if c0 == 0 else ge
            P.tt(ge[:, :, 0:n], t1[:, :, 0:n], yT[:, :, 0:n], ALU.mult)
            for op_ in range(2):
                pa = [P.ps(), P.ps()]; pg = [P.ps(), P.ps()]
                items = []
                for q in range(2):
                    oc = op_ * 2 + q
                    items.append((pa[q][:, 0:n], (lambda k, oc=oc: wglu[:, k, oc * 128:(oc + 1) * 128]), lambda k: ge[:, k, 0:n]))
                    items.append((pg[q][:, 0:n], (lambda k, oc=oc: wglu[:, k, 512 + oc * 128:512 + (oc + 1) * 128]), lambda k: ge[:, k, 0:n]))
                mmg(P, items, 4)
                for q in range(2):
                    oc = op_ * 2 + q
                    s = sg()
                    P.act(s[:, 0:n], pg[q][:, 0:n], AF.Sigmoid)
                    P.tt(srcs[1][:, oc, 0:n], pa[q][:, 0:n], s[:, 0:n], ALU.mult)
            for oc in range(8):
                a = acc()
                pbs = [P.ps() for _ in range(3)]
                mmg(P, [(pbs[br][:, 0:n], (lambda k, br=br: wb[br][:, k, oc * 128:(oc + 1) * 128]), (lambda k, br=br: srcs[br][:, k, 0:n])) for br in range(3)], 4)
                for br in range(3):
                    pb = pbs[br]
                    if br == 0:
                        P.tt(a[:, 0:n], pb[:, 0:n], gates[:, br * 8 + oc, 0:n], ALU.mult)
                    else:
                        tm = tmpm()
                        P.tt(tm[:, 0:n], pb[:, 0:n], gates[:, br * 8 + oc, 0:n], ALU.mult)
                        if br == 1:
                            P.tt(a[:, 0:n], a[:, 0:n], tm[:, 0:n], ALU.add, eng="pool")
                        else:
                            P.tt(merged[:, oc, 0:n], a[:, 0:n], tm[:, 0:n], ALU.add, eng="pool")
            def mix(ti):
                xt = xtR(); ts_ = tsR()
                P.dma(xt, (I["ctx"] if is_ctx else I["x_own"]).rows(r0 + ti * 128))
                pms = [P.ps(), P.ps()]
                mmg(P, [(pms[half][:, 0:512], lambda k: merged[:, k, ti * 128:(ti + 1) * 128],
                         (lambda k, half=half: wout[:, k, half * 512:(half + 1) * 512])) for half in range(2)], 8)
                for half in range(2):
                    P.tt(ts_[:, half * 512:(half + 1) * 512], pms[half][:, 0:512], g1b[tix][:, half * 512:(half + 1) * 512], ALU.mult)
                P.stt(ts_, xt, ALPHA, ts_, ALU.mult, ALU.add)
                return ts_

            def chain(ti, ts_):
                st = stR(); mv = mvR(); rs = rsR(); xm = xmR()
                for hh in range(2):
                    P.bn_stats(st[:, hh, :], ts_[:, hh * 512:(hh + 1) * 512])
                P.bn_aggr(mv, st)
                rstd_from_ms(P, rs, mv[:, 1:2], 1)
                P.stt(xm, ts_, mv[:, 0:1], lng, ALU.subtract, ALU.mult)
                P.stt(xm, xm, rs, lnb, ALU.mult, ALU.add)
                P.dma(G.xmidD.v(G.xmidD.ap[c0 + ti * 128:c0 + (ti + 1) * 128, :]), xm)
                st = stR(); mv = mvR(); rs = rsR(); xn = xnR()
                for hh in range(2):
                    P.bn_stats(st[:, hh, :], xm[:, hh * 512:(hh + 1) * 512])
                P.bn_aggr(mv, st)
                rstd_from_ms(P, rs, mv[:, 1:2], 1)
                P.ts(xn, xm, mv[:, 0:1], rs, op0=ALU.subtract, op1=ALU.mult)
                return xn

            def xpose(ti, xn):
                ps = P.ps()
                psb = ps.v(ps.ap.bitcast(BF16))
                for kc in range(8):
                    P.transpose(psb[:, kc * 128:(kc + 1) * 128], xn[:, kc * 128:(kc + 1) * 128], G.ident)
                for kc in range(8):
                    P.act(h2T[:, kc, ti * 128:(ti + 1) * 128], psb[:, kc * 128:(kc + 1) * 128], AF.Identity,
                          bias=G.modT[:, 3 * 8 + kc, tix:tix + 1], scale=G.modT[:, 4 * 8 + kc, tix:tix + 1])
            nti = n // 128
            ts_cur = mix(0)
            for ti in range(nti):
                ts_nxt = mix(ti + 1) if ti + 1 < nti else None
                xn = chain(ti, ts_cur)
                xpose(ti, xn)
                ts_cur = ts_nxt
            for kc in range(8):
                P.dma(G.h2D.v(G.h2D.ap[kc * 128:(kc + 1) * 128, c0:c0 + n]), h2T[:, kc, 0:n])


def stage_B4b(P, I, G, last, out_own, out_ctx):
    NQ = G.NQ
    NT = NQ // 128
    with P.scope():
        g2b = [P.sb([128, D], F32, f"g2b{t}") for t in range(2)]
        for t in range(2):
            P.dma(g2b[t], dT(G.modD.ap[t, 5 * D:6 * D].partition_broadcast(128), "modD_r"))
        lng = P.sb([128, D], F32, "lng"); bcast_load(P, lng, I["ln2_g"].ap)
        lnb = P.sb([128, D], F32, "lnb"); bcast_load(P, lnb, I["ln2_b"].ap)
        h2T = P.sb([128, 8, NQ], BF16, "h2Tall")
        for kc in range(8):
            P.dma(h2T[:, kc, :], G.h2D.v(G.h2D.ap[kc * 128:(kc + 1) * 128, :]))
        tsum = P.sb([128, NT, D], F32, "tsum2")
        wuR = Rot(P, [128, 8, 512], BF16, "wu", 2)
        wdR = Rot(P, [128, 4, D], BF16, "wd", 2)
        stg = Rot(P, [128, 1024], F32, "stg", 3)
        aR = Rot(P, [128, 4, 512], BF16, "aog", 3)
        rl = Rot(P, [128, 512], BF16, "rl", 4)
        xmR = Rot(P, [128, D], F32, "xm", 3)
        stR = Rot(P, [128, 2, 6], F32, "bnst", 3); mvR = Rot(P, [128, 2], F32, "mv", 3); rsR = Rot(P, [128, 1], F32, "rs", 3)
        def ep_a(tile, c0, ti, tix):
            xm = xmR()
            P.dma(xm, G.xmidD.v(G.xmidD.ap[c0 + ti * 128:c0 + (ti + 1) * 128, :]))
            P.tt(tsum[:, tile, :], tsum[:, tile, :], g2b[tix], ALU.mult, eng="pool")
            P.stt(tsum[:, tile, :], xm, ALPHA, tsum[:, tile, :], ALU.mult, ALU.add)
            st = stR(); mv = mvR(); rs = rsR()
            for hh in range(2):
                P.bn_stats(st[:, hh, :], tsum[:, tile, hh * 512:(hh + 1) * 512])
            P.bn_aggr(mv, st)
            rstd_from_ms(P, rs, mv[:, 1:2], 1)
            return (xm, mv, rs)

        def ep_b(tile, r0, ti, is_ctx, stt_):
            xm, mv, rs = stt_
            o = xm
            P.stt(o, tsum[:, tile, :], mv[:, 0:1], lng, ALU.subtract, ALU.mult)
            P.stt(o, o, rs, lnb, ALU.mult, ALU.add)
            dst = out_ctx if is_ctx else out_own
            P.dma(dst.rows(r0 + ti * 128), o)

        def epilogue(blk):
            (c0, n, is_ctx, r0) = blk
            tix = 1 if is_ctx else 0
            nti = n // 128
            cur = ep_a(c0 // 128, c0, 0, tix)
            for ti in range(nti):
                nxt = ep_a(c0 // 128 + ti + 1, c0, ti + 1, tix) if ti + 1 < nti else None
                ep_b(c0 // 128 + ti, r0, ti, is_ctx, cur)
                cur = nxt

        for og in range(8):
            wu = wuR(); wd = wdR()
            for kc in range(8):
                wload(P, stg, wu[:, kc, :], dT(I["w_up"].ap[kc * 128:(kc + 1) * 128, og * 512:(og + 1) * 512], "w"), engs=("pool", "act"))
            for oc in range(4):
                wload(P, stg, wd[:, oc, :], dT(I["w_down"].ap[og * 512 + oc * 128:og * 512 + (oc + 1) * 128, :], "w"), engs=("pool", "act"))
            blks = own_blocks(last)

            def up(blk):
                (c0, n, is_ctx, r0) = blk
                a = aR()
                pus = [P.ps() for _ in range(4)]
                mmg(P, [(pus[oc][:, 0:n], (lambda k, oc=oc: wu[:, k, oc * 128:(oc + 1) * 128]), lambda k: h2T[:, k, c0:c0 + n]) for oc in range(4)], 8)
                for oc in range(4):
                    r = rl()
                    P.act(r[:, 0:n], pus[oc][:, 0:n], AF.Relu)
                    P.tt(a[:, oc, 0:n], pus[oc][:, 0:n], r[:, 0:n], ALU.mult)
                return a

            def down(blk, a):
                (c0, n, is_ctx, r0) = blk
                combos = [(ti, half) for ti in range(n // 128) for half in range(2)]
                for g0 in range(0, len(combos), 4):
                    grp = combos[g0:g0 + 4]
                    pds = [P.ps() for _ in grp]
                    mmg(P, [(pds[i][:, 0:512], (lambda k, ti=ti: a[:, k, ti * 128:(ti + 1) * 128]),
                             (lambda k, half=half: wd[:, k, half * 512:(half + 1) * 512])) for i, (ti, half) in enumerate(grp)], 4)
                    for i, (ti, half) in enumerate(grp):
                        tile = c0 // 128 + ti
                        dst = tsum[:, tile, half * 512:(half + 1) * 512]
                        if og == 0:
                            P.copy(dst, pds[i][:, 0:512], eng="act")
                        else:
                            P.tt(dst, pds[i][:, 0:512], dst, ALU.add)
            a_cur = up(blks[0])
            for bi, blk in enumerate(blks):
                a_nxt = up(blks[bi + 1]) if bi + 1 < len(blks) else None
                down(blk, a_cur)
                a_cur = a_nxt
                if og == 7:
                    epilogue(blk)


def bc(t, pattern):
    a = t.ap
    return t.v(bass.AP(a.tensor, a.offset, [list(a.ap[0])] + [list(p) for p in pattern]))


MAGIC = 12582912.0
TWO_PI = 6.283185307179586


def s5_load(P, I, G):
    Lt = Ctx()
    Lt.are = P.sb([128, NGD], F32, "are"); Lt.aim = P.sb([128, NGD], F32, "aim"); Lt.ldt = P.sb([128, NGD], F32, "ldt")
    Lt.Bre = P.sb([128, NGD, 16], F32, "Bre"); Lt.Bim = P.sb([128, NGD, 16], F32, "Bim")
    Lt.Cre = P.sb([128, NGD, 16], F32, "Cre"); Lt.Cim = P.sb([128, NGD, 16], F32, "Cim")
    for hf in range(2):
        sl = slice(hf * 64, (hf + 1) * 64)
        P.dma(Lt.are[sl, :], dT(I["s5_a_re"].ap.rearrange("d g p -> p (d g)"), "a"), allow_slow_non_contiguous=True, eng="act")
        P.dma(Lt.aim[sl, :], dT(I["s5_a_im"].ap.rearrange("d g p -> p (d g)"), "a"), allow_slow_non_contiguous=True, eng="act")
        P.dma(Lt.Bre[sl], dT(I["s5_b_re"].ap.rearrange("d g p c -> p (d g) c"), "a"))
        P.dma(Lt.Bim[sl], dT(I["s5_b_im"].ap.rearrange("d g p c -> p (d g) c"), "a"))
        P.dma(Lt.Cre[sl], dT(I["s5_c_re"].ap.rearrange("d g c p -> p (d g) c"), "a"), allow_slow_non_contiguous=True, eng="act")
        P.dma(Lt.Cim[sl], dT(I["s5_c_im"].ap.rearrange("d g c p -> p (d g) c"), "a"), allow_slow_non_contiguous=True, eng="act")
    P.dma(Lt.ldt, dT(I["s5_log_dt"].ap.rearrange("d g -> (d g)").partition_broadcast(128), "a"))
    G.s5l = Lt


def s5_alloc(P, I, G):
    PR = P.sb([128, 32, NGD], F32, "PR"); NPI = P.sb([128, 32, NGD], F32, "NPI")
    DA = P.sb([128, 10, NGD], F32, "DA"); DB = P.sb([128, 10, NGD], F32, "DB")
    BX1 = P.sb([128, NGD, 16], F32, "BX1"); BX2 = P.sb([128, NGD, 16], F32, "BX2")
    CX1 = P.sb([128, NGD, 16], F32, "CX1"); CX2 = P.sb([128, NGD, 16], F32, "CX2")
    Dcol = P.sb([128, NGH], F32, "Dcol")
    identF = G.ident
    swapF = P.sb([128, 128], BF16, "swapF"); P.dma(swapF, I["swap"], eng="pool")
    maskf = P.sb([128, 128], BF16, "maskf"); P.dma(maskf, I["mask_f"], eng="pool")
    maskb = P.sb([128, 128], BF16, "maskb"); P.dma(maskb, I["mask_b"], eng="pool")
    sgn = P.sb([128, 1], F32, "sgn"); P.memset(sgn[0:64, :], 1.0); P.memset(sgn[64:128, :], -1.0)
    for i in range(8):
        P.dma(Dcol[i * 16:(i + 1) * 16, :], dT(I["s5_d"].ap.rearrange("(g c) -> c g", c=16), "s5d"), allow_slow_non_contiguous=True)
    G.s5t = dict(PR=PR, NPI=NPI, DA=DA, DB=DB, BX1=BX1, BX2=BX2, CX1=CX1, CX2=CX2, Dcol=Dcol, identF=identF, swapF=swapF, maskf=maskf, maskb=maskb, sgn=sgn)


def s5_params(P, I, G):
    PR = G.s5t["PR"]
    NPI = G.s5t["NPI"]
    DA = G.s5t["DA"]
    DB = G.s5t["DB"]
    BX1 = G.s5t["BX1"]
    BX2 = G.s5t["BX2"]
    CX1 = G.s5t["CX1"]
    CX2 = G.s5t["CX2"]
    Dcol = G.s5t["Dcol"]
    identF = G.s5t["identF"]
    swapF = G.s5t["swapF"]
    maskf = G.s5t["maskf"]
    maskb = G.s5t["maskb"]
    sgn = G.s5t["sgn"]
    with P.scope():
        are, aim, ldt = G.s5l.are, G.s5l.aim, G.s5l.ldt
        dt_ = P.sb([128, NGD], F32, "dt")
        P.act(dt_, ldt, AF.Exp)
        lr = P.sb([128, NGD], F32, "lr"); li = P.sb([128, NGD], F32, "li")
        P.tt(lr, are, dt_, ALU.mult); P.tt(li, aim, dt_, ALU.mult)
        with P.scope():
            elist = [t - 7 for t in range(16)] + [8 - t for t in range(16)]
            LR = P.sb([128, 32, NGD], F32, "LR"); LI = P.sb([128, 32, NGD], F32, "LI")
            for idx, e in enumerate(elist):
                P.ts(LR[:, idx, :], lr, float(e), None, op0=ALU.mult)
                P.ts(LI[:, idx, :], li, float(e), None, op0=ALU.mult, eng="pool")
            mag = P.sb([128, 32, NGD], F32, "mag")
            P.act(mag, LR, AF.Exp)
            rr = P.sb([128, 32, NGD], F32, "rr"); kk = P.sb([128, 32, NGD], F32, "kk")

            def sin_of(dst, ang_t, shift):
                P.ts(rr, ang_t, 1.0 / TWO_PI, shift / TWO_PI, op0=ALU.mult, op1=ALU.add)
                P.ts(kk, rr, MAGIC, None, op0=ALU.add)
                P.ts(kk, kk, MAGIC, None, op0=ALU.subtract)
                P.tt(rr, rr, kk, ALU.subtract)
                P.ts(rr, rr, TWO_PI, None, op0=ALU.mult)
                P.ts(rr, rr, 3.1415925, -3.1415925, op0=ALU.min, op1=ALU.max)
                P.act(dst, rr, AF.Sin)
            sn = LR
            sin_of(sn, LI, 0.0)
            P.stt(NPI, mag, -1.0, sn, ALU.mult, ALU.mult)
            sin_of(sn, LI, TWO_PI / 4)
            P.tt(PR, mag, sn, ALU.mult)
        cr_ = P.sb([128, NGD], F32, "cr"); ci_ = P.sb([128, NGD], F32, "ci")
        t1 = P.sb([128, NGD], F32, "t1"); t2 = P.sb([128, NGD], F32, "t2")
        P.copy(DA[:, 0, :], PR[:, 15, :])
        P.ts(DB[:, 0, :], NPI[:, 15, :], -1.0, None, op0=ALU.mult)
        for m in range(1, 10):
            P.tt(t1, DA[:, m - 1, :], DA[:, m - 1, :], ALU.mult)
            P.tt(t2, DB[:, m - 1, :], DB[:, m - 1, :], ALU.mult)
            P.tt(DA[:, m, :], t1, t2, ALU.subtract)
            P.stt(DB[:, m, :], DA[:, m - 1, :], 2.0, DB[:, m - 1, :], ALU.mult, ALU.mult)
        P.ts(DB, DB, sgn[:, 0:1], None, op0=ALU.mult)
        den = P.sb([128, NGD], F32, "den"); nr = P.sb([128, NGD], F32, "nr"); abi = P.sb([128, NGD], F32, "abi")
        P.tt(t1, are, are, ALU.mult); P.tt(t2, aim, aim, ALU.mult); P.tt(den, t1, t2, ALU.add); P.recip(den, den)
        P.ts(nr, PR[:, 8, :], -1.0, None, op0=ALU.add)
        P.ts(abi, NPI[:, 8, :], -1.0, None, op0=ALU.mult)
        P.tt(t1, nr, are, ALU.mult); P.tt(t2, abi, aim, ALU.mult); P.tt(cr_, t1, t2, ALU.add); P.tt(cr_, cr_, den, ALU.mult)
        P.tt(t1, abi, are, ALU.mult); P.tt(t2, nr, aim, ALU.mult); P.tt(ci_, t1, t2, ALU.subtract); P.tt(ci_, ci_, den, ALU.mult)
        crb = bc(cr_, [[1, NGD], [0, 16]]); cib = bc(ci_, [[1, NGD], [0, 16]])
        Bre, Bim, Cre, Cim = G.s5l.Bre, G.s5l.Bim, G.s5l.Cre, G.s5l.Cim
        bbr = P.sb([128, NGD, 16], F32, "bbr"); bbi = P.sb([128, NGD, 16], F32, "bbi"); t3 = P.sb([128, NGD, 16], F32, "t3")
        P.tt(bbr, Bre, crb, ALU.mult); P.tt(t3, Bim, cib, ALU.mult); P.tt(bbr, bbr, t3, ALU.subtract)
        P.tt(bbi, Bim, crb, ALU.mult); P.tt(t3, Bre, cib, ALU.mult); P.tt(bbi, bbi, t3, ALU.add)
        P.copy(BX1[0:64], bbr[0:64]); P.copy(BX1[64:128], bbi[64:128])
        P.copy(BX2[0:64], bbi[0:64]); P.ts(BX2[64:128], bbr[64:128], -1.0, None, op0=ALU.mult)
        P.copy(CX1[0:64], Cre[0:64]); P.ts(CX1[64:128], Cim[64:128], -1.0, None, op0=ALU.mult)
        P.copy(CX2[0:64], Cim[0:64]); P.copy(CX2[64:128], Cre[64:128])


def stage_S5(P, I, G, last):
    NQ = LH + (0 if last else CTX)
    G.ysD = P.dram([512, NQ], BF16, "ysD")
    with P.scope():
        PR = G.s5t["PR"]
        NPI = G.s5t["NPI"]
        DA = G.s5t["DA"]
        DB = G.s5t["DB"]
        BX1 = G.s5t["BX1"]
        BX2 = G.s5t["BX2"]
        CX1 = G.s5t["CX1"]
        CX2 = G.s5t["CX2"]
        Dcol = G.s5t["Dcol"]
        identF = G.s5t["identF"]
        swapF = G.s5t["swapF"]
        maskf = G.s5t["maskf"]
        maskb = G.s5t["maskb"]
        sgn = G.s5t["sgn"]
        Sel = P.sb([128, 64, 128], BF16, "Sel"); SelT = P.sb([128, 64, 128], BF16, "SelT")
        for q in range(4):
            P.dma(Sel[:, q * 16:(q + 1) * 16, :], dT(I["sel8"].ap[:, q * 16:(q + 1) * 16, :], "sel8"), eng="pool")
            P.dma(SelT[:, q * 16:(q + 1) * 16, :], dT(I["sel8T"].ap[:, q * 16:(q + 1) * 16, :], "sel8T"), eng="pool")
        KQ = {nm: Rot(P, [128, 16, 16], BF16, nm, 2) for nm in ["Kf", "Qf", "Kb", "Qb"]}
        tA = Rot(P, [128, 16, 16], F32, "tA", 1); tB = Rot(P, [128, 16, 16], F32, "tB", 1)
        tC = Rot(P, [128, 16, 16], F32, "tC", 1); tD = Rot(P, [128, 16, 16], F32, "tD", 1)
        SgR = Rot(P, [128, 128], BF16, "Sg", 2)
        WeR = Rot(P, [128, 128], BF16, "We", 4)
        s1R = Rot(P, [128, 128], F32, "s1", 1); s2R = Rot(P, [128, 128], F32, "s2", 1)
        UcR = Rot(P, [128, 576], BF16, "Uc", 2)
        XR = {d: [P.sb([128, 545], BF16, f"X{d}{i}") for i in range(2)] for d in "fb"}
        for d in "fb":
            for x in XR[d]:
                P.memset(x, 0.0)
        MdR = {d: Rot(P, [128, 10, 128], BF16, "Md" + d, 2) for d in "fb"}
        mA = Rot(P, [128, 10, 128], BF16, "mA", 1); mB = Rot(P, [128, 10, 128], BF16, "mB", 1)
        Yt = P.sb([128, 8, 544], BF16, "Yt")
        ybufR = Rot(P, [128, L + CTX], BF16, "ybuf", 2)
        yown = Rot(P, [128, LH], BF16, "yown", 1)
        ysend = [P.dram([128, L + CTX], BF16, f"ysend{j}") for j in range(2)]
        ygath = [P.dram([256, L + CTX], BF16, f"ygath{j}") for j in range(2)]
        identB = G.ident
        flip = 0
        spec = {"Kf": (17, 15), "Qf": (7, 9), "Kb": (7, 8), "Qb": (16, 16)}

        def m128(t, lo):
            return t.v(t.ap[:, lo:lo + 8, :].rearrange("p a b -> p (a b)"))

        def build(g):
            stt_ = {"mats": {}, "We": {}, "Mall": {}}
            th = []
            for dname, gd in (("f", g), ("b", NGH + g)):
                for kind, X1, X2, eng in (("K", BX1, BX2, "dve"), ("Q", CX1, CX2, "pool")):
                    t0_, ne = spec[kind + dname]
                    out = KQ[kind + dname]()[:, 0:ne, :]
                    a_ = (tA() if kind == "K" else tC())[:, 0:ne, :]
                    b_ = (tB() if kind == "K" else tD())[:, 0:ne, :]
                    prb = bc(PR[:, t0_, gd:gd + 1], [[NGD, ne], [0, 16]]); npb = bc(NPI[:, t0_, gd:gd + 1], [[NGD, ne], [0, 16]])
                    x1b = bc(X1[:, gd, :], [[0, ne], [1, 16]]); x2b = bc(X2[:, gd, :], [[0, ne], [1, 16]])
                    th.append(lambda a_=a_, prb=prb, x1b=x1b, eng=eng: P.tt(a_, prb, x1b, ALU.mult, eng=eng))
                    th.append(lambda b_=b_, npb=npb, x2b=x2b, eng=eng: P.tt(b_, npb, x2b, ALU.mult, eng=eng))
                    th.append(lambda out=out, a_=a_, b_=b_, eng=eng: P.tt(out, a_, b_, ALU.add, eng=eng))
                    stt_["mats"][kind + dname] = out
            Sg = SgR()
            stt_["Sg"] = Sg
            mt = stt_["mats"]

            def th_S():
                psf = P.ps(); psb_ = P.ps()
                P.mm(psf[:, 0:128], m128(mt["Kf"], 7), m128(mt["Qf"], 0))
                P.mm(psb_[:, 0:128], m128(mt["Kb"], 0), m128(mt["Qb"], 8))
                s1 = s1R(); s2 = s2R()
                P.tt(s1, psf[:, 0:128], maskf, ALU.mult)
                P.tt(s2, psb_[:, 0:128], maskb, ALU.mult)
                P.tt(s1, s1, s2, ALU.add)
                P.stt(Sg, identF, Dcol[:, g:g + 1], s1, ALU.mult, ALU.add)
            th.append(th_S)
            for dname, key in (("f", "Kf"), ("b", "Kb")):
                w = WeR()
                stt_["We"][dname] = w

                def th_W(w=w, key=key):
                    pt = P.ps()
                    ptb = pt.v(pt.ap.bitcast(BF16))
                    P.transpose(ptb[:, 0:128], m128(mt[key], 0), identB)
                    P.copy(w, ptb[:, 0:128], eng="act")
                th.append(th_W)
            stt_["Wo"] = {"f": m128(mt["Qf"], 1), "b": m128(mt["Qb"], 0)}
            for dname, gd in (("f", g), ("b", NGH + g)):
                Mall = MdR[dname]()
                stt_["Mall"][dname] = Mall
                ta = mA(); tb = mB()
                idb = bc(identF, [[0, 10], [1, 128]]); swb = bc(swapF, [[0, 10], [1, 128]])
                dab = bc(DA[:, :, gd], [[NGD, 10], [0, 128]]); dbb = bc(DB[:, :, gd], [[NGD, 10], [0, 128]])
                th.append(lambda ta=ta, idb=idb, dab=dab: P.tt(ta, idb, dab, ALU.mult, eng="pool"))
                th.append(lambda tb=tb, swb=swb, dbb=dbb: P.tt(tb, swb, dbb, ALU.mult, eng="pool"))
                th.append(lambda Mall=Mall, ta=ta, tb=tb: P.tt(Mall, ta, tb, ALU.add, eng="pool"))
            return stt_, th

        cur_state, th0 = build(0)
        for t_ in th0:
            t_()
        for g in range(NGH):
            j, gl = g // 8, g % 8
            if g + 1 < NGH:
                nxt_state, pend = build(g + 1)
            else:
                nxt_state, pend = None, []
            per_step = (len(pend) + 9) // 10
            Sg = cur_state["Sg"]; We = cur_state["We"]; Wo = cur_state["Wo"]
            Uc = UcR()
            puA = P.ps(); puB = P.ps(); pu2 = P.ps()
            for i in range(8):
                P.mm(puA[:, 0:256], Sel[:, gl * 8 + i, :], G.uT[:, j, i, 32:288], start=(i == 0), stop=(i == 7))
                P.mm(puB[:, 0:256], Sel[:, gl * 8 + i, :], G.uT[:, j, i, 288:544], start=(i == 0), stop=(i == 7))
                P.mm(pu2[:, 0:32], Sel[:, gl * 8 + i, :], G.uT[:, j, i, 0:32], start=(i == 0), stop=(i == 7))
            P.copy(Uc[:, 32:288], puA[:, 0:256], eng="act")
            P.copy(Uc[:, 288:544], puB[:, 0:256])
            P.copy(Uc[:, 0:32], pu2[:, 0:32], eng="act")
            P.copy(Uc[:, 544:576], pu2[:, 0:32])
            st_ = {}
            for dname, gd in (("f", g), ("b", NGH + g)):
                ucoff = 0 if dname == "f" else 32
                xoff = 1 if dname == "f" else 0
                cur = XR[dname][0]; nxt = XR[dname][1]
                pa = P.ps(); pb2 = P.ps()
                P.mm(pa[:, 0:512], We[dname], Uc[:, ucoff:ucoff + 512])
                P.mm(pb2[:, 0:32], We[dname], Uc[:, ucoff + 512:ucoff + 544])
                P.copy(cur[:, xoff:xoff + 512], pa[:, 0:512], eng="act")
                P.copy(cur[:, xoff + 512:xoff + 544], pb2[:, 0:32])
                st_[dname] = [cur, nxt, xoff, cur_state["Mall"][dname]]
            for m in range(10):
                d = 1 << m
                work = []
                for dname in ("f", "b"):
                    cur, nxt, xoff, Mall = st_[dname]
                    for (lo, hi) in ((0, 272), (272, 544)):
                        ps = P.ps()
                        if dname == "f":
                            s_ = max(lo, d)
                            has = s_ < hi
                            shift = (ps[:, s_ - lo:hi - lo], cur[:, xoff + s_ - d:xoff + hi - d]) if has else None
                        else:
                            e_ = min(hi, 544 - d)
                            has = lo < e_
                            shift = (ps[:, 0:e_ - lo], cur[:, xoff + lo + d:xoff + e_ + d]) if has else None
                        work.append((dname, ps, lo, hi, shift, cur, nxt, xoff, Mall))
                for (dname, ps, lo, hi, shift, cur, nxt, xoff, Mall) in work:
                    P.mm(ps[:, 0:hi - lo], identB, cur[:, xoff + lo:xoff + hi], start=True, stop=(shift is None))
                for (dname, ps, lo, hi, shift, cur, nxt, xoff, Mall) in work:
                    if shift is not None:
                        P.mm(shift[0], Mall[:, m, :], shift[1], start=False, stop=True)
                for (dname, ps, lo, hi, shift, cur, nxt, xoff, Mall) in work:
                    flip ^= 1
                    P.copy(nxt[:, xoff + lo:xoff + hi], ps[:, 0:hi - lo], eng=("act" if flip else "dve"))
                for dname in ("f", "b"):
                    st_[dname][0], st_[dname][1] = st_[dname][1], st_[dname][0]
                for _ in range(per_step):
                    if pend:
                        pend.pop(0)()
            while pend:
                pend.pop(0)()
            Xfin = {dname: st_[dname][0] for dname in ("f", "b")}
            Xf, Xb = Xfin["f"], Xfin["b"]
            py = P.ps()
            pyc = P.ps() if not last else None
            ytl = [(Sg, Uc[:, 32:544], Uc[:, 0:32]), (Wo["f"], Xf[:, 32:544], Xf[:, 0:32]), (Wo["b"], Xb[:, 1:513], Xb[:, 513:545])]
            for q, (lh, r1, r2) in enumerate(ytl):
                P.mm(py[:, 0:512], lh, r1, start=(q == 0), stop=(q == 2))
                if not last:
                    P.mm(pyc[:, 0:32], lh, r2, start=(q == 0), stop=(q == 2))
            P.copy(Yt[:, gl, 0:512], py[:, 0:512], eng="act")
            if not last:
                P.copy(Yt[:, gl, 512:544], pyc[:, 0:32])
            cur_state = nxt_state
            if gl == 7:
                yb = ybufR()
                for i in range(8):
                    ps = P.ps()
                    ps2 = P.ps() if not last else None
                    for g2 in range(8):
                        P.mm(ps[:, 0:512], SelT[:, g2 * 8 + i, :], Yt[:, g2, 0:512], start=(g2 == 0), stop=(g2 == 7))
                        if not last:
                            P.mm(ps2[:, 0:32], SelT[:, g2 * 8 + i, :], Yt[:, g2, 512:544], start=(g2 == 0), stop=(g2 == 7))
                    P.copy(yb.v(yb.ap[:, i:L:8]), ps[:, 0:512], eng=("act" if i % 2 else "dve"))
                    if not last:
                        P.copy(yb.v(yb.ap[:, L + i:L + CTX:8]), ps2[:, 0:32])
                P.dma(ysend[j], yb)
        P.flush()
        groups = [[0, 1], [2, 3], [4, 5], [6, 7]]
        for j in range(2):
            P.add("pool", lambda e, j=j: e.collective_compute("AllGather", ALU.bypass, replica_groups=groups,
                                                              ins=[ysend[j].ap.opt()], outs=[ygath[j].ap.opt()]), [ysend[j]], [ygath[j]])
            P.add("pool", None, [ygath[j]], [])
        P.flush()
        for J in range(4):
            src = ygath[J % 2]
            r0_ = 128 * (J // 2)
            yb = ybufR()
            P.dma(yb, src.v(src.ap[r0_:r0_ + 128, :]))
            yo = yown()
            P.ts(yb[:, 0:LH], yb[:, 0:LH], G.sel[:, 0:1], None, op0=ALU.mult)
            P.stt(yo, yb[:, LH:L], G.sel[:, 1:2], yb[:, 0:LH], ALU.mult, ALU.add)
            P.dma(G.ysD.v(G.ysD.ap[J * 128:(J + 1) * 128, 0:LH]), yo)
            if not last:
                P.dma(G.ysD.v(G.ysD.ap[J * 128:(J + 1) * 128, LH:LH + CTX]), yb[:, L:L + CTX])


def emit_layer(P, I, G, last, out_own, out_ctx):
    stage_prep(P, I, G)
    with P.scope():
        alloc_persist(P, G)
        with P.scope():
            G.uT = P.sb([128, 2, 8, NK // 8], BF16, "uT")
            s5_alloc(P, I, G)
            with P.scope():
                s5_load(P, I, G)
                stage_A(P, I, G)
                s5_params(P, I, G)
            stage_S5(P, I, G, last)
        stage_B1(P, I, G, last)
        stage_B2(P, I, G, last)
        stage_B3(P, I, G, last)
    stage_B4a(P, I, G, last)
    stage_B4b(P, I, G, last, out_own, out_ctx)
    P.flush()


def build_fused():
    nc = bass.Bass("TRN2", target_bir_lowering=False)
    Cn = declare_consts(nc)
    W = [declare_weights(nc, l) for l in range(2)]
    y_out = dT(nc.dram_tensor("y_own", [LH, D], F32, kind="ExternalOutput").ap(), "y_own")
    with ExitStack() as st:
        P = Prog(nc, st)
        P.init_psum()
        NCH = 4
        CR = LH // NCH
        x1o = [P.dram([CR, D], F32, f"x1o{c}") for c in range(NCH)]
        x1g = [P.dram([2 * CR, D], F32, f"x1g{c}") for c in range(NCH)]
        ctx1 = P.dram([CTX, D], F32, "ctx1")
        own_src = RowSrc(lambda r0: x1o[r0 // CR].v(x1o[r0 // CR].ap[r0 % CR:r0 % CR + 128, :]))

        def all_fn(r0):
            half, rr = r0 // LH, r0 % LH
            c, i = rr // CR, rr % CR
            return x1g[c].v(x1g[c].ap[half * CR + i:half * CR + i + 128, :])
        with P.scope():
            G = Ctx()
            I0 = dict(Cn); I0.update(W[0])
            for k in ("x_all", "x_own", "ctx"):
                I0[k] = flat_src(Cn[k])
            emit_layer(P, I0, G, False, own_src, flat_src(ctx1))
        groups = [[0, 1], [2, 3], [4, 5], [6, 7]]
        for c in range(NCH):
            P.add("pool", lambda e, c=c: e.collective_compute("AllGather", ALU.bypass, replica_groups=groups,
                                                              ins=[x1o[c].ap.opt()], outs=[x1g[c].ap.opt()]), [x1o[c]], [x1g[c]])
            P.add("pool", None, [x1g[c]], [])
        P.flush()
        with P.scope():
            G = Ctx()
            I1 = dict(Cn); I1.update(W[1])
            I1["x_all"] = RowSrc(all_fn); I1["x_own"] = own_src; I1["ctx"] = flat_src(ctx1)
            emit_layer(P, I1, G, True, flat_src(y_out), None)
    return nc


_NC_CACHE = {}


def kernel(**inputs):
    inputs = {k: np.asarray(v) for k, v in inputs.items()}
    C = host_constants()
    if "nc" not in _NC_CACHE:
        _NC_CACHE["nc"] = build_fused()
    nc = _NC_CACHE["nc"]
    in_maps = [per_core_inputs(inputs, core, C) for core in range(8)]
    res = run_bass_kernel_spmd(nc, in_maps, core_ids=list(range(8)))
    out = np.empty((4, L, D), np.float32)
    for core in range(8):
        b, hh = core // 2, core % 2
        out[b, hh * LH:(hh + 1) * LH] = np.asarray(res.results[core]["y_own"])
    return out
```

```python
import numpy as np
import concourse.bass as bass
import concourse.mybir as mybir
from concourse.bass_utils import run_bass_kernel_spmd
from contextlib import ExitStack, contextmanager

F32 = mybir.dt.float32
BF16 = mybir.dt.bfloat16
I32 = mybir.dt.int32
AF = mybir.ActivationFunctionType
ALU = mybir.AluOpType
AX = mybir.AxisListType

ENGS = ["pe", "act", "dve", "pool", "sp"]
DMA_WIN = 8
SAME_ENG_SYNC = True


class T:
    __slots__ = ("ap", "keys")

    def __init__(self, ap, keys):
        self.ap = ap
        self.keys = tuple(keys)

    def __getitem__(self, sl):
        return T(self.ap[sl], self.keys)

    def v(self, ap):
        return T(ap, self.keys)

    def k(self, *sub):
        return T(self.ap, [(self.keys[0],) + tuple(sub)])


class Prog:
    def __init__(self, nc, stack):
        self.nc = nc
        self.stack = stack
        self.cur = stack
        self.streams = {e: [] for e in ENGS}
        self.last_writer = {}
        self.readers = {}
        self.ndma = {e: 0 for e in ENGS}
        self.sigcount = {e: 0 for e in ENGS}
        self.waited = {e: {} for e in ENGS}
        self.nt = 0
        self.psum_banks = []
        self.psum_i = 0
        self.sem = {e: stack.enter_context(nc.semaphore(f"s_{e}")) for e in ENGS}
        self.dsem = {e: [stack.enter_context(nc.semaphore(f"d_{e}{i}")) for i in range(DMA_WIN)]
                     for e in ("sp", "pool", "act")}
        self.dbg = {}
        self.nops = {e: 0 for e in ENGS}

    def sb(self, shape, dt, name=None):
        self.nt += 1
        name = name or "t"
        nm = f"{name}_{self.nt}"
        t = self.cur.enter_context(self.nc.sbuf_tensor(nm, list(shape), dt))
        return T(t[:], [nm])

    def dram(self, shape, dt, name):
        self.nt += 1
        nm = f"{name}_{self.nt}"
        t = self.nc.dram_tensor(nm, list(shape), dt, kind="Internal")
        return T(t.ap(), [nm])

    def init_psum(self, n=8):
        for i in range(n):
            t = self.stack.enter_context(self.nc.psum_tensor(f"bank{i}", [128, 512], F32))
            self.psum_banks.append(T(t[:], [f"bank{i}"]))

    def ps(self):
        b = self.psum_banks[self.psum_i % len(self.psum_banks)]
        self.psum_i += 1
        return b

    @contextmanager
    def scope(self):
        prev = self.cur
        with ExitStack() as st:
            self.cur = st
            yield
            self.flush()
        self.cur = prev

    def add(self, eng, fn, reads=(), writes=(), dma=False):
        deps = set()
        rk = [k for t in reads for k in t.keys]
        wk = [k for t in writes for k in t.keys]
        for k in rk:
            if k in self.last_writer:
                deps.add(self.last_writer[k])
        for k in wk:
            if k in self.last_writer:
                deps.add(self.last_writer[k])
            for r in self.readers.get(k, ()):
                deps.add(r)
        idx = len(self.streams[eng])
        me = (eng, idx)
        deps.discard(me)
        op = dict(fn=fn, deps=deps, dma=dma, signal=False, dman=None)
        if dma:
            op["dman"] = self.ndma[eng]
            self.ndma[eng] += 1
        self.streams[eng].append(op)
        for k in rk:
            self.readers.setdefault(k, []).append(me)
        for k in wk:
            self.last_writer[k] = me
            self.readers[k] = []
        return me

    def dma(self, out, in_, eng="sp", **kw):
        o = out.ap
        i = in_.ap
        return self.add(eng, lambda e: e.dma_start(out=o, in_=i, **kw), [in_], [out], dma=True)

    def mm(self, out, lhsT, rhs, start=True, stop=True, **kw):
        return self.add("pe", lambda e: e.matmul(out.ap, lhsT.ap, rhs.ap, start=start, stop=stop, **kw),
                        [lhsT, rhs], [out])

    def transpose(self, out, in_, ident):
        return self.add("pe", lambda e: e.transpose(out.ap, in_.ap, ident.ap), [in_, ident], [out])

    def act(self, out, in_, func, bias=None, scale=None, eng="act", accum_out=None):
        reads = [in_]
        kw = {}
        if bias is not None:
            if isinstance(bias, T):
                reads.append(bias); kw["bias"] = bias.ap
            else:
                kw["bias"] = bias
        if scale is not None:
            if isinstance(scale, T):
                reads.append(scale); kw["scale"] = scale.ap
            else:
                kw["scale"] = scale
        writes = [out]
        if accum_out is not None:
            kw["accum_out"] = accum_out.ap; writes.append(accum_out)
        return self.add(eng, lambda e: e.activation(out.ap, in_.ap, func, **kw), reads, writes)

    def tt(self, out, a, b, op, eng="dve"):
        return self.add(eng, lambda e: e.tensor_tensor(out.ap, a.ap, b.ap, op), [a, b], [out])

    def ts(self, out, a, s1, s2=None, op0=ALU.mult, op1=None, eng="dve"):
        reads = [a]
        v1 = s1.ap if isinstance(s1, T) else s1
        if isinstance(s1, T): reads.append(s1)
        v2 = s2.ap if isinstance(s2, T) else s2
        if isinstance(s2, T): reads.append(s2)
        if op1 is None:
            return self.add(eng, lambda e: e.tensor_scalar(out.ap, a.ap, v1, None, op0), reads, [out])
        return self.add(eng, lambda e: e.tensor_scalar(out.ap, a.ap, v1, v2, op0, op1), reads, [out])

    def stt(self, out, a, s, b, op0, op1, eng="dve"):
        reads = [a, b]
        v = s.ap if isinstance(s, T) else s
        if isinstance(s, T): reads.append(s)
        return self.add(eng, lambda e: e.scalar_tensor_tensor(out.ap, a.ap, v, b.ap, op0, op1), reads, [out])

    def copy(self, out, in_, eng="dve"):
        if eng == "act":
            return self.add("act", lambda e: e.copy(out.ap, in_.ap), [in_], [out])
        return self.add(eng, lambda e: e.tensor_copy(out.ap, in_.ap), [in_], [out])

    def memset(self, out, val, eng="pool"):
        return self.add(eng, lambda e: e.memset(out.ap, val), [], [out])

    def recip(self, out, in_, eng="dve"):
        return self.add(eng, lambda e: e.reciprocal(out.ap, in_.ap), [in_], [out])

    def recip_fast(self, out, in_):
        return self.add("dve", lambda e: e.reciprocal_approx_fast(out.ap, in_.ap), [in_], [out])

    def bn_stats(self, out, in_):
        return self.add("dve", lambda e: e.bn_stats(out.ap, in_.ap), [in_], [out])

    def bn_aggr(self, out, in_):
        return self.add("dve", lambda e: e.bn_aggr(out.ap, in_.ap), [in_], [out])

    def debug_out(self, name, t, shape, dt=F32):
        d = self.nc.dram_tensor(name, list(shape), dt, kind="ExternalOutput").ap()
        self.dbg[name] = d
        return self.dma(T(d, [name]), t)

    def flush(self):
        nc = self.nc
        streams = self.streams
        lasts = []
        for e in ENGS:
            for j in range(len(streams[e]) - 1, -1, -1):
                op = streams[e][j]
                if op["fn"] is not None and not op["dma"]:
                    lasts.append((e, j))
                    break
        dmas = [(e, j) for e in ENGS for j, op in enumerate(streams[e]) if op["dma"]]
        for e in ENGS:
            deps = set(l for l in lasts if l[0] != e) | set(dmas)
            streams[e].append(dict(fn=None, deps=deps, dma=False, signal=False, dman=None))
        for e in ENGS:
            for op in streams[e]:
                for (f, j) in op["deps"]:
                    d = streams[f][j]
                    if not d["dma"]:
                        if f == e and not SAME_ENG_SYNC:
                            continue
                        d["signal"] = True
        for e in ENGS:
            for op in streams[e]:
                if op["signal"]:
                    self.sigcount[e] += 1
                    op["sigval"] = self.sigcount[e]
        sem, dsem = self.sem, self.dsem

        def run(ename):
            def body(eng):
                waited = self.waited[ename]

                def wait(s, v, key):
                    if waited.get(key, 0) >= v:
                        return
                    waited[key] = v
                    eng.wait_ge(s, v)

                for op in streams[ename]:
                    for (f, j) in sorted(op["deps"]):
                        d = streams[f][j]
                        if d["dma"]:
                            n = d["dman"]
                            wait(dsem[f][n % DMA_WIN], 16 * (n // DMA_WIN + 1), (f, n % DMA_WIN))
                        else:
                            if f == ename and not SAME_ENG_SYNC:
                                continue
                            wait(sem[f], d["sigval"], f)
                    if op["dma"]:
                        n = op["dman"]
                        if n >= DMA_WIN:
                            wait(dsem[ename][n % DMA_WIN], 16 * (n // DMA_WIN), (ename, n % DMA_WIN))
                    if op["fn"] is None:
                        continue
                    ins = op["fn"](eng)
                    if op["dma"]:
                        ins.then_inc(dsem[ename][op["dman"] % DMA_WIN], 16)
                    elif op["signal"]:
                        ins.then_inc(sem[ename], 1)
            return body

        with nc.Block() as block:
            block.tensor(run("pe"))
            block.scalar(run("act"))
            block.vector(run("dve"))
            block.gpsimd(run("pool"))
            block.sync(run("sp"))
        for e in ENGS:
            self.nops[e] += len(streams[e])
        self.streams = {e: [] for e in ENGS}
        self.last_writer = {}
        self.readers = {}


class Rot:
    def __init__(self, P, shape, dt, name, n):
        self.tiles = [P.sb(shape, dt, f"{name}{i}") for i in range(n)]
        self.i = 0

    def __call__(self):
        t = self.tiles[self.i % len(self.tiles)]
        self.i += 1
        return t


def _ps6(self):
    b = self.psum_banks[self.psum_i % 6]
    self.psum_i += 1
    return b


def _psacc(self):
    self.acc_i = getattr(self, "acc_i", 0) + 1
    return self.psum_banks[6 + self.acc_i % 2]


Prog.ps = _ps6
Prog.ps_acc = _psacc


D = 1024
L = 4096
LH = 2048
CTX = 256
NK = CTX + L
NKT = NK // 128
EPS = 1e-6
OFF_AK, OFF_AV, OFF_CKV, OFF_CKR, OFF_U, NST = 0, 128, 256, 512, 544, 1056
OFF_AQ, OFF_CQ, OFF_GATE, NIN = 1056, 1568, 2336, 5408
ALPHA = (2.0 * 2) ** 0.25
NGH = 16
NGD = 2 * NGH


def dT(ap, name):
    return T(ap, [name])


class Ctx:
    pass


class RowSrc:
    def __init__(self, fn):
        self.fn = fn

    def rows(self, r0):
        return self.fn(r0)


def flat_src(t):
    return RowSrc(lambda r0: t.v(t.ap[r0:r0 + 128, :]))


WEIGHT_SHAPES = {
    "w_mod": [D, 6 * D], "b_mod": [6 * D], "w_in": [D, NIN], "a_q_gain": [64], "a_k_gain": [64],
    "c_q_a_gain": [768], "c_kv_a_gain": [256], "c_w_qb": [768, 768], "c_w_kvb": [256, 1024],
    "s5_a_re": [2, 16, 64], "s5_a_im": [2, 16, 64], "s5_log_dt": [2, 16],
    "s5_b_re": [2, 16, 64, 16], "s5_b_im": [2, 16, 64, 16], "s5_c_re": [2, 16, 16, 64], "s5_c_im": [2, 16, 16, 64],
    "s5_d": [256], "s5_w_glu": [512, 1024], "w_branch_a": [512, D], "w_branch_s5": [512, D], "w_branch_c": [512, D],
    "w_out": [D, D], "ln1_g": [D], "ln1_b": [D], "w_up": [D, 4 * D], "w_down": [4 * D, D], "ln2_g": [D], "ln2_b": [D],
}
CONST_SHAPES = {
    "x_all": [L, D], "x_own": [LH, D], "ctx": [CTX, D], "cvec": [2, D],
    "ident": [128, 128], "blk64": [128, 128], "perm64": [128, 128], "perm32": [32, 32],
    "ropek_cos": [128, L], "ropek_sin": [128, L], "ropeq_cos": [128, LH], "ropeq_sin": [128, LH],
    "rope32k_cos": [32, L], "rope32k_sin": [32, L], "rope32q_cos": [32, LH], "rope32q_sin": [32, LH],
    "sel": [128, 2], "swap": [128, 128], "mask_f": [128, 128], "mask_b": [128, 128],
    "sel8": [128, 64, 128], "sel8T": [128, 64, 128],
}


def declare_consts(nc):
    return {k: dT(nc.dram_tensor(k, list(v), F32, kind="ExternalInput").ap(), k) for k, v in CONST_SHAPES.items()}


def declare_weights(nc, l):
    return {k: dT(nc.dram_tensor(f"{k}_{l}", list(v), F32, kind="ExternalInput").ap(), f"{k}_{l}")
            for k, v in WEIGHT_SHAPES.items()}


def declare_inputs(nc, last):
    I = declare_consts(nc)
    I.update(declare_weights(nc, 1 if last else 0))
    return I


def host_constants():
    import math
    C = {}
    C["ident"] = np.eye(128, dtype=np.float32)
    blk = np.zeros((128, 128), np.float32); blk[:64, :64] = 1 / 64; blk[64:, 64:] = 1 / 64
    C["blk64"] = blk

    def perm_and_sign(dim):
        half = dim // 2; q = half // 2
        Pm = np.zeros((dim, dim), np.float32)
        sg = np.zeros(dim, np.float32)
        for m in range(dim):
            if (m % half) < q:
                Pm[m + q, m] = 1; sg[m] = -1
            else:
                Pm[m - q, m] = 1; sg[m] = 1
        return Pm, sg
    P64, s64 = perm_and_sign(64)
    p128 = np.zeros((128, 128), np.float32); p128[:64, :64] = P64; p128[64:, 64:] = P64
    C["perm64"] = p128
    P32, s32 = perm_and_sign(32)
    C["perm32"] = P32

    def tables(dim):
        half = dim // 2
        inv = 10000.0 ** (-np.arange(0, half, 2, dtype=np.float32) / half)
        rows = L // 64
        row = np.repeat(np.arange(rows, dtype=np.float32), 64)
        col = np.tile(np.arange(64, dtype=np.float32), rows)
        ang_r = row[:, None] * inv; ang_c = col[:, None] * inv
        ang = np.concatenate([ang_r, ang_r, ang_c, ang_c], axis=-1).astype(np.float32)
        return np.cos(ang).T.astype(np.float32), np.sin(ang).T.astype(np.float32)
    c64, s64t = tables(64)
    s64t = s64t * s64[:, None]
    C["ropek_cos"] = np.concatenate([c64, c64], 0); C["ropek_sin"] = np.concatenate([s64t, s64t], 0)
    c32, s32t = tables(32)
    s32t = s32t * s32[:, None]
    C["rope32k_cos"] = c32; C["rope32k_sin"] = s32t
    sw = np.zeros((128, 128), np.float32)
    for p in range(64):
        sw[p, 64 + p] = 1; sw[64 + p, p] = 1
    C["swap"] = sw
    ii = np.arange(128) // 16
    C["mask_f"] = (ii[:, None] <= ii[None, :]).astype(np.float32)
    C["mask_b"] = (ii[:, None] >= ii[None, :]).astype(np.float32)
    sel = np.zeros((128, 64, 128), np.float32)
    for gl in range(8):
        for i in range(8):
            for c in range(16):
                sel[gl * 16 + c, gl * 8 + i, i * 16 + c] = 1
    C["sel8"] = sel
    C["sel8T"] = np.ascontiguousarray(sel.transpose(2, 1, 0))
    return C


def per_core_inputs(inputs, core, C, layers=(0, 1)):
    b, hh = core // 2, core % 2
    m = {}
    xb = inputs["x"][b]
    m["x_all"] = xb; m["x_own"] = xb[hh * LH:(hh + 1) * LH]; m["ctx"] = inputs["ctx"][b]
    m["cvec"] = np.stack([inputs["c"][b], inputs["c_ctx"]], 0)
    gs = slice(16 * hh, 16 * hh + 16)
    for l in layers:
        for k in WEIGHT_SHAPES:
            v = inputs[k][l]
            if k in ("s5_a_re", "s5_a_im", "s5_b_re", "s5_b_im", "s5_c_re", "s5_c_im", "s5_log_dt"):
                v = v[:, gs]
            elif k == "s5_d":
                v = v[256 * hh:256 * hh + 256]
            elif k == "w_in":
                v = np.concatenate([v[:, :OFF_U], v[:, OFF_U + 256 * hh:OFF_U + 256 * hh + 256],
                                    v[:, OFF_U + 256 * (1 - hh):OFF_U + 256 * (1 - hh) + 256], v[:, NST:]], axis=1)
            m[f"{k}_{l}"] = v
    for k in ["ident", "blk64", "perm64", "perm32", "ropek_cos", "ropek_sin", "rope32k_cos", "rope32k_sin",
              "swap", "mask_f", "mask_b", "sel8", "sel8T"]:
        m[k] = C[k]
    sl = slice(hh * LH, (hh + 1) * LH)
    m["ropeq_cos"] = C["ropek_cos"][:, sl]; m["ropeq_sin"] = C["ropek_sin"][:, sl]
    m["rope32q_cos"] = C["rope32k_cos"][:, sl]; m["rope32q_sin"] = C["rope32k_sin"][:, sl]
    s = np.zeros((128, 2), np.float32); s[:, hh] = 1
    m["sel"] = s
    return {k: np.ascontiguousarray(v, dtype=np.float32) for k, v in m.items()}


def rstd_from_ms(P, out, ms, n, eps=EPS, eng_a="act"):
    P.ts(out, ms, eps, None, op0=ALU.add)
    P.act(out, out, AF.Sqrt)
    if n > 1:
        P.recip(out, out)
    else:
        P.recip(out, out)


def stage_prep(P, I, G):
    G.ident = P.sb([128, 128], BF16, "ident"); P.dma(G.ident, I["ident"], eng="pool")
    G.blk64 = P.sb([128, 128], BF16, "blk64"); P.dma(G.blk64, I["blk64"], eng="pool")
    G.perm64 = P.sb([128, 128], BF16, "perm64"); P.dma(G.perm64, I["perm64"], eng="pool")
    G.perm32 = P.sb([32, 32], BF16, "perm32"); P.dma(G.perm32, I["perm32"], eng="pool")
    G.ones = P.sb([128, 128], BF16, "ones"); P.memset(G.ones, 1.0)
    G.modT = P.sb([128, 48, 2], F32, "modT")
    G.sel = P.sb([128, 2], F32, "sel"); P.dma(G.sel, I["sel"])
    with P.scope():
        cT = P.sb([128, 8, 2], F32, "cT")
        for t in range(2):
            src = I["cvec"].ap[t, :].rearrange("(k p) -> p k", p=128)
            P.dma(cT[:, :, t], dT(src, "cvec"), allow_slow_non_contiguous=True)
        sT = P.sb([128, 8, 2], BF16, "sT")
        P.act(sT, cT, AF.Silu)
        bT = P.sb([128, 48], F32, "bT")
        P.dma(bT, dT(I["b_mod"].ap.rearrange("(j p) -> p j", p=128), "b_mod"), allow_slow_non_contiguous=True)
        wbufs = [P.sb([128, 8, 128], BF16, f"wm{i}") for i in range(3)]
        for j in range(48):
            wb = wbufs[j % 3]
            src = I["w_mod"].ap[:, j * 128:(j + 1) * 128].rearrange("(k p) c -> p k c", p=128)
            P.dma(wb, dT(src, "w_mod"), eng="pool")
            ps = P.ps()
            for kc in range(8):
                P.mm(ps[:, 0:2], wb[:, kc, :], sT[:, kc, :], start=(kc == 0), stop=(kc == 7))
            P.ts(G.modT[:, j, :], ps[:, 0:2], bT[:, j:j + 1], None, op0=ALU.add)
        for w in (1, 4):
            P.ts(G.modT[:, w * 8:(w + 1) * 8, :], G.modT[:, w * 8:(w + 1) * 8, :], 1.0, None, op0=ALU.add)
        G.modD = P.dram([2, 6 * D], F32, "modD")
        for t in range(2):
            dst = G.modD.ap[t, :].rearrange("(j p) -> p j", p=128)
            P.dma(G.modD.v(dst), G.modT[:, :, t], allow_slow_non_contiguous=True)


def ln_tile_to_hT(P, G, xt, hT_dst, t_idx, which_sh, which_sc):
    st = P.sb([128, 2, 6], F32, "bnst")
    mv = P.sb([128, 2], F32, "mv")
    for hh in range(2):
        P.bn_stats(st[:, hh, :], xt[:, hh * 512:(hh + 1) * 512])
    P.bn_aggr(mv, st)
    rs = P.sb([128, 1], F32, "rs")
    rstd_from_ms(P, rs, mv[:, 1:2], 1)
    xn = P.sb([128, D], BF16, "xn")
    P.ts(xn, xt, mv[:, 0:1], rs, op0=ALU.subtract, op1=ALU.mult)
    ps = P.ps()
    psb = ps.v(ps.ap.bitcast(BF16))
    for kc in range(8):
        P.transpose(psb[:, kc * 128:(kc + 1) * 128], xn[:, kc * 128:(kc + 1) * 128], G.ident)
    for kc in range(8):
        P.act(hT_dst[:, kc, :], psb[:, kc * 128:(kc + 1) * 128], AF.Identity,
              bias=G.modT[:, which_sh * 8 + kc, t_idx:t_idx + 1], scale=G.modT[:, which_sc * 8 + kc, t_idx:t_idx + 1])


_CAST = {"i": 0}


def wload(P, stg, dst, src, engs=("pool", "dve", "act")):
    shape = list(dst.ap.shape)[1:]
    n = 1
    for v in shape:
        n *= v
    st = stg()
    sv = st.ap[0:dst.ap.shape[0], 0:n]
    if len(shape) == 2:
        sv = sv.rearrange("p (a b) -> p a b", a=shape[0])
    svt = st.v(sv)
    P.dma(svt, src)
    e = engs[_CAST["i"] % len(engs)]
    _CAST["i"] += 1
    P.copy(dst, svt, eng=e)


def mmg(P, items, K):
    for k in range(K):
        for (out, lf, rf) in items:
            P.mm(out, lf(k), rf(k), start=(k == 0), stop=(k == K - 1))


def make_ln_pools(P, nb=2):
    R = Ctx()
    R.xt = Rot(P, [128, D], F32, "xt", nb)
    R.st = Rot(P, [128, 2, 6], F32, "bnst", nb)
    R.mv = Rot(P, [128, 2], F32, "mv", nb)
    R.rs = Rot(P, [128, 1], F32, "rs", nb)
    R.xn = Rot(P, [128, D], BF16, "xn", nb)
    return R


def ln_part1(P, G, R, src_dram):
    xt = R.xt()
    P.dma(xt, src_dram)
    st = R.st(); mv = R.mv(); rs = R.rs(); xn = R.xn()
    for hh in range(2):
        P.bn_stats(st[:, hh, :], xt[:, hh * 512:(hh + 1) * 512])
    P.bn_aggr(mv, st)
    rstd_from_ms(P, rs, mv[:, 1:2], 1)
    P.ts(xn, xt, mv[:, 0:1], rs, op0=ALU.subtract, op1=ALU.mult)
    return xn


def ln_part2(P, G, xn, hT_dst, t_idx, which_sh, which_sc):
    ps = P.ps()
    psb = ps.v(ps.ap.bitcast(BF16))
    for kc in range(8):
        P.transpose(psb[:, kc * 128:(kc + 1) * 128], xn[:, kc * 128:(kc + 1) * 128], G.ident)
    for kc in range(8):
        bias = G.modT[:, which_sh * 8 + kc, t_idx:t_idx + 1]
        scale = G.modT[:, which_sc * 8 + kc, t_idx:t_idx + 1]
        if kc % 2 == 0:
            P.act(hT_dst[:, kc, :], psb[:, kc * 128:(kc + 1) * 128], AF.Identity, bias=bias, scale=scale)
        else:
            P.ts(hT_dst[:, kc, :], psb[:, kc * 128:(kc + 1) * 128], scale, bias, op0=ALU.mult, op1=ALU.add)


def ln_tile_to_hT2(P, G, R, src_dram, hT_dst, t_idx, which_sh, which_sc):
    xn = ln_part1(P, G, R, src_dram)
    ln_part2(P, G, xn, hT_dst, t_idx, which_sh, which_sc)


def rope_apply(P, dst, src_bf, perm, cos, sin, tmp, n, rows=128, psfn=None):
    ps = (psfn or P.ps)()
    P.mm(ps[0:rows, 0:n], perm, src_bf)
    P.tt(tmp, src_bf, cos, ALU.mult)
    P.tt(dst, ps[0:rows, 0:n], sin, ALU.mult)
    P.tt(dst, dst, tmp, ALU.add)


def alloc_persist(P, G):
    G.kT = P.sb([128, NK], BF16, "kT")
    G.Vg = P.sb([128, NKT, 2, 128], BF16, "Vg")
    G.ckvT = P.sb([128, 2, NK], BF16, "ckvT")
    G.krT = P.sb([32, NK], BF16, "krT")


def stage_A(P, I, G):
    P.memset(G.Vg[:, :, :, 64:128], 1.0)
    with P.scope():
        w_st = P.sb([128, 8, NST], BF16, "w_st")
        with P.scope():
            stg = Rot(P, [128, NST], F32, "stg", 2)
            for kc in range(8):
                wload(P, stg, w_st[:, kc, :], dT(I["w_in"].ap[kc * 128:(kc + 1) * 128, 0:NST], "w_in"))
        kg = P.sb([128, 1], F32, "kg")
        for r in range(2):
            P.dma(kg[r * 64:(r + 1) * 64, :], dT(I["a_k_gain"].ap.rearrange("(p o) -> p o", o=1), "akg"))
        cg = P.sb([128, 2], F32, "cg")
        P.dma(cg, dT(I["c_kv_a_gain"].ap.rearrange("(j p) -> p j", p=128), "ckg"), allow_slow_non_contiguous=True)
        R = make_ln_pools(P, 3)
        hTs = Rot(P, [128, 8, 512], BF16, "hT", 2)
        sq = Rot(P, [128, 512], BF16, "sq", 2)
        rst = Rot(P, [128, 512], F32, "rst", 1)
        knb = Rot(P, [128, 512], BF16, "knb", 2)
        tmp = Rot(P, [128, 512], F32, "tmp", 1)
        cosb = Rot(P, [128, 512], F32, "cosb", 1)
        sinb = Rot(P, [128, 512], F32, "sinb", 1)
        cos32 = Rot(P, [32, 512], BF16, "cos32", 1)
        sin32 = Rot(P, [32, 512], BF16, "sin32", 1)
        blocks = [(0, 2, True)] + [(2 + 4 * i, 4, False) for i in range(8)]
        alltiles = [(t0 + ti, is_ctx) for (t0, nt, is_ctx) in blocks for ti in range(nt)]

        def a_p1(q):
            t, is_ctx = alltiles[q]
            return ln_part1(P, G, R, I["ctx"].rows(t * 128) if is_ctx else I["x_all"].rows((t - 2) * 128))
        qi = 0
        xn_cur = a_p1(0)
        for (t0, nt, is_ctx) in blocks:
            n = nt * 128
            c0 = t0 * 128
            hT = hTs()
            for ti in range(nt):
                xn_nxt = a_p1(qi + 1) if qi + 1 < len(alltiles) else None
                ln_part2(P, G, xn_cur, hT[:, :, ti * 128:(ti + 1) * 128], 1 if is_ctx else 0, 0, 1)
                xn_cur = xn_nxt
                qi += 1
            if not is_ctx:
                lc = c0 - CTX
                cb, sb_, c32, s32 = cosb(), sinb(), cos32(), sin32()
                P.dma(cb[:, 0:n], dT(I["ropek_cos"].ap[:, lc:lc + n], "rc"))
                P.dma(sb_[:, 0:n], dT(I["ropek_sin"].ap[:, lc:lc + n], "rs"))
                P.dma(c32[:, 0:n], dT(I["rope32k_cos"].ap[:, lc:lc + n], "rc32"), eng="pool")
                P.dma(s32[:, 0:n], dT(I["rope32k_sin"].ap[:, lc:lc + n], "rs32"), eng="pool")
            pk = P.ps(); pc = [P.ps(), P.ps()]; pr = P.ps()
            mmg(P, [(pk[:, 0:n], lambda k: w_st[:, k, OFF_AK:OFF_AK + 128], lambda k: hT[:, k, 0:n]),
                    (pc[0][:, 0:n], lambda k: w_st[:, k, OFF_CKV:OFF_CKV + 128], lambda k: hT[:, k, 0:n]),
                    (pc[1][:, 0:n], lambda k: w_st[:, k, OFF_CKV + 128:OFF_CKV + 256], lambda k: hT[:, k, 0:n]),
                    (pr[0:32, 0:n], lambda k: w_st[:, k, OFF_CKR:OFF_CKR + 32], lambda k: hT[:, k, 0:n])], 8)
            s = sq()
            P.act(s[:, 0:n], pk[:, 0:n], AF.Square)
            pm = P.ps()
            P.mm(pm[:, 0:n], G.blk64, s[:, 0:n])
            rs = rst()
            rstd_from_ms(P, rs[:, 0:n], pm[:, 0:n], n)
            kn = knb()
            P.stt(kn[:, 0:n], pk[:, 0:n], kg[:, 0:1], rs[:, 0:n], ALU.mult, ALU.mult)
            if is_ctx:
                P.copy(G.kT[:, c0:c0 + n], kn[:, 0:n])
            else:
                rope_apply(P, G.kT[:, c0:c0 + n], kn[:, 0:n], G.perm64, cb[:, 0:n], sb_[:, 0:n], tmp()[:, 0:n], n)
            ss = [sq(), sq()]
            for j in range(2):
                P.act(ss[j][:, 0:n], pc[j][:, 0:n], AF.Square)
            pm = P.ps()
            for j in range(2):
                P.mm(pm[:, 0:n], G.ones, ss[j][:, 0:n], start=(j == 0), stop=(j == 1))
            rs = rst()
            P.ts(rs[:, 0:n], pm[:, 0:n], 1.0 / 256, EPS, op0=ALU.mult, op1=ALU.add)
            P.act(rs[:, 0:n], rs[:, 0:n], AF.Sqrt)
            P.recip(rs[:, 0:n], rs[:, 0:n])
            for j in range(2):
                P.stt(G.ckvT[:, j, c0:c0 + n], pc[j][:, 0:n], cg[:, j:j + 1], rs[:, 0:n], ALU.mult, ALU.mult)
            if is_ctx:
                P.copy(G.krT[:, c0:c0 + n], pr[0:32, 0:n])
            else:
                kr = knb()
                P.copy(kr[0:32, 0:n], pr[0:32, 0:n])
                rope_apply(P, G.krT[:, c0:c0 + n], kr[0:32, 0:n], G.perm32, c32[:, 0:n], s32[:, 0:n], tmp()[0:32, 0:n], n, rows=32)
            pvs = [P.ps() for _ in range(nt)]
            mmg(P, [(pvs[ti][:, 0:128], (lambda k, ti=ti: hT[:, k, ti * 128:(ti + 1) * 128]),
                     lambda k: w_st[:, k, OFF_AV:OFF_AV + 128]) for ti in range(nt)], 8)
            for ti in range(nt):
                pv = pvs[ti]
                P.copy(G.Vg[:, t0 + ti, :, 0:64], pv.v(pv.ap[:, 0:128].rearrange("p (a b) -> p a b", a=2)), eng="act")
            pus = [P.ps() for _ in range(2)]
            mmg(P, [(pus[j][:, 0:n], (lambda k, j=j: w_st[:, k, OFF_U + j * 128:OFF_U + (j + 1) * 128]),
                     lambda k: hT[:, k, 0:n]) for j in range(2)], 8)
            for j in range(2):
                dst = G.uT.v(G.uT.ap[:, j, :, c0 // 8:(c0 + n) // 8].rearrange("p i c -> p c i"))
                src = pus[j].v(pus[j].ap[:, 0:n].rearrange("p (c i) -> p c i", i=8))
                P.copy(dst, src, eng=("act" if j % 2 else "dve"))


def own_blocks(last):
    bl = [(i * 512, 512, False, i * 512) for i in range(4)]
    if not last:
        bl.append((LH, 256, True, 0))
    return bl


def stage_B1(P, I, G, last):
    NQ = LH + (0 if last else CTX)
    G.NQ = NQ
    G.qg = P.sb([128, 4, NQ], BF16, "qg")
    G.qm = P.sb([96, 8, NQ], BF16, "qm")
    G.gD = P.dram([3 * D, NQ], BF16, "gD")
    with P.scope():
        hT = P.sb([128, 8, NQ], BF16, "hTall")
        with P.scope():
            R = make_ln_pools(P, 4)
            tiles = [(c0 + ti * 128, r0 + ti * 128, is_ctx) for (c0, n, is_ctx, r0) in own_blocks(last) for ti in range(n // 128)]

            def p1(q):
                col, row, is_ctx = tiles[q]
                return ln_part1(P, G, R, (I["ctx"] if is_ctx else I["x_own"]).rows(row))
            xn_cur = p1(0)
            for q in range(len(tiles)):
                xn_nxt = p1(q + 1) if q + 1 < len(tiles) else None
                col, row, is_ctx = tiles[q]
                ln_part2(P, G, xn_cur, hT[:, :, col:col + 128], 1 if is_ctx else 0, 0, 1)
                xn_cur = xn_nxt
        qgain = P.sb([128, 1], F32, "qgain")
        for r in range(2):
            P.dma(qgain[r * 64:(r + 1) * 64, :], dT(I["a_q_gain"].ap.rearrange("(p o) -> p o", o=1), "aqg"))
        cqg = P.sb([128, 6], F32, "cqg")
        P.dma(cqg, dT(I["c_q_a_gain"].ap.rearrange("(j p) -> p j", p=128), "cqg"), allow_slow_non_contiguous=True)
        perm32h = P.sb([96, 32], BF16, "perm32h")
        P.dma(perm32h[64:96, :], I["perm32"], eng="pool")
        cosqR = Rot(P, [128, 512], F32, "cosq", 2)
        sinqR = Rot(P, [128, 512], F32, "sinq", 2)
        cos32R = Rot(P, [96, 512], F32, "cos32q", 2)
        sin32R = Rot(P, [96, 512], F32, "sin32q", 2)
        sq = Rot(P, [128, 512], BF16, "sq", 6)
        rst = Rot(P, [128, 512], F32, "rst", 2)
        knb = Rot(P, [128, 512], BF16, "knb", 2)
        tmp = Rot(P, [128, 512], F32, "tmp", 2)
        with P.scope():
            wq = P.sb([128, 8, 4, 128], BF16, "wq")
            stg = Rot(P, [128, 768], F32, "stg", 2)
            for kc in range(8):
                for hf in range(2):
                    src = I["w_in"].ap[kc * 128:(kc + 1) * 128, OFF_AQ + hf * 256:OFF_AQ + (hf + 1) * 256].rearrange("p (a b) -> p a b", a=4)
                    wload(P, stg, wq[:, kc, :, hf * 64:(hf + 1) * 64], dT(src, "w_in"))
            for (c0, n, is_ctx, r0) in own_blocks(last):
                if not is_ctx:
                    cosq = cosqR(); sinq = sinqR()
                    P.dma(cosq, dT(I["ropeq_cos"].ap[:, c0:c0 + n], "rqc"))
                    P.dma(sinq, dT(I["ropeq_sin"].ap[:, c0:c0 + n], "rqs"))
                pks = [P.ps() for _ in range(4)]
                mmg(P, [(pks[hd][:, 0:n], (lambda k, hd=hd: wq[:, k, hd, :]), lambda k: hT[:, k, c0:c0 + n]) for hd in range(4)], 8)
                for hd in range(4):
                    pk = pks[hd]
                    s = sq()
                    P.act(s[:, 0:n], pk[:, 0:n], AF.Square)
                    pm = P.ps_acc()
                    P.mm(pm[:, 0:n], G.blk64, s[:, 0:n])
                    rs = rst()
                    rstd_from_ms(P, rs[:, 0:n], pm[:, 0:n], n)
                    if is_ctx:
                        P.stt(G.qg[:, hd, c0:c0 + n], pk[:, 0:n], qgain[:, 0:1], rs[:, 0:n], ALU.mult, ALU.mult)
                    else:
                        kn = knb()
                        P.stt(kn[:, 0:n], pk[:, 0:n], qgain[:, 0:1], rs[:, 0:n], ALU.mult, ALU.mult)
                        rope_apply(P, G.qg[:, hd, c0:c0 + n], kn[:, 0:n], G.perm64, cosq[:, 0:n], sinq[:, 0:n],
                                   tmp()[:, 0:n], n, psfn=P.ps_acc)
        with P.scope():
            wc = P.sb([128, 8, 768], BF16, "wc")
            stg = Rot(P, [128, 768], F32, "stg", 1)
            for kc in range(8):
                wload(P, stg, wc[:, kc, :], dT(I["w_in"].ap[kc * 128:(kc + 1) * 128, OFF_CQ:OFF_CQ + 768], "w_in"))
            wqb = P.sb([128, 6, 768], BF16, "wqb")
            for j in range(6):
                wload(P, stg, wqb[:, j, :], dT(I["c_w_qb"].ap[j * 128:(j + 1) * 128, :], "wqb"))
            cqn = P.sb([128, 6, 512], BF16, "cqn")
            qrb = Rot(P, [96, 512], BF16, "qrb", 2)
            for (c0, n, is_ctx, r0) in own_blocks(last):
                if not is_ctx:
                    cos32 = cos32R(); sin32 = sin32R()
                    P.dma(cos32[64:96, :], dT(I["rope32q_cos"].ap[:, c0:c0 + n], "rqc32"))
                    P.dma(sin32[64:96, :], dT(I["rope32q_sin"].ap[:, c0:c0 + n], "rqs32"))
                pcs = [P.ps() for _ in range(6)]
                mmg(P, [(pcs[j][:, 0:n], (lambda k, j=j: wc[:, k, j * 128:(j + 1) * 128]), lambda k: hT[:, k, c0:c0 + n]) for j in range(6)], 8)
                sqs = []
                for j in range(6):
                    s = sq()
                    P.act(s[:, 0:n], pcs[j][:, 0:n], AF.Square)
                    sqs.append(s)
                pm = P.ps_acc()
                for j in range(6):
                    P.mm(pm[:, 0:n], G.ones, sqs[j][:, 0:n], start=(j == 0), stop=(j == 5))
                rs = rst()
                P.ts(rs[:, 0:n], pm[:, 0:n], 1.0 / 768, EPS, op0=ALU.mult, op1=ALU.add)
                P.act(rs[:, 0:n], rs[:, 0:n], AF.Sqrt)
                P.recip(rs[:, 0:n], rs[:, 0:n])
                for j in range(6):
                    P.stt(cqn[:, j, 0:n], pcs[j][:, 0:n], cqg[:, j:j + 1], rs[:, 0:n], ALU.mult, ALU.mult)
                for hg in range(2):
                    pqs = [P.ps() for _ in range(4)]
                    mmg(P, [(pqs[i][0:96, 0:n], (lambda k, h=hg * 4 + i: wqb[:, k, h * 96:(h + 1) * 96]), lambda k: cqn[:, k, 0:n]) for i in range(4)], 6)
                    for i in range(4):
                        h = hg * 4 + i
                        pq = pqs[i]
                        if is_ctx:
                            P.copy(G.qm[:, h, c0:c0 + n], pq[0:96, 0:n], eng="act")
                        else:
                            P.copy(G.qm[0:64, h, c0:c0 + n], pq[0:64, 0:n], eng="act")
                            qr = qrb()
                            P.copy(qr[64:96, 0:n], pq[64:96, 0:n])
                            pr = P.ps_acc()
                            P.mm(pr[64:96, 0:n], perm32h[64:96, :], qr[64:96, 0:n])
                            t = tmp()
                            P.tt(t[64:96, 0:n], qr[64:96, 0:n], cos32[64:96, 0:n], ALU.mult)
                            t2 = tmp()
                            P.tt(t2[64:96, 0:n], pr[64:96, 0:n], sin32[64:96, 0:n], ALU.mult)
                            P.tt(G.qm[64:96, h, c0:c0 + n], t[64:96, 0:n], t2[64:96, 0:n], ALU.add)
        with P.scope():
            wg = Rot(P, [128, 8, 512], BF16, "wg", 2)
            stg = Rot(P, [128, 512], F32, "stg", 3)
            gb = Rot(P, [128, 512], BF16, "gb", 8)
            def load_g(gi):
                w = wg()
                for kc in range(8):
                    wload(P, stg, w[:, kc, :], dT(I["w_in"].ap[kc * 128:(kc + 1) * 128, OFF_GATE + gi * 512:OFF_GATE + (gi + 1) * 512], "w_in"), engs=("dve", "act"))
                return w
            w_nxt = load_g(0)
            for gi in range(6):
                w = w_nxt
                w_nxt = load_g(gi + 1) if gi + 1 < 6 else None
                for (c0, n, is_ctx, r0) in own_blocks(last):
                    pgs = [P.ps() for _ in range(4)]
                    mmg(P, [(pgs[oc][:, 0:n], (lambda k, oc=oc: w[:, k, oc * 128:(oc + 1) * 128]), lambda k: hT[:, k, c0:c0 + n]) for oc in range(4)], 8)
                    for oc in range(4):
                        g = gb()
                        P.act(g[:, 0:n], pgs[oc][:, 0:n], AF.Sigmoid)
                        row = (gi * 4 + oc) * 128
                        P.dma(G.gD.v(G.gD.ap[row:row + 128, c0:c0 + n]), g[:, 0:n])


def run_attn(P, chains, pT, scale):
    LA = 2
    nkt = chains[0][1]
    pls = [dict() for _ in chains]
    for kt in range(nkt + LA):
        if kt < nkt:
            for ci, (po, _, n, slf, srhs, vlf) in enumerate(chains):
                pss = P.ps()
                P.mm(pss[:, 0:n], slf(kt), srhs)
                p = pT()
                P.act(p[:, 0:n], pss[:, 0:n], AF.Exp, scale=scale)
                pls[ci][kt] = p
        jj = kt - LA
        if jj >= 0:
            for ci, (po, _, n, slf, srhs, vlf) in enumerate(chains):
                P.mm(po[:, 0:n], vlf(jj), pls[ci].pop(jj)[:, 0:n], start=(jj == 0), stop=(jj == nkt - 1))


def block_groups(last):
    bl = own_blocks(last)
    groups = [bl[0:2], bl[2:4]]
    if not last:
        groups.append(bl[4:5])
    return groups


def attn_finish(P, po, n, rec, yo, dst):
    r = rec()
    P.recip(r[64:128, 0:n], po[64:128, 0:n])
    y = yo()
    P.tt(y[:, 0:n], po[0:64, 0:n], r[64:128, 0:n], ALU.mult)
    P.dma(dst, y[:, 0:n])


def stage_B2(P, I, G, last):
    NQ = G.NQ
    G.yaD = P.dram([512, NQ], BF16, "yaD")
    with P.scope():
        pT = Rot(P, [128, 512], BF16, "pT", 8)
        rec = Rot(P, [128, 512], F32, "rec", 2)
        yo = Rot(P, [64, 512], BF16, "yo", 2)
        kTp = P.sb([128, 2, NK], BF16, "kTp")
        P.memset(kTp, 0.0)
        P.copy(kTp[0:64, 0, :], G.kT[0:64, :], eng="pool")
        P.copy(kTp[64:128, 1, :], G.kT[64:128, :], eng="dve")
        for hd in range(4):
            for kvh in range(2):
                head = hd + 4 * kvh
                for grp in block_groups(last):
                    chains = []
                    for (c0, n, is_ctx, r0) in grp:
                        nkt = 2 if is_ctx else NKT
                        chains.append((P.ps_acc(), nkt, n, (lambda kt: kTp[:, kvh, kt * 128:(kt + 1) * 128]),
                                       G.qg[:, hd, c0:c0 + n], (lambda kt: G.Vg[:, kt, kvh, :])))
                    run_attn(P, chains, pT, 0.125)
                    for (po, _, n, _, _, _), (c0, _, _, _) in zip(chains, grp):
                        attn_finish(P, po, n, rec, yo, G.yaD.v(G.yaD.ap[head * 64:(head + 1) * 64, c0:c0 + n]))


def stage_B3(P, I, G, last):
    NQ = G.NQ
    G.ycD = P.dram([512, NQ], BF16, "ycD")
    with P.scope():
        wkv = P.sb([128, 2, 1024], BF16, "wkv")
        stg = Rot(P, [128, 1024], F32, "stg", 1)
        for j in range(2):
            wload(P, stg, wkv[:, j, :], dT(I["c_w_kvb"].ap[j * 128:(j + 1) * 128, :], "wkvb"))
        Kh = Rot(P, [96, NK], BF16, "Kh", 2)
        Vh = [P.sb([128, NKT, 128], BF16, f"Vh{i}") for i in range(2)]
        for v in Vh:
            P.memset(v[:, :, 64:128], 1.0)
        pT = Rot(P, [128, 512], BF16, "pT", 8)
        rec = Rot(P, [128, 512], F32, "rec", 2)
        yo = Rot(P, [64, 512], BF16, "yo", 2)
        scale = 96 ** -0.5
        def gen(h):
            K = Kh(); V = Vh[h % 2]
            cbs = [(cb * 512, min(512, NK - cb * 512)) for cb in range(9)]
            for g0 in range(0, 9, 3):
                grp = cbs[g0:g0 + 3]
                pks = [P.ps() for _ in grp]
                mmg(P, [(pks[i][0:64, 0:n], lambda k: wkv[:, k, h * 128:h * 128 + 64], (lambda k, k0=k0, n=n: G.ckvT[:, k, k0:k0 + n]))
                        for i, (k0, n) in enumerate(grp)], 2)
                for i, (k0, n) in enumerate(grp):
                    P.copy(K[0:64, k0:k0 + n], pks[i][0:64, 0:n], eng=("pool_never" if False else ("act" if i % 2 else "dve")))
            P.copy(K[64:96, :], G.krT[0:32, :], eng="pool")
            for g0 in range(0, NKT, 4):
                kts = list(range(g0, min(g0 + 4, NKT)))
                pvs = [P.ps() for _ in kts]
                mmg(P, [(pvs[i][:, 0:64], (lambda k, kt=kt: G.ckvT[:, k, kt * 128:(kt + 1) * 128]),
                         lambda k: wkv[:, k, h * 128 + 64:h * 128 + 128]) for i, kt in enumerate(kts)], 2)
                for i, kt in enumerate(kts):
                    P.copy(V[:, kt, 0:64], pvs[i][:, 0:64], eng="dve")
            return K, V

        def attend(h, K, V):
            for grp in block_groups(last):
                chains = []
                for (c0, n, is_ctx, r0) in grp:
                    nkt = 2 if is_ctx else NKT
                    chains.append((P.ps_acc(), nkt, n, (lambda kt: K[0:96, kt * 128:(kt + 1) * 128]),
                                   G.qm[0:96, h, c0:c0 + n], (lambda kt: V[:, kt, :])))
                run_attn(P, chains, pT, scale)
                for (po, _, n, _, _, _), (c0, _, _, _) in zip(chains, grp):
                    attn_finish(P, po, n, rec, yo, G.ycD.v(G.ycD.ap[h * 64:(h + 1) * 64, c0:c0 + n]))
        kv_cur = gen(0)
        for h in range(8):
            kv_nxt = gen(h + 1) if h + 1 < 8 else None
            attend(h, *kv_cur)
            kv_cur = kv_nxt


def bcast_load(P, dst, src_ap_1d):
    P.dma(dst, dT(src_ap_1d.partition_broadcast(128), "bc"))


def stage_B4a(P, I, G, last):
    NQ = G.NQ
    G.xmidD = P.dram([NQ, D], F32, "xmidD")
    G.h2D = P.dram([D, NQ], BF16, "h2D")
    with P.scope():
        wglu = P.sb([128, 4, 1024], BF16, "wglu")
        wb = [P.sb([128, 4, 1024], BF16, f"wb{i}") for i in range(3)]
        wout = P.sb([128, 8, 1024], BF16, "wout")
        with P.scope():
            stg = Rot(P, [128, 1024], F32, "stg", 3)
            for j in range(4):
                wload(P, stg, wglu[:, j, :], dT(I["s5_w_glu"].ap[j * 128:(j + 1) * 128, :], "w"))
                for i, nm in enumerate(["w_branch_a", "w_branch_s5", "w_branch_c"]):
                    wload(P, stg, wb[i][:, j, :], dT(I[nm].ap[j * 128:(j + 1) * 128, :], "w"))
            for kc in range(8):
                wload(P, stg, wout[:, kc, :], dT(I["w_out"].ap[kc * 128:(kc + 1) * 128, :], "w"))
        g1b = [P.sb([128, D], F32, f"g1b{t}") for t in range(2)]
        for t in range(2):
            P.dma(g1b[t], dT(G.modD.ap[t, 2 * D:3 * D].partition_broadcast(128), "modD_r"))
        lng = P.sb([128, D], F32, "lng"); bcast_load(P, lng, I["ln1_g"].ap)
        lnb = P.sb([128, D], F32, "lnb"); bcast_load(P, lnb, I["ln1_b"].ap)
        class _InR:
            def __init__(self):
                gts = P.sb([128, 24, 512], BF16, "gates")
                self.sets = [(P.sb([128, 4, 512], BF16, f"yT{i}"), P.sb([128, 4, 512], BF16, f"sa{i}"),
                              P.sb([128, 4, 512], BF16, f"sc{i}"), gts) for i in range(2)]
                self.i = 0

            def __call__(self):
                r = self.sets[self.i % 2]
                self.i += 1
                return r
        inR = _InR()
        srcs = [None, P.sb([128, 4, 512], BF16, "src1"), None]
        t1 = P.sb([128, 4, 512], F32, "t1")
        sg = Rot(P, [128, 512], F32, "sg", 2)
        acc = Rot(P, [128, 512], F32, "acc", 2)
        tmpm = Rot(P, [128, 512], F32, "tmpm", 2)
        merged = P.sb([128, 8, 512], BF16, "merged")
        xtR = Rot(P, [128, D], F32, "xt", 2)
        tsR = Rot(P, [128, D], F32, "tsum", 3)
        xmR = Rot(P, [128, D], F32, "xm", 3)
        stR = Rot(P, [128, 2, 6], F32, "bnst", 6); mvR = Rot(P, [128, 2], F32, "mv", 6); rsR = Rot(P, [128, 1], F32, "rs", 6)
        xnR = Rot(P, [128, D], BF16, "xn", 2)
        h2T = P.sb([128, 8, 512], BF16, "h2T")
        def load_blk(blk):
            (c0, n, is_ctx, r0) = blk
            bufs = inR()
            yT_, sa_, sc_, gates_ = bufs
            for j in range(4):
                P.dma(yT_[:, j, 0:n], G.ysD.v(G.ysD.ap[j * 128:(j + 1) * 128, c0:c0 + n]))
                P.dma(sa_[:, j, 0:n], G.yaD.v(G.yaD.ap[j * 128:(j + 1) * 128, c0:c0 + n]))
                P.dma(sc_[:, j, 0:n], G.ycD.v(G.ycD.ap[j * 128:(j + 1) * 128, c0:c0 + n]))
            return bufs
        blks_ = own_blocks(last)
        nxt_bufs = load_blk(blks_[0])
        for bi_, (c0, n, is_ctx, r0) in enumerate(blks_):
            tix = 1 if is_ctx else 0
            yT, srcs[0], srcs[2], gates = nxt_bufs
            for gi in range(24):
                P.dma(gates[:, gi, 0:n], G.gD.v(G.gD.ap[gi * 128:(gi + 1) * 128, c0:c0 + n]))
            nxt_bufs = load_blk(blks_[bi_ + 1]) if bi_ + 1 < len(blks_) else None
            P.tt(t1[:, :, 0:n], yT[:, :, 0:n], yT[:, :, 0:n], ALU.mult)
            P.ts(t1[:, :, 0:n], t1[:, :, 0:n], 0.044715, 1.0, op0=ALU.mult, op1=ALU.add)
            P.tt(t1[:, :, 0:n], t1[:, :, 0:n], yT[:, :, 0:n], ALU.mult)
            P.act(t1[:, :, 0:n], t1[:, :, 0:n], AF.Sigmoid, scale=1.5957691216)
            ge = P.sb([128, 4, 512], BF16, "ge") if c0 == 0 else ge
            P.tt(ge[:, :, 0:n], t1[:, :, 0:n], yT[:, :, 0:n], ALU.mult)
            for op_ in range(2):
                pa = [P.ps(), P.ps()]; pg = [P.ps(), P.ps()]
                items = []
                for q in range(2):
                    oc = op_ * 2 + q
                    items.append((pa[q][:, 0:n], (lambda k, oc=oc: wglu[:, k, oc * 128:(oc + 1) * 128]), lambda k: ge[:, k, 0:n]))
                    items.append((pg[q][:, 0:n], (lambda k, oc=oc: wglu[:, k, 512 + oc * 128:512 + (oc + 1) * 128]), lambda k: ge[:, k, 0:n]))
                mmg(P, items, 4)
                for q in range(2):
                    oc = op_ * 2 + q
                    s = sg()
                    P.act(s[:, 0:n], pg[q][:, 0:n], AF.Sigmoid)
                    P.tt(srcs[1][:, oc, 0:n], pa[q][:, 0:n], s[:, 0:n], ALU.mult)
            for oc in range(8):
                a = acc()
                pbs = [P.ps() for _ in range(3)]
                mmg(P, [(pbs[br][:, 0:n], (lambda k, br=br: wb[br][:, k, oc * 128:(oc + 1) * 128]), (lambda k, br=br: srcs[br][:, k, 0:n])) for br in range(3)], 4)
                for br in range(3):
                    pb = pbs[br]
                    if br == 0:
                        P.tt(a[:, 0:n], pb[:, 0:n], gates[:, br * 8 + oc, 0:n], ALU.mult)
                    else:
                        tm = tmpm()
                        P.tt(tm[:, 0:n], pb[:, 0:n], gates[:, br * 8 + oc, 0:n], ALU.mult)
                        if br == 1:
                            P.tt(a[:, 0:n], a[:, 0:n], tm[:, 0:n], ALU.add, eng="pool")
                        else:
                            P.tt(merged[:, oc, 0:n], a[:, 0:n], tm[:, 0:n], ALU.add, eng="pool")
            def mix(ti):
                xt = xtR(); ts_ = tsR()
                P.dma(xt, (I["ctx"] if is_ctx else I["x_own"]).rows(r0 + ti * 128))
                pms = [P.ps(), P.ps()]
                mmg(P, [(pms[half][:, 0:512], lambda k: merged[:, k, ti * 128:(ti + 1) * 128],
                         (lambda k, half=half: wout[:, k, half * 512:(half + 1) * 512])) for half in range(2)], 8)
                for half in range(2):
                    P.tt(ts_[:, half * 512:(half + 1) * 512], pms[half][:, 0:512], g1b[tix][:, half * 512:(half + 1) * 512], ALU.mult)
                P.stt(ts_, xt, ALPHA, ts_, ALU.mult, ALU.add)
                return ts_

            def rs_pre(ms_t):
                rs = rsR()
                P.ts(rs, ms_t, EPS, None, op0=ALU.add)
                P.act(rs, rs, AF.Sqrt)
                return rs

            def c1a(ti, ts_):
                st = stR(); mv = mvR()
                for hh in range(2):
                    P.bn_stats(st[:, hh, :], ts_[:, hh * 512:(hh + 1) * 512])
                P.bn_aggr(mv, st)
                return dict(ts=ts_, mv=mv, rs=rs_pre(mv[:, 1:2]))

            def c1b(ti, stt_):
                xm = xmR()
                P.recip(stt_["rs"], stt_["rs"])
                P.stt(xm, stt_["ts"], stt_["mv"][:, 0:1], lng, ALU.subtract, ALU.mult)
                P.stt(xm, xm, stt_["rs"], lnb, ALU.mult, ALU.add)
                P.dma(G.xmidD.v(G.xmidD.ap[c0 + ti * 128:c0 + (ti + 1) * 128, :]), xm)
                return xm

            def c2a(ti, xm):
                st = stR(); mv = mvR()
                for hh in range(2):
                    P.bn_stats(st[:, hh, :], xm[:, hh * 512:(hh + 1) * 512])
                P.bn_aggr(mv, st)
                return dict(xm=xm, mv=mv, rs=rs_pre(mv[:, 1:2]))

            def c2b(ti, stt_):
                xn = xnR()
                P.recip(stt_["rs"], stt_["rs"])
                P.ts(xn, stt_["xm"], stt_["mv"][:, 0:1], stt_["rs"], op0=ALU.subtract, op1=ALU.mult)
                return xn

            def xpose(ti, xn):
                ps = P.ps()
                psb = ps.v(ps.ap.bitcast(BF16))
                for kc in range(8):
                    P.transpose(psb[:, kc * 128:(kc + 1) * 128], xn[:, kc * 128:(kc + 1) * 128], G.ident)
                for kc in range(8):
                    P.act(h2T[:, kc, ti * 128:(ti + 1) * 128], psb[:, kc * 128:(kc + 1) * 128], AF.Identity,
                          bias=G.modT[:, 3 * 8 + kc, tix:tix + 1], scale=G.modT[:, 4 * 8 + kc, tix:tix + 1])
            nti = n // 128
            tsq = {0: mix(0)}
            s1 = {0: c1a(0, tsq[0])}
            if nti > 1:
                tsq[1] = mix(1)
            xm0 = c1b(0, s1[0])
            s2 = {0: c2a(0, xm0)}
            for ti in range(nti):
                if ti + 1 < nti:
                    s1[ti + 1] = c1a(ti + 1, tsq[ti + 1])
                xn = c2b(ti, s2[ti])
                if ti + 1 < nti:
                    xm_n = c1b(ti + 1, s1[ti + 1])
                    s2[ti + 1] = c2a(ti + 1, xm_n)
                if ti + 2 < nti:
                    tsq[ti + 2] = mix(ti + 2)
                xpose(ti, xn)
            for kc in range(8):
                P.dma(G.h2D.v(G.h2D.ap[kc * 128:(kc + 1) * 128, c0:c0 + n]), h2T[:, kc, 0:n])


def stage_B4b(P, I, G, last, out_own, out_ctx):
    NQ = G.NQ
    NT = NQ // 128
    with P.scope():
        g2b = [P.sb([128, D], F32, f"g2b{t}") for t in range(2)]
        for t in range(2):
            P.dma(g2b[t], dT(G.modD.ap[t, 5 * D:6 * D].partition_broadcast(128), "modD_r"))
        lng = P.sb([128, D], F32, "lng"); bcast_load(P, lng, I["ln2_g"].ap)
        lnb = P.sb([128, D], F32, "lnb"); bcast_load(P, lnb, I["ln2_b"].ap)
        h2T = P.sb([128, 8, NQ], BF16, "h2Tall")
        for kc in range(8):
            P.dma(h2T[:, kc, :], G.h2D.v(G.h2D.ap[kc * 128:(kc + 1) * 128, :]))
        tsum = P.sb([128, NT, D], F32, "tsum2")
        wuR = Rot(P, [128, 8, 512], BF16, "wu", 2)
        wdR = Rot(P, [128, 4, D], BF16, "wd", 2)
        stg = Rot(P, [128, 1024], F32, "stg", 3)
        aR = Rot(P, [128, 4, 512], BF16, "aog", 3)
        rl = Rot(P, [128, 512], BF16, "rl", 4)
        xmR = Rot(P, [128, D], F32, "xm", 3)
        stR = Rot(P, [128, 2, 6], F32, "bnst", 3); mvR = Rot(P, [128, 2], F32, "mv", 3); rsR = Rot(P, [128, 1], F32, "rs", 3)
        def ep_a(tile, c0, ti, tix):
            xm = xmR()
            P.dma(xm, G.xmidD.v(G.xmidD.ap[c0 + ti * 128:c0 + (ti + 1) * 128, :]))
            P.tt(tsum[:, tile, :], tsum[:, tile, :], g2b[tix], ALU.mult, eng="pool")
            P.stt(tsum[:, tile, :], xm, ALPHA, tsum[:, tile, :], ALU.mult, ALU.add)
            st = stR(); mv = mvR(); rs = rsR()
            for hh in range(2):
                P.bn_stats(st[:, hh, :], tsum[:, tile, hh * 512:(hh + 1) * 512])
            P.bn_aggr(mv, st)
            rstd_from_ms(P, rs, mv[:, 1:2], 1)
            return (xm, mv, rs)

        def ep_b(tile, r0, ti, is_ctx, stt_):
            xm, mv, rs = stt_
            o = xm
            P.stt(o, tsum[:, tile, :], mv[:, 0:1], lng, ALU.subtract, ALU.mult)
            P.stt(o, o, rs, lnb, ALU.mult, ALU.add)
            dst = out_ctx if is_ctx else out_own
            P.dma(dst.rows(r0 + ti * 128), o)

        def epilogue(blk):
            (c0, n, is_ctx, r0) = blk
            tix = 1 if is_ctx else 0
            nti = n // 128
            cur = ep_a(c0 // 128, c0, 0, tix)
            for ti in range(nti):
                nxt = ep_a(c0 // 128 + ti + 1, c0, ti + 1, tix) if ti + 1 < nti else None
                ep_b(c0 // 128 + ti, r0, ti, is_ctx, cur)
                cur = nxt

        for og in range(8):
            wu = wuR(); wd = wdR()
            for kc in range(8):
                wload(P, stg, wu[:, kc, :], dT(I["w_up"].ap[kc * 128:(kc + 1) * 128, og * 512:(og + 1) * 512], "w"), engs=("pool", "act"))
            for oc in range(4):
                wload(P, stg, wd[:, oc, :], dT(I["w_down"].ap[og * 512 + oc * 128:og * 512 + (oc + 1) * 128, :], "w"), engs=("pool", "act"))
            blks = own_blocks(last)

            def up(blk):
                (c0, n, is_ctx, r0) = blk
                a = aR()
                pus = [P.ps() for _ in range(4)]
                mmg(P, [(pus[oc][:, 0:n], (lambda k, oc=oc: wu[:, k, oc * 128:(oc + 1) * 128]), lambda k: h2T[:, k, c0:c0 + n]) for oc in range(4)], 8)
                for oc in range(4):
                    r = rl()
                    P.act(r[:, 0:n], pus[oc][:, 0:n], AF.Relu)
                    P.tt(a[:, oc, 0:n], pus[oc][:, 0:n], r[:, 0:n], ALU.mult)
                return a

            def down(blk, a):
                (c0, n, is_ctx, r0) = blk
                combos = [(ti, half) for ti in range(n // 128) for half in range(2)]
                for g0 in range(0, len(combos), 4):
                    grp = combos[g0:g0 + 4]
                    pds = [P.ps() for _ in grp]
                    mmg(P, [(pds[i][:, 0:512], (lambda k, ti=ti: a[:, k, ti * 128:(ti + 1) * 128]),
                             (lambda k, half=half: wd[:, k, half * 512:(half + 1) * 512])) for i, (ti, half) in enumerate(grp)], 4)
                    for i, (ti, half) in enumerate(grp):
                        tile = c0 // 128 + ti
                        dst = tsum[:, tile, half * 512:(half + 1) * 512]
                        if og == 0:
                            P.copy(dst, pds[i][:, 0:512], eng="act")
                        else:
                            P.tt(dst, pds[i][:, 0:512], dst, ALU.add)
            a_cur = up(blks[0])
            for bi, blk in enumerate(blks):
                a_nxt = up(blks[bi + 1]) if bi + 1 < len(blks) else None
                down(blk, a_cur)
                a_cur = a_nxt
                if og == 7:
                    epilogue(blk)


def bc(t, pattern):
    a = t.ap
    return t.v(bass.AP(a.tensor, a.offset, [list(a.ap[0])] + [list(p) for p in pattern]))


MAGIC = 12582912.0
TWO_PI = 6.283185307179586


def s5_load(P, I, G):
    Lt = Ctx()
    Lt.are = P.sb([128, NGD], F32, "are"); Lt.aim = P.sb([128, NGD], F32, "aim"); Lt.ldt = P.sb([128, NGD], F32, "ldt")
    Lt.Bre = P.sb([128, NGD, 16], F32, "Bre"); Lt.Bim = P.sb([128, NGD, 16], F32, "Bim")
    Lt.Cre = P.sb([128, NGD, 16], F32, "Cre"); Lt.Cim = P.sb([128, NGD, 16], F32, "Cim")
    for hf in range(2):
        sl = slice(hf * 64, (hf + 1) * 64)
        P.dma(Lt.are[sl, :], dT(I["s5_a_re"].ap.rearrange("d g p -> p (d g)"), "a"), allow_slow_non_contiguous=True, eng="act")
        P.dma(Lt.aim[sl, :], dT(I["s5_a_im"].ap.rearrange("d g p -> p (d g)"), "a"), allow_slow_non_contiguous=True, eng="act")
        P.dma(Lt.Bre[sl], dT(I["s5_b_re"].ap.rearrange("d g p c -> p (d g) c"), "a"))
        P.dma(Lt.Bim[sl], dT(I["s5_b_im"].ap.rearrange("d g p c -> p (d g) c"), "a"))
        P.dma(Lt.Cre[sl], dT(I["s5_c_re"].ap.rearrange("d g c p -> p (d g) c"), "a"), allow_slow_non_contiguous=True, eng="act")
        P.dma(Lt.Cim[sl], dT(I["s5_c_im"].ap.rearrange("d g c p -> p (d g) c"), "a"), allow_slow_non_contiguous=True, eng="act")
    P.dma(Lt.ldt, dT(I["s5_log_dt"].ap.rearrange("d g -> (d g)").partition_broadcast(128), "a"))
    G.s5l = Lt


def s5_alloc(P, I, G):
    PR = P.sb([128, 32, NGD], F32, "PR"); NPI = P.sb([128, 32, NGD], F32, "NPI")
    DA = P.sb([128, 10, NGD], F32, "DA"); DB = P.sb([128, 10, NGD], F32, "DB")
    BX1 = P.sb([128, NGD, 16], F32, "BX1"); BX2 = P.sb([128, NGD, 16], F32, "BX2")
    CX1 = P.sb([128, NGD, 16], F32, "CX1"); CX2 = P.sb([128, NGD, 16], F32, "CX2")
    Dcol = P.sb([128, NGH], F32, "Dcol")
    identF = G.ident
    swapF = P.sb([128, 128], BF16, "swapF"); P.dma(swapF, I["swap"], eng="pool")
    maskf = P.sb([128, 128], BF16, "maskf"); P.dma(maskf, I["mask_f"], eng="pool")
    maskb = P.sb([128, 128], BF16, "maskb"); P.dma(maskb, I["mask_b"], eng="pool")
    sgn = P.sb([128, 1], F32, "sgn"); P.memset(sgn[0:64, :], 1.0); P.memset(sgn[64:128, :], -1.0)
    for i in range(8):
        P.dma(Dcol[i * 16:(i + 1) * 16, :], dT(I["s5_d"].ap.rearrange("(g c) -> c g", c=16), "s5d"), allow_slow_non_contiguous=True)
    G.s5t = dict(PR=PR, NPI=NPI, DA=DA, DB=DB, BX1=BX1, BX2=BX2, CX1=CX1, CX2=CX2, Dcol=Dcol, identF=identF, swapF=swapF, maskf=maskf, maskb=maskb, sgn=sgn)


def s5_params(P, I, G):
    PR = G.s5t["PR"]
    NPI = G.s5t["NPI"]
    DA = G.s5t["DA"]
    DB = G.s5t["DB"]
    BX1 = G.s5t["BX1"]
    BX2 = G.s5t["BX2"]
    CX1 = G.s5t["CX1"]
    CX2 = G.s5t["CX2"]
    Dcol = G.s5t["Dcol"]
    identF = G.s5t["identF"]
    swapF = G.s5t["swapF"]
    maskf = G.s5t["maskf"]
    maskb = G.s5t["maskb"]
    sgn = G.s5t["sgn"]
    with P.scope():
        are, aim, ldt = G.s5l.are, G.s5l.aim, G.s5l.ldt
        dt_ = P.sb([128, NGD], F32, "dt")
        P.act(dt_, ldt, AF.Exp)
        lr = P.sb([128, NGD], F32, "lr"); li = P.sb([128, NGD], F32, "li")
        P.tt(lr, are, dt_, ALU.mult); P.tt(li, aim, dt_, ALU.mult)
        with P.scope():
            elist = [t - 7 for t in range(16)] + [8 - t for t in range(16)]
            LR = P.sb([128, 32, NGD], F32, "LR"); LI = P.sb([128, 32, NGD], F32, "LI")
            for idx, e in enumerate(elist):
                P.ts(LR[:, idx, :], lr, float(e), None, op0=ALU.mult)
                P.ts(LI[:, idx, :], li, float(e), None, op0=ALU.mult, eng="pool")
            mag = P.sb([128, 32, NGD], F32, "mag")
            P.act(mag, LR, AF.Exp)
            rr = P.sb([128, 32, NGD], F32, "rr"); kk = P.sb([128, 32, NGD], F32, "kk")

            def sin_of(dst, ang_t, shift):
                P.ts(rr, ang_t, 1.0 / TWO_PI, shift / TWO_PI, op0=ALU.mult, op1=ALU.add)
                P.ts(kk, rr, MAGIC, None, op0=ALU.add)
                P.ts(kk, kk, MAGIC, None, op0=ALU.subtract)
                P.tt(rr, rr, kk, ALU.subtract)
                P.ts(rr, rr, TWO_PI, None, op0=ALU.mult)
                P.ts(rr, rr, 3.1415925, -3.1415925, op0=ALU.min, op1=ALU.max)
                P.act(dst, rr, AF.Sin)
            sn = LR
            sin_of(sn, LI, 0.0)
            P.stt(NPI, mag, -1.0, sn, ALU.mult, ALU.mult)
            sin_of(sn, LI, TWO_PI / 4)
            P.tt(PR, mag, sn, ALU.mult)
        cr_ = P.sb([128, NGD], F32, "cr"); ci_ = P.sb([128, NGD], F32, "ci")
        t1 = P.sb([128, NGD], F32, "t1"); t2 = P.sb([128, NGD], F32, "t2")
        P.copy(DA[:, 0, :], PR[:, 15, :])
        P.ts(DB[:, 0, :], NPI[:, 15, :], -1.0, None, op0=ALU.mult)
        for m in range(1, 10):
            P.tt(t1, DA[:, m - 1, :], DA[:, m - 1, :], ALU.mult)
            P.tt(t2, DB[:, m - 1, :], DB[:, m - 1, :], ALU.mult)
            P.tt(DA[:, m, :], t1, t2, ALU.subtract)
            P.stt(DB[:, m, :], DA[:, m - 1, :], 2.0, DB[:, m - 1, :], ALU.mult, ALU.mult)
        P.ts(DB, DB, sgn[:, 0:1], None, op0=ALU.mult)
        den = P.sb([128, NGD], F32, "den"); nr = P.sb([128, NGD], F32, "nr"); abi = P.sb([128, NGD], F32, "abi")
        P.tt(t1, are, are, ALU.mult); P.tt(t2, aim, aim, ALU.mult); P.tt(den, t1, t2, ALU.add); P.recip(den, den)
        P.ts(nr, PR[:, 8, :], -1.0, None, op0=ALU.add)
        P.ts(abi, NPI[:, 8, :], -1.0, None, op0=ALU.mult)
        P.tt(t1, nr, are, ALU.mult); P.tt(t2, abi, aim, ALU.mult); P.tt(cr_, t1, t2, ALU.add); P.tt(cr_, cr_, den, ALU.mult)
        P.tt(t1, abi, are, ALU.mult); P.tt(t2, nr, aim, ALU.mult); P.tt(ci_, t1, t2, ALU.subtract); P.tt(ci_, ci_, den, ALU.mult)
        crb = bc(cr_, [[1, NGD], [0, 16]]); cib = bc(ci_, [[1, NGD], [0, 16]])
        Bre, Bim, Cre, Cim = G.s5l.Bre, G.s5l.Bim, G.s5l.Cre, G.s5l.Cim
        bbr = P.sb([128, NGD, 16], F32, "bbr"); bbi = P.sb([128, NGD, 16], F32, "bbi"); t3 = P.sb([128, NGD, 16], F32, "t3")
        P.tt(bbr, Bre, crb, ALU.mult); P.tt(t3, Bim, cib, ALU.mult); P.tt(bbr, bbr, t3, ALU.subtract)
        P.tt(bbi, Bim, crb, ALU.mult); P.tt(t3, Bre, cib, ALU.mult); P.tt(bbi, bbi, t3, ALU.add)
        P.copy(BX1[0:64], bbr[0:64]); P.copy(BX1[64:128], bbi[64:128])
        P.copy(BX2[0:64], bbi[0:64]); P.ts(BX2[64:128], bbr[64:128], -1.0, None, op0=ALU.mult)
        P.copy(CX1[0:64], Cre[0:64]); P.ts(CX1[64:128], Cim[64:128], -1.0, None, op0=ALU.mult)
        P.copy(CX2[0:64], Cim[0:64]); P.copy(CX2[64:128], Cre[64:128])


def stage_S5(P, I, G, last):
    NQ = LH + (0 if last else CTX)
    G.ysD = P.dram([512, NQ], BF16, "ysD")
    with P.scope():
        PR = G.s5t["PR"]
        NPI = G.s5t["NPI"]
        DA = G.s5t["DA"]
        DB = G.s5t["DB"]
        BX1 = G.s5t["BX1"]
        BX2 = G.s5t["BX2"]
        CX1 = G.s5t["CX1"]
        CX2 = G.s5t["CX2"]
        Dcol = G.s5t["Dcol"]
        identF = G.s5t["identF"]
        swapF = G.s5t["swapF"]
        maskf = G.s5t["maskf"]
        maskb = G.s5t["maskb"]
        sgn = G.s5t["sgn"]
        Sel = P.sb([128, 64, 128], BF16, "Sel"); SelT = P.sb([128, 64, 128], BF16, "SelT")
        for q in range(4):
            P.dma(Sel[:, q * 16:(q + 1) * 16, :], dT(I["sel8"].ap[:, q * 16:(q + 1) * 16, :], "sel8"), eng="pool")
            P.dma(SelT[:, q * 16:(q + 1) * 16, :], dT(I["sel8T"].ap[:, q * 16:(q + 1) * 16, :], "sel8T"), eng="pool")
        KQ = {nm: Rot(P, [128, 16, 16], BF16, nm, 2) for nm in ["Kf", "Qf", "Kb", "Qb"]}
        tA = Rot(P, [128, 16, 16], F32, "tA", 1); tB = Rot(P, [128, 16, 16], F32, "tB", 1)
        tC = Rot(P, [128, 16, 16], F32, "tC", 1); tD = Rot(P, [128, 16, 16], F32, "tD", 1)
        SgR = Rot(P, [128, 128], BF16, "Sg", 2)
        WeR = Rot(P, [128, 128], BF16, "We", 4)
        s1R = Rot(P, [128, 128], F32, "s1", 1); s2R = Rot(P, [128, 128], F32, "s2", 1)
        UcR = Rot(P, [128, 576], BF16, "Uc", 2)
        XR = {d: [P.sb([128, 545], BF16, f"X{d}{i}") for i in range(2)] for d in "fb"}
        for d in "fb":
            for x in XR[d]:
                P.memset(x, 0.0)
        MdR = {d: Rot(P, [128, 10, 128], BF16, "Md" + d, 2) for d in "fb"}
        mA = Rot(P, [128, 10, 128], BF16, "mA", 1); mB = Rot(P, [128, 10, 128], BF16, "mB", 1)
        Yt = P.sb([128, 8, 544], BF16, "Yt")
        ybufR = Rot(P, [128, L + CTX], BF16, "ybuf", 2)
        yown = Rot(P, [128, LH], BF16, "yown", 1)
        ysend = [P.dram([128, L + CTX], BF16, f"ysend{j}") for j in range(2)]
        ygath = [P.dram([256, L + CTX], BF16, f"ygath{j}") for j in range(2)]
        identB = G.ident
        flip = 0
        spec = {"Kf": (17, 15), "Qf": (7, 9), "Kb": (7, 8), "Qb": (16, 16)}

        def m128(t, lo):
            return t.v(t.ap[:, lo:lo + 8, :].rearrange("p a b -> p (a b)"))

        def build(g):
            stt_ = {"mats": {}, "We": {}, "Mall": {}}
            th = []
            for dname, gd in (("f", g), ("b", NGH + g)):
                for kind, X1, X2, eng in (("K", BX1, BX2, "dve"), ("Q", CX1, CX2, "pool")):
                    t0_, ne = spec[kind + dname]
                    out = KQ[kind + dname]()[:, 0:ne, :]
                    a_ = (tA() if kind == "K" else tC())[:, 0:ne, :]
                    b_ = (tB() if kind == "K" else tD())[:, 0:ne, :]
                    prb = bc(PR[:, t0_, gd:gd + 1], [[NGD, ne], [0, 16]]); npb = bc(NPI[:, t0_, gd:gd + 1], [[NGD, ne], [0, 16]])
                    x1b = bc(X1[:, gd, :], [[0, ne], [1, 16]]); x2b = bc(X2[:, gd, :], [[0, ne], [1, 16]])
                    th.append(lambda a_=a_, prb=prb, x1b=x1b, eng=eng: P.tt(a_, prb, x1b, ALU.mult, eng=eng))
                    th.append(lambda b_=b_, npb=npb, x2b=x2b, eng=eng: P.tt(b_, npb, x2b, ALU.mult, eng=eng))
                    th.append(lambda out=out, a_=a_, b_=b_, eng=eng: P.tt(out, a_, b_, ALU.add, eng=eng))
                    stt_["mats"][kind + dname] = out
            Sg = SgR()
            stt_["Sg"] = Sg
            mt = stt_["mats"]

            def th_S():
                psf = P.ps(); psb_ = P.ps()
                P.mm(psf[:, 0:128], m128(mt["Kf"], 7), m128(mt["Qf"], 0))
                P.mm(psb_[:, 0:128], m128(mt["Kb"], 0), m128(mt["Qb"], 8))
                s1 = s1R(); s2 = s2R()
                P.tt(s1, psf[:, 0:128], maskf, ALU.mult)
                P.tt(s2, psb_[:, 0:128], maskb, ALU.mult)
                P.tt(s1, s1, s2, ALU.add)
                P.stt(Sg, identF, Dcol[:, g:g + 1], s1, ALU.mult, ALU.add)
            th.append(th_S)
            for dname, key in (("f", "Kf"), ("b", "Kb")):
                w = WeR()
                stt_["We"][dname] = w

                def th_W(w=w, key=key):
                    pt = P.ps()
                    ptb = pt.v(pt.ap.bitcast(BF16))
                    P.transpose(ptb[:, 0:128], m128(mt[key], 0), identB)
                    P.copy(w, ptb[:, 0:128], eng="act")
                th.append(th_W)
            stt_["Wo"] = {"f": m128(mt["Qf"], 1), "b": m128(mt["Qb"], 0)}
            for dname, gd in (("f", g), ("b", NGH + g)):
                Mall = MdR[dname]()
                stt_["Mall"][dname] = Mall
                ta = mA(); tb = mB()
                idb = bc(identF, [[0, 10], [1, 128]]); swb = bc(swapF, [[0, 10], [1, 128]])
                dab = bc(DA[:, :, gd], [[NGD, 10], [0, 128]]); dbb = bc(DB[:, :, gd], [[NGD, 10], [0, 128]])
                th.append(lambda ta=ta, idb=idb, dab=dab: P.tt(ta, idb, dab, ALU.mult, eng="pool"))
                th.append(lambda tb=tb, swb=swb, dbb=dbb: P.tt(tb, swb, dbb, ALU.mult, eng="pool"))
                th.append(lambda Mall=Mall, ta=ta, tb=tb: P.tt(Mall, ta, tb, ALU.add, eng="pool"))
            return stt_, th

        cur_state, th0 = build(0)
        for t_ in th0:
            t_()
        for g in range(NGH):
            j, gl = g // 8, g % 8
            if g + 1 < NGH:
                nxt_state, pend = build(g + 1)
            else:
                nxt_state, pend = None, []
            per_step = (len(pend) + 9) // 10
            Sg = cur_state["Sg"]; We = cur_state["We"]; Wo = cur_state["Wo"]
            Uc = UcR()
            puA = P.ps(); puB = P.ps(); pu2 = P.ps()
            for i in range(8):
                P.mm(puA[:, 0:256], Sel[:, gl * 8 + i, :], G.uT[:, j, i, 32:288], start=(i == 0), stop=(i == 7))
                P.mm(puB[:, 0:256], Sel[:, gl * 8 + i, :], G.uT[:, j, i, 288:544], start=(i == 0), stop=(i == 7))
                P.mm(pu2[:, 0:32], Sel[:, gl * 8 + i, :], G.uT[:, j, i, 0:32], start=(i == 0), stop=(i == 7))
            P.copy(Uc[:, 32:288], puA[:, 0:256], eng="act")
            P.copy(Uc[:, 288:544], puB[:, 0:256])
            P.copy(Uc[:, 0:32], pu2[:, 0:32], eng="act")
            P.copy(Uc[:, 544:576], pu2[:, 0:32])
            st_ = {}
            for dname, gd in (("f", g), ("b", NGH + g)):
                ucoff = 0 if dname == "f" else 32
                xoff = 1 if dname == "f" else 0
                cur = XR[dname][0]; nxt = XR[dname][1]
                pa = P.ps(); pb2 = P.ps()
                P.mm(pa[:, 0:512], We[dname], Uc[:, ucoff:ucoff + 512])
                P.mm(pb2[:, 0:32], We[dname], Uc[:, ucoff + 512:ucoff + 544])
                P.copy(cur[:, xoff:xoff + 512], pa[:, 0:512], eng="act")
                P.copy(cur[:, xoff + 512:xoff + 544], pb2[:, 0:32])
                st_[dname] = [cur, nxt, xoff, cur_state["Mall"][dname]]
            for m in range(10):
                d = 1 << m
                work = []
                for dname in ("f", "b"):
                    cur, nxt, xoff, Mall = st_[dname]
                    for (lo, hi) in ((0, 272), (272, 544)):
                        ps = P.ps()
                        if dname == "f":
                            s_ = max(lo, d)
                            has = s_ < hi
                            shift = (ps[:, s_ - lo:hi - lo], cur[:, xoff + s_ - d:xoff + hi - d]) if has else None
                        else:
                            e_ = min(hi, 544 - d)
                            has = lo < e_
                            shift = (ps[:, 0:e_ - lo], cur[:, xoff + lo + d:xoff + e_ + d]) if has else None
                        work.append((dname, ps, lo, hi, shift, cur, nxt, xoff, Mall))
                for (dname, ps, lo, hi, shift, cur, nxt, xoff, Mall) in work:
                    P.mm(ps[:, 0:hi - lo], identB, cur[:, xoff + lo:xoff + hi], start=True, stop=(shift is None))
                for (dname, ps, lo, hi, shift, cur, nxt, xoff, Mall) in work:
                    if shift is not None:
                        P.mm(shift[0], Mall[:, m, :], shift[1], start=False, stop=True)
                for (dname, ps, lo, hi, shift, cur, nxt, xoff, Mall) in work:
                    flip ^= 1
                    P.copy(nxt[:, xoff + lo:xoff + hi], ps[:, 0:hi - lo], eng=("act" if flip else "dve"))
                for dname in ("f", "b"):
                    st_[dname][0], st_[dname][1] = st_[dname][1], st_[dname][0]
                for _ in range(per_step):
                    if pend:
                        pend.pop(0)()
            while pend:
                pend.pop(0)()
            Xfin = {dname: st_[dname][0] for dname in ("f", "b")}
            Xf, Xb = Xfin["f"], Xfin["b"]
            py = P.ps()
            pyc = P.ps() if not last else None
            ytl = [(Sg, Uc[:, 32:544], Uc[:, 0:32]), (Wo["f"], Xf[:, 32:544], Xf[:, 0:32]), (Wo["b"], Xb[:, 1:513], Xb[:, 513:545])]
            for q, (lh, r1, r2) in enumerate(ytl):
                P.mm(py[:, 0:512], lh, r1, start=(q == 0), stop=(q == 2))
                if not last:
                    P.mm(pyc[:, 0:32], lh, r2, start=(q == 0), stop=(q == 2))
            P.copy(Yt[:, gl, 0:512], py[:, 0:512], eng="act")
            if not last:
                P.copy(Yt[:, gl, 512:544], pyc[:, 0:32])
            cur_state = nxt_state
            if gl == 7:
                yb = ybufR()
                for i in range(8):
                    ps = P.ps()
                    ps2 = P.ps() if not last else None
                    for g2 in range(8):
                        P.mm(ps[:, 0:512], SelT[:, g2 * 8 + i, :], Yt[:, g2, 0:512], start=(g2 == 0), stop=(g2 == 7))
                        if not last:
                            P.mm(ps2[:, 0:32], SelT[:, g2 * 8 + i, :], Yt[:, g2, 512:544], start=(g2 == 0), stop=(g2 == 7))
                    P.copy(yb.v(yb.ap[:, i:L:8]), ps[:, 0:512], eng=("act" if i % 2 else "dve"))
                    if not last:
                        P.copy(yb.v(yb.ap[:, L + i:L + CTX:8]), ps2[:, 0:32])
                P.dma(ysend[j], yb)
        P.flush()
        groups = [[0, 1], [2, 3], [4, 5], [6, 7]]
        for j in range(2):
            P.add("pool", lambda e, j=j: e.collective_compute("AllGather", ALU.bypass, replica_groups=groups,
                                                              ins=[ysend[j].ap.opt()], outs=[ygath[j].ap.opt()]), [ysend[j]], [ygath[j]])
            P.add("pool", None, [ygath[j]], [])
        P.flush()
        for J in range(4):
            src = ygath[J % 2]
            r0_ = 128 * (J // 2)
            yb = ybufR()
            P.dma(yb, src.v(src.ap[r0_:r0_ + 128, :]))
            yo = yown()
            P.ts(yb[:, 0:LH], yb[:, 0:LH], G.sel[:, 0:1], None, op0=ALU.mult)
            P.stt(yo, yb[:, LH:L], G.sel[:, 1:2], yb[:, 0:LH], ALU.mult, ALU.add)
            P.dma(G.ysD.v(G.ysD.ap[J * 128:(J + 1) * 128, 0:LH]), yo)
            if not last:
                P.dma(G.ysD.v(G.ysD.ap[J * 128:(J + 1) * 128, LH:LH + CTX]), yb[:, L:L + CTX])


def emit_layer(P, I, G, last, out_own, out_ctx):
    stage_prep(P, I, G)
    with P.scope():
        alloc_persist(P, G)
        with P.scope():
            G.uT = P.sb([128, 2, 8, NK // 8], BF16, "uT")
            s5_alloc(P, I, G)
            with P.scope():
                s5_load(P, I, G)
                stage_A(P, I, G)
                s5_params(P, I, G)
            stage_S5(P, I, G, last)
        stage_B1(P, I, G, last)
        stage_B2(P, I, G, last)
        stage_B3(P, I, G, last)
    stage_B4a(P, I, G, last)
    stage_B4b(P, I, G, last, out_own, out_ctx)
    P.flush()


def build_fused():
    nc = bass.Bass("TRN2", target_bir_lowering=False)
    Cn = declare_consts(nc)
    W = [declare_weights(nc, l) for l in range(2)]
    y_out = dT(nc.dram_tensor("y_own", [LH, D], F32, kind="ExternalOutput").ap(), "y_own")
    with ExitStack() as st:
        P = Prog(nc, st)
        P.init_psum()
        NCH = 4
        CR = LH // NCH
        x1o = [P.dram([CR, D], F32, f"x1o{c}") for c in range(NCH)]
        x1g = [P.dram([2 * CR, D], F32, f"x1g{c}") for c in range(NCH)]
        ctx1 = P.dram([CTX, D], F32, "ctx1")
        own_src = RowSrc(lambda r0: x1o[r0 // CR].v(x1o[r0 // CR].ap[r0 % CR:r0 % CR + 128, :]))

        def all_fn(r0):
            half, rr = r0 // LH, r0 % LH
            c, i = rr // CR, rr % CR
            return x1g[c].v(x1g[c].ap[half * CR + i:half * CR + i + 128, :])
        with P.scope():
            G = Ctx()
            I0 = dict(Cn); I0.update(W[0])
            for k in ("x_all", "x_own", "ctx"):
                I0[k] = flat_src(Cn[k])
            emit_layer(P, I0, G, False, own_src, flat_src(ctx1))
        groups = [[0, 1], [2, 3], [4, 5], [6, 7]]
        for c in range(NCH):
            P.add("pool", lambda e, c=c: e.collective_compute("AllGather", ALU.bypass, replica_groups=groups,
                                                              ins=[x1o[c].ap.opt()], outs=[x1g[c].ap.opt()]), [x1o[c]], [x1g[c]])
            P.add("pool", None, [x1g[c]], [])
        P.flush()
        with P.scope():
            G = Ctx()
            I1 = dict(Cn); I1.update(W[1])
            I1["x_all"] = RowSrc(all_fn); I1["x_own"] = own_src; I1["ctx"] = flat_src(ctx1)
            emit_layer(P, I1, G, True, flat_src(y_out), None)
    return nc


_NC_CACHE = {}


def kernel(**inputs):
    inputs = {k: np.asarray(v) for k, v in inputs.items()}
    C = host_constants()
    if "nc" not in _NC_CACHE:
        _NC_CACHE["nc"] = build_fused()
    nc = _NC_CACHE["nc"]
    in_maps = [per_core_inputs(inputs, core, C) for core in range(8)]
    res = run_bass_kernel_spmd(nc, in_maps, core_ids=list(range(8)))
    out = np.empty((4, L, D), np.float32)
    for core in range(8):
        b, hh = core // 2, core % 2
        out[b, hh * LH:(hh + 1) * LH] = np.asarray(res.results[core]["y_own"])
    return out
```

```python
import numpy as np
import concourse.bass as bass
import concourse.mybir as mybir
from concourse.bass_utils import run_bass_kernel_spmd
from contextlib import ExitStack, contextmanager

F32 = mybir.dt.float32
BF16 = mybir.dt.bfloat16
I32 = mybir.dt.int32
AF = mybir.ActivationFunctionType
ALU = mybir.AluOpType
AX = mybir.AxisListType

ENGS = ["pe", "act", "dve", "pool", "sp"]
DMA_WIN = 8
SAME_ENG_SYNC = True


class T:
    __slots__ = ("ap", "keys")

    def __init__(self, ap, keys):
        self.ap = ap
        self.keys = tuple(keys)

    def __getitem__(self, sl):
        return T(self.ap[sl], self.keys)

    def v(self, ap):
        return T(ap, self.keys)

    def k(self, *sub):
        return T(self.ap, [(self.keys[0],) + tuple(sub)])


class Prog:
    def __init__(self, nc, stack):
        self.nc = nc
        self.stack = stack
        self.cur = stack
        self.streams = {e: [] for e in ENGS}
        self.last_writer = {}
        self.readers = {}
        self.ndma = {e: 0 for e in ENGS}
        self.sigcount = {e: 0 for e in ENGS}
        self.waited = {e: {} for e in ENGS}
        self.nt = 0
        self.psum_banks = []
        self.psum_i = 0
        self.sem = {e: stack.enter_context(nc.semaphore(f"s_{e}")) for e in ENGS}
        self.dsem = {e: [stack.enter_context(nc.semaphore(f"d_{e}{i}")) for i in range(DMA_WIN)]
                     for e in ("sp", "pool", "act")}
        self.dbg = {}
        self.nops = {e: 0 for e in ENGS}

    def sb(self, shape, dt, name=None):
        self.nt += 1
        name = name or "t"
        nm = f"{name}_{self.nt}"
        t = self.cur.enter_context(self.nc.sbuf_tensor(nm, list(shape), dt))
        return T(t[:], [nm])

    def dram(self, shape, dt, name):
        self.nt += 1
        nm = f"{name}_{self.nt}"
        t = self.nc.dram_tensor(nm, list(shape), dt, kind="Internal")
        return T(t.ap(), [nm])

    def init_psum(self, n=8):
        for i in range(n):
            t = self.stack.enter_context(self.nc.psum_tensor(f"bank{i}", [128, 512], F32))
            self.psum_banks.append(T(t[:], [f"bank{i}"]))

    def ps(self):
        b = self.psum_banks[self.psum_i % len(self.psum_banks)]
        self.psum_i += 1
        return b

    @contextmanager
    def scope(self):
        prev = self.cur
        with ExitStack() as st:
            self.cur = st
            yield
            self.flush()
        self.cur = prev

    def add(self, eng, fn, reads=(), writes=(), dma=False):
        deps = set()
        rk = [k for t in reads for k in t.keys]
        wk = [k for t in writes for k in t.keys]
        for k in rk:
            if k in self.last_writer:
                deps.add(self.last_writer[k])
        for k in wk:
            if k in self.last_writer:
                deps.add(self.last_writer[k])
            for r in self.readers.get(k, ()):
                deps.add(r)
        idx = len(self.streams[eng])
        me = (eng, idx)
        deps.discard(me)
        op = dict(fn=fn, deps=deps, dma=dma, signal=False, dman=None)
        if dma:
            op["dman"] = self.ndma[eng]
            self.ndma[eng] += 1
        self.streams[eng].append(op)
        for k in rk:
            self.readers.setdefault(k, []).append(me)
        for k in wk:
            self.last_writer[k] = me
            self.readers[k] = []
        return me

    def dma(self, out, in_, eng="sp", **kw):
        o = out.ap
        i = in_.ap
        return self.add(eng, lambda e: e.dma_start(out=o, in_=i, **kw), [in_], [out], dma=True)

    def mm(self, out, lhsT, rhs, start=True, stop=True, **kw):
        return self.add("pe", lambda e: e.matmul(out.ap, lhsT.ap, rhs.ap, start=start, stop=stop, **kw),
                        [lhsT, rhs], [out])

    def transpose(self, out, in_, ident):
        return self.add("pe", lambda e: e.transpose(out.ap, in_.ap, ident.ap), [in_, ident], [out])

    def act(self, out, in_, func, bias=None, scale=None, eng="act", accum_out=None):
        reads = [in_]
        kw = {}
        if bias is not None:
            if isinstance(bias, T):
                reads.append(bias); kw["bias"] = bias.ap
            else:
                kw["bias"] = bias
        if scale is not None:
            if isinstance(scale, T):
                reads.append(scale); kw["scale"] = scale.ap
            else:
                kw["scale"] = scale
        writes = [out]
        if accum_out is not None:
            kw["accum_out"] = accum_out.ap; writes.append(accum_out)
        return self.add(eng, lambda e: e.activation(out.ap, in_.ap, func, **kw), reads, writes)

    def tt(self, out, a, b, op, eng="dve"):
        return self.add(eng, lambda e: e.tensor_tensor(out.ap, a.ap, b.ap, op), [a, b], [out])

    def ts(self, out, a, s1, s2=None, op0=ALU.mult, op1=None, eng="dve"):
        reads = [a]
        v1 = s1.ap if isinstance(s1, T) else s1
        if isinstance(s1, T): reads.append(s1)
        v2 = s2.ap if isinstance(s2, T) else s2
        if isinstance(s2, T): reads.append(s2)
        if op1 is None:
            return self.add(eng, lambda e: e.tensor_scalar(out.ap, a.ap, v1, None, op0), reads, [out])
        return self.add(eng, lambda e: e.tensor_scalar(out.ap, a.ap, v1, v2, op0, op1), reads, [out])

    def stt(self, out, a, s, b, op0, op1, eng="dve"):
        reads = [a, b]
        v = s.ap if isinstance(s, T) else s
        if isinstance(s, T): reads.append(s)
        return self.add(eng, lambda e: e.scalar_tensor_tensor(out.ap, a.ap, v, b.ap, op0, op1), reads, [out])

    def copy(self, out, in_, eng="dve"):
        if eng == "act":
            return self.add("act", lambda e: e.copy(out.ap, in_.ap), [in_], [out])
        return self.add(eng, lambda e: e.tensor_copy(out.ap, in_.ap), [in_], [out])

    def memset(self, out, val, eng="pool"):
        return self.add(eng, lambda e: e.memset(out.ap, val), [], [out])

    def recip(self, out, in_, eng="dve"):
        return self.add(eng, lambda e: e.reciprocal(out.ap, in_.ap), [in_], [out])

    def recip_fast(self, out, in_):
        return self.add("dve", lambda e: e.reciprocal_approx_fast(out.ap, in_.ap), [in_], [out])

    def bn_stats(self, out, in_):
        return self.add("dve", lambda e: e.bn_stats(out.ap, in_.ap), [in_], [out])

    def bn_aggr(self, out, in_):
        return self.add("dve", lambda e: e.bn_aggr(out.ap, in_.ap), [in_], [out])

    def debug_out(self, name, t, shape, dt=F32):
        d = self.nc.dram_tensor(name, list(shape), dt, kind="ExternalOutput").ap()
        self.dbg[name] = d
        return self.dma(T(d, [name]), t)

    def flush(self):
        nc = self.nc
        streams = self.streams
        lasts = []
        for e in ENGS:
            for j in range(len(streams[e]) - 1, -1, -1):
                op = streams[e][j]
                if op["fn"] is not None and not op["dma"]:
                    lasts.append((e, j))
                    break
        dmas = [(e, j) for e in ENGS for j, op in enumerate(streams[e]) if op["dma"]]
        for e in ENGS:
            deps = set(l for l in lasts if l[0] != e) | set(dmas)
            streams[e].append(dict(fn=None, deps=deps, dma=False, signal=False, dman=None))
        for e in ENGS:
            for op in streams[e]:
                for (f, j) in op["deps"]:
                    d = streams[f][j]
                    if not d["dma"]:
                        if f == e and not SAME_ENG_SYNC:
                            continue
                        d["signal"] = True
        for e in ENGS:
            for op in streams[e]:
                if op["signal"]:
                    self.sigcount[e] += 1
                    op["sigval"] = self.sigcount[e]
        sem, dsem = self.sem, self.dsem

        def run(ename):
            def body(eng):
                waited = self.waited[ename]

                def wait(s, v, key):
                    if waited.get(key, 0) >= v:
                        return
                    waited[key] = v
                    eng.wait_ge(s, v)

                for op in streams[ename]:
                    for (f, j) in sorted(op["deps"]):
                        d = streams[f][j]
                        if d["dma"]:
                            n = d["dman"]
                            wait(dsem[f][n % DMA_WIN], 16 * (n // DMA_WIN + 1), (f, n % DMA_WIN))
                        else:
                            if f == ename and not SAME_ENG_SYNC:
                                continue
                            wait(sem[f], d["sigval"], f)
                    if op["dma"]:
                        n = op["dman"]
                        if n >= DMA_WIN:
                            wait(dsem[ename][n % DMA_WIN], 16 * (n // DMA_WIN), (ename, n % DMA_WIN))
                    if op["fn"] is None:
                        continue
                    ins = op["fn"](eng)
                    if op["dma"]:
                        ins.then_inc(dsem[ename][op["dman"] % DMA_WIN], 16)
                    elif op["signal"]:
                        ins.then_inc(sem[ename], 1)
            return body

        with nc.Block() as block:
            block.tensor(run("pe"))
            block.scalar(run("act"))
            block.vector(run("dve"))
            block.gpsimd(run("pool"))
            block.sync(run("sp"))
        for e in ENGS:
            self.nops[e] += len(streams[e])
        self.streams = {e: [] for e in ENGS}
        self.last_writer = {}
        self.readers = {}


class Rot:
    def __init__(self, P, shape, dt, name, n):
        self.tiles = [P.sb(shape, dt, f"{name}{i}") for i in range(n)]
        self.i = 0

    def __call__(self):
        t = self.tiles[self.i % len(self.tiles)]
        self.i += 1
        return t


def _ps6(self):
    b = self.psum_banks[self.psum_i % 6]
    self.psum_i += 1
    return b


def _psacc(self):
    self.acc_i = getattr(self, "acc_i", 0) + 1
    return self.psum_banks[6 + self.acc_i % 2]


Prog.ps = _ps6
Prog.ps_acc = _psacc


D = 1024
L = 4096
LH = 2048
CTX = 256
NK = CTX + L
NKT = NK // 128
EPS = 1e-6
OFF_AK, OFF_AV, OFF_CKV, OFF_CKR, OFF_U, NST = 0, 128, 256, 512, 544, 1056
OFF_AQ, OFF_CQ, OFF_GATE, NIN = 1056, 1568, 2336, 5408
ALPHA = (2.0 * 2) ** 0.25
NGH = 16
NGD = 2 * NGH


def dT(ap, name):
    return T(ap, [name])


class Ctx:
    pass


class RowSrc:
    def __init__(self, fn):
        self.fn = fn

    def rows(self, r0):
        return self.fn(r0)


def flat_src(t):
    return RowSrc(lambda r0: t.v(t.ap[r0:r0 + 128, :]))


WEIGHT_SHAPES = {
    "w_mod": [D, 6 * D], "b_mod": [6 * D], "w_in": [D, NIN], "a_q_gain": [64], "a_k_gain": [64],
    "c_q_a_gain": [768], "c_kv_a_gain": [256], "c_w_qb": [768, 768], "c_w_kvb": [256, 1024],
    "s5_a_re": [2, 16, 64], "s5_a_im": [2, 16, 64], "s5_log_dt": [2, 16],
    "s5_b_re": [2, 16, 64, 16], "s5_b_im": [2, 16, 64, 16], "s5_c_re": [2, 16, 16, 64], "s5_c_im": [2, 16, 16, 64],
    "s5_d": [256], "s5_w_glu": [512, 1024], "w_branch_a": [512, D], "w_branch_s5": [512, D], "w_branch_c": [512, D],
    "w_out": [D, D], "ln1_g": [D], "ln1_b": [D], "w_up": [D, 4 * D], "w_down": [4 * D, D], "ln2_g": [D], "ln2_b": [D],
}
CONST_SHAPES = {
    "x_all": [L, D], "x_own": [LH, D], "ctx": [CTX, D], "cvec": [2, D],
    "ident": [128, 128], "blk64": [128, 128], "perm64": [128, 128], "perm32": [32, 32],
    "ropek_cos": [128, L], "ropek_sin": [128, L], "ropeq_cos": [128, LH], "ropeq_sin": [128, LH],
    "rope32k_cos": [32, L], "rope32k_sin": [32, L], "rope32q_cos": [32, LH], "rope32q_sin": [32, LH],
    "sel": [128, 2], "swap": [128, 128], "mask_f": [128, 128], "mask_b": [128, 128],
    "sel8": [128, 64, 128], "sel8T": [128, 64, 128],
}


def declare_consts(nc):
    return {k: dT(nc.dram_tensor(k, list(v), F32, kind="ExternalInput").ap(), k) for k, v in CONST_SHAPES.items()}


def declare_weights(nc, l):
    return {k: dT(nc.dram_tensor(f"{k}_{l}", list(v), F32, kind="ExternalInput").ap(), f"{k}_{l}")
            for k, v in WEIGHT_SHAPES.items()}


def declare_inputs(nc, last):
    I = declare_consts(nc)
    I.update(declare_weights(nc, 1 if last else 0))
    return I


def host_constants():
    import math
    C = {}
    C["ident"] = np.eye(128, dtype=np.float32)
    blk = np.zeros((128, 128), np.float32); blk[:64, :64] = 1 / 64; blk[64:, 64:] = 1 / 64
    C["blk64"] = blk

    def perm_and_sign(dim):
        half = dim // 2; q = half // 2
        Pm = np.zeros((dim, dim), np.float32)
        sg = np.zeros(dim, np.float32)
        for m in range(dim):
            if (m % half) < q:
                Pm[m + q, m] = 1; sg[m] = -1
            else:
                Pm[m - q, m] = 1; sg[m] = 1
        return Pm, sg
    P64, s64 = perm_and_sign(64)
    p128 = np.zeros((128, 128), np.float32); p128[:64, :64] = P64; p128[64:, 64:] = P64
    C["perm64"] = p128
    P32, s32 = perm_and_sign(32)
    C["perm32"] = P32

    def tables(dim):
        half = dim // 2
        inv = 10000.0 ** (-np.arange(0, half, 2, dtype=np.float32) / half)
        rows = L // 64
        row = np.repeat(np.arange(rows, dtype=np.float32), 64)
        col = np.tile(np.arange(64, dtype=np.float32), rows)
        ang_r = row[:, None] * inv; ang_c = col[:, None] * inv
        ang = np.concatenate([ang_r, ang_r, ang_c, ang_c], axis=-1).astype(np.float32)
        return np.cos(ang).T.astype(np.float32), np.sin(ang).T.astype(np.float32)
    c64, s64t = tables(64)
    s64t = s64t * s64[:, None]
    C["ropek_cos"] = np.concatenate([c64, c64], 0); C["ropek_sin"] = np.concatenate([s64t, s64t], 0)
    c32, s32t = tables(32)
    s32t = s32t * s32[:, None]
    C["rope32k_cos"] = c32; C["rope32k_sin"] = s32t
    sw = np.zeros((128, 128), np.float32)
    for p in range(64):
        sw[p, 64 + p] = 1; sw[64 + p, p] = 1
    C["swap"] = sw
    ii = np.arange(128) // 16
    C["mask_f"] = (ii[:, None] <= ii[None, :]).astype(np.float32)
    C["mask_b"] = (ii[:, None] >= ii[None, :]).astype(np.float32)
    sel = np.zeros((128, 64, 128), np.float32)
    for gl in range(8):
        for i in range(8):
            for c in range(16):
                sel[gl * 16 + c, gl * 8 + i, i * 16 + c] = 1
    C["sel8"] = sel
    C["sel8T"] = np.ascontiguousarray(sel.transpose(2, 1, 0))
    return C


def per_core_inputs(inputs, core, C, layers=(0, 1)):
    b, hh = core // 2, core % 2
    m = {}
    xb = inputs["x"][b]
    m["x_all"] = xb; m["x_own"] = xb[hh * LH:(hh + 1) * LH]; m["ctx"] = inputs["ctx"][b]
    m["cvec"] = np.stack([inputs["c"][b], inputs["c_ctx"]], 0)
    gs = slice(16 * hh, 16 * hh + 16)
    for l in layers:
        for k in WEIGHT_SHAPES:
            v = inputs[k][l]
            if k in ("s5_a_re", "s5_a_im", "s5_b_re", "s5_b_im", "s5_c_re", "s5_c_im", "s5_log_dt"):
                v = v[:, gs]
            elif k == "s5_d":
                v = v[256 * hh:256 * hh + 256]
            elif k == "w_in":
                v = np.concatenate([v[:, :OFF_U], v[:, OFF_U + 256 * hh:OFF_U + 256 * hh + 256],
                                    v[:, OFF_U + 256 * (1 - hh):OFF_U + 256 * (1 - hh) + 256], v[:, NST:]], axis=1)
            m[f"{k}_{l}"] = v
    for k in ["ident", "blk64", "perm64", "perm32", "ropek_cos", "ropek_sin", "rope32k_cos", "rope32k_sin",
              "swap", "mask_f", "mask_b", "sel8", "sel8T"]:
        m[k] = C[k]
    sl = slice(hh * LH, (hh + 1) * LH)
    m["ropeq_cos"] = C["ropek_cos"][:, sl]; m["ropeq_sin"] = C["ropek_sin"][:, sl]
    m["rope32q_cos"] = C["rope32k_cos"][:, sl]; m["rope32q_sin"] = C["rope32k_sin"][:, sl]
    s = np.zeros((128, 2), np.float32); s[:, hh] = 1
    m["sel"] = s
    return {k: np.ascontiguousarray(v, dtype=np.float32) for k, v in m.items()}


def rstd_from_ms(P, out, ms, n, eps=EPS, eng_a="act"):
    P.ts(out, ms, eps, None, op0=ALU.add)
    P.act(out, out, AF.Sqrt)
    if n > 1:
        P.recip(out, out)
    else:
        P.recip(out, out)


def stage_prep(P, I, G):
    G.ident = P.sb([128, 128], BF16, "ident"); P.dma(G.ident, I["ident"], eng="pool")
    G.blk64 = P.sb([128, 128], BF16, "blk64"); P.dma(G.blk64, I["blk64"], eng="pool")
    G.perm64 = P.sb([128, 128], BF16, "perm64"); P.dma(G.perm64, I["perm64"], eng="pool")
    G.perm32 = P.sb([32, 32], BF16, "perm32"); P.dma(G.perm32, I["perm32"], eng="pool")
    G.ones = P.sb([128, 128], BF16, "ones"); P.memset(G.ones, 1.0)
    G.modT = P.sb([128, 48, 2], F32, "modT")
    G.sel = P.sb([128, 2], F32, "sel"); P.dma(G.sel, I["sel"])
    with P.scope():
        cT = P.sb([128, 8, 2], F32, "cT")
        for t in range(2):
            src = I["cvec"].ap[t, :].rearrange("(k p) -> p k", p=128)
            P.dma(cT[:, :, t], dT(src, "cvec"), allow_slow_non_contiguous=True)
        sT = P.sb([128, 8, 2], BF16, "sT")
        P.act(sT, cT, AF.Silu)
        bT = P.sb([128, 48], F32, "bT")
        P.dma(bT, dT(I["b_mod"].ap.rearrange("(j p) -> p j", p=128), "b_mod"), allow_slow_non_contiguous=True)
        wbufs = [P.sb([128, 8, 128], BF16, f"wm{i}") for i in range(3)]
        for j in range(48):
            wb = wbufs[j % 3]
            src = I["w_mod"].ap[:, j * 128:(j + 1) * 128].rearrange("(k p) c -> p k c", p=128)
            P.dma(wb, dT(src, "w_mod"), eng="pool")
            ps = P.ps()
            for kc in range(8):
                P.mm(ps[:, 0:2], wb[:, kc, :], sT[:, kc, :], start=(kc == 0), stop=(kc == 7))
            P.ts(G.modT[:, j, :], ps[:, 0:2], bT[:, j:j + 1], None, op0=ALU.add)
        for w in (1, 4):
            P.ts(G.modT[:, w * 8:(w + 1) * 8, :], G.modT[:, w * 8:(w + 1) * 8, :], 1.0, None, op0=ALU.add)
        G.modD = P.dram([2, 6 * D], F32, "modD")
        for t in range(2):
            dst = G.modD.ap[t, :].rearrange("(j p) -> p j", p=128)
            P.dma(G.modD.v(dst), G.modT[:, :, t], allow_slow_non_contiguous=True)


def ln_tile_to_hT(P, G, xt, hT_dst, t_idx, which_sh, which_sc):
    st = P.sb([128, 2, 6], F32, "bnst")
    mv = P.sb([128, 2], F32, "mv")
    for hh in range(2):
        P.bn_stats(st[:, hh, :], xt[:, hh * 512:(hh + 1) * 512])
    P.bn_aggr(mv, st)
    rs = P.sb([128, 1], F32, "rs")
    rstd_from_ms(P, rs, mv[:, 1:2], 1)
    xn = P.sb([128, D], BF16, "xn")
    P.ts(xn, xt, mv[:, 0:1], rs, op0=ALU.subtract, op1=ALU.mult)
    ps = P.ps()
    psb = ps.v(ps.ap.bitcast(BF16))
    for kc in range(8):
        P.transpose(psb[:, kc * 128:(kc + 1) * 128], xn[:, kc * 128:(kc + 1) * 128], G.ident)
    for kc in range(8):
        P.act(hT_dst[:, kc, :], psb[:, kc * 128:(kc + 1) * 128], AF.Identity,
              bias=G.modT[:, which_sh * 8 + kc, t_idx:t_idx + 1], scale=G.modT[:, which_sc * 8 + kc, t_idx:t_idx + 1])


_CAST = {"i": 0}


def wload(P, stg, dst, src, engs=("pool", "dve", "act")):
    shape = list(dst.ap.shape)[1:]
    n = 1
    for v in shape:
        n *= v
    st = stg()
    sv = st.ap[0:dst.ap.shape[0], 0:n]
    if len(shape) == 2:
        sv = sv.rearrange("p (a b) -> p a b", a=shape[0])
    svt = st.v(sv)
    P.dma(svt, src)
    e = engs[_CAST["i"] % len(engs)]
    _CAST["i"] += 1
    P.copy(dst, svt, eng=e)


def mmg(P, items, K):
    for k in range(K):
        for (out, lf, rf) in items:
            P.mm(out, lf(k), rf(k), start=(k == 0), stop=(k == K - 1))


def make_ln_pools(P, nb=2):
    R = Ctx()
    R.xt = Rot(P, [128, D], F32, "xt", nb)
    R.st = Rot(P, [128, 2, 6], F32, "bnst", nb)
    R.mv = Rot(P, [128, 2], F32, "mv", nb)
    R.rs = Rot(P, [128, 1], F32, "rs", nb)
    R.xn = Rot(P, [128, D], BF16, "xn", nb)
    return R


def ln_part1(P, G, R, src_dram):
    xt = R.xt()
    P.dma(xt, src_dram)
    st = R.st(); mv = R.mv(); rs = R.rs(); xn = R.xn()
    for hh in range(2):
        P.bn_stats(st[:, hh, :], xt[:, hh * 512:(hh + 1) * 512])
    P.bn_aggr(mv, st)
    rstd_from_ms(P, rs, mv[:, 1:2], 1)
    P.ts(xn, xt, mv[:, 0:1], rs, op0=ALU.subtract, op1=ALU.mult)
    return xn


def ln_part2(P, G, xn, hT_dst, t_idx, which_sh, which_sc):
    ps = P.ps()
    psb = ps.v(ps.ap.bitcast(BF16))
    for kc in range(8):
        P.transpose(psb[:, kc * 128:(kc + 1) * 128], xn[:, kc * 128:(kc + 1) * 128], G.ident)
    for kc in range(8):
        bias = G.modT[:, which_sh * 8 + kc, t_idx:t_idx + 1]
        scale = G.modT[:, which_sc * 8 + kc, t_idx:t_idx + 1]
        if kc % 2 == 0:
            P.act(hT_dst[:, kc, :], psb[:, kc * 128:(kc + 1) * 128], AF.Identity, bias=bias, scale=scale)
        else:
            P.ts(hT_dst[:, kc, :], psb[:, kc * 128:(kc + 1) * 128], scale, bias, op0=ALU.mult, op1=ALU.add)


def ln_tile_to_hT2(P, G, R, src_dram, hT_dst, t_idx, which_sh, which_sc):
    xn = ln_part1(P, G, R, src_dram)
    ln_part2(P, G, xn, hT_dst, t_idx, which_sh, which_sc)


def rope_apply(P, dst, src_bf, perm, cos, sin, tmp, n, rows=128, psfn=None):
    ps = (psfn or P.ps)()
    P.mm(ps[0:rows, 0:n], perm, src_bf)
    P.tt(tmp, src_bf, cos, ALU.mult)
    P.tt(dst, ps[0:rows, 0:n], sin, ALU.mult)
    P.tt(dst, dst, tmp, ALU.add)


def alloc_persist(P, G):
    G.kT = P.sb([128, NK], BF16, "kT")
    G.Vg = P.sb([128, NKT, 2, 128], BF16, "Vg")
    G.ckvT = P.sb([128, 2, NK], BF16, "ckvT")
    G.krT = P.sb([32, NK], BF16, "krT")


def stage_A(P, I, G):
    P.memset(G.Vg[:, :, :, 64:128], 1.0)
    with P.scope():
        w_st = P.sb([128, 8, NST], BF16, "w_st")
        with P.scope():
            stg = Rot(P, [128, NST], F32, "stg", 2)
            for kc in range(8):
                wload(P, stg, w_st[:, kc, :], dT(I["w_in"].ap[kc * 128:(kc + 1) * 128, 0:NST], "w_in"))
        kg = P.sb([128, 1], F32, "kg")
        for r in range(2):
            P.dma(kg[r * 64:(r + 1) * 64, :], dT(I["a_k_gain"].ap.rearrange("(p o) -> p o", o=1), "akg"))
        cg = P.sb([128, 2], F32, "cg")
        P.dma(cg, dT(I["c_kv_a_gain"].ap.rearrange("(j p) -> p j", p=128), "ckg"), allow_slow_non_contiguous=True)
        R = make_ln_pools(P, 3)
        hTs = Rot(P, [128, 8, 512], BF16, "hT", 2)
        sq = Rot(P, [128, 512], BF16, "sq", 2)
        rst = Rot(P, [128, 512], F32, "rst", 1)
        knb = Rot(P, [128, 512], BF16, "knb", 2)
        tmp = Rot(P, [128, 512], F32, "tmp", 1)
        cosb = Rot(P, [128, 512], F32, "cosb", 1)
        sinb = Rot(P, [128, 512], F32, "sinb", 1)
        cos32 = Rot(P, [32, 512], BF16, "cos32", 1)
        sin32 = Rot(P, [32, 512], BF16, "sin32", 1)
        blocks = [(0, 2, True)] + [(2 + 4 * i, 4, False) for i in range(8)]
        alltiles = [(t0 + ti, is_ctx) for (t0, nt, is_ctx) in blocks for ti in range(nt)]

        def a_p1(q):
            t, is_ctx = alltiles[q]
            return ln_part1(P, G, R, I["ctx"].rows(t * 128) if is_ctx else I["x_all"].rows((t - 2) * 128))
        qi = 0
        xn_cur = a_p1(0)
        for (t0, nt, is_ctx) in blocks:
            n = nt * 128
            c0 = t0 * 128
            hT = hTs()
            for ti in range(nt):
                xn_nxt = a_p1(qi + 1) if qi + 1 < len(alltiles) else None
                ln_part2(P, G, xn_cur, hT[:, :, ti * 128:(ti + 1) * 128], 1 if is_ctx else 0, 0, 1)
                xn_cur = xn_nxt
                qi += 1
            if not is_ctx:
                lc = c0 - CTX
                cb, sb_, c32, s32 = cosb(), sinb(), cos32(), sin32()
                P.dma(cb[:, 0:n], dT(I["ropek_cos"].ap[:, lc:lc + n], "rc"))
                P.dma(sb_[:, 0:n], dT(I["ropek_sin"].ap[:, lc:lc + n], "rs"))
                P.dma(c32[:, 0:n], dT(I["rope32k_cos"].ap[:, lc:lc + n], "rc32"), eng="pool")
                P.dma(s32[:, 0:n], dT(I["rope32k_sin"].ap[:, lc:lc + n], "rs32"), eng="pool")
            pk = P.ps(); pc = [P.ps(), P.ps()]; pr = P.ps()
            mmg(P, [(pk[:, 0:n], lambda k: w_st[:, k, OFF_AK:OFF_AK + 128], lambda k: hT[:, k, 0:n]),
                    (pc[0][:, 0:n], lambda k: w_st[:, k, OFF_CKV:OFF_CKV + 128], lambda k: hT[:, k, 0:n]),
                    (pc[1][:, 0:n], lambda k: w_st[:, k, OFF_CKV + 128:OFF_CKV + 256], lambda k: hT[:, k, 0:n]),
                    (pr[0:32, 0:n], lambda k: w_st[:, k, OFF_CKR:OFF_CKR + 32], lambda k: hT[:, k, 0:n])], 8)
            s = sq()
            P.act(s[:, 0:n], pk[:, 0:n], AF.Square)
            pm = P.ps()
            P.mm(pm[:, 0:n], G.blk64, s[:, 0:n])
            rs = rst()
            rstd_from_ms(P, rs[:, 0:n], pm[:, 0:n], n)
            kn = knb()
            P.stt(kn[:, 0:n], pk[:, 0:n], kg[:, 0:1], rs[:, 0:n], ALU.mult, ALU.mult)
            if is_ctx:
                P.copy(G.kT[:, c0:c0 + n], kn[:, 0:n])
            else:
                rope_apply(P, G.kT[:, c0:c0 + n], kn[:, 0:n], G.perm64, cb[:, 0:n], sb_[:, 0:n], tmp()[:, 0:n], n)
            ss = [sq(), sq()]
            for j in range(2):
                P.act(ss[j][:, 0:n], pc[j][:, 0:n], AF.Square)
            pm = P.ps()
            for j in range(2):
                P.mm(pm[:, 0:n], G.ones, ss[j][:, 0:n], start=(j == 0), stop=(j == 1))
            rs = rst()
            P.ts(rs[:, 0:n], pm[:, 0:n], 1.0 / 256, EPS, op0=ALU.mult, op1=ALU.add)
            P.act(rs[:, 0:n], rs[:, 0:n], AF.Sqrt)
            P.recip(rs[:, 0:n], rs[:, 0:n])
            for j in range(2):
                P.stt(G.ckvT[:, j, c0:c0 + n], pc[j][:, 0:n], cg[:, j:j + 1], rs[:, 0:n], ALU.mult, ALU.mult)
            if is_ctx:
                P.copy(G.krT[:, c0:c0 + n], pr[0:32, 0:n])
            else:
                kr = knb()
                P.copy(kr[0:32, 0:n], pr[0:32, 0:n])
                rope_apply(P, G.krT[:, c0:c0 + n], kr[0:32, 0:n], G.perm32, c32[:, 0:n], s32[:, 0:n], tmp()[0:32, 0:n], n, rows=32)
            pvs = [P.ps() for _ in range(nt)]
            mmg(P, [(pvs[ti][:, 0:128], (lambda k, ti=ti: hT[:, k, ti * 128:(ti + 1) * 128]),
                     lambda k: w_st[:, k, OFF_AV:OFF_AV + 128]) for ti in range(nt)], 8)
            for ti in range(nt):
                pv = pvs[ti]
                P.copy(G.Vg[:, t0 + ti, :, 0:64], pv.v(pv.ap[:, 0:128].rearrange("p (a b) -> p a b", a=2)), eng="act")
            pus = [P.ps() for _ in range(2)]
            mmg(P, [(pus[j][:, 0:n], (lambda k, j=j: w_st[:, k, OFF_U + j * 128:OFF_U + (j + 1) * 128]),
                     lambda k: hT[:, k, 0:n]) for j in range(2)], 8)
            for j in range(2):
                dst = G.uT.v(G.uT.ap[:, j, :, c0 // 8:(c0 + n) // 8].rearrange("p i c -> p c i"))
                src = pus[j].v(pus[j].ap[:, 0:n].rearrange("p (c i) -> p c i", i=8))
                P.copy(dst, src, eng=("act" if j % 2 else "dve"))


def own_blocks(last):
    bl = [(i * 512, 512, False, i * 512) for i in range(4)]
    if not last:
        bl.append((LH, 256, True, 0))
    return bl


def stage_B1(P, I, G, last):
    NQ = LH + (0 if last else CTX)
    G.NQ = NQ
    G.qg = P.sb([128, 4, NQ], BF16, "qg")
    G.qm = P.sb([96, 8, NQ], BF16, "qm")
    G.gD = P.dram([3 * D, NQ], BF16, "gD")
    with P.scope():
        hT = P.sb([128, 8, NQ], BF16, "hTall")
        with P.scope():
            R = make_ln_pools(P, 4)
            tiles = [(c0 + ti * 128, r0 + ti * 128, is_ctx) for (c0, n, is_ctx, r0) in own_blocks(last) for ti in range(n // 128)]

            def p1(q):
                col, row, is_ctx = tiles[q]
                return ln_part1(P, G, R, (I["ctx"] if is_ctx else I["x_own"]).rows(row))
            xn_cur = p1(0)
            for q in range(len(tiles)):
                xn_nxt = p1(q + 1) if q + 1 < len(tiles) else None
                col, row, is_ctx = tiles[q]
                ln_part2(P, G, xn_cur, hT[:, :, col:col + 128], 1 if is_ctx else 0, 0, 1)
                xn_cur = xn_nxt
        qgain = P.sb([128, 1], F32, "qgain")
        for r in range(2):
            P.dma(qgain[r * 64:(r + 1) * 64, :], dT(I["a_q_gain"].ap.rearrange("(p o) -> p o", o=1), "aqg"))
        cqg = P.sb([128, 6], F32, "cqg")
        P.dma(cqg, dT(I["c_q_a_gain"].ap.rearrange("(j p) -> p j", p=128), "cqg"), allow_slow_non_contiguous=True)
        perm32h = P.sb([96, 32], BF16, "perm32h")
        P.dma(perm32h[64:96, :], I["perm32"], eng="pool")
        cosqR = Rot(P, [128, 512], F32, "cosq", 2)
        sinqR = Rot(P, [128, 512], F32, "sinq", 2)
        cos32R = Rot(P, [96, 512], F32, "cos32q", 2)
        sin32R = Rot(P, [96, 512], F32, "sin32q", 2)
        sq = Rot(P, [128, 512], BF16, "sq", 6)
        rst = Rot(P, [128, 512], F32, "rst", 2)
        knb = Rot(P, [128, 512], BF16, "knb", 2)
        tmp = Rot(P, [128, 512], F32, "tmp", 2)
        with P.scope():
            wq = P.sb([128, 8, 4, 128], BF16, "wq")
            stg = Rot(P, [128, 768], F32, "stg", 2)
            for kc in range(8):
                for hf in range(2):
                    src = I["w_in"].ap[kc * 128:(kc + 1) * 128, OFF_AQ + hf * 256:OFF_AQ + (hf + 1) * 256].rearrange("p (a b) -> p a b", a=4)
                    wload(P, stg, wq[:, kc, :, hf * 64:(hf + 1) * 64], dT(src, "w_in"))
            for (c0, n, is_ctx, r0) in own_blocks(last):
                if not is_ctx:
                    cosq = cosqR(); sinq = sinqR()
                    P.dma(cosq, dT(I["ropeq_cos"].ap[:, c0:c0 + n], "rqc"))
                    P.dma(sinq, dT(I["ropeq_sin"].ap[:, c0:c0 + n], "rqs"))
                pks = [P.ps() for _ in range(4)]
                mmg(P, [(pks[hd][:, 0:n], (lambda k, hd=hd: wq[:, k, hd, :]), lambda k: hT[:, k, c0:c0 + n]) for hd in range(4)], 8)
                for hd in range(4):
                    pk = pks[hd]
                    s = sq()
                    P.act(s[:, 0:n], pk[:, 0:n], AF.Square)
                    pm = P.ps_acc()
                    P.mm(pm[:, 0:n], G.blk64, s[:, 0:n])
                    rs = rst()
                    rstd_from_ms(P, rs[:, 0:n], pm[:, 0:n], n)
                    if is_ctx:
                        P.stt(G.qg[:, hd, c0:c0 + n], pk[:, 0:n], qgain[:, 0:1], rs[:, 0:n], ALU.mult, ALU.mult)
                    else:
                        kn = knb()
                        P.stt(kn[:, 0:n], pk[:, 0:n], qgain[:, 0:1], rs[:, 0:n], ALU.mult, ALU.mult)
                        rope_apply(P, G.qg[:, hd, c0:c0 + n], kn[:, 0:n], G.perm64, cosq[:, 0:n], sinq[:, 0:n],
                                   tmp()[:, 0:n], n, psfn=P.ps_acc)
        with P.scope():
            wc = P.sb([128, 8, 768], BF16, "wc")
            stg = Rot(P, [128, 768], F32, "stg", 1)
            for kc in range(8):
                wload(P, stg, wc[:, kc, :], dT(I["w_in"].ap[kc * 128:(kc + 1) * 128, OFF_CQ:OFF_CQ + 768], "w_in"))
            wqb = P.sb([128, 6, 768], BF16, "wqb")
            for j in range(6):
                wload(P, stg, wqb[:, j, :], dT(I["c_w_qb"].ap[j * 128:(j + 1) * 128, :], "wqb"))
            cqn = P.sb([128, 6, 512], BF16, "cqn")
            qrb = Rot(P, [96, 512], BF16, "qrb", 2)
            for (c0, n, is_ctx, r0) in own_blocks(last):
                if not is_ctx:
                    cos32 = cos32R(); sin32 = sin32R()
                    P.dma(cos32[64:96, :], dT(I["rope32q_cos"].ap[:, c0:c0 + n], "rqc32"))
                    P.dma(sin32[64:96, :], dT(I["rope32q_sin"].ap[:, c0:c0 + n], "rqs32"))
                pcs = [P.ps() for _ in range(6)]
                mmg(P, [(pcs[j][:, 0:n], (lambda k, j=j: wc[:, k, j * 128:(j + 1) * 128]), lambda k: hT[:, k, c0:c0 + n]) for j in range(6)], 8)
                sqs = []
                for j in range(6):
                    s = sq()
                    P.act(s[:, 0:n], pcs[j][:, 0:n], AF.Square)
                    sqs.append(s)
                pm = P.ps_acc()
                for j in range(6):
                    P.mm(pm[:, 0:n], G.ones, sqs[j][:, 0:n], start=(j == 0), stop=(j == 5))
                rs = rst()
                P.ts(rs[:, 0:n], pm[:, 0:n], 1.0 / 768, EPS, op0=ALU.mult, op1=ALU.add)
                P.act(rs[:, 0:n], rs[:, 0:n], AF.Sqrt)
                P.recip(rs[:, 0:n], rs[:, 0:n])
                for j in range(6):
                    P.stt(cqn[:, j, 0:n], pcs[j][:, 0:n], cqg[:, j:j + 1], rs[:, 0:n], ALU.mult, ALU.mult)
                for hg in range(2):
                    pqs = [P.ps() for _ in range(4)]
                    mmg(P, [(pqs[i][0:96, 0:n], (lambda k, h=hg * 4 + i: wqb[:, k, h * 96:(h + 1) * 96]), lambda k: cqn[:, k, 0:n]) for i in range(4)], 6)
                    for i in range(4):
                        h = hg * 4 + i
                        pq = pqs[i]
                        if is_ctx:
                            P.copy(G.qm[:, h, c0:c0 + n], pq[0:96, 0:n], eng="act")
                        else:
                            P.copy(G.qm[0:64, h, c0:c0 + n], pq[0:64, 0:n], eng="act")
                            qr = qrb()
                            P.copy(qr[64:96, 0:n], pq[64:96, 0:n])
                            pr = P.ps_acc()
                            P.mm(pr[64:96, 0:n], perm32h[64:96, :], qr[64:96, 0:n])
                            t = tmp()
                            P.tt(t[64:96, 0:n], qr[64:96, 0:n], cos32[64:96, 0:n], ALU.mult)
                            t2 = tmp()
                            P.tt(t2[64:96, 0:n], pr[64:96, 0:n], sin32[64:96, 0:n], ALU.mult)
                            P.tt(G.qm[64:96, h, c0:c0 + n], t[64:96, 0:n], t2[64:96, 0:n], ALU.add)
        with P.scope():
            wg = Rot(P, [128, 8, 512], BF16, "wg", 2)
            stg = Rot(P, [128, 512], F32, "stg", 3)
            gb = Rot(P, [128, 512], BF16, "gb", 8)
            def load_g(gi):
                w = wg()
                for kc in range(8):
                    wload(P, stg, w[:, kc, :], dT(I["w_in"].ap[kc * 128:(kc + 1) * 128, OFF_GATE + gi * 512:OFF_GATE + (gi + 1) * 512], "w_in"), engs=("dve", "act"))
                return w
            w_nxt = load_g(0)
            for gi in range(6):
                w = w_nxt
                w_nxt = load_g(gi + 1) if gi + 1 < 6 else None
                for (c0, n, is_ctx, r0) in own_blocks(last):
                    pgs = [P.ps() for _ in range(4)]
                    mmg(P, [(pgs[oc][:, 0:n], (lambda k, oc=oc: w[:, k, oc * 128:(oc + 1) * 128]), lambda k: hT[:, k, c0:c0 + n]) for oc in range(4)], 8)
                    for oc in range(4):
                        g = gb()
                        P.act(g[:, 0:n], pgs[oc][:, 0:n], AF.Sigmoid)
                        row = (gi * 4 + oc) * 128
                        P.dma(G.gD.v(G.gD.ap[row:row + 128, c0:c0 + n]), g[:, 0:n])


def run_attn(P, chains, pT, scale):
    LA = 2
    nkt = chains[0][1]
    pls = [dict() for _ in chains]
    for kt in range(nkt + LA):
        if kt < nkt:
            for ci, (po, _, n, slf, srhs, vlf) in enumerate(chains):
                pss = P.ps()
                P.mm(pss[:, 0:n], slf(kt), srhs)
                p = pT()
                P.act(p[:, 0:n], pss[:, 0:n], AF.Exp, scale=scale)
                pls[ci][kt] = p
        jj = kt - LA
        if jj >= 0:
            for ci, (po, _, n, slf, srhs, vlf) in enumerate(chains):
                P.mm(po[:, 0:n], vlf(jj), pls[ci].pop(jj)[:, 0:n], start=(jj == 0), stop=(jj == nkt - 1))


def block_groups(last):
    bl = own_blocks(last)
    groups = [bl[0:2], bl[2:4]]
    if not last:
        groups.append(bl[4:5])
    return groups


def attn_finish(P, po, n, rec, yo, dst):
    r = rec()
    P.recip(r[64:128, 0:n], po[64:128, 0:n])
    y = yo()
    P.tt(y[:, 0:n], po[0:64, 0:n], r[64:128, 0:n], ALU.mult)
    P.dma(dst, y[:, 0:n])


def stage_B2(P, I, G, last):
    NQ = G.NQ
    G.yaD = P.dram([512, NQ], BF16, "yaD")
    with P.scope():
        pT = Rot(P, [128, 512], BF16, "pT", 8)
        rec = Rot(P, [128, 512], F32, "rec", 2)
        yo = Rot(P, [64, 512], BF16, "yo", 2)
        kTp = P.sb([128, 2, NK], BF16, "kTp")
        P.memset(kTp, 0.0)
        P.copy(kTp[0:64, 0, :], G.kT[0:64, :], eng="pool")
        P.copy(kTp[64:128, 1, :], G.kT[64:128, :], eng="dve")
        for hd in range(4):
            for kvh in range(2):
                head = hd + 4 * kvh
                for grp in block_groups(last):
                    chains = []
                    for (c0, n, is_ctx, r0) in grp:
                        nkt = 2 if is_ctx else NKT
                        chains.append((P.ps_acc(), nkt, n, (lambda kt: kTp[:, kvh, kt * 128:(kt + 1) * 128]),
                                       G.qg[:, hd, c0:c0 + n], (lambda kt: G.Vg[:, kt, kvh, :])))
                    run_attn(P, chains, pT, 0.125)
                    for (po, _, n, _, _, _), (c0, _, _, _) in zip(chains, grp):
                        attn_finish(P, po, n, rec, yo, G.yaD.v(G.yaD.ap[head * 64:(head + 1) * 64, c0:c0 + n]))


def stage_B3(P, I, G, last):
    NQ = G.NQ
    G.ycD = P.dram([512, NQ], BF16, "ycD")
    with P.scope():
        wkv = P.sb([128, 2, 1024], BF16, "wkv")
        stg = Rot(P, [128, 1024], F32, "stg", 1)
        for j in range(2):
            wload(P, stg, wkv[:, j, :], dT(I["c_w_kvb"].ap[j * 128:(j + 1) * 128, :], "wkvb"))
        Kh = Rot(P, [96, NK], BF16, "Kh", 2)
        Vh = [P.sb([128, NKT, 128], BF16, f"Vh{i}") for i in range(2)]
        for v in Vh:
            P.memset(v[:, :, 64:128], 1.0)
        pT = Rot(P, [128, 512], BF16, "pT", 8)
        rec = Rot(P, [128, 512], F32, "rec", 2)
        yo = Rot(P, [64, 512], BF16, "yo", 2)
        scale = 96 ** -0.5
        def gen(h):
            K = Kh(); V = Vh[h % 2]
            cbs = [(cb * 512, min(512, NK - cb * 512)) for cb in range(9)]
            for g0 in range(0, 9, 3):
                grp = cbs[g0:g0 + 3]
                pks = [P.ps() for _ in grp]
                mmg(P, [(pks[i][0:64, 0:n], lambda k: wkv[:, k, h * 128:h * 128 + 64], (lambda k, k0=k0, n=n: G.ckvT[:, k, k0:k0 + n]))
                        for i, (k0, n) in enumerate(grp)], 2)
                for i, (k0, n) in enumerate(grp):
                    P.copy(K[0:64, k0:k0 + n], pks[i][0:64, 0:n], eng=("pool_never" if False else ("act" if i % 2 else "dve")))
            P.copy(K[64:96, :], G.krT[0:32, :], eng="pool")
            for g0 in range(0, NKT, 4):
                kts = list(range(g0, min(g0 + 4, NKT)))
                pvs = [P.ps() for _ in kts]
                mmg(P, [(pvs[i][:, 0:64], (lambda k, kt=kt: G.ckvT[:, k, kt * 128:(kt + 1) * 128]),
                         lambda k: wkv[:, k, h * 128 + 64:h * 128 + 128]) for i, kt in enumerate(kts)], 2)
                for i, kt in enumerate(kts):
                    P.copy(V[:, kt, 0:64], pvs[i][:, 0:64], eng="dve")
            return K, V

        def attend(h, K, V):
            for grp in block_groups(last):
                chains = []
                for (c0, n, is_ctx, r0) in grp:
                    nkt = 2 if is_ctx else NKT
                    chains.append((P.ps_acc(), nkt, n, (lambda kt: K[0:96, kt * 128:(kt + 1) * 128]),
                                   G.qm[0:96, h, c0:c0 + n], (lambda kt: V[:, kt, :])))
                run_attn(P, chains, pT, scale)
                for (po, _, n, _, _, _), (c0, _, _, _) in zip(chains, grp):
                    attn_finish(P, po, n, rec, yo, G.ycD.v(G.ycD.ap[h * 64:(h + 1) * 64, c0:c0 + n]))
        kv_cur = gen(0)
        for h in range(8):
            kv_nxt = gen(h + 1) if h + 1 < 8 else None
            attend(h, *kv_cur)
            kv_cur = kv_nxt


def bcast_load(P, dst, src_ap_1d):
    P.dma(dst, dT(src_ap_1d.partition_broadcast(128), "bc"))


def stage_B4a(P, I, G, last):
    NQ = G.NQ
    G.xmidD = P.dram([NQ, D], F32, "xmidD")
    G.h2D = P.dram([D, NQ], BF16, "h2D")
    with P.scope():
        wglu = P.sb([128, 4, 1024], BF16, "wglu")
        wb = [P.sb([128, 4, 1024], BF16, f"wb{i}") for i in range(3)]
        wout = P.sb([128, 8, 1024], BF16, "wout")
        with P.scope():
            stg = Rot(P, [128, 1024], F32, "stg", 3)
            for j in range(4):
                wload(P, stg, wglu[:, j, :], dT(I["s5_w_glu"].ap[j * 128:(j + 1) * 128, :], "w"))
                for i, nm in enumerate(["w_branch_a", "w_branch_s5", "w_branch_c"]):
                    wload(P, stg, wb[i][:, j, :], dT(I[nm].ap[j * 128:(j + 1) * 128, :], "w"))
            for kc in range(8):
                wload(P, stg, wout[:, kc, :], dT(I["w_out"].ap[kc * 128:(kc + 1) * 128, :], "w"))
        g1b = [P.sb([128, D], F32, f"g1b{t}") for t in range(2)]
        for t in range(2):
            P.dma(g1b[t], dT(G.modD.ap[t, 2 * D:3 * D].partition_broadcast(128), "modD_r"))
        lng = P.sb([128, D], F32, "lng"); bcast_load(P, lng, I["ln1_g"].ap)
        lnb = P.sb([128, D], F32, "lnb"); bcast_load(P, lnb, I["ln1_b"].ap)
        class _InR:
            def __init__(self):
                gts = P.sb([128, 24, 512], BF16, "gates")
                self.sets = [(P.sb([128, 4, 512], BF16, f"yT{i}"), P.sb([128, 4, 512], BF16, f"sa{i}"),
                              P.sb([128, 4, 512], BF16, f"sc{i}"), gts) for i in range(2)]
                self.i = 0

            def __call__(self):
                r = self.sets[self.i % 2]
                self.i += 1
                return r
        inR = _InR()
        srcs = [None, P.sb([128, 4, 512], BF16, "src1"), None]
        t1 = P.sb([128, 4, 512], F32, "t1")
        sg = Rot(P, [128, 512], F32, "sg", 2)
        acc = Rot(P, [128, 512], F32, "acc", 2)
        tmpm = Rot(P, [128, 512], F32, "tmpm", 2)
        merged = P.sb([128, 8, 512], BF16, "merged")
        xtR = Rot(P, [128, D], F32, "xt", 2)
        tsR = Rot(P, [128, D], F32, "tsum", 3)
        xmR = Rot(P, [128, D], F32, "xm", 3)
        stR = Rot(P, [128, 2, 6], F32, "bnst", 6); mvR = Rot(P, [128, 2], F32, "mv", 6); rsR = Rot(P, [128, 1], F32, "rs", 6)
        xnR = Rot(P, [128, D], BF16, "xn", 2)
        h2T = P.sb([128, 8, 512], BF16, "h2T")
        def load_blk(blk):
            (c0, n, is_ctx, r0) = blk
            bufs = inR()
            yT_, sa_, sc_, gates_ = bufs
            for j in range(4):
                P.dma(yT_[:, j, 0:n], G.ysD.v(G.ysD.ap[j * 128:(j + 1) * 128, c0:c0 + n]))
                P.dma(sa_[:, j, 0:n], G.yaD.v(G.yaD.ap[j * 128:(j + 1) * 128, c0:c0 + n]))
                P.dma(sc_[:, j, 0:n], G.ycD.v(G.ycD.ap[j * 128:(j + 1) * 128, c0:c0 + n]))
            return bufs
        blks_ = own_blocks(last)
        nxt_bufs = load_blk(blks_[0])
        for bi_, (c0, n, is_ctx, r0) in enumerate(blks_):
            tix = 1 if is_ctx else 0
            yT, srcs[0], srcs[2], gates = nxt_bufs
            for gi in range(24):
                P.dma(gates[:, gi, 0:n], G.gD.v(G.gD.ap[gi * 128:(gi + 1) * 128, c0:c0 + n]))
            nxt_bufs = load_blk(blks_[bi_ + 1]) if bi_ + 1 < len(blks_) else None
            P.tt(t1[:, :, 0:n], yT[:, :, 0:n], yT[:, :, 0:n], ALU.mult)
            P.ts(t1[:, :, 0:n], t1[:, :, 0:n], 0.044715, 1.0, op0=ALU.mult, op1=ALU.add)
            P.tt(t1[:, :, 0:n], t1[:, :, 0:n], yT[:, :, 0:n], ALU.mult)
            P.act(t1[:, :, 0:n], t1[:, :, 0:n], AF.Sigmoid, scale=1.5957691216)
            ge = P.sb([128, 4, 512], BF16, "ge") if c0 == 0 else ge
            P.tt(ge[:, :, 0:n], t1[:, :, 0:n], yT[:, :, 0:n], ALU.mult)
            for op_ in range(2):
                pa = [P.ps(), P.ps()]; pg = [P.ps(), P.ps()]
                items = []
                for q in range(2):
                    oc = op_ * 2 + q
                    items.append((pa[q][:, 0:n], (lambda k, oc=oc: wglu[:, k, oc * 128:(oc + 1) * 128]), lambda k: ge[:, k, 0:n]))
                    items.append((pg[q][:, 0:n], (lambda k, oc=oc: wglu[:, k, 512 + oc * 128:512 + (oc + 1) * 128]), lambda k: ge[:, k, 0:n]))
                mmg(P, items, 4)
                for q in range(2):
                    oc = op_ * 2 + q
                    s = sg()
                    P.act(s[:, 0:n], pg[q][:, 0:n], AF.Sigmoid)
                    P.tt(srcs[1][:, oc, 0:n], pa[q][:, 0:n], s[:, 0:n], ALU.mult)
            for oc in range(8):
                a = acc()
                pbs = [P.ps() for _ in range(3)]
                mmg(P, [(pbs[br][:, 0:n], (lambda k, br=br: wb[br][:, k, oc * 128:(oc + 1) * 128]), (lambda k, br=br: srcs[br][:, k, 0:n])) for br in range(3)], 4)
                for br in range(3):
                    pb = pbs[br]
                    if br == 0:
                        P.tt(a[:, 0:n], pb[:, 0:n], gates[:, br * 8 + oc, 0:n], ALU.mult)
                    else:
                        tm = tmpm()
                        P.tt(tm[:, 0:n], pb[:, 0:n], gates[:, br * 8 + oc, 0:n], ALU.mult)
                        if br == 1:
                            P.tt(a[:, 0:n], a[:, 0:n], tm[:, 0:n], ALU.add, eng="pool")
                        else:
                            P.tt(merged[:, oc, 0:n], a[:, 0:n], tm[:, 0:n], ALU.add, eng="pool")
            def mix(ti):
                xt = xtR(); ts_ = tsR()
                P.dma(xt, (I["ctx"] if is_ctx else I["x_own"]).rows(r0 + ti * 128))
                pms = [P.ps(), P.ps()]
                mmg(P, [(pms[half][:, 0:512], lambda k: merged[:, k, ti * 128:(ti + 1) * 128],
                         (lambda k, half=half: wout[:, k, half * 512:(half + 1) * 512])) for half in range(2)], 8)
                for half in range(2):
                    P.tt(ts_[:, half * 512:(half + 1) * 512], pms[half][:, 0:512], g1b[tix][:, half * 512:(half + 1) * 512], ALU.mult)
                P.stt(ts_, xt, ALPHA, ts_, ALU.mult, ALU.add)
                return ts_

            def rs_pre(ms_t):
                rs = rsR()
                P.ts(rs, ms_t, EPS, None, op0=ALU.add)
                P.act(rs, rs, AF.Sqrt)
                return rs

            def c1a(ti, ts_):
                st = stR(); mv = mvR()
                for hh in range(2):
                    P.bn_stats(st[:, hh, :], ts_[:, hh * 512:(hh + 1) * 512])
                P.bn_aggr(mv, st)
                return dict(ts=ts_, mv=mv, rs=rs_pre(mv[:, 1:2]))

            def c1b(ti, stt_):
                xm = xmR()
                P.recip(stt_["rs"], stt_["rs"])
                P.stt(xm, stt_["ts"], stt_["mv"][:, 0:1], lng, ALU.subtract, ALU.mult)
                P.stt(xm, xm, stt_["rs"], lnb, ALU.mult, ALU.add)
                P.dma(G.xmidD.v(G.xmidD.ap[c0 + ti * 128:c0 + (ti + 1) * 128, :]), xm)
                return xm

            def c2a(ti, xm):
                st = stR(); mv = mvR()
                for hh in range(2):
                    P.bn_stats(st[:, hh, :], xm[:, hh * 512:(hh + 1) * 512])
                P.bn_aggr(mv, st)
                return dict(xm=xm, mv=mv, rs=rs_pre(mv[:, 1:2]))

            def c2b(ti, stt_):
                xn = xnR()
                P.recip(stt_["rs"], stt_["rs"])
                P.ts(xn, stt_["xm"], stt_["mv"][:, 0:1], stt_["rs"], op0=ALU.subtract, op1=ALU.mult)
                return xn

            def xpose(ti, xn):
                ps = P.ps()
                psb = ps.v(ps.ap.bitcast(BF16))
                for kc in range(8):
                    P.transpose(psb[:, kc * 128:(kc + 1) * 128], xn[:, kc * 128:(kc + 1) * 128], G.ident)
                for kc in range(8):
                    P.act(h2T[:, kc, ti * 128:(ti + 1) * 128], psb[:, kc * 128:(kc + 1) * 128], AF.Identity,
                          bias=G.modT[:, 3 * 8 + kc, tix:tix + 1], scale=G.modT[:, 4 * 8 + kc, tix:tix + 1])
            nti = n // 128
            tsq = {0: mix(0)}
            s1 = {0: c1a(0, tsq[0])}
            if nti > 1:
                tsq[1] = mix(1)
            xm0 = c1b(0, s1[0])
            s2 = {0: c2a(0, xm0)}
            for ti in range(nti):
                if ti + 1 < nti:
                    s1[ti + 1] = c1a(ti + 1, tsq[ti + 1])
                xn = c2b(ti, s2[ti])
                if ti + 1 < nti:
                    xm_n = c1b(ti + 1, s1[ti + 1])
                    s2[ti + 1] = c2a(ti + 1, xm_n)
                if ti + 2 < nti:
                    tsq[ti + 2] = mix(ti + 2)
                xpose(ti, xn)
            for kc in range(8):
                P.dma(G.h2D.v(G.h2D.ap[kc * 128:(kc + 1) * 128, c0:c0 + n]), h2T[:, kc, 0:n])


def stage_B4b(P, I, G, last, out_own, out_ctx):
    NQ = G.NQ
    NT = NQ // 128
    with P.scope():
        g2b = [P.sb([128, D], F32, f"g2b{t}") for t in range(2)]
        for t in range(2):
            P.dma(g2b[t], dT(G.modD.ap[t, 5 * D:6 * D].partition_broadcast(128), "modD_r"))
        lng = P.sb([128, D], F32, "lng"); bcast_load(P, lng, I["ln2_g"].ap)
        lnb = P.sb([128, D], F32, "lnb"); bcast_load(P, lnb, I["ln2_b"].ap)
        h2T = P.sb([128, 8, NQ], BF16, "h2Tall")
        for kc in range(8):
            P.dma(h2T[:, kc, :], G.h2D.v(G.h2D.ap[kc * 128:(kc + 1) * 128, :]))
        tsum = P.sb([128, NT, D], F32, "tsum2")
        wuR = Rot(P, [128, 8, 512], BF16, "wu", 2)
        wdR = Rot(P, [128, 4, D], BF16, "wd", 2)
        stg = Rot(P, [128, 1024], F32, "stg", 3)
        aR = Rot(P, [128, 4, 512], BF16, "aog", 3)
        rl = Rot(P, [128, 512], BF16, "rl", 4)
        xmR = Rot(P, [128, D], F32, "xm", 3)
        stR = Rot(P, [128, 2, 6], F32, "bnst", 3); mvR = Rot(P, [128, 2], F32, "mv", 3); rsR = Rot(P, [128, 1], F32, "rs", 3)
        def ep_a(tile, c0, ti, tix):
            xm = xmR()
            P.dma(xm, G.xmidD.v(G.xmidD.ap[c0 + ti * 128:c0 + (ti + 1) * 128, :]))
            P.tt(tsum[:, tile, :], tsum[:, tile, :], g2b[tix], ALU.mult, eng="pool")
            P.stt(tsum[:, tile, :], xm, ALPHA, tsum[:, tile, :], ALU.mult, ALU.add)
            st = stR(); mv = mvR(); rs = rsR()
            for hh in range(2):
                P.bn_stats(st[:, hh, :], tsum[:, tile, hh * 512:(hh + 1) * 512])
            P.bn_aggr(mv, st)
            rstd_from_ms(P, rs, mv[:, 1:2], 1)
            return (xm, mv, rs)

        def ep_b(tile, r0, ti, is_ctx, stt_):
            xm, mv, rs = stt_
            o = xm
            P.stt(o, tsum[:, tile, :], mv[:, 0:1], lng, ALU.subtract, ALU.mult)
            P.stt(o, o, rs, lnb, ALU.mult, ALU.add)
            dst = out_ctx if is_ctx else out_own
            P.dma(dst.rows(r0 + ti * 128), o)

        def epilogue(blk):
            (c0, n, is_ctx, r0) = blk
            tix = 1 if is_ctx else 0
            nti = n // 128
            cur = ep_a(c0 // 128, c0, 0, tix)
            for ti in range(nti):
                nxt = ep_a(c0 // 128 + ti + 1, c0, ti + 1, tix) if ti + 1 < nti else None
                ep_b(c0 // 128 + ti, r0, ti, is_ctx, cur)
                cur = nxt

        for og in range(8):
            wu = wuR(); wd = wdR()
            for kc in range(8):
                wload(P, stg, wu[:, kc, :], dT(I["w_up"].ap[kc * 128:(kc + 1) * 128, og * 512:(og + 1) * 512], "w"), engs=("pool", "act"))
            for oc in range(4):
                wload(P, stg, wd[:, oc, :], dT(I["w_down"].ap[og * 512 + oc * 128:og * 512 + (oc + 1) * 128, :], "w"), engs=("pool", "act"))
            blks = own_blocks(last)

            def up(blk):
                (c0, n, is_ctx, r0) = blk
                a = aR()
                pus = [P.ps() for _ in range(4)]
                mmg(P, [(pus[oc][:, 0:n], (lambda k, oc=oc: wu[:, k, oc * 128:(oc + 1) * 128]), lambda k: h2T[:, k, c0:c0 + n]) for oc in range(4)], 8)
                for oc in range(4):
                    r = rl()
                    P.act(r[:, 0:n], pus[oc][:, 0:n], AF.Relu)
                    P.tt(a[:, oc, 0:n], pus[oc][:, 0:n], r[:, 0:n], ALU.mult)
                return a

            def down(blk, a):
                (c0, n, is_ctx, r0) = blk
                combos = [(ti, half) for ti in range(n // 128) for half in range(2)]
                for g0 in range(0, len(combos), 4):
                    grp = combos[g0:g0 + 4]
                    pds = [P.ps() for _ in grp]
                    mmg(P, [(pds[i][:, 0:512], (lambda k, ti=ti: a[:, k, ti * 128:(ti + 1) * 128]),
                             (lambda k, half=half: wd[:, k, half * 512:(half + 1) * 512])) for i, (ti, half) in enumerate(grp)], 4)
                    for i, (ti, half) in enumerate(grp):
                        tile = c0 // 128 + ti
                        dst = tsum[:, tile, half * 512:(half + 1) * 512]
                        if og == 0:
                            P.copy(dst, pds[i][:, 0:512], eng="act")
                        else:
                            P.tt(dst, pds[i][:, 0:512], dst, ALU.add)
            a_cur = up(blks[0])
            for bi, blk in enumerate(blks):
                a_nxt = up(blks[bi + 1]) if bi + 1 < len(blks) else None
                down(blk, a_cur)
                a_cur = a_nxt
                if og == 7:
                    epilogue(blk)


def bc(t, pattern):
    a = t.ap
    return t.v(bass.AP(a.tensor, a.offset, [list(a.ap[0])] + [list(p) for p in pattern]))


MAGIC = 12582912.0
TWO_PI = 6.283185307179586


def s5_load(P, I, G):
    Lt = Ctx()
    Lt.are = P.sb([128, NGD], F32, "are"); Lt.aim = P.sb([128, NGD], F32, "aim"); Lt.ldt = P.sb([128, NGD], F32, "ldt")
    Lt.Bre = P.sb([128, NGD, 16], F32, "Bre"); Lt.Bim = P.sb([128, NGD, 16], F32, "Bim")
    Lt.Cre = P.sb([128, NGD, 16], F32, "Cre"); Lt.Cim = P.sb([128, NGD, 16], F32, "Cim")
    for hf in range(2):
        sl = slice(hf * 64, (hf + 1) * 64)
        P.dma(Lt.are[sl, :], dT(I["s5_a_re"].ap.rearrange("d g p -> p (d g)"), "a"), allow_slow_non_contiguous=True, eng="act")
        P.dma(Lt.aim[sl, :], dT(I["s5_a_im"].ap.rearrange("d g p -> p (d g)"), "a"), allow_slow_non_contiguous=True, eng="act")
        P.dma(Lt.Bre[sl], dT(I["s5_b_re"].ap.rearrange("d g p c -> p (d g) c"), "a"))
        P.dma(Lt.Bim[sl], dT(I["s5_b_im"].ap.rearrange("d g p c -> p (d g) c"), "a"))
        P.dma(Lt.Cre[sl], dT(I["s5_c_re"].ap.rearrange("d g c p -> p (d g) c"), "a"), allow_slow_non_contiguous=True, eng="act")
        P.dma(Lt.Cim[sl], dT(I["s5_c_im"].ap.rearrange("d g c p -> p (d g) c"), "a"), allow_slow_non_contiguous=True, eng="act")
    P.dma(Lt.ldt, dT(I["s5_log_dt"].ap.rearrange("d g -> (d g)").partition_broadcast(128), "a"))
    G.s5l = Lt


def s5_alloc(P, I, G):
    PR = P.sb([128, 32, NGD], F32, "PR"); NPI = P.sb([128, 32, NGD], F32, "NPI")
    DA = P.sb([128, 10, NGD], F32, "DA"); DB = P.sb([128, 10, NGD], F32, "DB")
    BX1 = P.sb([128, NGD, 16], F32, "BX1"); BX2 = P.sb([128, NGD, 16], F32, "BX2")
    CX1 = P.sb([128, NGD, 16], F32, "CX1"); CX2 = P.sb([128, NGD, 16], F32, "CX2")
    Dcol = P.sb([128, NGH], F32, "Dcol")
    identF = G.ident
    swapF = P.sb([128, 128], BF16, "swapF"); P.dma(swapF, I["swap"], eng="pool")
    maskf = P.sb([128, 128], BF16, "maskf"); P.dma(maskf, I["mask_f"], eng="pool")
    maskb = P.sb([128, 128], BF16, "maskb"); P.dma(maskb, I["mask_b"], eng="pool")
    sgn = P.sb([128, 1], F32, "sgn"); P.memset(sgn[0:64, :], 1.0); P.memset(sgn[64:128, :], -1.0)
    for i in range(8):
        P.dma(Dcol[i * 16:(i + 1) * 16, :], dT(I["s5_d"].ap.rearrange("(g c) -> c g", c=16), "s5d"), allow_slow_non_contiguous=True)
    G.s5t = dict(PR=PR, NPI=NPI, DA=DA, DB=DB, BX1=BX1, BX2=BX2, CX1=CX1, CX2=CX2, Dcol=Dcol, identF=identF, swapF=swapF, maskf=maskf, maskb=maskb, sgn=sgn)


def s5_params(P, I, G):
    PR = G.s5t["PR"]
    NPI = G.s5t["NPI"]
    DA = G.s5t["DA"]
    DB = G.s5t["DB"]
    BX1 = G.s5t["BX1"]
    BX2 = G.s5t["BX2"]
    CX1 = G.s5t["CX1"]
    CX2 = G.s5t["CX2"]
    Dcol = G.s5t["Dcol"]
    identF = G.s5t["identF"]
    swapF = G.s5t["swapF"]
    maskf = G.s5t["maskf"]
    maskb = G.s5t["maskb"]
    sgn = G.s5t["sgn"]
    with P.scope():
        are, aim, ldt = G.s5l.are, G.s5l.aim, G.s5l.ldt
        dt_ = P.sb([128, NGD], F32, "dt")
        P.act(dt_, ldt, AF.Exp)
        lr = P.sb([128, NGD], F32, "lr"); li = P.sb([128, NGD], F32, "li")
        P.tt(lr, are, dt_, ALU.mult); P.tt(li, aim, dt_, ALU.mult)
        with P.scope():
            elist = [t - 7 for t in range(16)] + [8 - t for t in range(16)]
            LR = P.sb([128, 32, NGD], F32, "LR"); LI = P.sb([128, 32, NGD], F32, "LI")
            for idx, e in enumerate(elist):
                P.ts(LR[:, idx, :], lr, float(e), None, op0=ALU.mult)
                P.ts(LI[:, idx, :], li, float(e), None, op0=ALU.mult, eng="pool")
            mag = P.sb([128, 32, NGD], F32, "mag")
            P.act(mag, LR, AF.Exp)
            rr = P.sb([128, 32, NGD], F32, "rr"); kk = P.sb([128, 32, NGD], F32, "kk")

            def sin_of(dst, ang_t, shift):
                P.ts(rr, ang_t, 1.0 / TWO_PI, shift / TWO_PI, op0=ALU.mult, op1=ALU.add)
                P.ts(kk, rr, MAGIC, None, op0=ALU.add)
                P.ts(kk, kk, MAGIC, None, op0=ALU.subtract)
                P.tt(rr, rr, kk, ALU.subtract)
                P.ts(rr, rr, TWO_PI, None, op0=ALU.mult)
                P.ts(rr, rr, 3.1415925, -3.1415925, op0=ALU.min, op1=ALU.max)
                P.act(dst, rr, AF.Sin)
            sn = LR
            sin_of(sn, LI, 0.0)
            P.stt(NPI, mag, -1.0, sn, ALU.mult, ALU.mult)
            sin_of(sn, LI, TWO_PI / 4)
            P.tt(PR, mag, sn, ALU.mult)
        cr_ = P.sb([128, NGD], F32, "cr"); ci_ = P.sb([128, NGD], F32, "ci")
        t1 = P.sb([128, NGD], F32, "t1"); t2 = P.sb([128, NGD], F32, "t2")
        P.copy(DA[:, 0, :], PR[:, 15, :])
        P.ts(DB[:, 0, :], NPI[:, 15, :], -1.0, None, op0=ALU.mult)
        for m in range(1, 10):
            P.tt(t1, DA[:, m - 1, :], DA[:, m - 1, :], ALU.mult)
            P.tt(t2, DB[:, m - 1, :], DB[:, m - 1, :], ALU.mult)
            P.tt(DA[:, m, :], t1, t2, ALU.subtract)
            P.stt(DB[:, m, :], DA[:, m - 1, :], 2.0, DB[:, m - 1, :], ALU.mult, ALU.mult)
        P.ts(DB, DB, sgn[:, 0:1], None, op0=ALU.mult)
        den = P.sb([128, NGD], F32, "den"); nr = P.sb([128, NGD], F32, "nr"); abi = P.sb([128, NGD], F32, "abi")
        P.tt(t1, are, are, ALU.mult); P.tt(t2, aim, aim, ALU.mult); P.tt(den, t1, t2, ALU.add); P.recip(den, den)
        P.ts(nr, PR[:, 8, :], -1.0, None, op0=ALU.add)
        P.ts(abi, NPI[:, 8, :], -1.0, None, op0=ALU.mult)
        P.tt(t1, nr, are, ALU.mult); P.tt(t2, abi, aim, ALU.mult); P.tt(cr_, t1, t2, ALU.add); P.tt(cr_, cr_, den, ALU.mult)
        P.tt(t1, abi, are, ALU.mult); P.tt(t2, nr, aim, ALU.mult); P.tt(ci_, t1, t2, ALU.subtract); P.tt(ci_, ci_, den, ALU.mult)
        crb = bc(cr_, [[1, NGD], [0, 16]]); cib = bc(ci_, [[1, NGD], [0, 16]])
        Bre, Bim, Cre, Cim = G.s5l.Bre, G.s5l.Bim, G.s5l.Cre, G.s5l.Cim
        bbr = P.sb([128, NGD, 16], F32, "bbr"); bbi = P.sb([128, NGD, 16], F32, "bbi"); t3 = P.sb([128, NGD, 16], F32, "t3")
        P.tt(bbr, Bre, crb, ALU.mult); P.tt(t3, Bim, cib, ALU.mult); P.tt(bbr, bbr, t3, ALU.subtract)
        P.tt(bbi, Bim, crb, ALU.mult); P.tt(t3, Bre, cib, ALU.mult); P.tt(bbi, bbi, t3, ALU.add)
        P.copy(BX1[0:64], bbr[0:64]); P.copy(BX1[64:128], bbi[64:128])
        P.copy(BX2[0:64], bbi[0:64]); P.ts(BX2[64:128], bbr[64:128], -1.0, None, op0=ALU.mult)
        P.copy(CX1[0:64], Cre[0:64]); P.ts(CX1[64:128], Cim[64:128], -1.0, None, op0=ALU.mult)
        P.copy(CX2[0:64], Cim[0:64]); P.copy(CX2[64:128], Cre[64:128])


def stage_S5(P, I, G, last):
    NQ = LH + (0 if last else CTX)
    G.ysD = P.dram([512, NQ], BF16, "ysD")
    with P.scope():
        PR = G.s5t["PR"]
        NPI = G.s5t["NPI"]
        DA = G.s5t["DA"]
        DB = G.s5t["DB"]
        BX1 = G.s5t["BX1"]
        BX2 = G.s5t["BX2"]
        CX1 = G.s5t["CX1"]
        CX2 = G.s5t["CX2"]
        Dcol = G.s5t["Dcol"]
        identF = G.s5t["identF"]
        swapF = G.s5t["swapF"]
        maskf = G.s5t["maskf"]
        maskb = G.s5t["maskb"]
        sgn = G.s5t["sgn"]
        Sel = P.sb([128, 64, 128], BF16, "Sel"); SelT = P.sb([128, 64, 128], BF16, "SelT")
        for q in range(4):
            P.dma(Sel[:, q * 16:(q + 1) * 16, :], dT(I["sel8"].ap[:, q * 16:(q + 1) * 16, :], "sel8"), eng="pool")
            P.dma(SelT[:, q * 16:(q + 1) * 16, :], dT(I["sel8T"].ap[:, q * 16:(q + 1) * 16, :], "sel8T"), eng="pool")
        KQ = {nm: Rot(P, [128, 16, 16], BF16, nm, 2) for nm in ["Kf", "Qf", "Kb", "Qb"]}
        tA = Rot(P, [128, 16, 16], F32, "tA", 1); tB = Rot(P, [128, 16, 16], F32, "tB", 1)
        tC = Rot(P, [128, 16, 16], F32, "tC", 1); tD = Rot(P, [128, 16, 16], F32, "tD", 1)
        SgR = Rot(P, [128, 128], BF16, "Sg", 2)
        WeR = Rot(P, [128, 128], BF16, "We", 4)
        s1R = Rot(P, [128, 128], F32, "s1", 1); s2R = Rot(P, [128, 128], F32, "s2", 1)
        UcR = Rot(P, [128, 576], BF16, "Uc", 2)
        XR = {d: [P.sb([128, 545], BF16, f"X{d}{i}") for i in range(2)] for d in "fb"}
        for d in "fb":
            for x in XR[d]:
                P.memset(x, 0.0)
        MdR = {d: Rot(P, [128, 10, 128], BF16, "Md" + d, 2) for d in "fb"}
        mA = Rot(P, [128, 10, 128], BF16, "mA", 1); mB = Rot(P, [128, 10, 128], BF16, "mB", 1)
        Yt = P.sb([128, 8, 544], BF16, "Yt")
        ybufR = Rot(P, [128, L + CTX], BF16, "ybuf", 2)
        yown = Rot(P, [128, LH], BF16, "yown", 1)
        ysend = [P.dram([128, L + CTX], BF16, f"ysend{j}") for j in range(2)]
        ygath = [P.dram([256, L + CTX], BF16, f"ygath{j}") for j in range(2)]
        identB = G.ident
        flip = 0
        spec = {"Kf": (17, 15), "Qf": (7, 9), "Kb": (7, 8), "Qb": (16, 16)}

        def m128(t, lo):
            return t.v(t.ap[:, lo:lo + 8, :].rearrange("p a b -> p (a b)"))

        def build(g):
            stt_ = {"mats": {}, "We": {}, "Mall": {}}
            th = []
            for dname, gd in (("f", g), ("b", NGH + g)):
                for kind, X1, X2, eng in (("K", BX1, BX2, "dve"), ("Q", CX1, CX2, "pool")):
                    t0_, ne = spec[kind + dname]
                    out = KQ[kind + dname]()[:, 0:ne, :]
                    a_ = (tA() if kind == "K" else tC())[:, 0:ne, :]
                    b_ = (tB() if kind == "K" else tD())[:, 0:ne, :]
                    prb = bc(PR[:, t0_, gd:gd + 1], [[NGD, ne], [0, 16]]); npb = bc(NPI[:, t0_, gd:gd + 1], [[NGD, ne], [0, 16]])
                    x1b = bc(X1[:, gd, :], [[0, ne], [1, 16]]); x2b = bc(X2[:, gd, :], [[0, ne], [1, 16]])
                    th.append(lambda a_=a_, prb=prb, x1b=x1b, eng=eng: P.tt(a_, prb, x1b, ALU.mult, eng=eng))
                    th.append(lambda b_=b_, npb=npb, x2b=x2b, eng=eng: P.tt(b_, npb, x2b, ALU.mult, eng=eng))
                    th.append(lambda out=out, a_=a_, b_=b_, eng=eng: P.tt(out, a_, b_, ALU.add, eng=eng))
                    stt_["mats"][kind + dname] = out
            Sg = SgR()
            stt_["Sg"] = Sg
            mt = stt_["mats"]

            def th_S():
                psf = P.ps(); psb_ = P.ps()
                P.mm(psf[:, 0:128], m128(mt["Kf"], 7), m128(mt["Qf"], 0))
                P.mm(psb_[:, 0:128], m128(mt["Kb"], 0), m128(mt["Qb"], 8))
                s1 = s1R(); s2 = s2R()
                P.tt(s1, psf[:, 0:128], maskf, ALU.mult)
                P.tt(s2, psb_[:, 0:128], maskb, ALU.mult)
                P.tt(s1, s1, s2, ALU.add)
                P.stt(Sg, identF, Dcol[:, g:g + 1], s1, ALU.mult, ALU.add)
            th.append(th_S)
            for dname, key in (("f", "Kf"), ("b", "Kb")):
                w = WeR()
                stt_["We"][dname] = w

                def th_W(w=w, key=key):
                    pt = P.ps()
                    ptb = pt.v(pt.ap.bitcast(BF16))
                    P.transpose(ptb[:, 0:128], m128(mt[key], 0), identB)
                    P.copy(w, ptb[:, 0:128], eng="act")
                th.append(th_W)
            stt_["Wo"] = {"f": m128(mt["Qf"], 1), "b": m128(mt["Qb"], 0)}
            for dname, gd in (("f", g), ("b", NGH + g)):
                Mall = MdR[dname]()
                stt_["Mall"][dname] = Mall
                ta = mA(); tb = mB()
                idb = bc(identF, [[0, 10], [1, 128]]); swb = bc(swapF, [[0, 10], [1, 128]])
                dab = bc(DA[:, :, gd], [[NGD, 10], [0, 128]]); dbb = bc(DB[:, :, gd], [[NGD, 10], [0, 128]])
                th.append(lambda ta=ta, idb=idb, dab=dab: P.tt(ta, idb, dab, ALU.mult, eng="pool"))
                th.append(lambda tb=tb, swb=swb, dbb=dbb: P.tt(tb, swb, dbb, ALU.mult, eng="pool"))
                th.append(lambda Mall=Mall, ta=ta, tb=tb: P.tt(Mall, ta, tb, ALU.add, eng="pool"))
            return stt_, th

        cur_state, th0 = build(0)
        for t_ in th0:
            t_()
        for g in range(NGH):
            j, gl = g // 8, g % 8
            if g + 1 < NGH:
                nxt_state, pend = build(g + 1)
            else:
                nxt_state, pend = None, []
            per_step = (len(pend) + 9) // 10
            Sg = cur_state["Sg"]; We = cur_state["We"]; Wo = cur_state["Wo"]
            Uc = UcR()
            puA = P.ps(); puB = P.ps(); pu2 = P.ps()
            for i in range(8):
                P.mm(puA[:, 0:256], Sel[:, gl * 8 + i, :], G.uT[:, j, i, 32:288], start=(i == 0), stop=(i == 7))
                P.mm(puB[:, 0:256], Sel[:, gl * 8 + i, :], G.uT[:, j, i, 288:544], start=(i == 0), stop=(i == 7))
                P.mm(pu2[:, 0:32], Sel[:, gl * 8 + i, :], G.uT[:, j, i, 0:32], start=(i == 0), stop=(i == 7))
            P.copy(Uc[:, 32:288], puA[:, 0:256], eng="act")
            P.copy(Uc[:, 288:544], puB[:, 0:256])
            P.copy(Uc[:, 0:32], pu2[:, 0:32], eng="act")
            P.copy(Uc[:, 544:576], pu2[:, 0:32])
            st_ = {}
            for dname, gd in (("f", g), ("b", NGH + g)):
                ucoff = 0 if dname == "f" else 32
                xoff = 1 if dname == "f" else 0
                cur = XR[dname][0]; nxt = XR[dname][1]
                pa = P.ps(); pb2 = P.ps()
                P.mm(pa[:, 0:512], We[dname], Uc[:, ucoff:ucoff + 512])
                P.mm(pb2[:, 0:32], We[dname], Uc[:, ucoff + 512:ucoff + 544])
                P.copy(cur[:, xoff:xoff + 512], pa[:, 0:512], eng="act")
                P.copy(cur[:, xoff + 512:xoff + 544], pb2[:, 0:32])
                st_[dname] = [cur, nxt, xoff, cur_state["Mall"][dname]]
            for m in range(10):
                d = 1 << m
                work = []
                for dname in ("f", "b"):
                    cur, nxt, xoff, Mall = st_[dname]
                    for (lo, hi) in ((0, 272), (272, 544)):
                        ps = P.ps()
                        if dname == "f":
                            s_ = max(lo, d)
                            has = s_ < hi
                            shift = (ps[:, s_ - lo:hi - lo], cur[:, xoff + s_ - d:xoff + hi - d]) if has else None
                        else:
                            e_ = min(hi, 544 - d)
                            has = lo < e_
                            shift = (ps[:, 0:e_ - lo], cur[:, xoff + lo + d:xoff + e_ + d]) if has else None
                        work.append((dname, ps, lo, hi, shift, cur, nxt, xoff, Mall))
                for (dname, ps, lo, hi, shift, cur, nxt, xoff, Mall) in work:
                    P.mm(ps[:, 0:hi - lo], identB, cur[:, xoff + lo:xoff + hi], start=True, stop=(shift is None))
                for (dname, ps, lo, hi, shift, cur, nxt, xoff, Mall) in work:
                    if shift is not None:
                        P.mm(shift[0], Mall[:, m, :], shift[1], start=False, stop=True)
                for (dname, ps, lo, hi, shift, cur, nxt, xoff, Mall) in work:
                    flip ^= 1
                    P.copy(nxt[:, xoff + lo:xoff + hi], ps[:, 0:hi - lo], eng=("act" if flip else "dve"))
                for dname in ("f", "b"):
                    st_[dname][0], st_[dname][1] = st_[dname][1], st_[dname][0]
                for _ in range(per_step):
                    if pend:
                        pend.pop(0)()
            while pend:
                pend.pop(0)()
            Xfin = {dname: st_[dname][0] for dname in ("f", "b")}
            Xf, Xb = Xfin["f"], Xfin["b"]
            py = P.ps()
            pyc = P.ps() if not last else None
            ytl = [(Sg, Uc[:, 32:544], Uc[:, 0:32]), (Wo["f"], Xf[:, 32:544], Xf[:, 0:32]), (Wo["b"], Xb[:, 1:513], Xb[:, 513:545])]
            for q, (lh, r1, r2) in enumerate(ytl):
                P.mm(py[:, 0:512], lh, r1, start=(q == 0), stop=(q == 2))
                if not last:
                    P.mm(pyc[:, 0:32], lh, r2, start=(q == 0), stop=(q == 2))
            P.copy(Yt[:, gl, 0:512], py[:, 0:512], eng="act")
            if not last:
                P.copy(Yt[:, gl, 512:544], pyc[:, 0:32])
            cur_state = nxt_state
            if gl == 7:
                yb = ybufR()
                for i in range(8):
                    ps = P.ps()
                    ps2 = P.ps() if not last else None
                    for g2 in range(8):
                        P.mm(ps[:, 0:512], SelT[:, g2 * 8 + i, :], Yt[:, g2, 0:512], start=(g2 == 0), stop=(g2 == 7))
                        if not last:
                            P.mm(ps2[:, 0:32], SelT[:, g2 * 8 + i, :], Yt[:, g2, 512:544], start=(g2 == 0), stop=(g2 == 7))
                    P.copy(yb.v(yb.ap[:, i:L:8]), ps[:, 0:512], eng=("act" if i % 2 else "dve"))
                    if not last:
                        P.copy(yb.v(yb.ap[:, L + i:L + CTX:8]), ps2[:, 0:32])
                P.dma(ysend[j], yb)
        groups = [[0, 1], [2, 3], [4, 5], [6, 7]]
        for j in range(2):
            P.add("pool", lambda e, j=j: e.collective_compute("AllGather", ALU.bypass, replica_groups=groups,
                                                              ins=[ysend[j].ap.opt()], outs=[ygath[j].ap.opt()]), [ysend[j]], [ygath[j]])
            P.add("pool", None, [ygath[j]], [])
        P.flush()
        for J in range(4):
            src = ygath[J % 2]
            r0_ = 128 * (J // 2)
            yb = ybufR()
            P.dma(yb, src.v(src.ap[r0_:r0_ + 128, :]))
            yo = yown()
            P.ts(yb[:, 0:LH], yb[:, 0:LH], G.sel[:, 0:1], None, op0=ALU.mult)
            P.stt(yo, yb[:, LH:L], G.sel[:, 1:2], yb[:, 0:LH], ALU.mult, ALU.add)
            P.dma(G.ysD.v(G.ysD.ap[J * 128:(J + 1) * 128, 0:LH]), yo)
            if not last:
                P.dma(G.ysD.v(G.ysD.ap[J * 128:(J + 1) * 128, LH:LH + CTX]), yb[:, L:L + CTX])


def emit_layer(P, I, G, last, out_own, out_ctx):
    stage_prep(P, I, G)
    with P.scope():
        alloc_persist(P, G)
        with P.scope():
            G.uT = P.sb([128, 2, 8, NK // 8], BF16, "uT")
            s5_alloc(P, I, G)
            with P.scope():
                s5_load(P, I, G)
                stage_A(P, I, G)
                s5_params(P, I, G)
            stage_S5(P, I, G, last)
        stage_B1(P, I, G, last)
        stage_B2(P, I, G, last)
        stage_B3(P, I, G, last)
    stage_B4a(P, I, G, last)
    stage_B4b(P, I, G, last, out_own, out_ctx)
    P.flush()


def build_fused():
    nc = bass.Bass("TRN2", target_bir_lowering=False)
    Cn = declare_consts(nc)
    W = [declare_weights(nc, l) for l in range(2)]
    y_out = dT(nc.dram_tensor("y_own", [LH, D], F32, kind="ExternalOutput").ap(), "y_own")
    with ExitStack() as st:
        P = Prog(nc, st)
        P.init_psum()
        NCH = 4
        CR = LH // NCH
        x1o = [P.dram([CR, D], F32, f"x1o{c}") for c in range(NCH)]
        x1g = [P.dram([2 * CR, D], F32, f"x1g{c}") for c in range(NCH)]
        ctx1 = P.dram([CTX, D], F32, "ctx1")
        own_src = RowSrc(lambda r0: x1o[r0 // CR].v(x1o[r0 // CR].ap[r0 % CR:r0 % CR + 128, :]))

        def all_fn(r0):
            half, rr = r0 // LH, r0 % LH
            c, i = rr // CR, rr % CR
            return x1g[c].v(x1g[c].ap[half * CR + i:half * CR + i + 128, :])
        with P.scope():
            G = Ctx()
            I0 = dict(Cn); I0.update(W[0])
            for k in ("x_all", "x_own", "ctx"):
                I0[k] = flat_src(Cn[k])
            emit_layer(P, I0, G, False, own_src, flat_src(ctx1))
        groups = [[0, 1], [2, 3], [4, 5], [6, 7]]
        for c in range(NCH):
            P.add("pool", lambda e, c=c: e.collective_compute("AllGather", ALU.bypass, replica_groups=groups,
                                                              ins=[x1o[c].ap.opt()], outs=[x1g[c].ap.opt()]), [x1o[c]], [x1g[c]])
            P.add("pool", None, [x1g[c]], [])
        P.flush()
        with P.scope():
            G = Ctx()
            I1 = dict(Cn); I1.update(W[1])
            I1["x_all"] = RowSrc(all_fn); I1["x_own"] = own_src; I1["ctx"] = flat_src(ctx1)
            emit_layer(P, I1, G, True, flat_src(y_out), None)
    return nc


_NC_CACHE = {}


def kernel(**inputs):
    inputs = {k: np.asarray(v) for k, v in inputs.items()}
    C = host_constants()
    if "nc" not in _NC_CACHE:
        _NC_CACHE["nc"] = build_fused()
    nc = _NC_CACHE["nc"]
    in_maps = [per_core_inputs(inputs, core, C) for core in range(8)]
    res = run_bass_kernel_spmd(nc, in_maps, core_ids=list(range(8)))
    out = np.empty((4, L, D), np.float32)
    for core in range(8):
        b, hh = core // 2, core % 2
        out[b, hh * LH:(hh + 1) * LH] = np.asarray(res.results[core]["y_own"])
    return out
```
